# Optimizing a Trainium2 kernel written in Bass

```python
import math
import jax, jax.numpy as jnp
from jax import lax
import numpy as np

D_MODEL = 1024
BATCH = 16
SEQ = 2048
DEPTH = 2

PLE_DIM = 256
CONV_CH = 512
CONV_WIDTH = 31
ATTN_HEADS = 4
ATTN_HD = 64
ATTN_VD = 2 * ATTN_HD
ATTN_QK = ATTN_HEADS * 2 * ATTN_HD
ATTN_W = ATTN_HEADS * ATTN_VD
POOL_CH = 512
POOL_WINDOWS = (2, 4, 8, 16)
POOL_GROUPS = 4
POOL_GC = POOL_CH // POOL_GROUPS
N_BRANCH = 3
D_FF = 4 * D_MODEL
REL_BUCKETS = 32
REL_MAX_DIST = 128
Q_BLOCK = 128
EPS = 1e-6
IN_SPLITS = (CONV_CH, CONV_CH, ATTN_QK, ATTN_QK, ATTN_W, POOL_CH, N_BRANCH * D_MODEL)
D_IN = CONV_CH * 2 + ATTN_QK * 2 + ATTN_W + POOL_CH + N_BRANCH * D_MODEL

kernel_name = "hybrid_gated_conv_diffattn_pool_block"


def rmsnorm(x, g):
    xf = x.astype(jnp.float32)
    y = xf * lax.rsqrt(jnp.mean(xf * xf, axis=-1, keepdims=True) + EPS) * g
    return y.astype(x.dtype)


def layernorm(x, g, b):
    xf = x.astype(jnp.float32)
    mu = jnp.mean(xf, axis=-1, keepdims=True)
    xc = xf - mu
    var = jnp.mean(xc * xc, axis=-1, keepdims=True)
    return (xc * lax.rsqrt(var + EPS) * g + b).astype(x.dtype)


def split_cols(t):
    out, off = [], 0
    for w in IN_SPLITS:
        out.append(t[..., off:off + w])
        off += w
    return out


def rel_bucket(dist):
    n = jnp.maximum(dist, 0)
    max_exact = REL_BUCKETS // 2
    nf = jnp.maximum(n, 1).astype(jnp.float32)
    large = max_exact + (jnp.log(nf / max_exact) / math.log(REL_MAX_DIST / max_exact)
                         * (REL_BUCKETS - max_exact)).astype(jnp.int32)
    large = jnp.minimum(large, REL_BUCKETS - 1)
    return jnp.where(n < max_exact, n, large)


def conv_module(a, b, w_dw, b_dw, ln_g, ln_b, w_pw):
    u = a * jax.nn.sigmoid(b)
    u = lax.conv_general_dilated(u, w_dw, window_strides=(1,),
                                 padding=[(CONV_WIDTH - 1, 0)],
                                 dimension_numbers=("NWC", "WIO", "NWC"),
                                 feature_group_count=CONV_CH) + b_dw
    u = jax.nn.silu(layernorm(u, ln_g, ln_b))
    return u @ w_pw


def diff_attention(q, k, v, rel_bias, lam_p, subln_g, lambda_init):
    B, S = q.shape[0], q.shape[1]
    lp = lam_p.astype(jnp.float32)
    lam = jnp.exp(jnp.sum(lp[0] * lp[1])) - jnp.exp(jnp.sum(lp[2] * lp[3])) + lambda_init
    q = q * (ATTN_HD ** -0.5)
    outs = []
    for start in range(0, S, Q_BLOCK):
        end = start + Q_BLOCK
        qb, kb, vb = q[:, start:end], k[:, :end], v[:, :end]
        logits = jnp.einsum("bqhmd,bkhmd->bhmqk", qb, kb).astype(jnp.float32)
        dist = jnp.arange(start, end)[:, None] - jnp.arange(end)[None, :]
        bias = rel_bias[rel_bucket(dist)].astype(jnp.float32)
        bias = jnp.transpose(bias, (2, 0, 1))[None, :, None]
        logits = jnp.where(dist >= 0, logits + bias, -1e30)
        probs = jax.nn.softmax(logits, axis=-1)
        attn = probs[:, :, 0] - lam * probs[:, :, 1]
        outs.append(jnp.einsum("bhqk,bkhe->bqhe", attn.astype(vb.dtype), vb))
    o = jnp.concatenate(outs, axis=1)
    o = rmsnorm(o, subln_g) * (1.0 - lambda_init)
    return o.reshape(B, S, ATTN_W)


def pool_mixer(u, w_group, scale, w_proj):
    B, S, _ = u.shape
    uf = u.astype(jnp.float32).reshape(B, S, POOL_GROUPS, POOL_GC)
    c = jnp.pad(jnp.cumsum(uf, axis=1), ((0, 0), (1, 0), (0, 0), (0, 0)))
    t = jnp.arange(S)
    groups = []
    for g, w in enumerate(POOL_WINDOWS):
        lo = jnp.maximum(t + 1 - w, 0)
        cnt = jnp.minimum(t + 1, w).astype(jnp.float32)[None, :, None]
        mean = (c[:, 1:, g] - c[:, lo, g]) / cnt
        groups.append(mean - uf[:, :, g])
    pooled = jnp.stack(groups, axis=2)
    y = jnp.einsum("bsgc,gcd->bsgd", pooled, w_group.astype(jnp.float32))
    y = (y.reshape(B, S, POOL_CH) * scale).astype(u.dtype)
    return y @ w_proj


def setup_inputs(seed: int = 0) -> dict:
    key = jax.random.key(seed)
    ks = jax.random.split(key, 32)
    f32 = jnp.float32

    def nrm(k, shape, fan_in):
        return jax.random.normal(k, shape, f32) * (fan_in ** -0.5)

    def gain(k, shape):
        return 1.0 + 0.05 * jax.random.normal(k, shape, f32)

    def small(k, shape):
        return 0.02 * jax.random.normal(k, shape, f32)

    L = DEPTH
    return {
        "x": jax.random.normal(ks[0], (BATCH, SEQ, D_MODEL), f32),
        "p": jax.random.normal(ks[1], (DEPTH, BATCH, SEQ, PLE_DIM), f32),
        "rel_bias": 0.5 * jax.random.normal(ks[2], (REL_BUCKETS, ATTN_HEADS), f32),
        "g_pre_mix": gain(ks[3], (L, D_MODEL)),
        "w_in": nrm(ks[4], (L, D_MODEL, D_IN), D_MODEL),
        "conv_dw_w": nrm(ks[5], (L, CONV_WIDTH, 1, CONV_CH), CONV_WIDTH),
        "conv_dw_b": small(ks[6], (L, CONV_CH)),
        "conv_ln_g": gain(ks[7], (L, CONV_CH)),
        "conv_ln_b": small(ks[8], (L, CONV_CH)),
        "w_conv_out": nrm(ks[9], (L, CONV_CH, D_MODEL), CONV_CH),
        "lam_p": 0.1 * jax.random.normal(ks[10], (L, 4, ATTN_HD), f32),
        "subln_g": gain(ks[11], (L, ATTN_VD)),
        "w_attn_out": nrm(ks[12], (L, ATTN_W, D_MODEL), ATTN_W),
        "pool_w": nrm(ks[13], (L, POOL_GROUPS, POOL_GC, POOL_GC), POOL_GC),
        "pool_scale": gain(ks[14], (L, POOL_CH)),
        "w_pool_out": nrm(ks[15], (L, POOL_CH, D_MODEL), POOL_CH),
        "w_out": nrm(ks[16], (L, D_MODEL, D_MODEL), D_MODEL),
        "g_post_mix": gain(ks[17], (L, D_MODEL)),
        "g_pre_mlp": gain(ks[18], (L, D_MODEL)),
        "w_mlp_in": nrm(ks[19], (L, D_MODEL, D_FF), D_MODEL),
        "w_mlp_out": nrm(ks[20], (L, D_FF, D_MODEL), D_FF),
        "g_post_mlp": gain(ks[21], (L, D_MODEL)),
        "w_ple_proj": nrm(ks[22], (L, PLE_DIM, D_MODEL), PLE_DIM),
        "w_ple_gate": nrm(ks[23], (L, D_MODEL, D_MODEL), D_MODEL),
    }


def reference(x, p, rel_bias, g_pre_mix, w_in, conv_dw_w, conv_dw_b, conv_ln_g, conv_ln_b,
              w_conv_out, lam_p, subln_g, w_attn_out, pool_w, pool_scale, w_pool_out,
              w_out, g_post_mix, g_pre_mlp, w_mlp_in, w_mlp_out, g_post_mlp,
              w_ple_proj, w_ple_gate):
    B, S, D = x.shape
    for i in range(DEPTH):
        lambda_init = 0.8 - 0.6 * math.exp(-0.3 * i)
        h = rmsnorm(x, g_pre_mix[i])
        ca, cb, q, k, v, pu, gl = split_cols(h @ w_in[i])
        y_conv = conv_module(ca, cb, conv_dw_w[i], conv_dw_b[i], conv_ln_g[i], conv_ln_b[i],
                             w_conv_out[i])
        o = diff_attention(q.reshape(B, S, ATTN_HEADS, 2, ATTN_HD),
                           k.reshape(B, S, ATTN_HEADS, 2, ATTN_HD),
                           v.reshape(B, S, ATTN_HEADS, ATTN_VD),
                           rel_bias, lam_p[i], subln_g[i], lambda_init)
        y_attn = o @ w_attn_out[i]
        y_pool = pool_mixer(pu, pool_w[i], pool_scale[i], w_pool_out[i])
        gates = jax.nn.sigmoid(gl).reshape(B, S, N_BRANCH, D)
        merged = gates[:, :, 0] * y_conv + gates[:, :, 1] * y_attn + gates[:, :, 2] * y_pool
        x = x + rmsnorm(merged @ w_out[i], g_post_mix[i])
        h = rmsnorm(x, g_pre_mlp[i])
        f = jnp.square(jax.nn.relu(h @ w_mlp_in[i])) @ w_mlp_out[i]
        x = x + rmsnorm(f, g_post_mlp[i])
        x = x + jax.nn.sigmoid(x @ w_ple_gate[i]) * (p[i] @ w_ple_proj[i])
    return x
```

```python
import math
from contextlib import ExitStack

import numpy as np
import concourse.bass as bass
import concourse.mybir as mybir
from concourse.bass_utils import run_bass_kernel_spmd

F32 = mybir.dt.float32
BF16 = mybir.dt.bfloat16
ALU = mybir.AluOpType
AF = mybir.ActivationFunctionType
AX = mybir.AxisListType

DEPTH = 2
NSEQ = 2
S = 2048
D = 1024
EPS = 1e-6
LAMBDA_INIT = [0.8 - 0.6 * math.exp(-0.3 * i) for i in range(DEPTH)]

COMPUTE = ("pe", "act", "dve", "pool")
STREAMS = ("pe", "act", "dve", "pool", "sp")


class Buf:
    __slots__ = ("name", "lw", "rd")

    def __init__(self, name):
        self.name = name
        self.lw = None
        self.rd = []


class Op:
    __slots__ = ("stream", "fn", "waits", "key", "idx", "signal", "is_dma", "cnt")


class Prog:
    def __init__(self, nc):
        self.nc = nc
        self.ops = {s: [] for s in STREAMS}
        self.cnt = {}
        self.vc = {s: {} for s in STREAMS}
        self.opvc = {}
        self.bykey = {}
        self.dma_keys = []
        self.pend = {s: {} for s in STREAMS}

    def barrier(self):
        snap = dict(self.cnt)
        for s in STREAMS:
            for k, i in snap.items():
                if self.pend[s].get(k, 0) < i:
                    self.pend[s][k] = i

    def _record(self, stream, key, fn, reads, writes, is_dma):
        deps = set()
        for b in reads:
            if b.lw is not None:
                deps.add(b.lw)
        for b in writes:
            if b.lw is not None:
                deps.add(b.lw)
            for r in b.rd:
                deps.add(r)
        for k, i in self.pend[stream].items():
            deps.add((k, i))
        self.pend[stream] = {}
        idx = self.cnt.get(key, 0) + 1
        self.cnt[key] = idx
        me = (key, idx)
        vc = self.vc[stream]
        waits = {}
        for (k, i) in deps:
            if k == stream and not is_dma:
                continue
            if vc.get(k, 0) >= i:
                continue
            if waits.get(k, 0) < i:
                waits[k] = i
        if stream in ("act", "dve", "pool") and not is_dma:
            for b in reads:
                if b.lw is not None and b.lw[0] == stream and vc.get(("self", stream), 0) < b.lw[1]:
                    if waits.get(stream, 0) < b.lw[1]:
                        waits[stream] = b.lw[1]
        for k, i in waits.items():
            if k == stream:
                vc[("self", stream)] = max(vc.get(("self", stream), 0), i)
                continue
            dvc = self.opvc.get((k, i))
            if dvc:
                for kk, ii in dvc.items():
                    if vc.get(kk, 0) < ii:
                        vc[kk] = ii
            if vc.get(k, 0) < i:
                vc[k] = i
        op = Op()
        op.stream, op.fn, op.waits, op.key, op.idx = stream, fn, waits, key, idx
        op.signal, op.is_dma, op.cnt = is_dma, is_dma, 0
        self.ops[stream].append(op)
        self.bykey[me] = op
        snap = {k: v for k, v in vc.items() if not isinstance(k, tuple) and k != stream}
        if not is_dma:
            snap[stream] = idx
        self.opvc[me] = snap
        for b in reads:
            b.rd.append(me)
        for b in writes:
            b.lw = me
            b.rd = []
        return op

    def op(self, stream, method, *args, reads=(), writes=(), **kw):
        return self._record(stream, stream, (method, args, kw), list(reads), list(writes), False)

    def dma(self, queue, semkey, out, in_, reads=(), writes=()):
        if semkey not in self.dma_keys:
            self.dma_keys.append(semkey)
        return self._record(queue, semkey, ("dma_start", (), {"out": out, "in_": in_}), list(reads), list(writes), True)

    def emit(self):
        nc = self.nc
        for s in STREAMS:
            for op in self.ops[s]:
                for k, i in op.waits.items():
                    if k in COMPUTE:
                        self.bykey[(k, i)].signal = True
        finals = {}
        for k in COMPUTE:
            if self.cnt.get(k, 0):
                self.bykey[(k, self.cnt[k])].signal = True
                finals[k] = self.cnt[k]
        for k in self.dma_keys:
            finals[k] = self.cnt[k]
        for k in COMPUTE:
            c = 0
            for i in range(1, self.cnt.get(k, 0) + 1):
                op = self.bykey[(k, i)]
                if op.signal:
                    c += 1
                op.cnt = c
        with ExitStack() as es:
            sems = {}
            for k in COMPUTE:
                if self.cnt.get(k, 0):
                    sems[k] = es.enter_context(nc.semaphore("s_" + k))
            for k in self.dma_keys:
                sems[k] = es.enter_context(nc.semaphore("d_" + str(k)))
            block = es.enter_context(nc.Block())

            def val(k, i):
                if k in COMPUTE:
                    return self.bykey[(k, i)].cnt
                return 16 * i

            def run(stream, eng):
                for op in self.ops[stream]:
                    for k, i in op.waits.items():
                        eng.wait_ge(sems[k], val(k, i))
                    meth, args, kw = op.fn
                    ins = getattr(eng, meth)(*args, **kw)
                    if op.is_dma:
                        ins.then_inc(sems[op.key], 16)
                    elif op.signal:
                        ins.then_inc(sems[op.key], 1)
                if stream == "sp":
                    for k, i in finals.items():
                        eng.wait_ge(sems[k], val(k, i))

            @block.tensor
            def _(e):
                run("pe", e)

            @block.scalar
            def _(e):
                run("act", e)

            @block.vector
            def _(e):
                run("dve", e)

            @block.gpsimd
            def _(e):
                run("pool", e)

            @block.sync
            def _(e):
                run("sp", e)
        return {s: len(self.ops[s]) for s in STREAMS}


def build_program(layers=(0, 1), nseq=NSEQ, halves=(0, 1)):
    nc = bass.Bass("TRN2", target_bir_lowering=False)
    P = Prog(nc)

    def dram(name, shape, kind="ExternalInput"):
        return nc.dram_tensor(name, list(shape), F32, kind=kind).ap()

    d_xT = dram("xT", [NSEQ, D, S])
    d_pT = dram("pT", [DEPTH, NSEQ, 256, S])
    d_out = dram("outT", [NSEQ, D, S], kind="ExternalOutput")
    d_win = dram("win", [DEPTH, 6, 128, 4096])
    d_mrgg = dram("mrgg", [DEPTH, 8, 128, 3072])
    d_mrgb = dram("mrgb", [DEPTH, 8, 128, 1536])
    d_wout = dram("wout", [DEPTH, 2, 128, 4096])
    d_w1 = dram("w1", [DEPTH, 8, 128, 4096])
    d_w2 = dram("w2", [DEPTH, 8, 128, 4096])
    d_wpg = dram("wpg", [DEPTH, 2, 128, 4096])
    d_wpe = dram("wpe", [DEPTH, 128, 2048])
    d_poolw = dram("poolw", [DEPTH, 128, 512])
    d_vecs = dram("vecs", [DEPTH, 128, 172])
    d_subln = dram("sublnb", [DEPTH, 128, 128])
    d_lam = dram("lamb", [DEPTH, 128, 256])
    d_bias = dram("biasT", [128, 4 * 768])
    d_ident = dram("ident", [128, 128])
    d_rc = dram("rc", [128, 16])

    es = ExitStack()
    with es:
        def sb(name, shape, dt):
            return es.enter_context(nc.sbuf_tensor(name, list(shape), dt))

        xT = sb("xT_s", [128, 8, 1024], F32)
        kvK = [sb(f"kvK{i}", [128, 4, 1024], BF16) for i in range(2)]
        kvV = [sb(f"kvV{i}", [128, 8, 4, 129], BF16) for i in range(2)]
        NRR = 39072
        RR = sb("RR", [128, NRR], BF16)
        S2 = [sb(f"S2_{i}", [128, 528], F32) for i in range(3)]
        sq = [sb(f"sq{i}", [128, 512], BF16) for i in range(2)]
        rstd = [sb(f"rstd{i}", [128, 512], F32) for i in range(2)]
        sg = [sb(f"sg{i}", [128, 512], F32) for i in range(2)]
        tmp = [sb(f"tmp{i}", [128, 512], F32) for i in range(2)]
        ring = [sb(f"ring{i}", [128, 4096], BF16) for i in range(3)]
        biasT = sb("biasT_s", [128, 4, 768], BF16)
        ident = sb("ident_s", [128, 128], BF16)
        onesD = sb("onesD", [128, 128], BF16)
        onesC = sb("onesC", [128, 128], BF16)
        vecs = sb("vecs_s", [128, DEPTH, 172], F32)
        vec2 = sb("vec2_s", [128, DEPTH, 12], F32)
        subln = sb("subln_s", [128, DEPTH, 128], F32)
        nlam = sb("nlam", [128, DEPTH], F32)
        rc = sb("rc_s", [128, 16], F32)
        nhalf = sb("nhalf", [128, 512], F32)
        poolw = sb("poolw_s", [128, 4, 128], BF16)
        uhalo = sb("uhalo", [128, DEPTH, 4, 30], BF16)
        puhalo = sb("puhalo", [128, DEPTH, 4, 16], F32)
        att_f = [sb(f"attf{i}", [128, 128], F32) for i in range(4)]
        att_b = [sb(f"attb{i}", [128, 128], BF16) for i in range(2)]
        att_s = [sb(f"atts{i}", [128, 8], F32) for i in range(2)]
        ps = [es.enter_context(nc.psum_tensor(f"ps{i}", [128, 512], F32)) for i in range(8)]

        b_ps = [Buf(f"ps{i}") for i in range(8)]
        b_ring = [Buf(f"ring{i}") for i in range(3)]
        b_x = [[Buf(f"x{k}_{t}") for t in range(2)] for k in range(8)]
        b_sq = [Buf("sq0"), Buf("sq1")]
        b_rstd = [Buf("rstd0"), Buf("rstd1")]
        b_sg = [Buf("sg0"), Buf("sg1")]
        b_tmp = [Buf("tmp0"), Buf("tmp1")]
        b_S2 = [Buf(f"S2_{i}") for i in range(3)]
        b_const = Buf("const")
        b_kvK = [[Buf(f"kvK{i}_{t}") for t in range(2)] for i in range(2)]
        b_kvV = [[Buf(f"kvV{i}_{tb}") for tb in range(8)] for i in range(2)]
        b_uh = [Buf(f"uh{l}") for l in range(DEPTH)]
        b_ph = [[Buf(f"ph{l}_{g}") for g in range(4)] for l in range(DEPTH)]
        b_attf = [Buf(f"attf{i}") for i in range(4)]
        b_attb = [Buf(f"attb{i}") for i in range(2)]
        b_atts = [Buf(f"atts{i}") for i in range(2)]
        b_poolw = Buf("poolw")

        def rr3(off, a, b):
            return RR[:, off:off + a * b].rearrange("p (a b) -> p a b", a=a)

        def rrf(off_bf16, a, b):
            v = RR[:, off_bf16:off_bf16 + 2 * a * b].bitcast(F32)
            return v.rearrange("p (a b) -> p a b", a=a)

        curK = rr3(0, 4, 1024)
        curV = RR[:, 4096:4096 + 4128].rearrange("p (a h e) -> p a h e", a=8, h=4)
        hT = rr3(8224, 8, 1024)
        uT = rr3(16416, 4, 1054)
        QT = rr3(20632, 4, 1024)
        p2T = rr3(24728, 4, 1024)
        cT = rr3(28824, 4, 1024)
        OT = rr3(32920, 4, 1024)
        stash = rrf(32920, 4, 512)
        ET = rr3(37016, 4, 512)
        mergedT = rr3(0, 8, 1024)
        yevE = rrf(16416, 8, 512)
        h2T = rr3(0, 8, 512)
        f1 = rr3(4096, 32, 512)
        relu_s = rrf(20480, 2, 512)
        yevF = rrf(22528, 8, 512)
        xbT = rr3(0, 8, 1024)
        pTs = rr3(8192, 2, 1024)
        b_curK = [Buf("curK0"), Buf("curK1")]
        b_curV = [Buf(f"curV{i}") for i in range(8)]
        b_hT = [[Buf(f"hT{k}_{t}") for t in range(2)] for k in range(8)]
        b_uT = [[Buf(f"uT{c}_{t}") for t in range(2)] for c in range(4)]
        b_QT = [[Buf(f"QT{c}_{t}") for t in range(2)] for c in range(4)]
        b_p2T = [[Buf(f"p2T{c}_{t}") for t in range(2)] for c in range(4)]
        b_cT = [[Buf(f"cT{c}_{t}") for t in range(2)] for c in range(4)]
        b_OT = [[Buf(f"OT{c}_{t}") for t in range(2)] for c in range(4)]
        b_ET = [Buf(f"ET{i}") for i in range(4)]
        b_mg = [[Buf(f"mg{k}_{t}") for t in range(2)] for k in range(8)]
        b_yev = [Buf(f"yev{m}") for m in range(8)]
        b_h2 = [Buf(f"h2_{k}") for k in range(8)]
        b_f1 = [Buf(f"f1_{j}") for j in range(32)]
        b_relu = [Buf("relu0"), Buf("relu1")]
        b_xb = [[Buf(f"xb{k}_{t}") for t in range(2)] for k in range(8)]
        b_pTs = Buf("pTs")

        ring_n = [0]

        def load_pack(src_ap, nelem):
            s = ring_n[0] % 3
            ring_n[0] += 1
            P.dma("pool", f"ring{s}", ring[s][:, 0:nelem], src_ap, writes=[b_ring[s]])
            return ring[s], b_ring[s]

        gen_n = [0]

        def gen_bank():
            b = gen_n[0] % 6
            gen_n[0] += 1
            return b

        ev_n = [0]

        def evac(out, in_, reads, writes):
            ev_n[0] += 1
            if ev_n[0] % 2:
                P.op("act", "activation", out=out, in_=in_, func=AF.Copy, reads=reads, writes=writes)
            else:
                P.op("dve", "tensor_copy", out, in_, reads=reads, writes=writes)

        def mm(out, lhsT, rhs, start, stop, reads, writes):
            P.op("pe", "matmul", out, lhsT, rhs, start=start, stop=stop, reads=reads, writes=writes)

        def tt(eng, out, in0, in1, op, reads, writes):
            P.op(eng, "tensor_tensor", out, in0, in1, op, reads=reads, writes=writes)

        def ts(eng, out, in0, s1, s2, op0, op1, reads, writes):
            if s2 is None:
                P.op(eng, "tensor_scalar", out, in0, s1, None, op0, reads=reads, writes=writes)
            else:
                P.op(eng, "tensor_scalar", out, in0, s1, s2, op0, op1, reads=reads, writes=writes)

        def stt(eng, out, in0, scalar, in1, op0, op1, reads, writes):
            P.op(eng, "scalar_tensor_tensor", out=out, in0=in0, scalar=scalar, in1=in1, op0=op0, op1=op1,
                 reads=reads, writes=writes)

        def act(out, in_, func, reads, writes, **kw):
            P.op("act", "activation", out=out, in_=in_, func=func, reads=reads, writes=writes, **kw)

        b_lamt, b_lt, b_nl, b_v2 = Buf("lamt"), Buf("lt"), Buf("nl"), Buf("v2")
        lamt = RR[:, 0:1024].bitcast(F32).rearrange("p (l c) -> p l c", l=DEPTH)
        lt = RR[:, 2048:2048 + 512].bitcast(F32)
        P.dma("sp", "c0", vecs[:], d_vecs.rearrange("l p c -> p l c"), writes=[b_const])
        P.dma("sp", "c0", subln[:], d_subln.rearrange("l p c -> p l c"), writes=[Buf("x1")])
        P.dma("sp", "c0", rc[:], d_rc, writes=[Buf("x2")])
        P.dma("sp", "c0", lamt, d_lam.rearrange("l p c -> p l c"), writes=[b_lamt])
        P.dma("pool", "c1", biasT[:], d_bias.rearrange("p (h c) -> p h c", h=4), writes=[Buf("x3")])
        P.dma("pool", "c1", ident[:], d_ident, writes=[Buf("x4")])
        P.barrier()
        P.op("dve", "memset", onesD[:], 1.0 / 1024.0, writes=[b_const])
        P.op("dve", "memset", nhalf[:], -0.5, writes=[b_const])
        P.op("dve", "memset", onesC[:], 1.0 / 512.0, writes=[b_const])
        P.op("dve", "memset", uhalo[:], 0.0, writes=[b_const])
        P.op("dve", "memset", puhalo[:], 0.0, writes=[b_const])
        for i in range(2):
            P.op("dve", "memset", kvV[i][:, :, :, 128:129], 1.0, writes=[b_const])
        ssum = att_s[0]
        for l in range(DEPTH):
            ts("dve", vec2[:, l, 0:4], vecs[:, l, 32:36], 2.0, None, ALU.mult, None, [b_const], [b_v2])
            ts("dve", vec2[:, l, 4:12], vecs[:, l, 36:44], 0.5, None, ALU.mult, None, [b_const], [b_v2])
            ts("dve", subln[:, l, :], subln[:, l, :], 1.0 - LAMBDA_INIT[l], None, ALU.mult, None, [b_const], [b_v2])
            for j in range(2):
                tt("dve", lt[:, 64 * j:64 * j + 64], lamt[:, l, 128 * j:128 * j + 64],
                   lamt[:, l, 128 * j + 64:128 * j + 128], ALU.mult, [b_lamt], [b_lt])
                P.op("dve", "reduce_sum", ssum[:, 2 * l + j:2 * l + j + 1], lt[:, 64 * j:64 * j + 64], AX.X,
                     reads=[b_lt], writes=[b_atts[0]])
        act(ssum[:, 4:8], ssum[:, 0:4], AF.Exp, [b_atts[0]], [b_atts[0]])
        for l in range(DEPTH):
            tt("dve", nlam[:, l:l + 1], ssum[:, 5 + 2 * l:6 + 2 * l], ssum[:, 4 + 2 * l:5 + 2 * l], ALU.subtract,
               [b_atts[0]], [b_nl])
            ts("dve", nlam[:, l:l + 1], nlam[:, l:l + 1], -LAMBDA_INIT[l], None, ALU.add, None, [b_nl], [b_nl])
        P.barrier()

        def norm_rstd(srcs, src_bufs, eps, r):
            bank = 6 + (r % 2)
            for k in range(8):
                q = k % 2
                act(sq[q][:], srcs[k], AF.Square, [src_bufs[k]], [b_sq[q]])
                mm(ps[bank][:], onesD[:], sq[q][:], k == 0, k == 7, [b_sq[q]], [b_ps[bank]])
            ts("dve", rstd[r][:], ps[bank][:], eps, None, ALU.add, None, [b_ps[bank]], [b_rstd[r]])
            tt("pool", rstd[r][:], rstd[r][:], nhalf[:], ALU.pow, [b_rstd[r]], [b_rstd[r]])

        def post_norm_residual(yev, t, gcol, l, eps, r):
            norm_rstd([yev[:, m, :] for m in range(8)], b_yev, eps, r)
            for m in range(8):
                q = m % 2
                stt("dve", tmp[q][:], yev[:, m, :], vecs[:, l, gcol + m:gcol + m + 1], rstd[r][:], ALU.mult, ALU.mult,
                    [b_yev[m], b_rstd[r]], [b_tmp[q]])
                xs = xT[:, m, t * 512:(t + 1) * 512]
                tt("pool", xs, xs, tmp[q][:], ALU.add, [b_tmp[q]], [b_x[m][t]])

        def pre_norm(dst, dst_bufs, t, gcol, l, r, tok0):
            norm_rstd([xT[:, k, t * 512:(t + 1) * 512] for k in range(8)], [b_x[k][t] for k in range(8)], EPS, r)
            for k in range(8):
                stt("dve", dst[:, k, tok0:tok0 + 512], xT[:, k, t * 512:(t + 1) * 512],
                    vecs[:, l, gcol + k:gcol + k + 1], rstd[r][:], ALU.mult, ALU.mult,
                    [b_x[k][t], b_rstd[r]], [dst_bufs[k]])

        for sq_i in range(nseq):
            for half in halves:
                tok_base = half * 1024
                for k in range(8):
                    P.dma("sp", "xin", xT[:, k, :], d_xT[sq_i, k * 128:(k + 1) * 128, tok_base:tok_base + 1024],
                          writes=[b_x[k][0], b_x[k][1]])
                P.barrier()
                for l in layers:
                    if half == 0:
                        Kd, Vd, bKd, bVd = kvK[l], kvV[l], b_kvK[l], b_kvV[l]
                    else:
                        Kd, Vd, bKd, bVd = curK, curV, b_curK, b_curV
                        P.op("dve", "memset", curV[:, :, :, 128:129], 1.0, writes=b_curV)
                    for t in range(2):
                        pre_norm(hT, [b_hT[k][t] for k in range(8)], t, 0, l, t, t * 512)
                    P.dma("pool", "pw", poolw[:], d_poolw[l].rearrange("p (g c) -> p g c", g=4), writes=[b_poolw])
                    wa, bwa = load_pack(d_win[l, 0], 4096)
                    wb, bwb = load_pack(d_win[l, 1], 4096)
                    wa3 = wa[:, :].rearrange("p (k c) -> p k c", k=8)
                    wb3 = wb[:, :].rearrange("p (k c) -> p k c", k=8)
                    for c in range(4):
                        if half == 0:
                            P.op("pool", "memset", uT[:, c, 0:30], 0.0, writes=[b_uT[c][0]])
                        else:
                            P.op("pool", "tensor_copy", uT[:, c, 0:30], uhalo[:, l, c, :], reads=[b_uh[l]], writes=[b_uT[c][0]])
                    for c in range(4):
                        for t in range(2):
                            ba, bb = gen_bank(), gen_bank()
                            for k in range(8):
                                mm(ps[ba][:], wa3[:, k, c * 128:(c + 1) * 128], hT[:, k, t * 512:(t + 1) * 512],
                                   k == 0, k == 7, [bwa, b_hT[k][t]], [b_ps[ba]])
                            for k in range(8):
                                mm(ps[bb][:], wb3[:, k, c * 128:(c + 1) * 128], hT[:, k, t * 512:(t + 1) * 512],
                                   k == 0, k == 7, [bwb, b_hT[k][t]], [b_ps[bb]])
                            q = t % 2
                            act(sg[q][:], ps[bb][:], AF.Tanh, [b_ps[bb]], [b_sg[q]], scale=0.5)
                            stt("dve", uT[:, c, 30 + t * 512:30 + (t + 1) * 512], sg[q][:], 1.0, ps[ba][:], ALU.add, ALU.mult,
                                [b_sg[q], b_ps[ba]], [b_uT[c][t]])
                    if half == 0:
                        for c in range(4):
                            P.op("pool", "tensor_copy", uhalo[:, l, c, :], uT[:, c, 1024:1054],
                                 reads=[b_uT[c][1]], writes=[b_uh[l]])
                    for which in (2, 3):
                        w_, bw_ = load_pack(d_win[l, which], 4096)
                        w3 = w_[:, :].rearrange("p (k c) -> p k c", k=8)
                        for c in range(4):
                            for t in range(2):
                                b = gen_bank()
                                for k in range(8):
                                    mm(ps[b][:], w3[:, k, c * 128:(c + 1) * 128], hT[:, k, t * 512:(t + 1) * 512],
                                       k == 0, k == 7, [bw_, b_hT[k][t]], [b_ps[b]])
                                if which == 2:
                                    act(QT[:, c, t * 512:(t + 1) * 512], ps[b][:], AF.Copy, [b_ps[b]], [b_QT[c][t]], scale=0.125)
                                else:
                                    P.op("dve", "tensor_copy", Kd[:, c, t * 512:(t + 1) * 512], ps[b][:],
                                         reads=[b_ps[b]], writes=[bKd[t]])
                    w_, bw_ = load_pack(d_win[l, 4], 4096)
                    w3 = w_[:, :].rearrange("p (k c) -> p k c", k=8)
                    for tb in range(8):
                        b = gen_bank()
                        for k in range(8):
                            mm(ps[b][:], hT[:, k, tb * 128:(tb + 1) * 128], w3[:, k, :],
                               k == 0, k == 7, [bw_, b_hT[k][tb // 4]], [b_ps[b]])
                        evac(Vd[:, tb, :, 0:128], ps[b][:].rearrange("p (h e) -> p h e", h=4), [b_ps[b]], [bVd[tb]])
                    w_, bw_ = load_pack(d_win[l, 5], 4096)
                    w3 = w_[:, :].rearrange("p (k c) -> p k c", k=8)
                    B3 = [(S2[0], b_S2[0]), (S2[1], b_S2[1]), (S2[2], b_S2[2])]
                    pu, bpu = B3[0]
                    for g in range(4):
                        win_w = 2 ** (g + 1)
                        for t in range(2):
                            b = gen_bank()
                            for k in range(8):
                                mm(ps[b][:], w3[:, k, g * 128:(g + 1) * 128], hT[:, k, t * 512:(t + 1) * 512],
                                   k == 0, k == 7, [bw_, b_hT[k][t]], [b_ps[b]])
                            if half == 0 and t == 0:
                                P.op("dve", "memset", pu[:, 0:16], 0.0, writes=[bpu])
                            else:
                                P.op("dve", "tensor_copy", pu[:, 0:16], puhalo[:, l, g, :], reads=[b_ph[l][g]], writes=[bpu])
                            act(pu[:, 16:528], ps[b][:], AF.Copy, [b_ps[b]], [bpu])
                            P.op("pool", "tensor_copy", puhalo[:, l, g, :], pu[:, 512:528], reads=[bpu], writes=[b_ph[l][g]])
                            si, sh, lo = 0, 1, 1
                            for st in range(g + 1):
                                di = 1 if si != 1 else 2
                                lo += sh
                                sv, sbf = B3[si]
                                dv, dbf = B3[di]
                                tt("dve", dv[:, lo:528], sv[:, lo:528], sv[:, lo - sh:528 - sh], ALU.add, [sbf], [dbf])
                                si, sh = di, sh * 2
                            av, abf = B3[si]
                            fi = 2 if si == 1 else 1
                            fv, fbf = B3[fi]
                            pooled = fv[:, 0:256].bitcast(BF16)
                            stt("dve", pooled, av[:, 16:528], 1.0 / win_w, pu[:, 16:528], ALU.mult, ALU.subtract,
                                [abf, bpu], [fbf])
                            if half == 0 and t == 0:
                                nfix = win_w - 1
                                tt("dve", tmp[0][:, 0:nfix], av[:, 16:16 + nfix], rc[:, 0:nfix], ALU.mult, [abf], [b_tmp[0]])
                                tt("dve", pooled[:, 0:nfix], tmp[0][:, 0:nfix], pu[:, 16:16 + nfix], ALU.subtract,
                                   [b_tmp[0], bpu], [fbf])
                            b2 = gen_bank()
                            mm(ps[b2][:], poolw[:, g, :], pooled, True, True, [b_poolw, fbf], [b_ps[b2]])
                            act(p2T[:, g, t * 512:(t + 1) * 512], ps[b2][:], AF.Copy, [b_ps[b2]], [b_p2T[g][t]],
                                scale=vecs[:, l, 44 + g:45 + g])
                    for t in range(2):
                        base = t * 512
                        for c in range(4):
                            acc, bacc = S2[c % 2], b_S2[c % 2]
                            av = acc[:, 0:512]
                            ts("dve", av, uT[:, c, base:base + 512], vecs[:, l, 48 + c * 31:49 + c * 31], vec2[:, l, c:c + 1],
                               ALU.mult, ALU.add, [b_uT[c][0], b_uT[c][1]], [bacc])
                            for j in range(1, 31):
                                dst = av if j < 30 else stash[:, c, :]
                                wr = [bacc] if j < 30 else [b_OT[c][0], b_OT[c][1]]
                                stt("dve", dst, uT[:, c, base + j:base + j + 512],
                                    vecs[:, l, 48 + c * 31 + j:49 + c * 31 + j], av, ALU.mult, ALU.add,
                                    [b_uT[c][0], b_uT[c][1], bacc], wr)
                        for c in range(4):
                            q = c % 2
                            act(sq[q][:], stash[:, c, :], AF.Copy, [b_OT[c][0]], [b_sq[q]])
                            mm(ps[6][:], onesC[:], sq[q][:], c == 0, c == 3, [b_sq[q]], [b_ps[6]])
                        for c in range(4):
                            q = c % 2
                            act(sq[q][:], stash[:, c, :], AF.Square, [b_OT[c][0]], [b_sq[q]])
                            mm(ps[7][:], onesC[:], sq[q][:], c == 0, c == 3, [b_sq[q]], [b_ps[7]])
                        P.op("dve", "tensor_copy", rstd[1][:], ps[6][:], reads=[b_ps[6]], writes=[b_rstd[1]])
                        tt("dve", rstd[0][:], rstd[1][:], rstd[1][:], ALU.mult, [b_rstd[1]], [b_rstd[0]])
                        tt("dve", rstd[0][:], ps[7][:], rstd[0][:], ALU.subtract, [b_ps[7], b_rstd[0]], [b_rstd[0]])
                        ts("dve", rstd[0][:], rstd[0][:], 4.0 * EPS, None, ALU.add, None, [b_rstd[0]], [b_rstd[0]])
                        tt("pool", rstd[0][:], rstd[0][:], nhalf[:], ALU.pow, [b_rstd[0]], [b_rstd[0]])
                        for c in range(4):
                            q = c % 2
                            tt("dve", tmp[q][:], stash[:, c, :], rstd[1][:], ALU.subtract, [b_OT[c][0], b_rstd[1]], [b_tmp[q]])
                            tt("dve", tmp[q][:], tmp[q][:], rstd[0][:], ALU.mult, [b_tmp[q], b_rstd[0]], [b_tmp[q]])
                            ts("dve", tmp[q][:], tmp[q][:], vec2[:, l, 4 + c:5 + c], vec2[:, l, 8 + c:9 + c], ALU.mult, ALU.add,
                               [b_tmp[q]], [b_tmp[q]])
                            act(sg[q][:], tmp[q][:], AF.Tanh, [b_tmp[q]], [b_sg[q]])
                            stt("dve", cT[:, c, t * 512:(t + 1) * 512], sg[q][:], 1.0, tmp[q][:], ALU.add, ALU.mult,
                                [b_sg[q], b_tmp[q]], [b_cT[c][t]])
                    et_n = tr_n = at_n = 0
                    for h in range(4):
                        for qt in range(2):
                            Q0 = tok_base + qt * 512
                            started = set()
                            for kb in range(0, Q0 // 128 + 4):
                                k0 = kb * 128
                                c0 = max(0, k0 - Q0)
                                n = 512 - c0
                                d0 = min(Q0 + c0 - k0, 256)
                                if half == 1 and kb >= 8:
                                    Ks, Vs, bKs, bVs = curK, curV, b_curK[(kb - 8) // 4], b_curV[kb - 8]
                                    kk0, vb = k0 - 1024, kb - 8
                                else:
                                    Ks, Vs, bKs, bVs = kvK[l], kvV[l], b_kvK[l][kb // 4], b_kvV[l][kb]
                                    kk0, vb = k0, kb
                                for mi in range(2):
                                    sbk = (et_n % 2) * 2 + mi
                                    mm(ps[sbk][:, 0:n], Ks[64 * mi:64 * mi + 64, h, kk0:kk0 + 128],
                                       QT[64 * mi:64 * mi + 64, h, qt * 512 + c0:qt * 512 + 512],
                                       True, False, [bKs, b_QT[h][qt]], [b_ps[sbk]])
                                    mm(ps[sbk][:, 0:n], ident[:], biasT[:, h, d0:d0 + n], False, True, [b_const], [b_ps[sbk]])
                                    act(ET[:, sbk, 0:n], ps[sbk][:, 0:n], AF.Exp, [b_ps[sbk]], [b_ET[sbk]])
                                    for j in range(c0 // 128, 4):
                                        a = j * 2 + mi
                                        bank, off = 4 + a // 3, (a % 3) * 160
                                        st1 = bank not in started
                                        started.add(bank)
                                        P.op("pe", "matmul", ps[bank][:, off:off + 129],
                                             ET[:, sbk, j * 128 - c0:j * 128 - c0 + 128], Vs[:, vb, h, :],
                                             start=st1, stop=(k0 == Q0 + j * 128), skip_group_check=True,
                                             reads=[b_ET[sbk], bVs], writes=[b_ps[bank]])
                                et_n += 1
                                if k0 >= Q0:
                                    j = (k0 - Q0) // 128
                                    a1, a2 = j * 2, j * 2 + 1
                                    O1 = ps[4 + a1 // 3][:, (a1 % 3) * 160:(a1 % 3) * 160 + 129]
                                    O2 = ps[4 + a2 // 3][:, (a2 % 3) * 160:(a2 % 3) * 160 + 129]
                                    bO1, bO2 = b_ps[4 + a1 // 3], b_ps[4 + a2 // 3]
                                    z = at_n % 2
                                    at_n += 1
                                    st_, bst = att_s[z], b_atts[z]
                                    fa, bfa = att_f[2 * z], b_attf[2 * z]
                                    fo, bfo = att_f[2 * z + 1], b_attf[2 * z + 1]
                                    on, bon = att_b[z], b_attb[z]
                                    P.op("dve", "reciprocal", st_[:, 0:1], O1[:, 128:129], reads=[bO1], writes=[bst])
                                    P.op("dve", "reciprocal", st_[:, 1:2], O2[:, 128:129], reads=[bO2], writes=[bst])
                                    P.op("dve", "memset", st_[:, 3:4], 0.0, writes=[bst])
                                    tt("dve", st_[:, 2:3], st_[:, 1:2], nlam[:, l:l + 1], ALU.mult, [bst], [bst])
                                    ts("dve", fa[:], O1[:, 0:128], st_[:, 0:1], None, ALU.mult, None, [bst, bO1], [bfa])
                                    stt("dve", fo[:], O2[:, 0:128], st_[:, 2:3], fa[:], ALU.mult, ALU.add, [bst, bO2, bfa], [bfo])
                                    act(fa[:], fo[:], AF.Square, [bfo, bst], [bfa, bst], accum_out=st_[:, 3:4])
                                    ts("dve", st_[:, 4:5], st_[:, 3:4], 1.0 / 128.0, EPS, ALU.mult, ALU.add, [bst], [bst])
                                    tt("pool", st_[:, 5:6], st_[:, 4:5], nhalf[:, 0:1], ALU.pow, [bst], [bst])
                                    stt("dve", on[:], fo[:], st_[:, 5:6], subln[:, l, :], ALU.mult, ALU.mult, [bst, bfo], [bon])
                                    tsl = tr_n % 4
                                    tr_n += 1
                                    mm(ps[7][:, tsl * 128:(tsl + 1) * 128], on[:], ident[:], True, True, [bon], [b_ps[7]])
                                    qa = qt * 512 + j * 128
                                    act(OT[:, h, qa:qa + 128], ps[7][:, tsl * 128:(tsl + 1) * 128], AF.Copy,
                                        [b_ps[7]], [b_OT[h][qt]])
                    P.barrier()
                    srcs = [(cT, b_cT), (OT, b_OT), (p2T, b_p2T)]
                    for m in range(8):
                        wg, bwg = load_pack(d_mrgg[l, m], 3072)
                        wbo, bwbo = load_pack(d_mrgb[l, m], 1536)
                        wg4 = wg[:, 0:3072].rearrange("p (b k j) -> p b k j", b=3, k=8)
                        wbo4 = wbo[:, 0:1536].rearrange("p (b k j) -> p b k j", b=3, k=4)
                        for t in range(2):
                            for br in range(3):
                                bg, by = gen_bank(), gen_bank()
                                for k in range(8):
                                    mm(ps[bg][:], wg4[:, br, k, :], hT[:, k, t * 512:(t + 1) * 512], k == 0, k == 7,
                                       [bwg, b_hT[k][t]], [b_ps[bg]])
                                sT, bsT = srcs[br]
                                for k in range(4):
                                    mm(ps[by][:], wbo4[:, br, k, :], sT[:, k, t * 512:(t + 1) * 512], k == 0, k == 3,
                                       [bwbo, bsT[k][t]], [b_ps[by]])
                                q = br % 2
                                act(sg[q][:], ps[bg][:], AF.Tanh, [b_ps[bg]], [b_sg[q]], scale=0.5)
                                if br == 0:
                                    stt("dve", tmp[0][:], sg[q][:], 1.0, ps[by][:], ALU.add, ALU.mult, [b_sg[q], b_ps[by]], [b_tmp[0]])
                                else:
                                    stt("dve", tmp[1][:], sg[q][:], 1.0, ps[by][:], ALU.add, ALU.mult, [b_sg[q], b_ps[by]], [b_tmp[1]])
                                    if br == 1:
                                        tt("pool", tmp[0][:], tmp[0][:], tmp[1][:], ALU.add, [b_tmp[1]], [b_tmp[0]])
                                    else:
                                        tt("pool", mergedT[:, m, t * 512:(t + 1) * 512], tmp[0][:], tmp[1][:], ALU.add,
                                           [b_tmp[0], b_tmp[1]], [b_mg[m][t]])
                    for t in range(2):
                        for cg in range(2):
                            w_, bw_ = load_pack(d_wout[l, cg], 4096)
                            w3 = w_[:, :].rearrange("p (k c) -> p k c", k=8)
                            for m4 in range(4):
                                m = cg * 4 + m4
                                b = gen_bank()
                                for k in range(8):
                                    mm(ps[b][:], w3[:, k, m4 * 128:(m4 + 1) * 128], mergedT[:, k, t * 512:(t + 1) * 512],
                                       k == 0, k == 7, [bw_, b_mg[k][t]], [b_ps[b]])
                                evac(yevE[:, m, :], ps[b][:], [b_ps[b]], [b_yev[m]])
                        post_norm_residual(yevE, t, 8, l, 4.0 * EPS, t)
                    P.barrier()
                    for t in range(2):
                        pre_norm(h2T, b_h2, t, 16, l, t, 0)
                        for fg in range(8):
                            w_, bw_ = load_pack(d_w1[l, fg], 4096)
                            w3 = w_[:, :].rearrange("p (k c) -> p k c", k=8)
                            for i in range(4):
                                jf = fg * 4 + i
                                b = gen_bank()
                                for k in range(8):
                                    mm(ps[b][:], w3[:, k, i * 128:(i + 1) * 128], h2T[:, k, :], k == 0, k == 7,
                                       [bw_, b_h2[k]], [b_ps[b]])
                                q = jf % 2
                                act(relu_s[:, q, :], ps[b][:], AF.Relu, [b_ps[b]], [b_relu[q]])
                                eng = "pool" if jf % 4 != 3 else "dve"
                                tt(eng, f1[:, jf, :], relu_s[:, q, :], relu_s[:, q, :], ALU.mult, [b_relu[q]], [b_f1[jf]])
                        for jg in range(8):
                            w_, bw_ = load_pack(d_w2[l, jg], 4096)
                            w3 = w_[:, :].rearrange("p (j c) -> p j c", j=4)
                            for jj in range(4):
                                jf = jg * 4 + jj
                                for m in range(8):
                                    mm(ps[m][:], w3[:, jj, m * 128:(m + 1) * 128], f1[:, jf, :], jf == 0, jf == 31,
                                       [bw_, b_f1[jf]], [b_ps[m]])
                        for m in range(8):
                            evac(yevF[:, m, :], ps[m][:], [b_ps[m]], [b_yev[m]])
                        post_norm_residual(yevF, t, 24, l, EPS, t)
                    P.barrier()
                    P.dma("pool", "pin", pTs, d_pT[l, sq_i, :, tok_base:tok_base + 1024].rearrange("(k p) t -> p k t", p=128),
                          writes=[b_pTs])
                    for k in range(8):
                        for t in range(2):
                            if (k + t) % 2:
                                act(xbT[:, k, t * 512:(t + 1) * 512], xT[:, k, t * 512:(t + 1) * 512], AF.Copy,
                                    [b_x[k][t]], [b_xb[k][t]])
                            else:
                                P.op("pool", "tensor_copy", xbT[:, k, t * 512:(t + 1) * 512], xT[:, k, t * 512:(t + 1) * 512],
                                     reads=[b_x[k][t]], writes=[b_xb[k][t]])
                    wpe_, bwpe = load_pack(d_wpe[l], 2048)
                    wpe3 = wpe_[:, 0:2048].rearrange("p (k c) -> p k c", k=2)
                    for cg in range(2):
                        w_, bw_ = load_pack(d_wpg[l, cg], 4096)
                        w3 = w_[:, :].rearrange("p (k c) -> p k c", k=8)
                        for m4 in range(4):
                            m = cg * 4 + m4
                            for t in range(2):
                                bg, bp = gen_bank(), gen_bank()
                                for k in range(8):
                                    mm(ps[bg][:], w3[:, k, m4 * 128:(m4 + 1) * 128], xbT[:, k, t * 512:(t + 1) * 512],
                                       k == 0, k == 7, [bw_, b_xb[k][t]], [b_ps[bg]])
                                for k in range(2):
                                    mm(ps[bp][:], wpe3[:, k, m * 128:(m + 1) * 128], pTs[:, k, t * 512:(t + 1) * 512],
                                       k == 0, k == 1, [bwpe, b_pTs], [b_ps[bp]])
                                q = t % 2
                                act(sg[q][:], ps[bg][:], AF.Tanh, [b_ps[bg]], [b_sg[q]], scale=0.5)
                                act(relu_s[:, q, :], ps[bp][:], AF.Copy, [b_ps[bp]], [b_relu[q]], scale=0.5)
                                stt("dve", tmp[q][:], sg[q][:], 1.0, relu_s[:, q, :], ALU.add, ALU.mult, [b_sg[q], b_relu[q]], [b_tmp[q]])
                                xs = xT[:, m, t * 512:(t + 1) * 512]
                                tt("pool", xs, xs, tmp[q][:], ALU.add, [b_tmp[q]], [b_x[m][t]])
                    P.barrier()
                for k in range(8):
                    P.dma("sp", "xout", d_out[sq_i, k * 128:(k + 1) * 128, tok_base:tok_base + 1024], xT[:, k, :],
                          reads=[b_x[k][0], b_x[k][1]])
                P.barrier()
        counts = P.emit()
    return nc, counts


def _rel_bucket(n):
    n = np.maximum(n, 0)
    nf = np.maximum(n, 1).astype(np.float32)
    large = 16 + (np.log(nf / np.float32(16)) / np.float32(math.log(128 / 16)) * np.float32(16)).astype(np.int32)
    large = np.minimum(large, 31)
    return np.where(n < 16, n, large)


def host_layout(inp):
    f = np.float32
    g = {k: np.asarray(v, dtype=f) for k, v in inp.items()}
    L = DEPTH
    sh = {}
    w_in = g["w_in"]
    sh["win"] = np.ascontiguousarray(
        w_in[:, :, :3072].reshape(L, 8, 128, 6, 512).transpose(0, 3, 2, 1, 4)).reshape(L, 6, 128, 4096)
    gates = w_in[:, :, 3072:].reshape(L, 8, 128, 3, 8, 128)
    sh["mrgg"] = np.ascontiguousarray(gates.transpose(0, 4, 2, 3, 1, 5)).reshape(L, 8, 128, 3072)
    bo = np.stack([g["w_conv_out"], g["w_attn_out"], g["w_pool_out"]], axis=1)
    bo = bo.reshape(L, 3, 4, 128, 8, 128)
    sh["mrgb"] = np.ascontiguousarray(bo.transpose(0, 4, 3, 1, 2, 5)).reshape(L, 8, 128, 1536)

    def colgroups(w, ng):
        return np.ascontiguousarray(w.reshape(L, 8, 128, ng, 512).transpose(0, 3, 2, 1, 4)).reshape(L, ng, 128, 4096)

    sh["wout"] = colgroups(g["w_out"], 2)
    sh["w1"] = colgroups(g["w_mlp_in"], 8)
    sh["wpg"] = colgroups(g["w_ple_gate"], 2)
    sh["w2"] = np.ascontiguousarray(g["w_mlp_out"].reshape(L, 8, 4, 128, 1024).transpose(0, 1, 3, 2, 4)).reshape(L, 8, 128, 4096)
    sh["wpe"] = np.ascontiguousarray(g["w_ple_proj"].reshape(L, 2, 128, 1024).transpose(0, 2, 1, 3)).reshape(L, 128, 2048)
    sh["poolw"] = np.ascontiguousarray(g["pool_w"].transpose(0, 2, 1, 3)).reshape(L, 128, 512)
    vecs = np.zeros((L, 128, 172), f)

    def cols(v, n):
        return v.reshape(L, n, 128).transpose(0, 2, 1)
    vecs[:, :, 0:8] = cols(g["g_pre_mix"], 8)
    vecs[:, :, 8:16] = cols(g["g_post_mix"], 8)
    vecs[:, :, 16:24] = cols(g["g_pre_mlp"], 8)
    vecs[:, :, 24:32] = cols(g["g_post_mlp"], 8)
    vecs[:, :, 32:36] = cols(g["conv_dw_b"], 4)
    vecs[:, :, 36:40] = cols(g["conv_ln_g"], 4)
    vecs[:, :, 40:44] = cols(g["conv_ln_b"], 4)
    vecs[:, :, 44:48] = cols(g["pool_scale"], 4)
    cw = g["conv_dw_w"][:, :, 0, :].reshape(L, 31, 4, 128).transpose(0, 3, 2, 1)
    vecs[:, :, 48:172] = cw.reshape(L, 128, 124)
    sh["vecs"] = vecs
    sh["sublnb"] = np.ascontiguousarray(np.broadcast_to(g["subln_g"][:, None, :], (L, 128, 128)))
    sh["lamb"] = np.ascontiguousarray(np.broadcast_to(g["lam_p"].reshape(L, 1, 256), (L, 128, 256)))
    kk = np.arange(128)[:, None]
    dd = np.arange(768)[None, :]
    nrel = dd - kk
    bidx = _rel_bucket(nrel)
    tab = g["rel_bias"][bidx]
    tab = np.where((nrel >= 0)[:, :, None], tab, f(-1e30))
    sh["biasT"] = np.ascontiguousarray(tab.transpose(0, 2, 1)).reshape(128, 4 * 768).astype(f)
    sh["ident"] = np.eye(128, dtype=f)
    rc = np.zeros((128, 16), f)
    rc[:, :] = 1.0 / (np.arange(16, dtype=f) + 1.0)
    sh["rc"] = rc
    x = g["x"]
    p = g["p"]
    per_core = []
    for c in range(8):
        d = dict(sh)
        d["xT"] = np.ascontiguousarray(x[2 * c:2 * c + 2].transpose(0, 2, 1))
        d["pT"] = np.ascontiguousarray(p[:, 2 * c:2 * c + 2].transpose(0, 1, 3, 2))
        per_core.append(d)
    return per_core


_CACHE = {}


def kernel(**inputs):
    if "nc" not in _CACHE:
        _CACHE["nc"] = build_program()[0]
    nc = _CACHE["nc"]
    in_maps = host_layout(inputs)
    res = run_bass_kernel_spmd(nc, in_maps, core_ids=list(range(8)))
    out = np.empty((16, S, D), np.float32)
    for c in range(8):
        out[2 * c:2 * c + 2] = res.results[c]["outT"].transpose(0, 2, 1)
    return out
```

```python
import math
from contextlib import ExitStack

import numpy as np
import concourse.bass as bass
import concourse.mybir as mybir
from concourse.bass_utils import run_bass_kernel_spmd

F32 = mybir.dt.float32
BF16 = mybir.dt.bfloat16
ALU = mybir.AluOpType
AF = mybir.ActivationFunctionType
AX = mybir.AxisListType

DEPTH = 2
NSEQ = 2
S = 2048
D = 1024
EPS = 1e-6
LAMBDA_INIT = [0.8 - 0.6 * math.exp(-0.3 * i) for i in range(DEPTH)]

COMPUTE = ("pe", "act", "dve", "pool")
STREAMS = ("pe", "act", "dve", "pool", "sp")


class Buf:
    __slots__ = ("name", "lw", "rd")

    def __init__(self, name):
        self.name = name
        self.lw = None
        self.rd = []


class Op:
    __slots__ = ("stream", "fn", "waits", "key", "idx", "signal", "is_dma", "cnt")


class Prog:
    def __init__(self, nc):
        self.nc = nc
        self.ops = {s: [] for s in STREAMS}
        self.cnt = {}
        self.vc = {s: {} for s in STREAMS}
        self.opvc = {}
        self.bykey = {}
        self.dma_keys = []
        self.pend = {s: {} for s in STREAMS}

    def barrier(self):
        snap = dict(self.cnt)
        for s in STREAMS:
            for k, i in snap.items():
                if self.pend[s].get(k, 0) < i:
                    self.pend[s][k] = i

    def _record(self, stream, key, fn, reads, writes, is_dma):
        deps = set()
        for b in reads:
            if b.lw is not None:
                deps.add(b.lw)
        for b in writes:
            if b.lw is not None:
                deps.add(b.lw)
            for r in b.rd:
                deps.add(r)
        for k, i in self.pend[stream].items():
            deps.add((k, i))
        self.pend[stream] = {}
        idx = self.cnt.get(key, 0) + 1
        self.cnt[key] = idx
        me = (key, idx)
        vc = self.vc[stream]
        waits = {}
        for (k, i) in deps:
            if k == stream and not is_dma:
                continue
            if vc.get(k, 0) >= i:
                continue
            if waits.get(k, 0) < i:
                waits[k] = i
        if stream in ("act", "dve", "pool") and not is_dma:
            for b in reads:
                if b.lw is not None and b.lw[0] == stream and vc.get(("self", stream), 0) < b.lw[1]:
                    if waits.get(stream, 0) < b.lw[1]:
                        waits[stream] = b.lw[1]
        for k, i in waits.items():
            if k == stream:
                vc[("self", stream)] = max(vc.get(("self", stream), 0), i)
                continue
            dvc = self.opvc.get((k, i))
            if dvc:
                for kk, ii in dvc.items():
                    if vc.get(kk, 0) < ii:
                        vc[kk] = ii
            if vc.get(k, 0) < i:
                vc[k] = i
        op = Op()
        op.stream, op.fn, op.waits, op.key, op.idx = stream, fn, waits, key, idx
        op.signal, op.is_dma, op.cnt = is_dma, is_dma, 0
        self.ops[stream].append(op)
        self.bykey[me] = op
        snap = {k: v for k, v in vc.items() if not isinstance(k, tuple) and k != stream}
        if not is_dma:
            snap[stream] = idx
        self.opvc[me] = snap
        for b in reads:
            b.rd.append(me)
        for b in writes:
            b.lw = me
            b.rd = []
        return op

    def op(self, stream, method, *args, reads=(), writes=(), **kw):
        return self._record(stream, stream, (method, args, kw), list(reads), list(writes), False)

    def dma(self, queue, semkey, out, in_, reads=(), writes=()):
        if semkey not in self.dma_keys:
            self.dma_keys.append(semkey)
        return self._record(queue, semkey, ("dma_start", (), {"out": out, "in_": in_}), list(reads), list(writes), True)

    def emit(self):
        nc = self.nc
        for s in STREAMS:
            for op in self.ops[s]:
                for k, i in op.waits.items():
                    if k in COMPUTE:
                        self.bykey[(k, i)].signal = True
        finals = {}
        for k in COMPUTE:
            if self.cnt.get(k, 0):
                self.bykey[(k, self.cnt[k])].signal = True
                finals[k] = self.cnt[k]
        for k in self.dma_keys:
            finals[k] = self.cnt[k]
        for k in COMPUTE:
            c = 0
            for i in range(1, self.cnt.get(k, 0) + 1):
                op = self.bykey[(k, i)]
                if op.signal:
                    c += 1
                op.cnt = c
        with ExitStack() as es:
            sems = {}
            for k in COMPUTE:
                if self.cnt.get(k, 0):
                    sems[k] = es.enter_context(nc.semaphore("s_" + k))
            for k in self.dma_keys:
                sems[k] = es.enter_context(nc.semaphore("d_" + str(k)))
            block = es.enter_context(nc.Block())

            def val(k, i):
                if k in COMPUTE:
                    return self.bykey[(k, i)].cnt
                return 16 * i

            def run(stream, eng):
                for op in self.ops[stream]:
                    for k, i in op.waits.items():
                        eng.wait_ge(sems[k], val(k, i))
                    meth, args, kw = op.fn
                    ins = getattr(eng, meth)(*args, **kw)
                    if op.is_dma:
                        ins.then_inc(sems[op.key], 16)
                    elif op.signal:
                        ins.then_inc(sems[op.key], 1)
                if stream == "sp":
                    for k, i in finals.items():
                        eng.wait_ge(sems[k], val(k, i))

            @block.tensor
            def _(e):
                run("pe", e)

            @block.scalar
            def _(e):
                run("act", e)

            @block.vector
            def _(e):
                run("dve", e)

            @block.gpsimd
            def _(e):
                run("pool", e)

            @block.sync
            def _(e):
                run("sp", e)
        return {s: len(self.ops[s]) for s in STREAMS}


def build_program(layers=(0, 1), nseq=NSEQ, halves=(0, 1)):
    nc = bass.Bass("TRN2", target_bir_lowering=False)
    P = Prog(nc)

    def dram(name, shape, kind="ExternalInput"):
        return nc.dram_tensor(name, list(shape), F32, kind=kind).ap()

    d_xT = dram("xT", [NSEQ, D, S])
    d_pT = dram("pT", [DEPTH, NSEQ, 256, S])
    d_out = dram("outT", [NSEQ, D, S], kind="ExternalOutput")
    d_win = dram("win", [DEPTH, 6, 128, 4096])
    d_mrgg = dram("mrgg", [DEPTH, 8, 128, 3072])
    d_mrgb = dram("mrgb", [DEPTH, 8, 128, 1536])
    d_wout = dram("wout", [DEPTH, 2, 128, 4096])
    d_w1 = dram("w1", [DEPTH, 8, 128, 4096])
    d_w2 = dram("w2", [DEPTH, 8, 128, 4096])
    d_wpg = dram("wpg", [DEPTH, 2, 128, 4096])
    d_wpe = dram("wpe", [DEPTH, 128, 2048])
    d_poolw = dram("poolw", [DEPTH, 128, 512])
    d_vecs = dram("vecs", [DEPTH, 128, 172])
    d_subln = dram("sublnb", [DEPTH, 128, 128])
    d_lam = dram("lamb", [DEPTH, 128, 256])
    d_bias = dram("biasT", [128, 4 * 768])
    d_ident = dram("ident", [128, 128])
    d_rc = dram("rc", [128, 16])

    es = ExitStack()
    with es:
        def sb(name, shape, dt):
            return es.enter_context(nc.sbuf_tensor(name, list(shape), dt))

        xT = sb("xT_s", [128, 8, 1024], F32)
        kvK = [sb(f"kvK{i}", [128, 4, 1024], BF16) for i in range(2)]
        kvV = [sb(f"kvV{i}", [128, 8, 4, 129], BF16) for i in range(2)]
        NRR = 39072
        RR = sb("RR", [128, NRR], BF16)
        S2 = [sb(f"S2_{i}", [128, 528], F32) for i in range(3)]
        sq = [sb(f"sq{i}", [128, 512], BF16) for i in range(2)]
        rstd = [sb(f"rstd{i}", [128, 512], F32) for i in range(2)]
        sg = [sb(f"sg{i}", [128, 512], F32) for i in range(2)]
        tmp = [sb(f"tmp{i}", [128, 512], F32) for i in range(2)]
        ring = [sb(f"ring{i}", [128, 4096], BF16) for i in range(3)]
        biasT = sb("biasT_s", [128, 4, 768], BF16)
        ident = sb("ident_s", [128, 128], BF16)
        onesD = sb("onesD", [128, 128], BF16)
        onesC = sb("onesC", [128, 128], BF16)
        vecs = sb("vecs_s", [128, DEPTH, 172], F32)
        vec2 = sb("vec2_s", [128, DEPTH, 12], F32)
        subln = sb("subln_s", [128, DEPTH, 128], F32)
        nlam = sb("nlam", [128, DEPTH], F32)
        rc = sb("rc_s", [128, 16], F32)
        nhalf = sb("nhalf", [128, 8], F32)
        onesF = sb("onesF", [1, 128], F32)
        rowb = [sb(f"rowb{i}", [1, 512], F32) for i in range(3)]
        identF = sb("identF", [128, 128], F32)
        onesFF = sb("onesFF", [128, 128], F32)
        colb = [sb(f"colb{i}", [128, 4], F32) for i in range(2)]
        poolw = sb("poolw_s", [128, 4, 128], BF16)
        uhalo = sb("uhalo", [128, DEPTH, 4, 30], BF16)
        puhalo = sb("puhalo", [128, DEPTH, 4, 16], F32)
        att_f = [sb(f"attf{i}", [128, 128], F32) for i in range(4)]
        att_b = [sb(f"attb{i}", [128, 128], BF16) for i in range(2)]
        att_s = [sb(f"atts{i}", [128, 8], F32) for i in range(2)]
        ps = [es.enter_context(nc.psum_tensor(f"ps{i}", [128, 512], F32)) for i in range(8)]

        b_ps = [Buf(f"ps{i}") for i in range(8)]
        b_ring = [Buf(f"ring{i}") for i in range(3)]
        b_x = [[Buf(f"x{k}_{t}") for t in range(2)] for k in range(8)]
        b_sq = [Buf("sq0"), Buf("sq1")]
        b_rstd = [Buf("rstd0"), Buf("rstd1")]
        b_sg = [Buf("sg0"), Buf("sg1")]
        b_tmp = [Buf("tmp0"), Buf("tmp1")]
        b_S2 = [Buf(f"S2_{i}") for i in range(3)]
        b_const = Buf("const")
        b_kvK = [[Buf(f"kvK{i}_{t}") for t in range(2)] for i in range(2)]
        b_kvV = [[Buf(f"kvV{i}_{tb}") for tb in range(8)] for i in range(2)]
        b_uh = [Buf(f"uh{l}") for l in range(DEPTH)]
        b_ph = [[Buf(f"ph{l}_{g}") for g in range(4)] for l in range(DEPTH)]
        b_attf = [Buf(f"attf{i}") for i in range(4)]
        b_attb = [Buf(f"attb{i}") for i in range(2)]
        b_atts = [Buf(f"atts{i}") for i in range(2)]
        b_poolw = Buf("poolw")
        b_row = [Buf(f"row{i}") for i in range(3)]
        b_col = [Buf(f"col{i}") for i in range(2)]

        def rr3(off, a, b):
            return RR[:, off:off + a * b].rearrange("p (a b) -> p a b", a=a)

        def rrf(off_bf16, a, b):
            v = RR[:, off_bf16:off_bf16 + 2 * a * b].bitcast(F32)
            return v.rearrange("p (a b) -> p a b", a=a)

        curK = rr3(0, 4, 1024)
        curV = RR[:, 4096:4096 + 4128].rearrange("p (a h e) -> p a h e", a=8, h=4)
        hT = rr3(8224, 8, 1024)
        uT = rr3(16416, 4, 1054)
        QT = rr3(20632, 4, 1024)
        p2T = rr3(24728, 4, 1024)
        cT = rr3(28824, 4, 1024)
        OT = rr3(32920, 4, 1024)
        stash = rrf(32920, 4, 512)
        ET = rr3(37016, 4, 512)
        mergedT = rr3(0, 8, 1024)
        yevE = rrf(16416, 8, 512)
        h2T = rr3(0, 8, 512)
        f1 = rr3(4096, 32, 512)
        relu_s = rrf(20480, 2, 512)
        yevF = rrf(22528, 8, 512)
        xbT = rr3(0, 8, 1024)
        pTs = rr3(8192, 2, 1024)
        b_curK = [Buf("curK0"), Buf("curK1")]
        b_curV = [Buf(f"curV{i}") for i in range(8)]
        b_hT = [[Buf(f"hT{k}_{t}") for t in range(2)] for k in range(8)]
        b_uT = [[Buf(f"uT{c}_{t}") for t in range(2)] for c in range(4)]
        b_QT = [[Buf(f"QT{c}_{t}") for t in range(2)] for c in range(4)]
        b_p2T = [[Buf(f"p2T{c}_{t}") for t in range(2)] for c in range(4)]
        b_cT = [[Buf(f"cT{c}_{t}") for t in range(2)] for c in range(4)]
        b_OT = [[Buf(f"OT{c}_{t}") for t in range(2)] for c in range(4)]
        b_ET = [Buf(f"ET{i}") for i in range(4)]
        b_mg = [[Buf(f"mg{k}_{t}") for t in range(2)] for k in range(8)]
        b_yev = [Buf(f"yev{m}") for m in range(8)]
        b_h2 = [Buf(f"h2_{k}") for k in range(8)]
        b_f1 = [Buf(f"f1_{j}") for j in range(32)]
        b_relu = [Buf("relu0"), Buf("relu1")]
        b_xb = [[Buf(f"xb{k}_{t}") for t in range(2)] for k in range(8)]
        b_pTs = Buf("pTs")

        ring_n = [0]

        def load_pack(src_ap, nelem):
            s = ring_n[0] % 3
            ring_n[0] += 1
            P.dma("pool", f"ring{s}", ring[s][:, 0:nelem], src_ap, writes=[b_ring[s]])
            return ring[s], b_ring[s]

        gen_n = [0]

        def gen_bank():
            b = gen_n[0] % 6
            gen_n[0] += 1
            return b

        ev_n = [0]

        def evac(out, in_, reads, writes):
            ev_n[0] += 1
            if ev_n[0] % 2:
                P.op("act", "activation", out=out, in_=in_, func=AF.Copy, reads=reads, writes=writes)
            else:
                P.op("dve", "tensor_copy", out, in_, reads=reads, writes=writes)

        def mm(out, lhsT, rhs, start, stop, reads, writes):
            P.op("pe", "matmul", out, lhsT, rhs, start=start, stop=stop, reads=reads, writes=writes)

        def tt(eng, out, in0, in1, op, reads, writes):
            P.op(eng, "tensor_tensor", out, in0, in1, op, reads=reads, writes=writes)

        def ts(eng, out, in0, s1, s2, op0, op1, reads, writes):
            if s2 is None:
                P.op(eng, "tensor_scalar", out, in0, s1, None, op0, reads=reads, writes=writes)
            else:
                P.op(eng, "tensor_scalar", out, in0, s1, s2, op0, op1, reads=reads, writes=writes)

        def stt(eng, out, in0, scalar, in1, op0, op1, reads, writes):
            P.op(eng, "scalar_tensor_tensor", out=out, in0=in0, scalar=scalar, in1=in1, op0=op0, op1=op1,
                 reads=reads, writes=writes)

        def act(out, in_, func, reads, writes, **kw):
            P.op("act", "activation", out=out, in_=in_, func=func, reads=reads, writes=writes, **kw)

        b_lamt, b_lt, b_nl, b_v2 = Buf("lamt"), Buf("lt"), Buf("nl"), Buf("v2")
        lamt = RR[:, 0:1024].bitcast(F32).rearrange("p (l c) -> p l c", l=DEPTH)
        lt = RR[:, 2048:2048 + 512].bitcast(F32)
        P.dma("sp", "c0", vecs[:], d_vecs.rearrange("l p c -> p l c"), writes=[b_const])
        P.dma("sp", "c0", subln[:], d_subln.rearrange("l p c -> p l c"), writes=[Buf("x1")])
        P.dma("sp", "c0", rc[:], d_rc, writes=[Buf("x2")])
        P.dma("sp", "c0", lamt, d_lam.rearrange("l p c -> p l c"), writes=[b_lamt])
        P.dma("pool", "c1", biasT[:], d_bias.rearrange("p (h c) -> p h c", h=4), writes=[Buf("x3")])
        P.dma("pool", "c1", ident[:], d_ident, writes=[Buf("x4")])
        P.dma("sp", "c0", identF[:], d_ident, writes=[Buf("x5")])
        P.barrier()
        P.op("dve", "memset", onesD[:], 1.0 / 1024.0, writes=[b_const])
        P.op("dve", "memset", nhalf[:], -0.5, writes=[b_const])
        P.op("dve", "memset", onesF[:], 1.0, writes=[b_const])
        P.op("dve", "memset", onesFF[:], 1.0, writes=[b_const])
        P.op("dve", "memset", onesC[:], 1.0 / 512.0, writes=[b_const])
        P.op("dve", "memset", uhalo[:], 0.0, writes=[b_const])
        P.op("dve", "memset", puhalo[:], 0.0, writes=[b_const])
        for i in range(2):
            P.op("dve", "memset", kvV[i][:, :, :, 128:129], 1.0, writes=[b_const])
        ssum = att_s[0]
        for l in range(DEPTH):
            ts("dve", vec2[:, l, 0:4], vecs[:, l, 32:36], 2.0, None, ALU.mult, None, [b_const], [b_v2])
            ts("dve", vec2[:, l, 4:12], vecs[:, l, 36:44], 0.5, None, ALU.mult, None, [b_const], [b_v2])
            ts("dve", subln[:, l, :], subln[:, l, :], 1.0 - LAMBDA_INIT[l], None, ALU.mult, None, [b_const], [b_v2])
            for j in range(2):
                tt("dve", lt[:, 64 * j:64 * j + 64], lamt[:, l, 128 * j:128 * j + 64],
                   lamt[:, l, 128 * j + 64:128 * j + 128], ALU.mult, [b_lamt], [b_lt])
                P.op("dve", "reduce_sum", ssum[:, 2 * l + j:2 * l + j + 1], lt[:, 64 * j:64 * j + 64], AX.X,
                     reads=[b_lt], writes=[b_atts[0]])
        act(ssum[:, 4:8], ssum[:, 0:4], AF.Exp, [b_atts[0]], [b_atts[0]])
        for l in range(DEPTH):
            tt("dve", nlam[:, l:l + 1], ssum[:, 5 + 2 * l:6 + 2 * l], ssum[:, 4 + 2 * l:5 + 2 * l], ALU.subtract,
               [b_atts[0]], [b_nl])
            ts("dve", nlam[:, l:l + 1], nlam[:, l:l + 1], -LAMBDA_INIT[l], None, ALU.add, None, [b_nl], [b_nl])
        P.barrier()

        def norm_rstd(srcs, src_bufs, eps, r):
            bank = 6 + (r % 2)
            for k in range(8):
                q = k % 2
                act(sq[q][:], srcs[k], AF.Square, [src_bufs[k]], [b_sq[q]])
                mm(ps[bank][0:1, :], onesD[:, 0:1], sq[q][:], k == 0, k == 7, [b_sq[q]], [b_ps[bank]])
            ts("dve", rowb[r][:], ps[bank][0:1, :], eps, None, ALU.add, None, [b_ps[bank]], [b_row[r]])
            return rsqrt_bcast(r, bank)

        def rsqrt_bcast(r, bank):
            for blk in range(4):
                mm(ps[bank][:, blk:blk + 1], rowb[r][0:1, blk * 128:(blk + 1) * 128], onesF[0:1, 0:1], True, True,
                   [b_row[r]], [b_ps[bank]])
            P.op("dve", "tensor_copy", colb[r % 2][:], ps[bank][:, 0:4], reads=[b_ps[bank]], writes=[b_col[r % 2]])
            tt("pool", colb[r % 2][:], colb[r % 2][:], nhalf[:, 0:4], ALU.pow, [b_col[r % 2]], [b_col[r % 2]])
            for blk in range(4):
                ts("dve", rstd[r % 2][:, blk * 128:(blk + 1) * 128], identF[:], colb[r % 2][:, blk:blk + 1], None, ALU.mult, None,
                   [b_col[r % 2]], [b_rstd[r % 2]])
            for blk in range(4):
                mm(ps[bank][:, blk * 128:(blk + 1) * 128], onesFF[:], rstd[r % 2][:, blk * 128:(blk + 1) * 128], True, True,
                   [b_rstd[r % 2]], [b_ps[bank]])
            return ps[bank][:], b_ps[bank]

        def post_norm_residual(yev, t, gcol, l, eps, r):
            rs, brs = norm_rstd([yev[:, m, :] for m in range(8)], b_yev, eps, r)
            for m in range(8):
                q = m % 2
                stt("dve", tmp[q][:], yev[:, m, :], vecs[:, l, gcol + m:gcol + m + 1], rs, ALU.mult, ALU.mult,
                    [b_yev[m], brs], [b_tmp[q]])
                xs = xT[:, m, t * 512:(t + 1) * 512]
                tt("pool", xs, xs, tmp[q][:], ALU.add, [b_tmp[q]], [b_x[m][t]])

        def pre_norm(dst, dst_bufs, t, gcol, l, r, tok0):
            rs, brs = norm_rstd([xT[:, k, t * 512:(t + 1) * 512] for k in range(8)], [b_x[k][t] for k in range(8)], EPS, r)
            for k in range(8):
                stt("dve", dst[:, k, tok0:tok0 + 512], xT[:, k, t * 512:(t + 1) * 512],
                    vecs[:, l, gcol + k:gcol + k + 1], rs, ALU.mult, ALU.mult,
                    [b_x[k][t], brs], [dst_bufs[k]])

        for sq_i in range(nseq):
            for half in halves:
                tok_base = half * 1024
                for k in range(8):
                    P.dma("sp", "xin", xT[:, k, :], d_xT[sq_i, k * 128:(k + 1) * 128, tok_base:tok_base + 1024],
                          writes=[b_x[k][0], b_x[k][1]])
                P.barrier()
                for l in layers:
                    if half == 0:
                        Kd, Vd, bKd, bVd = kvK[l], kvV[l], b_kvK[l], b_kvV[l]
                    else:
                        Kd, Vd, bKd, bVd = curK, curV, b_curK, b_curV
                        P.op("dve", "memset", curV[:, :, :, 128:129], 1.0, writes=b_curV)
                    for t in range(2):
                        pre_norm(hT, [b_hT[k][t] for k in range(8)], t, 0, l, t, t * 512)
                    P.dma("pool", "pw", poolw[:], d_poolw[l].rearrange("p (g c) -> p g c", g=4), writes=[b_poolw])
                    wa, bwa = load_pack(d_win[l, 0], 4096)
                    wb, bwb = load_pack(d_win[l, 1], 4096)
                    wa3 = wa[:, :].rearrange("p (k c) -> p k c", k=8)
                    wb3 = wb[:, :].rearrange("p (k c) -> p k c", k=8)
                    for c in range(4):
                        if half == 0:
                            P.op("pool", "memset", uT[:, c, 0:30], 0.0, writes=[b_uT[c][0]])
                        else:
                            P.op("pool", "tensor_copy", uT[:, c, 0:30], uhalo[:, l, c, :], reads=[b_uh[l]], writes=[b_uT[c][0]])
                    for c in range(4):
                        for t in range(2):
                            ba, bb = gen_bank(), gen_bank()
                            for k in range(8):
                                mm(ps[ba][:], wa3[:, k, c * 128:(c + 1) * 128], hT[:, k, t * 512:(t + 1) * 512],
                                   k == 0, k == 7, [bwa, b_hT[k][t]], [b_ps[ba]])
                            for k in range(8):
                                mm(ps[bb][:], wb3[:, k, c * 128:(c + 1) * 128], hT[:, k, t * 512:(t + 1) * 512],
                                   k == 0, k == 7, [bwb, b_hT[k][t]], [b_ps[bb]])
                            q = t % 2
                            act(sg[q][:], ps[bb][:], AF.Tanh, [b_ps[bb]], [b_sg[q]], scale=0.5)
                            stt("dve", uT[:, c, 30 + t * 512:30 + (t + 1) * 512], sg[q][:], 1.0, ps[ba][:], ALU.add, ALU.mult,
                                [b_sg[q], b_ps[ba]], [b_uT[c][t]])
                    if half == 0:
                        for c in range(4):
                            P.op("pool", "tensor_copy", uhalo[:, l, c, :], uT[:, c, 1024:1054],
                                 reads=[b_uT[c][1]], writes=[b_uh[l]])
                    for which in (2, 3):
                        w_, bw_ = load_pack(d_win[l, which], 4096)
                        w3 = w_[:, :].rearrange("p (k c) -> p k c", k=8)
                        for c in range(4):
                            for t in range(2):
                                b = gen_bank()
                                for k in range(8):
                                    mm(ps[b][:], w3[:, k, c * 128:(c + 1) * 128], hT[:, k, t * 512:(t + 1) * 512],
                                       k == 0, k == 7, [bw_, b_hT[k][t]], [b_ps[b]])
                                if which == 2:
                                    act(QT[:, c, t * 512:(t + 1) * 512], ps[b][:], AF.Copy, [b_ps[b]], [b_QT[c][t]], scale=0.125)
                                else:
                                    P.op("dve", "tensor_copy", Kd[:, c, t * 512:(t + 1) * 512], ps[b][:],
                                         reads=[b_ps[b]], writes=[bKd[t]])
                    w_, bw_ = load_pack(d_win[l, 4], 4096)
                    w3 = w_[:, :].rearrange("p (k c) -> p k c", k=8)
                    for tb in range(8):
                        b = gen_bank()
                        for k in range(8):
                            mm(ps[b][:], hT[:, k, tb * 128:(tb + 1) * 128], w3[:, k, :],
                               k == 0, k == 7, [bw_, b_hT[k][tb // 4]], [b_ps[b]])
                        evac(Vd[:, tb, :, 0:128], ps[b][:].rearrange("p (h e) -> p h e", h=4), [b_ps[b]], [bVd[tb]])
                    w_, bw_ = load_pack(d_win[l, 5], 4096)
                    w3 = w_[:, :].rearrange("p (k c) -> p k c", k=8)
                    B3 = [(S2[0], b_S2[0]), (S2[1], b_S2[1]), (S2[2], b_S2[2])]
                    pu, bpu = B3[0]
                    for g in range(4):
                        win_w = 2 ** (g + 1)
                        for t in range(2):
                            b = gen_bank()
                            for k in range(8):
                                mm(ps[b][:], w3[:, k, g * 128:(g + 1) * 128], hT[:, k, t * 512:(t + 1) * 512],
                                   k == 0, k == 7, [bw_, b_hT[k][t]], [b_ps[b]])
                            if half == 0 and t == 0:
                                P.op("dve", "memset", pu[:, 0:16], 0.0, writes=[bpu])
                            else:
                                P.op("dve", "tensor_copy", pu[:, 0:16], puhalo[:, l, g, :], reads=[b_ph[l][g]], writes=[bpu])
                            act(pu[:, 16:528], ps[b][:], AF.Copy, [b_ps[b]], [bpu])
                            P.op("pool", "tensor_copy", puhalo[:, l, g, :], pu[:, 512:528], reads=[bpu], writes=[b_ph[l][g]])
                            si, sh, lo = 0, 1, 1
                            for st in range(g + 1):
                                di = 1 if si != 1 else 2
                                lo += sh
                                sv, sbf = B3[si]
                                dv, dbf = B3[di]
                                tt("dve", dv[:, lo:528], sv[:, lo:528], sv[:, lo - sh:528 - sh], ALU.add, [sbf], [dbf])
                                si, sh = di, sh * 2
                            av, abf = B3[si]
                            fi = 2 if si == 1 else 1
                            fv, fbf = B3[fi]
                            pooled = fv[:, 0:256].bitcast(BF16)
                            stt("dve", pooled, av[:, 16:528], 1.0 / win_w, pu[:, 16:528], ALU.mult, ALU.subtract,
                                [abf, bpu], [fbf])
                            if half == 0 and t == 0:
                                nfix = win_w - 1
                                tt("dve", tmp[0][:, 0:nfix], av[:, 16:16 + nfix], rc[:, 0:nfix], ALU.mult, [abf], [b_tmp[0]])
                                tt("dve", pooled[:, 0:nfix], tmp[0][:, 0:nfix], pu[:, 16:16 + nfix], ALU.subtract,
                                   [b_tmp[0], bpu], [fbf])
                            b2 = gen_bank()
                            mm(ps[b2][:], poolw[:, g, :], pooled, True, True, [b_poolw, fbf], [b_ps[b2]])
                            act(p2T[:, g, t * 512:(t + 1) * 512], ps[b2][:], AF.Copy, [b_ps[b2]], [b_p2T[g][t]],
                                scale=vecs[:, l, 44 + g:45 + g])
                    for t in range(2):
                        base = t * 512
                        for c in range(4):
                            acc, bacc = S2[c % 2], b_S2[c % 2]
                            av = acc[:, 0:512]
                            ts("dve", av, uT[:, c, base:base + 512], vecs[:, l, 48 + c * 31:49 + c * 31], vec2[:, l, c:c + 1],
                               ALU.mult, ALU.add, [b_uT[c][0], b_uT[c][1]], [bacc])
                            for j in range(1, 31):
                                dst = av if j < 30 else stash[:, c, :]
                                wr = [bacc] if j < 30 else [b_OT[c][0], b_OT[c][1]]
                                stt("dve", dst, uT[:, c, base + j:base + j + 512],
                                    vecs[:, l, 48 + c * 31 + j:49 + c * 31 + j], av, ALU.mult, ALU.add,
                                    [b_uT[c][0], b_uT[c][1], bacc], wr)
                        for c in range(4):
                            q = c % 2
                            act(sq[q][:], stash[:, c, :], AF.Copy, [b_OT[c][0]], [b_sq[q]])
                            mm(ps[6][0:1, :], onesC[:, 0:1], sq[q][:], c == 0, c == 3, [b_sq[q]], [b_ps[6]])
                        for c in range(4):
                            q = c % 2
                            act(sq[q][:], stash[:, c, :], AF.Square, [b_OT[c][0]], [b_sq[q]])
                            mm(ps[7][0:1, :], onesC[:, 0:1], sq[q][:], c == 0, c == 3, [b_sq[q]], [b_ps[7]])
                        P.op("dve", "tensor_copy", rowb[2][:], ps[6][0:1, :], reads=[b_ps[6]], writes=[b_row[2]])
                        tt("dve", rowb[0][:], rowb[2][:], rowb[2][:], ALU.mult, [b_row[2]], [b_row[0]])
                        tt("dve", rowb[0][:], ps[7][0:1, :], rowb[0][:], ALU.subtract, [b_ps[7], b_row[0]], [b_row[0]])
                        ts("dve", rowb[0][:], rowb[0][:], 4.0 * EPS, None, ALU.add, None, [b_row[0]], [b_row[0]])
                        mm(ps[6][:], onesF[:], rowb[2][:], True, True, [b_row[2]], [b_ps[6]])
                        rsqrt_bcast(0, 7)
                        for c in range(4):
                            q = c % 2
                            tt("dve", tmp[q][:], stash[:, c, :], ps[6][:], ALU.subtract, [b_OT[c][0], b_ps[6]], [b_tmp[q]])
                            tt("dve", tmp[q][:], tmp[q][:], ps[7][:], ALU.mult, [b_tmp[q], b_ps[7]], [b_tmp[q]])
                            ts("dve", tmp[q][:], tmp[q][:], vec2[:, l, 4 + c:5 + c], vec2[:, l, 8 + c:9 + c], ALU.mult, ALU.add,
                               [b_tmp[q]], [b_tmp[q]])
                            act(sg[q][:], tmp[q][:], AF.Tanh, [b_tmp[q]], [b_sg[q]])
                            stt("dve", cT[:, c, t * 512:(t + 1) * 512], sg[q][:], 1.0, tmp[q][:], ALU.add, ALU.mult,
                                [b_sg[q], b_tmp[q]], [b_cT[c][t]])
                    et_n = tr_n = at_n = 0
                    for h in range(4):
                        for qt in range(2):
                            Q0 = tok_base + qt * 512
                            started = set()
                            for kb in range(0, Q0 // 128 + 4):
                                k0 = kb * 128
                                c0 = max(0, k0 - Q0)
                                n = 512 - c0
                                d0 = min(Q0 + c0 - k0, 256)
                                if half == 1 and kb >= 8:
                                    Ks, Vs, bKs, bVs = curK, curV, b_curK[(kb - 8) // 4], b_curV[kb - 8]
                                    kk0, vb = k0 - 1024, kb - 8
                                else:
                                    Ks, Vs, bKs, bVs = kvK[l], kvV[l], b_kvK[l][kb // 4], b_kvV[l][kb]
                                    kk0, vb = k0, kb
                                for mi in range(2):
                                    sbk = (et_n % 2) * 2 + mi
                                    mm(ps[sbk][:, 0:n], Ks[64 * mi:64 * mi + 64, h, kk0:kk0 + 128],
                                       QT[64 * mi:64 * mi + 64, h, qt * 512 + c0:qt * 512 + 512],
                                       True, False, [bKs, b_QT[h][qt]], [b_ps[sbk]])
                                    mm(ps[sbk][:, 0:n], ident[:], biasT[:, h, d0:d0 + n], False, True, [b_const], [b_ps[sbk]])
                                    act(ET[:, sbk, 0:n], ps[sbk][:, 0:n], AF.Exp, [b_ps[sbk]], [b_ET[sbk]])
                                    for j in range(c0 // 128, 4):
                                        a = j * 2 + mi
                                        bank, off = 4 + a // 3, (a % 3) * 160
                                        st1 = bank not in started
                                        started.add(bank)
                                        P.op("pe", "matmul", ps[bank][:, off:off + 129],
                                             ET[:, sbk, j * 128 - c0:j * 128 - c0 + 128], Vs[:, vb, h, :],
                                             start=st1, stop=(k0 == Q0 + j * 128), skip_group_check=True,
                                             reads=[b_ET[sbk], bVs], writes=[b_ps[bank]])
                                et_n += 1
                                if k0 >= Q0:
                                    j = (k0 - Q0) // 128
                                    a1, a2 = j * 2, j * 2 + 1
                                    O1 = ps[4 + a1 // 3][:, (a1 % 3) * 160:(a1 % 3) * 160 + 129]
                                    O2 = ps[4 + a2 // 3][:, (a2 % 3) * 160:(a2 % 3) * 160 + 129]
                                    bO1, bO2 = b_ps[4 + a1 // 3], b_ps[4 + a2 // 3]
                                    z = at_n % 2
                                    at_n += 1
                                    st_, bst = att_s[z], b_atts[z]
                                    fa, bfa = att_f[2 * z], b_attf[2 * z]
                                    fo, bfo = att_f[2 * z + 1], b_attf[2 * z + 1]
                                    on, bon = att_b[z], b_attb[z]
                                    P.op("dve", "reciprocal", st_[:, 0:1], O1[:, 128:129], reads=[bO1], writes=[bst])
                                    P.op("dve", "reciprocal", st_[:, 1:2], O2[:, 128:129], reads=[bO2], writes=[bst])
                                    P.op("dve", "memset", st_[:, 3:4], 0.0, writes=[bst])
                                    tt("dve", st_[:, 2:3], st_[:, 1:2], nlam[:, l:l + 1], ALU.mult, [bst], [bst])
                                    ts("dve", fa[:], O1[:, 0:128], st_[:, 0:1], None, ALU.mult, None, [bst, bO1], [bfa])
                                    stt("dve", fo[:], O2[:, 0:128], st_[:, 2:3], fa[:], ALU.mult, ALU.add, [bst, bO2, bfa], [bfo])
                                    act(fa[:], fo[:], AF.Square, [bfo, bst], [bfa, bst], accum_out=st_[:, 3:4])
                                    ts("dve", st_[:, 4:5], st_[:, 3:4], 1.0 / 128.0, EPS, ALU.mult, ALU.add, [bst], [bst])
                                    tt("pool", st_[:, 5:6], st_[:, 4:5], nhalf[:, 0:1], ALU.pow, [bst], [bst])
                                    stt("dve", on[:], fo[:], st_[:, 5:6], subln[:, l, :], ALU.mult, ALU.mult, [bst, bfo], [bon])
                                    tsl = tr_n % 4
                                    tr_n += 1
                                    mm(ps[7][:, tsl * 128:(tsl + 1) * 128], on[:], ident[:], True, True, [bon], [b_ps[7]])
                                    qa = qt * 512 + j * 128
                                    act(OT[:, h, qa:qa + 128], ps[7][:, tsl * 128:(tsl + 1) * 128], AF.Copy,
                                        [b_ps[7]], [b_OT[h][qt]])
                    P.barrier()
                    srcs = [(cT, b_cT), (OT, b_OT), (p2T, b_p2T)]
                    for m in range(8):
                        wg, bwg = load_pack(d_mrgg[l, m], 3072)
                        wbo, bwbo = load_pack(d_mrgb[l, m], 1536)
                        wg4 = wg[:, 0:3072].rearrange("p (b k j) -> p b k j", b=3, k=8)
                        wbo4 = wbo[:, 0:1536].rearrange("p (b k j) -> p b k j", b=3, k=4)
                        for t in range(2):
                            for br in range(3):
                                bg, by = gen_bank(), gen_bank()
                                for k in range(8):
                                    mm(ps[bg][:], wg4[:, br, k, :], hT[:, k, t * 512:(t + 1) * 512], k == 0, k == 7,
                                       [bwg, b_hT[k][t]], [b_ps[bg]])
                                sT, bsT = srcs[br]
                                for k in range(4):
                                    mm(ps[by][:], wbo4[:, br, k, :], sT[:, k, t * 512:(t + 1) * 512], k == 0, k == 3,
                                       [bwbo, bsT[k][t]], [b_ps[by]])
                                q = br % 2
                                act(sg[q][:], ps[bg][:], AF.Tanh, [b_ps[bg]], [b_sg[q]], scale=0.5)
                                if br == 0:
                                    stt("dve", tmp[0][:], sg[q][:], 1.0, ps[by][:], ALU.add, ALU.mult, [b_sg[q], b_ps[by]], [b_tmp[0]])
                                else:
                                    stt("dve", tmp[1][:], sg[q][:], 1.0, ps[by][:], ALU.add, ALU.mult, [b_sg[q], b_ps[by]], [b_tmp[1]])
                                    if br == 1:
                                        tt("pool", tmp[0][:], tmp[0][:], tmp[1][:], ALU.add, [b_tmp[1]], [b_tmp[0]])
                                    else:
                                        tt("pool", mergedT[:, m, t * 512:(t + 1) * 512], tmp[0][:], tmp[1][:], ALU.add,
                                           [b_tmp[0], b_tmp[1]], [b_mg[m][t]])
                    for t in range(2):
                        for cg in range(2):
                            w_, bw_ = load_pack(d_wout[l, cg], 4096)
                            w3 = w_[:, :].rearrange("p (k c) -> p k c", k=8)
                            for m4 in range(4):
                                m = cg * 4 + m4
                                b = gen_bank()
                                for k in range(8):
                                    mm(ps[b][:], w3[:, k, m4 * 128:(m4 + 1) * 128], mergedT[:, k, t * 512:(t + 1) * 512],
                                       k == 0, k == 7, [bw_, b_mg[k][t]], [b_ps[b]])
                                evac(yevE[:, m, :], ps[b][:], [b_ps[b]], [b_yev[m]])
                        post_norm_residual(yevE, t, 8, l, 4.0 * EPS, t)
                    P.barrier()
                    for t in range(2):
                        pre_norm(h2T, b_h2, t, 16, l, t, 0)
                        for fg in range(8):
                            w_, bw_ = load_pack(d_w1[l, fg], 4096)
                            w3 = w_[:, :].rearrange("p (k c) -> p k c", k=8)
                            for i in range(4):
                                jf = fg * 4 + i
                                b = gen_bank()
                                for k in range(8):
                                    mm(ps[b][:], w3[:, k, i * 128:(i + 1) * 128], h2T[:, k, :], k == 0, k == 7,
                                       [bw_, b_h2[k]], [b_ps[b]])
                                q = jf % 2
                                act(relu_s[:, q, :], ps[b][:], AF.Relu, [b_ps[b]], [b_relu[q]])
                                eng = "pool" if jf % 4 != 3 else "dve"
                                tt(eng, f1[:, jf, :], relu_s[:, q, :], relu_s[:, q, :], ALU.mult, [b_relu[q]], [b_f1[jf]])
                        for jg in range(8):
                            w_, bw_ = load_pack(d_w2[l, jg], 4096)
                            w3 = w_[:, :].rearrange("p (j c) -> p j c", j=4)
                            for jj in range(4):
                                jf = jg * 4 + jj
                                for m in range(8):
                                    mm(ps[m][:], w3[:, jj, m * 128:(m + 1) * 128], f1[:, jf, :], jf == 0, jf == 31,
                                       [bw_, b_f1[jf]], [b_ps[m]])
                        for m in range(8):
                            evac(yevF[:, m, :], ps[m][:], [b_ps[m]], [b_yev[m]])
                        post_norm_residual(yevF, t, 24, l, EPS, t)
                    P.barrier()
                    P.dma("pool", "pin", pTs, d_pT[l, sq_i, :, tok_base:tok_base + 1024].rearrange("(k p) t -> p k t", p=128),
                          writes=[b_pTs])
                    for k in range(8):
                        for t in range(2):
                            if (k + t) % 2:
                                act(xbT[:, k, t * 512:(t + 1) * 512], xT[:, k, t * 512:(t + 1) * 512], AF.Copy,
                                    [b_x[k][t]], [b_xb[k][t]])
                            else:
                                P.op("pool", "tensor_copy", xbT[:, k, t * 512:(t + 1) * 512], xT[:, k, t * 512:(t + 1) * 512],
                                     reads=[b_x[k][t]], writes=[b_xb[k][t]])
                    wpe_, bwpe = load_pack(d_wpe[l], 2048)
                    wpe3 = wpe_[:, 0:2048].rearrange("p (k c) -> p k c", k=2)
                    for cg in range(2):
                        w_, bw_ = load_pack(d_wpg[l, cg], 4096)
                        w3 = w_[:, :].rearrange("p (k c) -> p k c", k=8)
                        for m4 in range(4):
                            m = cg * 4 + m4
                            for t in range(2):
                                bg, bp = gen_bank(), gen_bank()
                                for k in range(8):
                                    mm(ps[bg][:], w3[:, k, m4 * 128:(m4 + 1) * 128], xbT[:, k, t * 512:(t + 1) * 512],
                                       k == 0, k == 7, [bw_, b_xb[k][t]], [b_ps[bg]])
                                for k in range(2):
                                    mm(ps[bp][:], wpe3[:, k, m * 128:(m + 1) * 128], pTs[:, k, t * 512:(t + 1) * 512],
                                       k == 0, k == 1, [bwpe, b_pTs], [b_ps[bp]])
                                q = t % 2
                                act(sg[q][:], ps[bg][:], AF.Tanh, [b_ps[bg]], [b_sg[q]], scale=0.5)
                                act(relu_s[:, q, :], ps[bp][:], AF.Copy, [b_ps[bp]], [b_relu[q]], scale=0.5)
                                stt("dve", tmp[q][:], sg[q][:], 1.0, relu_s[:, q, :], ALU.add, ALU.mult, [b_sg[q], b_relu[q]], [b_tmp[q]])
                                xs = xT[:, m, t * 512:(t + 1) * 512]
                                tt("pool", xs, xs, tmp[q][:], ALU.add, [b_tmp[q]], [b_x[m][t]])
                    P.barrier()
                for k in range(8):
                    P.dma("sp", "xout", d_out[sq_i, k * 128:(k + 1) * 128, tok_base:tok_base + 1024], xT[:, k, :],
                          reads=[b_x[k][0], b_x[k][1]])
                P.barrier()
        counts = P.emit()
    return nc, counts


def _rel_bucket(n):
    n = np.maximum(n, 0)
    nf = np.maximum(n, 1).astype(np.float32)
    large = 16 + (np.log(nf / np.float32(16)) / np.float32(math.log(128 / 16)) * np.float32(16)).astype(np.int32)
    large = np.minimum(large, 31)
    return np.where(n < 16, n, large)


def host_layout(inp):
    f = np.float32
    g = {k: np.asarray(v, dtype=f) for k, v in inp.items()}
    L = DEPTH
    sh = {}
    w_in = g["w_in"]
    sh["win"] = np.ascontiguousarray(
        w_in[:, :, :3072].reshape(L, 8, 128, 6, 512).transpose(0, 3, 2, 1, 4)).reshape(L, 6, 128, 4096)
    gates = w_in[:, :, 3072:].reshape(L, 8, 128, 3, 8, 128)
    sh["mrgg"] = np.ascontiguousarray(gates.transpose(0, 4, 2, 3, 1, 5)).reshape(L, 8, 128, 3072)
    bo = np.stack([g["w_conv_out"], g["w_attn_out"], g["w_pool_out"]], axis=1)
    bo = bo.reshape(L, 3, 4, 128, 8, 128)
    sh["mrgb"] = np.ascontiguousarray(bo.transpose(0, 4, 3, 1, 2, 5)).reshape(L, 8, 128, 1536)

    def colgroups(w, ng):
        return np.ascontiguousarray(w.reshape(L, 8, 128, ng, 512).transpose(0, 3, 2, 1, 4)).reshape(L, ng, 128, 4096)

    sh["wout"] = colgroups(g["w_out"], 2)
    sh["w1"] = colgroups(g["w_mlp_in"], 8)
    sh["wpg"] = colgroups(g["w_ple_gate"], 2)
    sh["w2"] = np.ascontiguousarray(g["w_mlp_out"].reshape(L, 8, 4, 128, 1024).transpose(0, 1, 3, 2, 4)).reshape(L, 8, 128, 4096)
    sh["wpe"] = np.ascontiguousarray(g["w_ple_proj"].reshape(L, 2, 128, 1024).transpose(0, 2, 1, 3)).reshape(L, 128, 2048)
    sh["poolw"] = np.ascontiguousarray(g["pool_w"].transpose(0, 2, 1, 3)).reshape(L, 128, 512)
    vecs = np.zeros((L, 128, 172), f)

    def cols(v, n):
        return v.reshape(L, n, 128).transpose(0, 2, 1)
    vecs[:, :, 0:8] = cols(g["g_pre_mix"], 8)
    vecs[:, :, 8:16] = cols(g["g_post_mix"], 8)
    vecs[:, :, 16:24] = cols(g["g_pre_mlp"], 8)
    vecs[:, :, 24:32] = cols(g["g_post_mlp"], 8)
    vecs[:, :, 32:36] = cols(g["conv_dw_b"], 4)
    vecs[:, :, 36:40] = cols(g["conv_ln_g"], 4)
    vecs[:, :, 40:44] = cols(g["conv_ln_b"], 4)
    vecs[:, :, 44:48] = cols(g["pool_scale"], 4)
    cw = g["conv_dw_w"][:, :, 0, :].reshape(L, 31, 4, 128).transpose(0, 3, 2, 1)
    vecs[:, :, 48:172] = cw.reshape(L, 128, 124)
    sh["vecs"] = vecs
    sh["sublnb"] = np.ascontiguousarray(np.broadcast_to(g["subln_g"][:, None, :], (L, 128, 128)))
    sh["lamb"] = np.ascontiguousarray(np.broadcast_to(g["lam_p"].reshape(L, 1, 256), (L, 128, 256)))
    kk = np.arange(128)[:, None]
    dd = np.arange(768)[None, :]
    nrel = dd - kk
    bidx = _rel_bucket(nrel)
    tab = g["rel_bias"][bidx]
    tab = np.where((nrel >= 0)[:, :, None], tab, f(-1e30))
    sh["biasT"] = np.ascontiguousarray(tab.transpose(0, 2, 1)).reshape(128, 4 * 768).astype(f)
    sh["ident"] = np.eye(128, dtype=f)
    rc = np.zeros((128, 16), f)
    rc[:, :] = 1.0 / (np.arange(16, dtype=f) + 1.0)
    sh["rc"] = rc
    x = g["x"]
    p = g["p"]
    per_core = []
    for c in range(8):
        d = dict(sh)
        d["xT"] = np.ascontiguousarray(x[2 * c:2 * c + 2].transpose(0, 2, 1))
        d["pT"] = np.ascontiguousarray(p[:, 2 * c:2 * c + 2].transpose(0, 1, 3, 2))
        per_core.append(d)
    return per_core


_CACHE = {}


def kernel(**inputs):
    if "nc" not in _CACHE:
        _CACHE["nc"] = build_program()[0]
    nc = _CACHE["nc"]
    in_maps = host_layout(inputs)
    res = run_bass_kernel_spmd(nc, in_maps, core_ids=list(range(8)))
    out = np.empty((16, S, D), np.float32)
    for c in range(8):
        out[2 * c:2 * c + 2] = res.results[c]["outT"].transpose(0, 2, 1)
    return out
```

```python
import math
from contextlib import ExitStack

import numpy as np
import concourse.bass as bass
import concourse.mybir as mybir
from concourse.bass_utils import run_bass_kernel_spmd

F32 = mybir.dt.float32
BF16 = mybir.dt.bfloat16
ALU = mybir.AluOpType
AF = mybir.ActivationFunctionType
AX = mybir.AxisListType

DEPTH = 2
NSEQ = 2
S = 2048
D = 1024
EPS = 1e-6
LAMBDA_INIT = [0.8 - 0.6 * math.exp(-0.3 * i) for i in range(DEPTH)]

COMPUTE = ("pe", "act", "dve", "pool")
STREAMS = ("pe", "act", "dve", "pool", "sp")


class Buf:
    __slots__ = ("name", "lw", "rd")

    def __init__(self, name):
        self.name = name
        self.lw = None
        self.rd = []


class Op:
    __slots__ = ("stream", "fn", "waits", "key", "idx", "signal", "is_dma", "cnt")


class Prog:
    def __init__(self, nc):
        self.nc = nc
        self.ops = {s: [] for s in STREAMS}
        self.cnt = {}
        self.vc = {s: {} for s in STREAMS}
        self.opvc = {}
        self.bykey = {}
        self.dma_keys = []
        self.pend = {s: {} for s in STREAMS}
        self.marks = []
        self.dry = False

    def mark(self, name):
        if not self.dry:
            self.marks.append((name, len(self.ops["pe"])))

    def barrier(self):
        if self.dry:
            return
        snap = dict(self.cnt)
        for s in STREAMS:
            for k, i in snap.items():
                if self.pend[s].get(k, 0) < i:
                    self.pend[s][k] = i

    def _record(self, stream, key, fn, reads, writes, is_dma):
        if self.dry:
            return None
        deps = set()
        for b in reads:
            if b.lw is not None:
                deps.add(b.lw)
        for b in writes:
            if b.lw is not None:
                deps.add(b.lw)
            for r in b.rd:
                deps.add(r)
        for k, i in self.pend[stream].items():
            deps.add((k, i))
        self.pend[stream] = {}
        idx = self.cnt.get(key, 0) + 1
        self.cnt[key] = idx
        me = (key, idx)
        vc = self.vc[stream]
        waits = {}
        for (k, i) in deps:
            if k == stream and not is_dma:
                continue
            if vc.get(k, 0) >= i:
                continue
            if waits.get(k, 0) < i:
                waits[k] = i
        if stream in ("act", "dve", "pool") and not is_dma:
            for b in reads:
                if b.lw is not None and b.lw[0] == stream and vc.get(("self", stream), 0) < b.lw[1]:
                    if waits.get(stream, 0) < b.lw[1]:
                        waits[stream] = b.lw[1]
        for k, i in waits.items():
            if k == stream:
                vc[("self", stream)] = max(vc.get(("self", stream), 0), i)
                continue
            dvc = self.opvc.get((k, i))
            if dvc:
                for kk, ii in dvc.items():
                    if vc.get(kk, 0) < ii:
                        vc[kk] = ii
            if vc.get(k, 0) < i:
                vc[k] = i
        op = Op()
        op.stream, op.fn, op.waits, op.key, op.idx = stream, fn, waits, key, idx
        op.signal, op.is_dma, op.cnt = is_dma, is_dma, 0
        self.ops[stream].append(op)
        self.bykey[me] = op
        snap = {k: v for k, v in vc.items() if not isinstance(k, tuple) and k != stream}
        if not is_dma:
            snap[stream] = idx
        self.opvc[me] = snap
        for b in reads:
            b.rd.append(me)
        for b in writes:
            b.lw = me
            b.rd = []
        return op

    def op(self, stream, method, *args, reads=(), writes=(), **kw):
        return self._record(stream, stream, (method, args, kw), list(reads), list(writes), False)

    def dma(self, queue, semkey, out, in_, reads=(), writes=()):
        if semkey not in self.dma_keys:
            self.dma_keys.append(semkey)
        return self._record(queue, semkey, ("dma_start", (), {"out": out, "in_": in_}), list(reads), list(writes), True)

    def emit(self):
        nc = self.nc
        for s in STREAMS:
            for op in self.ops[s]:
                for k, i in op.waits.items():
                    if k in COMPUTE:
                        self.bykey[(k, i)].signal = True
        finals = {}
        for k in COMPUTE:
            if self.cnt.get(k, 0):
                self.bykey[(k, self.cnt[k])].signal = True
                finals[k] = self.cnt[k]
        for k in self.dma_keys:
            finals[k] = self.cnt[k]
        for k in COMPUTE:
            c = 0
            for i in range(1, self.cnt.get(k, 0) + 1):
                op = self.bykey[(k, i)]
                if op.signal:
                    c += 1
                op.cnt = c
        with ExitStack() as es:
            sems = {}
            for k in COMPUTE:
                if self.cnt.get(k, 0):
                    sems[k] = es.enter_context(nc.semaphore("s_" + k))
            for k in self.dma_keys:
                sems[k] = es.enter_context(nc.semaphore("d_" + str(k)))
            block = es.enter_context(nc.Block())

            def val(k, i):
                if k in COMPUTE:
                    return self.bykey[(k, i)].cnt
                return 16 * i

            def run(stream, eng):
                for op in self.ops[stream]:
                    for k, i in op.waits.items():
                        eng.wait_ge(sems[k], val(k, i))
                    meth, args, kw = op.fn
                    ins = getattr(eng, meth)(*args, **kw)
                    if op.is_dma:
                        ins.then_inc(sems[op.key], 16)
                    elif op.signal:
                        ins.then_inc(sems[op.key], 1)
                if stream == "sp":
                    for k, i in finals.items():
                        eng.wait_ge(sems[k], val(k, i))

            @block.tensor
            def _(e):
                run("pe", e)

            @block.scalar
            def _(e):
                run("act", e)

            @block.vector
            def _(e):
                run("dve", e)

            @block.gpsimd
            def _(e):
                run("pool", e)

            @block.sync
            def _(e):
                run("sp", e)
        return {s: len(self.ops[s]) for s in STREAMS}


def build_program(layers=(0, 1), nseq=NSEQ, halves=(0, 1)):
    nc = bass.Bass("TRN2", target_bir_lowering=False)
    P = Prog(nc)

    def dram(name, shape, kind="ExternalInput"):
        return nc.dram_tensor(name, list(shape), F32, kind=kind).ap()

    d_xT = dram("xT", [NSEQ, D, S])
    d_pT = dram("pT", [DEPTH, NSEQ, 256, S])
    d_out = dram("outT", [NSEQ, D, S], kind="ExternalOutput")
    d_win = dram("win", [DEPTH, 6, 128, 4096])
    d_mrgg = dram("mrgg", [DEPTH, 8, 128, 3072])
    d_mrgb = dram("mrgb", [DEPTH, 8, 128, 1536])
    d_wout = dram("wout", [DEPTH, 2, 128, 4096])
    d_w1 = dram("w1", [DEPTH, 8, 128, 4096])
    d_w2 = dram("w2", [DEPTH, 8, 128, 4096])
    d_wpg = dram("wpg", [DEPTH, 2, 128, 4096])
    d_wpe = dram("wpe", [DEPTH, 128, 2048])
    d_poolw = dram("poolw", [DEPTH, 128, 512])
    d_vecs = dram("vecs", [DEPTH, 128, 172])
    d_subln = dram("sublnb", [DEPTH, 128, 128])
    d_lam = dram("lamb", [DEPTH, 128, 256])
    d_bias = dram("biasT", [128, 4 * 768])
    d_ident = dram("ident", [128, 128])
    d_rc = dram("rc", [128, 16])

    es = ExitStack()
    with es:
        def sb(name, shape, dt):
            return es.enter_context(nc.sbuf_tensor(name, list(shape), dt))

        xT = sb("xT_s", [128, 8, 1024], F32)
        kvK = [sb(f"kvK{i}", [128, 4, 1024], BF16) for i in range(2)]
        kvV = [sb(f"kvV{i}", [128, 8, 4, 129], BF16) for i in range(2)]
        NRR = 39072
        RR = sb("RR", [128, NRR], BF16)
        S2 = [sb(f"S2_{i}", [128, 528], F32) for i in range(3)]
        sq = [sb(f"sq{i}", [128, 512], BF16) for i in range(2)]
        rstd = [sb(f"rstd{i}", [128, 512], F32) for i in range(2)]
        sg = [sb(f"sg{i}", [128, 512], F32) for i in range(2)]
        tmp = [sb(f"tmp{i}", [128, 512], F32) for i in range(2)]
        ring = [sb(f"ring{i}", [128, 4096], BF16) for i in range(3)]
        biasT = sb("biasT_s", [128, 4, 768], BF16)
        ident = sb("ident_s", [128, 128], BF16)
        onesD = sb("onesD", [128, 128], BF16)
        onesC = sb("onesC", [128, 128], BF16)
        vecs = sb("vecs_s", [128, DEPTH, 172], F32)
        vec2 = sb("vec2_s", [128, DEPTH, 12], F32)
        subln = sb("subln_s", [128, DEPTH, 128], F32)
        nlam = sb("nlam", [128, DEPTH], F32)
        rc = sb("rc_s", [128, 16], F32)
        nhalf = sb("nhalf", [128, 8], F32)
        onesF = sb("onesF", [1, 128], F32)
        rowb = [sb(f"rowb{i}", [1, 512], F32) for i in range(3)]
        identF = sb("identF", [128, 128], F32)
        onesFF = sb("onesFF", [128, 128], F32)
        colb = [sb(f"colb{i}", [128, 4], F32) for i in range(2)]
        NDG = 6
        diag = [sb(f"diag{i}", [128, 128], BF16) for i in range(NDG)]
        poolw = sb("poolw_s", [128, 4, 128], BF16)
        uhalo = sb("uhalo", [128, DEPTH, 4, 30], BF16)
        puhalo = sb("puhalo", [128, DEPTH, 4, 16], F32)
        att_f = [sb(f"attf{i}", [128, 128], F32) for i in range(4)]
        att_b = [sb(f"attb{i}", [128, 128], BF16) for i in range(2)]
        att_s = [sb(f"atts{i}", [128, 8], F32) for i in range(2)]
        ps = [es.enter_context(nc.psum_tensor(f"ps{i}", [128, 512], F32)) for i in range(8)]

        b_ps = [Buf(f"ps{i}") for i in range(8)]
        b_ring = [Buf(f"ring{i}") for i in range(3)]
        b_x = [[Buf(f"x{k}_{t}") for t in range(2)] for k in range(8)]
        b_sq = [Buf("sq0"), Buf("sq1")]
        b_rstd = [Buf("rstd0"), Buf("rstd1")]
        b_sg = [Buf("sg0"), Buf("sg1")]
        b_tmp = [Buf("tmp0"), Buf("tmp1")]
        b_S2 = [Buf(f"S2_{i}") for i in range(3)]
        b_const = Buf("const")
        b_kvK = [[Buf(f"kvK{i}_{t}") for t in range(2)] for i in range(2)]
        b_kvV = [[Buf(f"kvV{i}_{tb}") for tb in range(8)] for i in range(2)]
        b_uh = [Buf(f"uh{l}") for l in range(DEPTH)]
        b_ph = [[Buf(f"ph{l}_{g}") for g in range(4)] for l in range(DEPTH)]
        b_attf = [Buf(f"attf{i}") for i in range(4)]
        b_attb = [Buf(f"attb{i}") for i in range(2)]
        b_atts = [Buf(f"atts{i}") for i in range(2)]
        b_poolw = Buf("poolw")
        b_row = [Buf(f"row{i}") for i in range(3)]
        b_col = [Buf(f"col{i}") for i in range(2)]
        b_diag = [Buf(f"diag{i}") for i in range(6)]
        dg_n = [0]

        def rr3(off, a, b):
            return RR[:, off:off + a * b].rearrange("p (a b) -> p a b", a=a)

        def rrf(off_bf16, a, b):
            v = RR[:, off_bf16:off_bf16 + 2 * a * b].bitcast(F32)
            return v.rearrange("p (a b) -> p a b", a=a)

        curK = rr3(0, 4, 1024)
        curV = RR[:, 4096:4096 + 4128].rearrange("p (a h e) -> p a h e", a=8, h=4)
        hT = rr3(8224, 8, 1024)
        uT = rr3(16416, 4, 1054)
        QT = rr3(20632, 4, 1024)
        p2T = rr3(24728, 4, 1024)
        cT = rr3(28824, 4, 1024)
        OT = rr3(32920, 4, 1024)
        stash = rrf(32920, 4, 512)
        ET = rr3(37016, 4, 512)
        mergedT = rr3(0, 8, 1024)
        yevE = rrf(16416, 8, 512)
        h2T = rr3(0, 8, 512)
        f1 = rr3(4096, 32, 512)
        relu_s = rrf(20480, 2, 512)
        yevF = rrf(22528, 8, 512)
        xbT = rr3(0, 8, 1024)
        pTs = rr3(8192, 2, 1024)
        b_curK = [Buf("curK0"), Buf("curK1")]
        b_curV = [Buf(f"curV{i}") for i in range(8)]
        b_hT = [[Buf(f"hT{k}_{t}") for t in range(2)] for k in range(8)]
        b_uT = [[Buf(f"uT{c}_{t}") for t in range(2)] for c in range(4)]
        b_QT = [[Buf(f"QT{c}_{t}") for t in range(2)] for c in range(4)]
        b_p2T = [[Buf(f"p2T{c}_{t}") for t in range(2)] for c in range(4)]
        b_cT = [[Buf(f"cT{c}_{t}") for t in range(2)] for c in range(4)]
        b_OT = [[Buf(f"OT{c}_{t}") for t in range(2)] for c in range(4)]
        b_ET = [Buf(f"ET{i}") for i in range(4)]
        b_mg = [[Buf(f"mg{k}_{t}") for t in range(2)] for k in range(8)]
        b_yev = [Buf(f"yev{m}") for m in range(8)]
        b_h2 = [Buf(f"h2_{k}") for k in range(8)]
        b_f1 = [Buf(f"f1_{j}") for j in range(32)]
        b_relu = [Buf("relu0"), Buf("relu1")]
        b_xb = [[Buf(f"xb{k}_{t}") for t in range(2)] for k in range(8)]
        b_pTs = Buf("pTs")

        NSLOT = 3
        plan = []
        lastuse = {}
        st = {"pk": 0, "issued": 0, "dry_pk": 0}

        class PackTok:
            def __init__(self, k):
                self.k = k

        def load_pack(src_ap, nelem):
            if P.dry:
                plan.append((src_ap, nelem))
                k = st["dry_pk"]
                st["dry_pk"] += 1
                lastuse[k] = k + 1
                return ring[0], PackTok(k)
            k = st["pk"]
            while st["issued"] < len(plan):
                i = st["issued"]
                if i > k + NSLOT - 1:
                    break
                if i >= NSLOT and lastuse[i - NSLOT] > k:
                    assert i > k, "ring too small: slot-mate of the requested pack is still live"
                    break
                s = i % NSLOT
                P.dma("pool", f"ring{s}", ring[s][:, 0:plan[i][1]], plan[i][0], writes=[b_ring[s]])
                st["issued"] += 1
            assert st["issued"] > k
            st["pk"] += 1
            return ring[k % NSLOT], b_ring[k % NSLOT]

        gen_n = [0]

        def gen_bank():
            b = gen_n[0] % 6
            gen_n[0] += 1
            return b

        ev_n = [0]

        def evac(out, in_, reads, writes):
            ev_n[0] += 1
            if ev_n[0] % 2:
                P.op("act", "activation", out=out, in_=in_, func=AF.Copy, reads=reads, writes=writes)
            else:
                P.op("dve", "tensor_copy", out, in_, reads=reads, writes=writes)

        def mm(out, lhsT, rhs, start, stop, reads, writes):
            if P.dry:
                for b in reads:
                    if isinstance(b, PackTok):
                        lastuse[b.k] = st["dry_pk"]
                return
            P.op("pe", "matmul", out, lhsT, rhs, start=start, stop=stop, reads=reads, writes=writes)

        def tt(eng, out, in0, in1, op, reads, writes):
            P.op(eng, "tensor_tensor", out, in0, in1, op, reads=reads, writes=writes)

        def ts(eng, out, in0, s1, s2, op0, op1, reads, writes):
            if s2 is None:
                P.op(eng, "tensor_scalar", out, in0, s1, None, op0, reads=reads, writes=writes)
            else:
                P.op(eng, "tensor_scalar", out, in0, s1, s2, op0, op1, reads=reads, writes=writes)

        def stt(eng, out, in0, scalar, in1, op0, op1, reads, writes):
            P.op(eng, "scalar_tensor_tensor", out=out, in0=in0, scalar=scalar, in1=in1, op0=op0, op1=op1,
                 reads=reads, writes=writes)

        def act(out, in_, func, reads, writes, **kw):
            P.op("act", "activation", out=out, in_=in_, func=func, reads=reads, writes=writes, **kw)

        b_lamt, b_lt, b_nl, b_v2 = Buf("lamt"), Buf("lt"), Buf("nl"), Buf("v2")
        lamt = RR[:, 0:1024].bitcast(F32).rearrange("p (l c) -> p l c", l=DEPTH)
        lt = RR[:, 2048:2048 + 512].bitcast(F32)
        P.dma("sp", "c0", vecs[:], d_vecs.rearrange("l p c -> p l c"), writes=[b_const])
        P.dma("sp", "c0", subln[:], d_subln.rearrange("l p c -> p l c"), writes=[Buf("x1")])
        P.dma("sp", "c0", rc[:], d_rc, writes=[Buf("x2")])
        P.dma("sp", "c0", lamt, d_lam.rearrange("l p c -> p l c"), writes=[b_lamt])
        P.dma("pool", "c1", biasT[:], d_bias.rearrange("p (h c) -> p h c", h=4), writes=[Buf("x3")])
        P.dma("pool", "c1", ident[:], d_ident, writes=[Buf("x4")])
        P.dma("sp", "c0", identF[:], d_ident, writes=[Buf("x5")])
        P.barrier()
        P.op("dve", "memset", onesD[:], 1.0 / 1024.0, writes=[b_const])
        P.op("dve", "memset", nhalf[:], -0.5, writes=[b_const])
        P.op("dve", "memset", onesF[:], 1.0, writes=[b_const])
        P.op("dve", "memset", onesFF[:], 1.0, writes=[b_const])
        P.op("dve", "memset", onesC[:], 1.0 / 512.0, writes=[b_const])
        P.op("dve", "memset", uhalo[:], 0.0, writes=[b_const])
        P.op("dve", "memset", puhalo[:], 0.0, writes=[b_const])
        for i in range(2):
            P.op("dve", "memset", kvV[i][:, :, :, 128:129], 1.0, writes=[b_const])
        ssum = att_s[0]
        for l in range(DEPTH):
            ts("dve", vec2[:, l, 0:4], vecs[:, l, 32:36], 2.0, None, ALU.mult, None, [b_const], [b_v2])
            ts("dve", vec2[:, l, 4:12], vecs[:, l, 36:44], 0.5, None, ALU.mult, None, [b_const], [b_v2])
            ts("dve", subln[:, l, :], subln[:, l, :], 1.0 - LAMBDA_INIT[l], None, ALU.mult, None, [b_const], [b_v2])
            for j in range(2):
                tt("dve", lt[:, 64 * j:64 * j + 64], lamt[:, l, 128 * j:128 * j + 64],
                   lamt[:, l, 128 * j + 64:128 * j + 128], ALU.mult, [b_lamt], [b_lt])
                P.op("dve", "reduce_sum", ssum[:, 2 * l + j:2 * l + j + 1], lt[:, 64 * j:64 * j + 64], AX.X,
                     reads=[b_lt], writes=[b_atts[0]])
        act(ssum[:, 4:8], ssum[:, 0:4], AF.Exp, [b_atts[0]], [b_atts[0]])
        for l in range(DEPTH):
            tt("dve", nlam[:, l:l + 1], ssum[:, 5 + 2 * l:6 + 2 * l], ssum[:, 4 + 2 * l:5 + 2 * l], ALU.subtract,
               [b_atts[0]], [b_nl])
            ts("dve", nlam[:, l:l + 1], nlam[:, l:l + 1], -LAMBDA_INIT[l], None, ALU.add, None, [b_nl], [b_nl])
        P.barrier()

        def norm_rstd(srcs, src_bufs, eps, r):
            bank = 6 + (r % 2)
            for k in range(8):
                q = k % 2
                act(sq[q][:], srcs[k], AF.Square, [src_bufs[k]], [b_sq[q]])
                mm(ps[bank][0:1, :], onesD[:, 0:1], sq[q][:], k == 0, k == 7, [b_sq[q]], [b_ps[bank]])
            ts("dve", rowb[r][:], ps[bank][0:1, :], eps, None, ALU.add, None, [b_ps[bank]], [b_row[r]])
            return rsqrt_bcast(r, bank)

        def rsqrt_bcast(r, bank):
            for blk in range(4):
                mm(ps[bank][:, blk:blk + 1], rowb[r][0:1, blk * 128:(blk + 1) * 128], onesF[0:1, 0:1], True, True,
                   [b_row[r]], [b_ps[bank]])
            P.op("dve", "tensor_copy", colb[r % 2][:], ps[bank][:, 0:4], reads=[b_ps[bank]], writes=[b_col[r % 2]])
            tt("pool", colb[r % 2][:], colb[r % 2][:], nhalf[:, 0:4], ALU.pow, [b_col[r % 2]], [b_col[r % 2]])
            for blk in range(4):
                ts("dve", rstd[r % 2][:, blk * 128:(blk + 1) * 128], identF[:], colb[r % 2][:, blk:blk + 1], None, ALU.mult, None,
                   [b_col[r % 2]], [b_rstd[r % 2]])
            for blk in range(4):
                mm(ps[bank][:, blk * 128:(blk + 1) * 128], onesFF[:], rstd[r % 2][:, blk * 128:(blk + 1) * 128], True, True,
                   [b_rstd[r % 2]], [b_ps[bank]])
            return ps[bank][:], b_ps[bank]

        def post_norm_residual(yev, t, gcol, l, eps, r):
            rs, brs = norm_rstd([yev[:, m, :] for m in range(8)], b_yev, eps, r)
            for m in range(8):
                q = m % 2
                stt("dve", tmp[q][:], yev[:, m, :], vecs[:, l, gcol + m:gcol + m + 1], rs, ALU.mult, ALU.mult,
                    [b_yev[m], brs], [b_tmp[q]])
                xs = xT[:, m, t * 512:(t + 1) * 512]
                tt("dve", xs, xs, tmp[q][:], ALU.add, [b_tmp[q]], [b_x[m][t]])

        def pre_norm(dst, dst_bufs, t, gcol, l, r, tok0):
            rs, brs = norm_rstd([xT[:, k, t * 512:(t + 1) * 512] for k in range(8)], [b_x[k][t] for k in range(8)], EPS, r)
            for k in range(8):
                stt("dve", dst[:, k, tok0:tok0 + 512], xT[:, k, t * 512:(t + 1) * 512],
                    vecs[:, l, gcol + k:gcol + k + 1], rs, ALU.mult, ALU.mult,
                    [b_x[k][t], brs], [dst_bufs[k]])

        def walk():
            gen_n[0] = 0
            ev_n[0] = 0
            dg_n[0] = 0
            for sq_i in range(nseq):
                for half in halves:
                    tok_base = half * 1024
                    for k in range(8):
                        P.dma("sp", "xin", xT[:, k, :], d_xT[sq_i, k * 128:(k + 1) * 128, tok_base:tok_base + 1024],
                              writes=[b_x[k][0], b_x[k][1]])
                    P.barrier()
                    for l in layers:
                        if half == 0:
                            Kd, Vd, bKd, bVd = kvK[l], kvV[l], b_kvK[l], b_kvV[l]
                        else:
                            Kd, Vd, bKd, bVd = curK, curV, b_curK, b_curV
                            P.op("dve", "memset", curV[:, :, :, 128:129], 1.0, writes=b_curV)
                        P.mark(f"A s{sq_i} h{half} l{l}")
                        for t in range(2):
                            pre_norm(hT, [b_hT[k][t] for k in range(8)], t, 0, l, t, t * 512)
                        P.dma("pool", "pw", poolw[:], d_poolw[l].rearrange("p (g c) -> p g c", g=4), writes=[b_poolw])
                        P.mark("B")
                        wa, bwa = load_pack(d_win[l, 0], 4096)
                        wb, bwb = load_pack(d_win[l, 1], 4096)
                        wa3 = wa[:, :].rearrange("p (k c) -> p k c", k=8)
                        wb3 = wb[:, :].rearrange("p (k c) -> p k c", k=8)
                        for c in range(4):
                            if half == 0:
                                P.op("dve", "memset", uT[:, c, 0:30], 0.0, writes=[b_uT[c][0]])
                            else:
                                P.op("dve", "tensor_copy", uT[:, c, 0:30], uhalo[:, l, c, :], reads=[b_uh[l]], writes=[b_uT[c][0]])
                        for c in range(4):
                            for t in range(2):
                                ba, bb = gen_bank(), gen_bank()
                                for k in range(8):
                                    mm(ps[ba][:], wa3[:, k, c * 128:(c + 1) * 128], hT[:, k, t * 512:(t + 1) * 512],
                                       k == 0, k == 7, [bwa, b_hT[k][t]], [b_ps[ba]])
                                for k in range(8):
                                    mm(ps[bb][:], wb3[:, k, c * 128:(c + 1) * 128], hT[:, k, t * 512:(t + 1) * 512],
                                       k == 0, k == 7, [bwb, b_hT[k][t]], [b_ps[bb]])
                                q = t % 2
                                act(sg[q][:], ps[bb][:], AF.Tanh, [b_ps[bb]], [b_sg[q]], scale=0.5)
                                stt("dve", uT[:, c, 30 + t * 512:30 + (t + 1) * 512], sg[q][:], 1.0, ps[ba][:], ALU.add, ALU.mult,
                                    [b_sg[q], b_ps[ba]], [b_uT[c][t]])
                        if half == 0:
                            for c in range(4):
                                P.op("dve", "tensor_copy", uhalo[:, l, c, :], uT[:, c, 1024:1054],
                                     reads=[b_uT[c][1]], writes=[b_uh[l]])
                        for which in (2, 3):
                            w_, bw_ = load_pack(d_win[l, which], 4096)
                            w3 = w_[:, :].rearrange("p (k c) -> p k c", k=8)
                            for c in range(4):
                                for t in range(2):
                                    b = gen_bank()
                                    for k in range(8):
                                        mm(ps[b][:], w3[:, k, c * 128:(c + 1) * 128], hT[:, k, t * 512:(t + 1) * 512],
                                           k == 0, k == 7, [bw_, b_hT[k][t]], [b_ps[b]])
                                    if which == 2:
                                        act(QT[:, c, t * 512:(t + 1) * 512], ps[b][:], AF.Copy, [b_ps[b]], [b_QT[c][t]], scale=0.125)
                                    else:
                                        P.op("dve", "tensor_copy", Kd[:, c, t * 512:(t + 1) * 512], ps[b][:],
                                             reads=[b_ps[b]], writes=[bKd[t]])
                        w_, bw_ = load_pack(d_win[l, 4], 4096)
                        w3 = w_[:, :].rearrange("p (k c) -> p k c", k=8)
                        for tb in range(8):
                            b = gen_bank()
                            for k in range(8):
                                mm(ps[b][:], hT[:, k, tb * 128:(tb + 1) * 128], w3[:, k, :],
                                   k == 0, k == 7, [bw_, b_hT[k][tb // 4]], [b_ps[b]])
                            evac(Vd[:, tb, :, 0:128], ps[b][:].rearrange("p (h e) -> p h e", h=4), [b_ps[b]], [bVd[tb]])
                        w_, bw_ = load_pack(d_win[l, 5], 4096)
                        w3 = w_[:, :].rearrange("p (k c) -> p k c", k=8)
                        B3 = [(S2[0], b_S2[0]), (S2[1], b_S2[1]), (S2[2], b_S2[2])]
                        pu, bpu = B3[0]
                        for g in range(4):
                            win_w = 2 ** (g + 1)
                            for t in range(2):
                                b = gen_bank()
                                for k in range(8):
                                    mm(ps[b][:], w3[:, k, g * 128:(g + 1) * 128], hT[:, k, t * 512:(t + 1) * 512],
                                       k == 0, k == 7, [bw_, b_hT[k][t]], [b_ps[b]])
                                if half == 0 and t == 0:
                                    P.op("dve", "memset", pu[:, 0:16], 0.0, writes=[bpu])
                                else:
                                    P.op("dve", "tensor_copy", pu[:, 0:16], puhalo[:, l, g, :], reads=[b_ph[l][g]], writes=[bpu])
                                act(pu[:, 16:528], ps[b][:], AF.Copy, [b_ps[b]], [bpu])
                                P.op("dve", "tensor_copy", puhalo[:, l, g, :], pu[:, 512:528], reads=[bpu], writes=[b_ph[l][g]])
                                si, sh, lo = 0, 1, 1
                                for st in range(g + 1):
                                    di = 1 if si != 1 else 2
                                    lo += sh
                                    sv, sbf = B3[si]
                                    dv, dbf = B3[di]
                                    tt("dve", dv[:, lo:528], sv[:, lo:528], sv[:, lo - sh:528 - sh], ALU.add, [sbf], [dbf])
                                    si, sh = di, sh * 2
                                av, abf = B3[si]
                                fi = 2 if si == 1 else 1
                                fv, fbf = B3[fi]
                                pooled = fv[:, 0:256].bitcast(BF16)
                                stt("dve", pooled, av[:, 16:528], 1.0 / win_w, pu[:, 16:528], ALU.mult, ALU.subtract,
                                    [abf, bpu], [fbf])
                                if half == 0 and t == 0:
                                    nfix = win_w - 1
                                    tt("dve", tmp[0][:, 0:nfix], av[:, 16:16 + nfix], rc[:, 0:nfix], ALU.mult, [abf], [b_tmp[0]])
                                    tt("dve", pooled[:, 0:nfix], tmp[0][:, 0:nfix], pu[:, 16:16 + nfix], ALU.subtract,
                                       [b_tmp[0], bpu], [fbf])
                                b2 = gen_bank()
                                mm(ps[b2][:], poolw[:, g, :], pooled, True, True, [b_poolw, fbf], [b_ps[b2]])
                                act(p2T[:, g, t * 512:(t + 1) * 512], ps[b2][:], AF.Copy, [b_ps[b2]], [b_p2T[g][t]],
                                    scale=vecs[:, l, 44 + g:45 + g])
                        P.mark("C")
                        for t in range(2):
                            base = t * 512
                            for c in range(4):
                                bk = gen_bank()
                                for j in range(31):
                                    sl = dg_n[0] % NDG
                                    dg_n[0] += 1
                                    ts("dve", diag[sl][:], ident[:], vecs[:, l, 48 + c * 31 + j:49 + c * 31 + j], None, ALU.mult, None,
                                       [b_const], [b_diag[sl]])
                                    mm(ps[bk][:], diag[sl][:], uT[:, c, base + j:base + j + 512], j == 0, j == 30,
                                       [b_diag[sl], b_uT[c][0], b_uT[c][1]], [b_ps[bk]])
                                ts("dve", stash[:, c, :], ps[bk][:], vec2[:, l, c:c + 1], None, ALU.add, None,
                                   [b_ps[bk]], [b_OT[c][0], b_OT[c][1]])
                            for c in range(4):
                                q = c % 2
                                act(sq[q][:], stash[:, c, :], AF.Copy, [b_OT[c][0]], [b_sq[q]])
                                mm(ps[6][0:1, :], onesC[:, 0:1], sq[q][:], c == 0, c == 3, [b_sq[q]], [b_ps[6]])
                            for c in range(4):
                                q = c % 2
                                act(sq[q][:], stash[:, c, :], AF.Square, [b_OT[c][0]], [b_sq[q]])
                                mm(ps[7][0:1, :], onesC[:, 0:1], sq[q][:], c == 0, c == 3, [b_sq[q]], [b_ps[7]])
                            P.op("dve", "tensor_copy", rowb[2][:], ps[6][0:1, :], reads=[b_ps[6]], writes=[b_row[2]])
                            tt("dve", rowb[0][:], rowb[2][:], rowb[2][:], ALU.mult, [b_row[2]], [b_row[0]])
                            tt("dve", rowb[0][:], ps[7][0:1, :], rowb[0][:], ALU.subtract, [b_ps[7], b_row[0]], [b_row[0]])
                            ts("dve", rowb[0][:], rowb[0][:], 4.0 * EPS, None, ALU.add, None, [b_row[0]], [b_row[0]])
                            mm(ps[6][:], onesF[:], rowb[2][:], True, True, [b_row[2]], [b_ps[6]])
                            rsqrt_bcast(0, 7)
                            for c in range(4):
                                q = c % 2
                                tt("dve", tmp[q][:], stash[:, c, :], ps[6][:], ALU.subtract, [b_OT[c][0], b_ps[6]], [b_tmp[q]])
                                tt("dve", tmp[q][:], tmp[q][:], ps[7][:], ALU.mult, [b_tmp[q], b_ps[7]], [b_tmp[q]])
                                ts("dve", tmp[q][:], tmp[q][:], vec2[:, l, 4 + c:5 + c], vec2[:, l, 8 + c:9 + c], ALU.mult, ALU.add,
                                   [b_tmp[q]], [b_tmp[q]])
                                act(sg[q][:], tmp[q][:], AF.Tanh, [b_tmp[q]], [b_sg[q]])
                                stt("dve", cT[:, c, t * 512:(t + 1) * 512], sg[q][:], 1.0, tmp[q][:], ALU.add, ALU.mult,
                                    [b_sg[q], b_tmp[q]], [b_cT[c][t]])
                        P.mark("D")
                        blocks = []
                        for h in range(4):
                            for qt in range(2):
                                Q0 = tok_base + qt * 512
                                for kb in range(0, Q0 // 128 + 4):
                                    blocks.append((h, qt, Q0, kb))
                        nb = len(blocks)
                        state = {"started": set(), "unit": None, "at_n": 0, "tr_n": 0}
                        pend_tr = {}

                        def blk_params(i):
                            h, qt, Q0, kb = blocks[i]
                            k0 = kb * 128
                            c0 = max(0, k0 - Q0)
                            n = 512 - c0
                            d0 = min(Q0 + c0 - k0, 256)
                            if half == 1 and kb >= 8:
                                Ks, Vs, bKs, bVs = curK, curV, b_curK[(kb - 8) // 4], b_curV[kb - 8]
                                kk0, vb = k0 - 1024, kb - 8
                            else:
                                Ks, Vs, bKs, bVs = kvK[l], kvV[l], b_kvK[l][kb // 4], b_kvV[l][kb]
                                kk0, vb = k0, kb
                            return h, qt, Q0, kb, k0, c0, n, d0, Ks, Vs, bKs, bVs, kk0, vb

                        def rec_S(i):
                            h, qt, Q0, kb, k0, c0, n, d0, Ks, Vs, bKs, bVs, kk0, vb = blk_params(i)
                            for mi in range(2):
                                sbk = (i % 2) * 2 + mi
                                mm(ps[sbk][:, 0:n], Ks[64 * mi:64 * mi + 64, h, kk0:kk0 + 128],
                                   QT[64 * mi:64 * mi + 64, h, qt * 512 + c0:qt * 512 + 512],
                                   True, False, [bKs, b_QT[h][qt]], [b_ps[sbk]])
                                mm(ps[sbk][:, 0:n], ident[:], biasT[:, h, d0:d0 + n], False, True, [b_const], [b_ps[sbk]])
                                act(ET[:, sbk, 0:n], ps[sbk][:, 0:n], AF.Exp, [b_ps[sbk]], [b_ET[sbk]])

                        def rec_AV(i):
                            h, qt, Q0, kb, k0, c0, n, d0, Ks, Vs, bKs, bVs, kk0, vb = blk_params(i)
                            if state["unit"] != (h, qt):
                                state["unit"] = (h, qt)
                                state["started"] = set()
                            started = state["started"]
                            for mi in range(2):
                                sbk = (i % 2) * 2 + mi
                                for j in range(c0 // 128, 4):
                                    a = j * 2 + mi
                                    bank, off = 4 + a // 3, (a % 3) * 160
                                    st1 = bank not in started
                                    started.add(bank)
                                    P.op("pe", "matmul", ps[bank][:, off:off + 129],
                                         ET[:, sbk, j * 128 - c0:j * 128 - c0 + 128], Vs[:, vb, h, :],
                                         start=st1, stop=(k0 == Q0 + j * 128), skip_group_check=True,
                                         reads=[b_ET[sbk], bVs], writes=[b_ps[bank]])
                            if k0 >= Q0:
                                j = (k0 - Q0) // 128
                                a1, a2 = j * 2, j * 2 + 1
                                O1 = ps[4 + a1 // 3][:, (a1 % 3) * 160:(a1 % 3) * 160 + 129]
                                O2 = ps[4 + a2 // 3][:, (a2 % 3) * 160:(a2 % 3) * 160 + 129]
                                bO1, bO2 = b_ps[4 + a1 // 3], b_ps[4 + a2 // 3]
                                z = state["at_n"] % 2
                                state["at_n"] += 1
                                st_, bst = att_s[z], b_atts[z]
                                fa, bfa = att_f[2 * z], b_attf[2 * z]
                                fo, bfo = att_f[2 * z + 1], b_attf[2 * z + 1]
                                on, bon = att_b[z], b_attb[z]
                                P.op("dve", "reciprocal", st_[:, 0:1], O1[:, 128:129], reads=[bO1], writes=[bst])
                                P.op("dve", "reciprocal", st_[:, 1:2], O2[:, 128:129], reads=[bO2], writes=[bst])
                                P.op("dve", "memset", st_[:, 3:4], 0.0, writes=[bst])
                                tt("dve", st_[:, 2:3], st_[:, 1:2], nlam[:, l:l + 1], ALU.mult, [bst], [bst])
                                ts("dve", fa[:], O1[:, 0:128], st_[:, 0:1], None, ALU.mult, None, [bst, bO1], [bfa])
                                stt("dve", fo[:], O2[:, 0:128], st_[:, 2:3], fa[:], ALU.mult, ALU.add, [bst, bO2, bfa], [bfo])
                                act(fa[:], fo[:], AF.Square, [bfo, bst], [bfa, bst], accum_out=st_[:, 3:4])
                                ts("dve", st_[:, 4:5], st_[:, 3:4], 1.0 / 128.0, EPS, ALU.mult, ALU.add, [bst], [bst])
                                tt("pool", st_[:, 5:6], st_[:, 4:5], nhalf[:, 0:1], ALU.pow, [bst], [bst])
                                stt("dve", on[:], fo[:], st_[:, 5:6], subln[:, l, :], ALU.mult, ALU.mult, [bst, bfo], [bon])
                                pend_tr[i] = (on, bon, h, qt * 512 + j * 128, qt)

                        def rec_TR(i):
                            if i not in pend_tr:
                                return
                            on, bon, h, qa, qt = pend_tr.pop(i)
                            tsl = state["tr_n"] % 4
                            state["tr_n"] += 1
                            mm(ps[7][:, tsl * 128:(tsl + 1) * 128], on[:], ident[:], True, True, [bon], [b_ps[7]])
                            act(OT[:, h, qa:qa + 128], ps[7][:, tsl * 128:(tsl + 1) * 128], AF.Copy, [b_ps[7]], [b_OT[h][qt]])

                        for i in range(nb + 2):
                            if i < nb:
                                rec_S(i)
                            if 0 <= i - 1 < nb:
                                rec_AV(i - 1)
                            if 0 <= i - 2 < nb:
                                rec_TR(i - 2)
                        P.barrier()
                        P.mark("E")
                        srcs = [(cT, b_cT), (OT, b_OT), (p2T, b_p2T)]
                        for m in range(8):
                            wg, bwg = load_pack(d_mrgg[l, m], 3072)
                            wbo, bwbo = load_pack(d_mrgb[l, m], 1536)
                            wg4 = wg[:, 0:3072].rearrange("p (b k j) -> p b k j", b=3, k=8)
                            wbo4 = wbo[:, 0:1536].rearrange("p (b k j) -> p b k j", b=3, k=4)
                            for t in range(2):
                                for br in range(3):
                                    bg, by = gen_bank(), gen_bank()
                                    for k in range(8):
                                        mm(ps[bg][:], wg4[:, br, k, :], hT[:, k, t * 512:(t + 1) * 512], k == 0, k == 7,
                                           [bwg, b_hT[k][t]], [b_ps[bg]])
                                    sT, bsT = srcs[br]
                                    for k in range(4):
                                        mm(ps[by][:], wbo4[:, br, k, :], sT[:, k, t * 512:(t + 1) * 512], k == 0, k == 3,
                                           [bwbo, bsT[k][t]], [b_ps[by]])
                                    q = br % 2
                                    act(sg[q][:], ps[bg][:], AF.Tanh, [b_ps[bg]], [b_sg[q]], scale=0.5)
                                    if br == 0:
                                        stt("dve", tmp[0][:], sg[q][:], 1.0, ps[by][:], ALU.add, ALU.mult, [b_sg[q], b_ps[by]], [b_tmp[0]])
                                    else:
                                        stt("dve", tmp[1][:], sg[q][:], 1.0, ps[by][:], ALU.add, ALU.mult, [b_sg[q], b_ps[by]], [b_tmp[1]])
                                        if br == 1:
                                            tt("dve", tmp[0][:], tmp[0][:], tmp[1][:], ALU.add, [b_tmp[1]], [b_tmp[0]])
                                        else:
                                            tt("dve", mergedT[:, m, t * 512:(t + 1) * 512], tmp[0][:], tmp[1][:], ALU.add,
                                               [b_tmp[0], b_tmp[1]], [b_mg[m][t]])
                        for t in range(2):
                            for cg in range(2):
                                w_, bw_ = load_pack(d_wout[l, cg], 4096)
                                w3 = w_[:, :].rearrange("p (k c) -> p k c", k=8)
                                for m4 in range(4):
                                    m = cg * 4 + m4
                                    b = gen_bank()
                                    for k in range(8):
                                        mm(ps[b][:], w3[:, k, m4 * 128:(m4 + 1) * 128], mergedT[:, k, t * 512:(t + 1) * 512],
                                           k == 0, k == 7, [bw_, b_mg[k][t]], [b_ps[b]])
                                    evac(yevE[:, m, :], ps[b][:], [b_ps[b]], [b_yev[m]])
                            post_norm_residual(yevE, t, 8, l, 4.0 * EPS, t)
                        P.barrier()
                        P.mark("F")
                        for t in range(2):
                            pre_norm(h2T, b_h2, t, 16, l, t, 0)
                            for fg in range(8):
                                w_, bw_ = load_pack(d_w1[l, fg], 4096)
                                w3 = w_[:, :].rearrange("p (k c) -> p k c", k=8)
                                for i in range(4):
                                    jf = fg * 4 + i
                                    b = gen_bank()
                                    for k in range(8):
                                        mm(ps[b][:], w3[:, k, i * 128:(i + 1) * 128], h2T[:, k, :], k == 0, k == 7,
                                           [bw_, b_h2[k]], [b_ps[b]])
                                    q = jf % 2
                                    act(relu_s[:, q, :], ps[b][:], AF.Relu, [b_ps[b]], [b_relu[q]])
                                    tt("dve", f1[:, jf, :], relu_s[:, q, :], relu_s[:, q, :], ALU.mult, [b_relu[q]], [b_f1[jf]])
                            for jg in range(8):
                                w_, bw_ = load_pack(d_w2[l, jg], 4096)
                                w3 = w_[:, :].rearrange("p (j c) -> p j c", j=4)
                                for jj in range(4):
                                    jf = jg * 4 + jj
                                    for m in range(8):
                                        mm(ps[m][:], w3[:, jj, m * 128:(m + 1) * 128], f1[:, jf, :], jf == 0, jf == 31,
                                           [bw_, b_f1[jf]], [b_ps[m]])
                            for m in range(8):
                                evac(yevF[:, m, :], ps[m][:], [b_ps[m]], [b_yev[m]])
                            post_norm_residual(yevF, t, 24, l, EPS, t)
                        P.barrier()
                        P.mark("G")
                        P.dma("pool", "pin", pTs, d_pT[l, sq_i, :, tok_base:tok_base + 1024].rearrange("(k p) t -> p k t", p=128),
                              writes=[b_pTs])
                        for k in range(8):
                            for t in range(2):
                                if (k + t) % 2:
                                    act(xbT[:, k, t * 512:(t + 1) * 512], xT[:, k, t * 512:(t + 1) * 512], AF.Copy,
                                        [b_x[k][t]], [b_xb[k][t]])
                                else:
                                    P.op("dve", "tensor_copy", xbT[:, k, t * 512:(t + 1) * 512], xT[:, k, t * 512:(t + 1) * 512],
                                         reads=[b_x[k][t]], writes=[b_xb[k][t]])
                        wpe_, bwpe = load_pack(d_wpe[l], 2048)
                        wpe3 = wpe_[:, 0:2048].rearrange("p (k c) -> p k c", k=2)
                        for cg in range(2):
                            w_, bw_ = load_pack(d_wpg[l, cg], 4096)
                            w3 = w_[:, :].rearrange("p (k c) -> p k c", k=8)
                            for m4 in range(4):
                                m = cg * 4 + m4
                                for t in range(2):
                                    bg, bp = gen_bank(), gen_bank()
                                    for k in range(8):
                                        mm(ps[bg][:], w3[:, k, m4 * 128:(m4 + 1) * 128], xbT[:, k, t * 512:(t + 1) * 512],
                                           k == 0, k == 7, [bw_, b_xb[k][t]], [b_ps[bg]])
                                    for k in range(2):
                                        mm(ps[bp][:], wpe3[:, k, m * 128:(m + 1) * 128], pTs[:, k, t * 512:(t + 1) * 512],
                                           k == 0, k == 1, [bwpe, b_pTs], [b_ps[bp]])
                                    q = t % 2
                                    act(sg[q][:], ps[bg][:], AF.Tanh, [b_ps[bg]], [b_sg[q]], scale=0.5)
                                    act(relu_s[:, q, :], ps[bp][:], AF.Copy, [b_ps[bp]], [b_relu[q]], scale=0.5)
                                    stt("dve", tmp[q][:], sg[q][:], 1.0, relu_s[:, q, :], ALU.add, ALU.mult, [b_sg[q], b_relu[q]], [b_tmp[q]])
                                    xs = xT[:, m, t * 512:(t + 1) * 512]
                                    tt("dve", xs, xs, tmp[q][:], ALU.add, [b_tmp[q]], [b_x[m][t]])
                        P.barrier()
                    for k in range(8):
                        P.dma("sp", "xout", d_out[sq_i, k * 128:(k + 1) * 128, tok_base:tok_base + 1024], xT[:, k, :],
                              reads=[b_x[k][0], b_x[k][1]])
                    P.barrier()
        P.dry = True
        walk()
        P.dry = False
        walk()
        counts = P.emit()
        counts["marks"] = P.marks
    return nc, counts


def _rel_bucket(n):
    n = np.maximum(n, 0)
    nf = np.maximum(n, 1).astype(np.float32)
    large = 16 + (np.log(nf / np.float32(16)) / np.float32(math.log(128 / 16)) * np.float32(16)).astype(np.int32)
    large = np.minimum(large, 31)
    return np.where(n < 16, n, large)


def host_layout(inp):
    f = np.float32
    g = {k: np.asarray(v, dtype=f) for k, v in inp.items()}
    L = DEPTH
    sh = {}
    w_in = g["w_in"]
    sh["win"] = np.ascontiguousarray(
        w_in[:, :, :3072].reshape(L, 8, 128, 6, 512).transpose(0, 3, 2, 1, 4)).reshape(L, 6, 128, 4096)
    gates = w_in[:, :, 3072:].reshape(L, 8, 128, 3, 8, 128)
    sh["mrgg"] = np.ascontiguousarray(gates.transpose(0, 4, 2, 3, 1, 5)).reshape(L, 8, 128, 3072)
    bo = np.stack([g["w_conv_out"], g["w_attn_out"], g["w_pool_out"]], axis=1)
    bo = bo.reshape(L, 3, 4, 128, 8, 128)
    sh["mrgb"] = np.ascontiguousarray(bo.transpose(0, 4, 3, 1, 2, 5)).reshape(L, 8, 128, 1536)

    def colgroups(w, ng):
        return np.ascontiguousarray(w.reshape(L, 8, 128, ng, 512).transpose(0, 3, 2, 1, 4)).reshape(L, ng, 128, 4096)

    sh["wout"] = colgroups(g["w_out"], 2)
    sh["w1"] = colgroups(g["w_mlp_in"], 8)
    sh["wpg"] = colgroups(g["w_ple_gate"], 2)
    sh["w2"] = np.ascontiguousarray(g["w_mlp_out"].reshape(L, 8, 4, 128, 1024).transpose(0, 1, 3, 2, 4)).reshape(L, 8, 128, 4096)
    sh["wpe"] = np.ascontiguousarray(g["w_ple_proj"].reshape(L, 2, 128, 1024).transpose(0, 2, 1, 3)).reshape(L, 128, 2048)
    sh["poolw"] = np.ascontiguousarray(g["pool_w"].transpose(0, 2, 1, 3)).reshape(L, 128, 512)
    vecs = np.zeros((L, 128, 172), f)

    def cols(v, n):
        return v.reshape(L, n, 128).transpose(0, 2, 1)
    vecs[:, :, 0:8] = cols(g["g_pre_mix"], 8)
    vecs[:, :, 8:16] = cols(g["g_post_mix"], 8)
    vecs[:, :, 16:24] = cols(g["g_pre_mlp"], 8)
    vecs[:, :, 24:32] = cols(g["g_post_mlp"], 8)
    vecs[:, :, 32:36] = cols(g["conv_dw_b"], 4)
    vecs[:, :, 36:40] = cols(g["conv_ln_g"], 4)
    vecs[:, :, 40:44] = cols(g["conv_ln_b"], 4)
    vecs[:, :, 44:48] = cols(g["pool_scale"], 4)
    cw = g["conv_dw_w"][:, :, 0, :].reshape(L, 31, 4, 128).transpose(0, 3, 2, 1)
    vecs[:, :, 48:172] = cw.reshape(L, 128, 124)
    sh["vecs"] = vecs
    sh["sublnb"] = np.ascontiguousarray(np.broadcast_to(g["subln_g"][:, None, :], (L, 128, 128)))
    sh["lamb"] = np.ascontiguousarray(np.broadcast_to(g["lam_p"].reshape(L, 1, 256), (L, 128, 256)))
    kk = np.arange(128)[:, None]
    dd = np.arange(768)[None, :]
    nrel = dd - kk
    bidx = _rel_bucket(nrel)
    tab = g["rel_bias"][bidx]
    tab = np.where((nrel >= 0)[:, :, None], tab, f(-1e30))
    sh["biasT"] = np.ascontiguousarray(tab.transpose(0, 2, 1)).reshape(128, 4 * 768).astype(f)
    sh["ident"] = np.eye(128, dtype=f)
    rc = np.zeros((128, 16), f)
    rc[:, :] = 1.0 / (np.arange(16, dtype=f) + 1.0)
    sh["rc"] = rc
    x = g["x"]
    p = g["p"]
    per_core = []
    for c in range(8):
        d = dict(sh)
        d["xT"] = np.ascontiguousarray(x[2 * c:2 * c + 2].transpose(0, 2, 1))
        d["pT"] = np.ascontiguousarray(p[:, 2 * c:2 * c + 2].transpose(0, 1, 3, 2))
        per_core.append(d)
    return per_core


_CACHE = {}


def kernel(**inputs):
    if "nc" not in _CACHE:
        _CACHE["nc"] = build_program()[0]
    nc = _CACHE["nc"]
    in_maps = host_layout(inputs)
    res = run_bass_kernel_spmd(nc, in_maps, core_ids=list(range(8)))
    out = np.empty((16, S, D), np.float32)
    for c in range(8):
        out[2 * c:2 * c + 2] = res.results[c]["outT"].transpose(0, 2, 1)
    return out
```

```python
import math
from contextlib import ExitStack

import numpy as np
import concourse.bass as bass
import concourse.mybir as mybir
from concourse.bass_utils import run_bass_kernel_spmd

F32 = mybir.dt.float32
BF16 = mybir.dt.bfloat16
ALU = mybir.AluOpType
AF = mybir.ActivationFunctionType
AX = mybir.AxisListType

DEPTH = 2
NSEQ = 2
S = 2048
D = 1024
EPS = 1e-6
LAMBDA_INIT = [0.8 - 0.6 * math.exp(-0.3 * i) for i in range(DEPTH)]

COMPUTE = ("pe", "act", "dve", "pool")
STREAMS = ("pe", "act", "dve", "pool", "sp")


class Buf:
    __slots__ = ("name", "lw", "rd")

    def __init__(self, name):
        self.name = name
        self.lw = None
        self.rd = []


class Op:
    __slots__ = ("stream", "fn", "waits", "key", "idx", "signal", "is_dma", "cnt")


class Prog:
    def __init__(self, nc):
        self.nc = nc
        self.ops = {s: [] for s in STREAMS}
        self.cnt = {}
        self.vc = {s: {} for s in STREAMS}
        self.opvc = {}
        self.bykey = {}
        self.dma_keys = []
        self.pend = {s: {} for s in STREAMS}
        self.marks = []
        self.dry = False

    def mark(self, name):
        if not self.dry:
            self.marks.append((name, len(self.ops["pe"])))

    def barrier(self):
        if self.dry:
            return
        snap = dict(self.cnt)
        for s in STREAMS:
            for k, i in snap.items():
                if self.pend[s].get(k, 0) < i:
                    self.pend[s][k] = i

    def _record(self, stream, key, fn, reads, writes, is_dma):
        if self.dry:
            return None
        deps = set()
        for b in reads:
            if b.lw is not None:
                deps.add(b.lw)
        for b in writes:
            if b.lw is not None:
                deps.add(b.lw)
            for r in b.rd:
                deps.add(r)
        for k, i in self.pend[stream].items():
            deps.add((k, i))
        self.pend[stream] = {}
        idx = self.cnt.get(key, 0) + 1
        self.cnt[key] = idx
        me = (key, idx)
        vc = self.vc[stream]
        waits = {}
        for (k, i) in deps:
            if k == stream and not is_dma:
                continue
            if vc.get(k, 0) >= i:
                continue
            if waits.get(k, 0) < i:
                waits[k] = i
        if stream in ("act", "dve", "pool") and not is_dma:
            for b in reads:
                if b.lw is not None and b.lw[0] == stream and vc.get(("self", stream), 0) < b.lw[1]:
                    if waits.get(stream, 0) < b.lw[1]:
                        waits[stream] = b.lw[1]
        for k, i in waits.items():
            if k == stream:
                vc[("self", stream)] = max(vc.get(("self", stream), 0), i)
                continue
            dvc = self.opvc.get((k, i))
            if dvc:
                for kk, ii in dvc.items():
                    if vc.get(kk, 0) < ii:
                        vc[kk] = ii
            if vc.get(k, 0) < i:
                vc[k] = i
        op = Op()
        op.stream, op.fn, op.waits, op.key, op.idx = stream, fn, waits, key, idx
        op.signal, op.is_dma, op.cnt = is_dma, is_dma, 0
        self.ops[stream].append(op)
        self.bykey[me] = op
        snap = {k: v for k, v in vc.items() if not isinstance(k, tuple) and k != stream}
        if not is_dma:
            snap[stream] = idx
        self.opvc[me] = snap
        for b in reads:
            b.rd.append(me)
        for b in writes:
            b.lw = me
            b.rd = []
        return op

    def op(self, stream, method, *args, reads=(), writes=(), **kw):
        return self._record(stream, stream, (method, args, kw), list(reads), list(writes), False)

    def dma(self, queue, semkey, out, in_, reads=(), writes=()):
        if semkey not in self.dma_keys:
            self.dma_keys.append(semkey)
        return self._record(queue, semkey, ("dma_start", (), {"out": out, "in_": in_}), list(reads), list(writes), True)

    def emit(self):
        nc = self.nc
        for s in STREAMS:
            for op in self.ops[s]:
                for k, i in op.waits.items():
                    if k in COMPUTE:
                        self.bykey[(k, i)].signal = True
        finals = {}
        for k in COMPUTE:
            if self.cnt.get(k, 0):
                self.bykey[(k, self.cnt[k])].signal = True
                finals[k] = self.cnt[k]
        for k in self.dma_keys:
            finals[k] = self.cnt[k]
        for k in COMPUTE:
            c = 0
            for i in range(1, self.cnt.get(k, 0) + 1):
                op = self.bykey[(k, i)]
                if op.signal:
                    c += 1
                op.cnt = c
        with ExitStack() as es:
            sems = {}
            for k in COMPUTE:
                if self.cnt.get(k, 0):
                    sems[k] = es.enter_context(nc.semaphore("s_" + k))
            for k in self.dma_keys:
                sems[k] = es.enter_context(nc.semaphore("d_" + str(k)))
            block = es.enter_context(nc.Block())

            def val(k, i):
                if k in COMPUTE:
                    return self.bykey[(k, i)].cnt
                return 16 * i

            def run(stream, eng):
                for op in self.ops[stream]:
                    for k, i in op.waits.items():
                        eng.wait_ge(sems[k], val(k, i))
                    meth, args, kw = op.fn
                    ins = getattr(eng, meth)(*args, **kw)
                    if op.is_dma:
                        ins.then_inc(sems[op.key], 16)
                    elif op.signal:
                        ins.then_inc(sems[op.key], 1)
                if stream == "sp":
                    for k, i in finals.items():
                        eng.wait_ge(sems[k], val(k, i))

            @block.tensor
            def _(e):
                run("pe", e)

            @block.scalar
            def _(e):
                run("act", e)

            @block.vector
            def _(e):
                run("dve", e)

            @block.gpsimd
            def _(e):
                run("pool", e)

            @block.sync
            def _(e):
                run("sp", e)
        return {s: len(self.ops[s]) for s in STREAMS}


def build_program(layers=(0, 1), nseq=NSEQ, halves=(0, 1)):
    nc = bass.Bass("TRN2", target_bir_lowering=False)
    P = Prog(nc)

    def dram(name, shape, kind="ExternalInput"):
        return nc.dram_tensor(name, list(shape), F32, kind=kind).ap()

    d_xT = dram("xT", [NSEQ, D, S])
    d_pT = dram("pT", [DEPTH, NSEQ, 256, S])
    d_out = dram("outT", [NSEQ, D, S], kind="ExternalOutput")
    d_win = dram("win", [DEPTH, 6, 128, 4096])
    d_mrgg = dram("mrgg", [DEPTH, 8, 128, 3072])
    d_mrgb = dram("mrgb", [DEPTH, 8, 128, 1536])
    d_wout = dram("wout", [DEPTH, 2, 128, 4096])
    d_w1 = dram("w1", [DEPTH, 8, 128, 4096])
    d_w2 = dram("w2", [DEPTH, 8, 128, 4096])
    d_wpg = dram("wpg", [DEPTH, 2, 128, 4096])
    d_wpe = dram("wpe", [DEPTH, 128, 2048])
    d_poolw = dram("poolw", [DEPTH, 128, 512])
    d_vecs = dram("vecs", [DEPTH, 128, 172])
    d_subln = dram("sublnb", [DEPTH, 128, 128])
    d_lam = dram("lamb", [DEPTH, 128, 256])
    d_bias = dram("biasT", [128, 4 * 768])
    d_ident = dram("ident", [128, 128])
    d_rc = dram("rc", [128, 16])
    d_cb = dram("cbias", [128, 4])

    es = ExitStack()
    with es:
        def sb(name, shape, dt):
            return es.enter_context(nc.sbuf_tensor(name, list(shape), dt))

        xT = sb("xT_s", [128, 8, 1024], F32)
        kvK = [sb(f"kvK{i}", [128, 4, 1024], BF16) for i in range(2)]
        kvV = [sb(f"kvV{i}", [128, 8, 4, 129], BF16) for i in range(2)]
        NRR = 39072
        RR = sb("RR", [128, NRR], BF16)
        S2 = [sb(f"S2_{i}", [128, 528], F32) for i in range(3)]
        sq = [sb(f"sq{i}", [128, 512], BF16) for i in range(2)]
        rstd = [sb(f"rstd{i}", [128, 512], F32) for i in range(2)]
        sg = [sb(f"sg{i}", [128, 512], F32) for i in range(2)]
        tmp = [sb(f"tmp{i}", [128, 512], F32) for i in range(2)]
        ring = [sb(f"ring{i}", [128, 4096], BF16) for i in range(3)]
        biasT = sb("biasT_s", [128, 4, 768], BF16)
        ident = sb("ident_s", [128, 128], BF16)
        onesD = sb("onesD", [128, 128], BF16)
        onesC = sb("onesC", [128, 128], BF16)
        vecs = sb("vecs_s", [128, DEPTH, 172], F32)
        vec2 = sb("vec2_s", [128, DEPTH, 12], F32)
        subln = sb("subln_s", [128, DEPTH, 128], F32)
        nlam = sb("nlam", [128, DEPTH], F32)
        rc = sb("rc_s", [128, 16], F32)
        nhalf = sb("nhalf", [128, 8], F32)
        onesF = sb("onesF", [1, 128], F32)
        cbias = sb("cbias_s", [128, 4], F32)
        identF = sb("identF", [128, 128], F32)
        onesFF = sb("onesFF", [128, 128], F32)
        colb = [sb(f"colb{i}", [128, 4], F32) for i in range(2)]
        NDG = 2
        DGT = 8
        diag = [sb(f"diag{i}", [128, DGT, 128], BF16) for i in range(NDG)]
        poolw = sb("poolw_s", [128, 4, 128], BF16)
        uhalo = sb("uhalo", [128, DEPTH, 4, 30], BF16)
        puhalo = sb("puhalo", [128, DEPTH, 4, 16], F32)
        att_f = [sb(f"attf{i}", [128, 128], F32) for i in range(4)]
        att_b = [sb(f"attb{i}", [128, 128], BF16) for i in range(2)]
        att_s = [sb(f"atts{i}", [128, 8], F32) for i in range(2)]
        ps = [es.enter_context(nc.psum_tensor(f"ps{i}", [128, 512], F32)) for i in range(8)]

        b_ps = [Buf(f"ps{i}") for i in range(8)]
        b_ring = [Buf(f"ring{i}") for i in range(3)]
        b_x = [[Buf(f"x{k}_{t}") for t in range(2)] for k in range(8)]
        b_sq = [Buf("sq0"), Buf("sq1")]
        b_rstd = [Buf("rstd0"), Buf("rstd1")]
        b_sg = [Buf("sg0"), Buf("sg1")]
        b_tmp = [Buf("tmp0"), Buf("tmp1")]
        b_S2 = [Buf(f"S2_{i}") for i in range(3)]
        b_const = Buf("const")
        b_kvK = [[Buf(f"kvK{i}_{t}") for t in range(2)] for i in range(2)]
        b_kvV = [[Buf(f"kvV{i}_{tb}") for tb in range(8)] for i in range(2)]
        b_uh = [Buf(f"uh{l}") for l in range(DEPTH)]
        b_ph = [[Buf(f"ph{l}_{g}") for g in range(4)] for l in range(DEPTH)]
        b_attf = [Buf(f"attf{i}") for i in range(4)]
        b_attb = [Buf(f"attb{i}") for i in range(2)]
        b_atts = [Buf(f"atts{i}") for i in range(2)]
        b_poolw = Buf("poolw")
        rowb = [rstd[0][0:1, :], rstd[1][0:1, :], tmp[1][0:1, :]]
        b_row = [b_rstd[0], b_rstd[1], b_tmp[1]]
        b_col = [Buf(f"col{i}") for i in range(2)]
        b_diag = [Buf(f"diag{i}") for i in range(6)]
        dg_n = [0]

        def rr3(off, a, b):
            return RR[:, off:off + a * b].rearrange("p (a b) -> p a b", a=a)

        def rrf(off_bf16, a, b):
            v = RR[:, off_bf16:off_bf16 + 2 * a * b].bitcast(F32)
            return v.rearrange("p (a b) -> p a b", a=a)

        curK = rr3(0, 4, 1024)
        curV = RR[:, 4096:4096 + 4128].rearrange("p (a h e) -> p a h e", a=8, h=4)
        hT = rr3(8224, 8, 1024)
        uT = rr3(16416, 4, 1054)
        QT = rr3(20632, 4, 1024)
        p2T = rr3(24728, 4, 1024)
        cT = rr3(28824, 4, 1024)
        OT = rr3(32920, 4, 1024)
        stash = rrf(32920, 4, 512)
        ET = rr3(37016, 4, 512)
        mergedT = rr3(0, 8, 1024)
        yevE = rrf(16416, 8, 512)
        h2T = rr3(0, 8, 512)
        f1 = rr3(4096, 32, 512)
        relu_s = rrf(20480, 2, 512)
        yevF = rrf(22528, 8, 512)
        xbT = rr3(0, 8, 1024)
        pTs = rr3(8192, 2, 1024)
        b_curK = [Buf("curK0"), Buf("curK1")]
        b_curV = [Buf(f"curV{i}") for i in range(8)]
        b_hT = [[Buf(f"hT{k}_{t}") for t in range(2)] for k in range(8)]
        b_uT = [[Buf(f"uT{c}_{t}") for t in range(2)] for c in range(4)]
        b_QT = [[Buf(f"QT{c}_{t}") for t in range(2)] for c in range(4)]
        b_p2T = [[Buf(f"p2T{c}_{t}") for t in range(2)] for c in range(4)]
        b_cT = [[Buf(f"cT{c}_{t}") for t in range(2)] for c in range(4)]
        b_OT = [[Buf(f"OT{c}_{t}") for t in range(2)] for c in range(4)]
        b_ET = [Buf(f"ET{i}") for i in range(4)]
        b_mg = [[Buf(f"mg{k}_{t}") for t in range(2)] for k in range(8)]
        b_yev = [Buf(f"yev{m}") for m in range(8)]
        b_h2 = [Buf(f"h2_{k}") for k in range(8)]
        b_f1 = [Buf(f"f1_{j}") for j in range(32)]
        b_relu = [Buf("relu0"), Buf("relu1")]
        b_xb = [[Buf(f"xb{k}_{t}") for t in range(2)] for k in range(8)]
        b_pTs = Buf("pTs")

        NSLOT = 3
        plan = []
        lastuse = {}
        st = {"pk": 0, "issued": 0, "dry_pk": 0}

        class PackTok:
            def __init__(self, k):
                self.k = k

        def load_pack(src_ap, nelem):
            if P.dry:
                plan.append((src_ap, nelem))
                k = st["dry_pk"]
                st["dry_pk"] += 1
                lastuse[k] = k + 1
                return ring[0], PackTok(k)
            k = st["pk"]
            while st["issued"] < len(plan):
                i = st["issued"]
                if i > k + NSLOT - 1:
                    break
                if i >= NSLOT and lastuse[i - NSLOT] > k:
                    assert i > k, "ring too small: slot-mate of the requested pack is still live"
                    break
                s = i % NSLOT
                P.dma("pool", f"ring{s}", ring[s][:, 0:plan[i][1]], plan[i][0], writes=[b_ring[s]])
                st["issued"] += 1
            assert st["issued"] > k
            st["pk"] += 1
            return ring[k % NSLOT], b_ring[k % NSLOT]

        gen_n = [0]

        def gen_bank():
            b = gen_n[0] % 6
            gen_n[0] += 1
            return b

        ev_n = [0]

        def evac(out, in_, reads, writes):
            ev_n[0] += 1
            if ev_n[0] % 2:
                P.op("act", "activation", out=out, in_=in_, func=AF.Copy, reads=reads, writes=writes)
            else:
                P.op("dve", "tensor_copy", out, in_, reads=reads, writes=writes)

        def mm(out, lhsT, rhs, start, stop, reads, writes):
            if P.dry:
                for b in reads:
                    if isinstance(b, PackTok):
                        lastuse[b.k] = st["dry_pk"]
                return
            P.op("pe", "matmul", out, lhsT, rhs, start=start, stop=stop, reads=reads, writes=writes)

        def tt(eng, out, in0, in1, op, reads, writes):
            P.op(eng, "tensor_tensor", out, in0, in1, op, reads=reads, writes=writes)

        def ts(eng, out, in0, s1, s2, op0, op1, reads, writes):
            if s2 is None:
                P.op(eng, "tensor_scalar", out, in0, s1, None, op0, reads=reads, writes=writes)
            else:
                P.op(eng, "tensor_scalar", out, in0, s1, s2, op0, op1, reads=reads, writes=writes)

        def stt(eng, out, in0, scalar, in1, op0, op1, reads, writes):
            P.op(eng, "scalar_tensor_tensor", out=out, in0=in0, scalar=scalar, in1=in1, op0=op0, op1=op1,
                 reads=reads, writes=writes)

        def act(out, in_, func, reads, writes, **kw):
            P.op("act", "activation", out=out, in_=in_, func=func, reads=reads, writes=writes, **kw)

        b_lamt, b_lt, b_nl, b_v2 = Buf("lamt"), Buf("lt"), Buf("nl"), Buf("v2")
        lamt = RR[:, 0:1024].bitcast(F32).rearrange("p (l c) -> p l c", l=DEPTH)
        lt = RR[:, 2048:2048 + 512].bitcast(F32)
        P.dma("sp", "c0", vecs[:], d_vecs.rearrange("l p c -> p l c"), writes=[b_const])
        P.dma("sp", "c0", subln[:], d_subln.rearrange("l p c -> p l c"), writes=[Buf("x1")])
        P.dma("sp", "c0", rc[:], d_rc, writes=[Buf("x2")])
        P.dma("sp", "c0", lamt, d_lam.rearrange("l p c -> p l c"), writes=[b_lamt])
        P.dma("pool", "c1", biasT[:], d_bias.rearrange("p (h c) -> p h c", h=4), writes=[Buf("x3")])
        P.dma("pool", "c1", ident[:], d_ident, writes=[Buf("x4")])
        P.dma("sp", "c0", identF[:], d_ident, writes=[Buf("x5")])
        P.dma("sp", "c0", cbias[:], d_cb, writes=[Buf("x6")])
        P.barrier()
        P.op("dve", "memset", onesD[:], 1.0 / 1024.0, writes=[b_const])
        P.op("dve", "memset", nhalf[:], -0.5, writes=[b_const])
        P.op("dve", "memset", onesF[:], 1.0, writes=[b_const])
        P.op("dve", "memset", onesFF[:], 1.0, writes=[b_const])
        P.op("dve", "memset", onesC[:], 1.0 / 512.0, writes=[b_const])
        P.op("dve", "memset", uhalo[:], 0.0, writes=[b_const])
        P.op("dve", "memset", puhalo[:], 0.0, writes=[b_const])
        for i in range(2):
            P.op("dve", "memset", kvV[i][:, :, :, 128:129], 1.0, writes=[b_const])
        ssum = att_s[0]
        for l in range(DEPTH):
            ts("dve", vec2[:, l, 0:4], vecs[:, l, 32:36], 2.0, None, ALU.mult, None, [b_const], [b_v2])
            ts("dve", vec2[:, l, 4:12], vecs[:, l, 36:44], 0.5, None, ALU.mult, None, [b_const], [b_v2])
            ts("dve", subln[:, l, :], subln[:, l, :], 1.0 - LAMBDA_INIT[l], None, ALU.mult, None, [b_const], [b_v2])
            for j in range(2):
                tt("dve", lt[:, 64 * j:64 * j + 64], lamt[:, l, 128 * j:128 * j + 64],
                   lamt[:, l, 128 * j + 64:128 * j + 128], ALU.mult, [b_lamt], [b_lt])
                P.op("dve", "reduce_sum", ssum[:, 2 * l + j:2 * l + j + 1], lt[:, 64 * j:64 * j + 64], AX.X,
                     reads=[b_lt], writes=[b_atts[0]])
        act(ssum[:, 4:8], ssum[:, 0:4], AF.Exp, [b_atts[0]], [b_atts[0]])
        for l in range(DEPTH):
            tt("dve", nlam[:, l:l + 1], ssum[:, 5 + 2 * l:6 + 2 * l], ssum[:, 4 + 2 * l:5 + 2 * l], ALU.subtract,
               [b_atts[0]], [b_nl])
            ts("dve", nlam[:, l:l + 1], nlam[:, l:l + 1], -LAMBDA_INIT[l], None, ALU.add, None, [b_nl], [b_nl])
        P.barrier()

        def norm_rstd(srcs, src_bufs, eps, r):
            bank = 6 + (r % 2)
            for k in range(8):
                q = k % 2
                act(sq[q][:], srcs[k], AF.Square, [src_bufs[k]], [b_sq[q]])
                mm(ps[bank][0:1, :], onesD[:, 0:1], sq[q][:], k == 0, k == 7, [b_sq[q]], [b_ps[bank]])
            ts("dve", rowb[r][:], ps[bank][0:1, :], eps, None, ALU.add, None, [b_ps[bank]], [b_row[r]])
            return rsqrt_bcast(r, bank)

        def rsqrt_bcast(r, bank):
            for blk in range(4):
                mm(ps[bank][:, blk:blk + 1], rowb[r][0:1, blk * 128:(blk + 1) * 128], onesF[0:1, 0:1], True, True,
                   [b_row[r]], [b_ps[bank]])
            P.op("dve", "tensor_copy", colb[r % 2][:], ps[bank][:, 0:4], reads=[b_ps[bank]], writes=[b_col[r % 2]])
            tt("pool", colb[r % 2][:], colb[r % 2][:], nhalf[:, 0:4], ALU.pow, [b_col[r % 2]], [b_col[r % 2]])
            for blk in range(4):
                ts("dve", rstd[r % 2][:, blk * 128:(blk + 1) * 128], identF[:], colb[r % 2][:, blk:blk + 1], None, ALU.mult, None,
                   [b_col[r % 2]], [b_rstd[r % 2]])
            for blk in range(4):
                mm(ps[bank][:, blk * 128:(blk + 1) * 128], onesFF[:], rstd[r % 2][:, blk * 128:(blk + 1) * 128], True, True,
                   [b_rstd[r % 2]], [b_ps[bank]])
            return ps[bank][:], b_ps[bank]

        def post_norm_residual(yev, t, gcol, l, eps, r):
            rs, brs = norm_rstd([yev[:, m, :] for m in range(8)], b_yev, eps, r)
            for m in range(8):
                q = m % 2
                stt("dve", tmp[q][:], yev[:, m, :], vecs[:, l, gcol + m:gcol + m + 1], rs, ALU.mult, ALU.mult,
                    [b_yev[m], brs], [b_tmp[q]])
                xs = xT[:, m, t * 512:(t + 1) * 512]
                tt("dve", xs, xs, tmp[q][:], ALU.add, [b_tmp[q]], [b_x[m][t]])

        def pre_norm(dst, dst_bufs, t, gcol, l, r, tok0):
            rs, brs = norm_rstd([xT[:, k, t * 512:(t + 1) * 512] for k in range(8)], [b_x[k][t] for k in range(8)], EPS, r)
            for k in range(8):
                stt("dve", dst[:, k, tok0:tok0 + 512], xT[:, k, t * 512:(t + 1) * 512],
                    vecs[:, l, gcol + k:gcol + k + 1], rs, ALU.mult, ALU.mult,
                    [b_x[k][t], brs], [dst_bufs[k]])

        def walk():
            gen_n[0] = 0
            ev_n[0] = 0
            dg_n[0] = 0
            for sq_i in range(nseq):
                for half in halves:
                    tok_base = half * 1024
                    for k in range(8):
                        P.dma("sp", "xin", xT[:, k, :], d_xT[sq_i, k * 128:(k + 1) * 128, tok_base:tok_base + 1024],
                              writes=[b_x[k][0], b_x[k][1]])
                    P.barrier()
                    for l in layers:
                        if half == 0:
                            Kd, Vd, bKd, bVd = kvK[l], kvV[l], b_kvK[l], b_kvV[l]
                        else:
                            Kd, Vd, bKd, bVd = curK, curV, b_curK, b_curV
                            P.op("dve", "memset", curV[:, :, :, 128:129], 1.0, writes=b_curV)
                        P.mark(f"A s{sq_i} h{half} l{l}")
                        for t in range(2):
                            pre_norm(hT, [b_hT[k][t] for k in range(8)], t, 0, l, t, t * 512)
                        P.dma("pool", "pw", poolw[:], d_poolw[l].rearrange("p (g c) -> p g c", g=4), writes=[b_poolw])
                        P.mark("B")
                        wa, bwa = load_pack(d_win[l, 0], 4096)
                        wb, bwb = load_pack(d_win[l, 1], 4096)
                        wa3 = wa[:, :].rearrange("p (k c) -> p k c", k=8)
                        wb3 = wb[:, :].rearrange("p (k c) -> p k c", k=8)
                        for c in range(4):
                            if half == 0:
                                P.op("dve", "memset", uT[:, c, 0:30], 0.0, writes=[b_uT[c][0]])
                            else:
                                P.op("dve", "tensor_copy", uT[:, c, 0:30], uhalo[:, l, c, :], reads=[b_uh[l]], writes=[b_uT[c][0]])
                        for c in range(4):
                            for t in range(2):
                                ba, bb = gen_bank(), gen_bank()
                                for k in range(8):
                                    mm(ps[ba][:], wa3[:, k, c * 128:(c + 1) * 128], hT[:, k, t * 512:(t + 1) * 512],
                                       k == 0, k == 7, [bwa, b_hT[k][t]], [b_ps[ba]])
                                for k in range(8):
                                    mm(ps[bb][:], wb3[:, k, c * 128:(c + 1) * 128], hT[:, k, t * 512:(t + 1) * 512],
                                       k == 0, k == 7, [bwb, b_hT[k][t]], [b_ps[bb]])
                                q = t % 2
                                act(sg[q][:], ps[bb][:], AF.Tanh, [b_ps[bb]], [b_sg[q]], scale=0.5)
                                stt("dve", uT[:, c, 30 + t * 512:30 + (t + 1) * 512], sg[q][:], 1.0, ps[ba][:], ALU.add, ALU.mult,
                                    [b_sg[q], b_ps[ba]], [b_uT[c][t]])
                        if half == 0:
                            for c in range(4):
                                P.op("dve", "tensor_copy", uhalo[:, l, c, :], uT[:, c, 1024:1054],
                                     reads=[b_uT[c][1]], writes=[b_uh[l]])
                        for which in (2, 3):
                            w_, bw_ = load_pack(d_win[l, which], 4096)
                            w3 = w_[:, :].rearrange("p (k c) -> p k c", k=8)
                            for c in range(4):
                                for t in range(2):
                                    b = gen_bank()
                                    for k in range(8):
                                        mm(ps[b][:], w3[:, k, c * 128:(c + 1) * 128], hT[:, k, t * 512:(t + 1) * 512],
                                           k == 0, k == 7, [bw_, b_hT[k][t]], [b_ps[b]])
                                    if which == 2:
                                        act(QT[:, c, t * 512:(t + 1) * 512], ps[b][:], AF.Copy, [b_ps[b]], [b_QT[c][t]], scale=0.125)
                                    else:
                                        P.op("dve", "tensor_copy", Kd[:, c, t * 512:(t + 1) * 512], ps[b][:],
                                             reads=[b_ps[b]], writes=[bKd[t]])
                        w_, bw_ = load_pack(d_win[l, 4], 4096)
                        w3 = w_[:, :].rearrange("p (k c) -> p k c", k=8)
                        for tb in range(8):
                            b = gen_bank()
                            for k in range(8):
                                mm(ps[b][:], hT[:, k, tb * 128:(tb + 1) * 128], w3[:, k, :],
                                   k == 0, k == 7, [bw_, b_hT[k][tb // 4]], [b_ps[b]])
                            evac(Vd[:, tb, :, 0:128], ps[b][:].rearrange("p (h e) -> p h e", h=4), [b_ps[b]], [bVd[tb]])
                        w_, bw_ = load_pack(d_win[l, 5], 4096)
                        w3 = w_[:, :].rearrange("p (k c) -> p k c", k=8)
                        B3 = [(S2[0], b_S2[0]), (S2[1], b_S2[1]), (S2[2], b_S2[2])]
                        pu, bpu = B3[0]
                        for g in range(4):
                            win_w = 2 ** (g + 1)
                            for t in range(2):
                                b = gen_bank()
                                for k in range(8):
                                    mm(ps[b][:], w3[:, k, g * 128:(g + 1) * 128], hT[:, k, t * 512:(t + 1) * 512],
                                       k == 0, k == 7, [bw_, b_hT[k][t]], [b_ps[b]])
                                if half == 0 and t == 0:
                                    P.op("dve", "memset", pu[:, 0:16], 0.0, writes=[bpu])
                                else:
                                    P.op("dve", "tensor_copy", pu[:, 0:16], puhalo[:, l, g, :], reads=[b_ph[l][g]], writes=[bpu])
                                act(pu[:, 16:528], ps[b][:], AF.Copy, [b_ps[b]], [bpu])
                                P.op("dve", "tensor_copy", puhalo[:, l, g, :], pu[:, 512:528], reads=[bpu], writes=[b_ph[l][g]])
                                si, sh, lo = 0, 1, 1
                                for st in range(g + 1):
                                    di = 1 if si != 1 else 2
                                    lo += sh
                                    sv, sbf = B3[si]
                                    dv, dbf = B3[di]
                                    tt("dve", dv[:, lo:528], sv[:, lo:528], sv[:, lo - sh:528 - sh], ALU.add, [sbf], [dbf])
                                    si, sh = di, sh * 2
                                av, abf = B3[si]
                                fi = 2 if si == 1 else 1
                                fv, fbf = B3[fi]
                                pooled = fv[:, 0:256].bitcast(BF16)
                                stt("dve", pooled, av[:, 16:528], 1.0 / win_w, pu[:, 16:528], ALU.mult, ALU.subtract,
                                    [abf, bpu], [fbf])
                                if half == 0 and t == 0:
                                    nfix = win_w - 1
                                    tt("dve", tmp[0][:, 0:nfix], av[:, 16:16 + nfix], rc[:, 0:nfix], ALU.mult, [abf], [b_tmp[0]])
                                    tt("dve", pooled[:, 0:nfix], tmp[0][:, 0:nfix], pu[:, 16:16 + nfix], ALU.subtract,
                                       [b_tmp[0], bpu], [fbf])
                                b2 = gen_bank()
                                mm(ps[b2][:], poolw[:, g, :], pooled, True, True, [b_poolw, fbf], [b_ps[b2]])
                                act(p2T[:, g, t * 512:(t + 1) * 512], ps[b2][:], AF.Copy, [b_ps[b2]], [b_p2T[g][t]],
                                    scale=vecs[:, l, 44 + g:45 + g])
                        P.mark("C")
                        for t in range(2):
                            base = t * 512
                            for c in range(4):
                                bk = gen_bank()
                                for j0 in range(0, 31, DGT):
                                    nt = min(DGT, 31 - j0)
                                    sl = dg_n[0] % NDG
                                    dg_n[0] += 1
                                    tt("dve", diag[sl][:, 0:nt, :], ident[:].unsqueeze(1).to_broadcast([128, nt, 128]),
                                       vecs[:, l, 48 + c * 31 + j0:48 + c * 31 + j0 + nt].unsqueeze(2).to_broadcast([128, nt, 128]),
                                       ALU.mult, [b_const], [b_diag[sl]])
                                    for i in range(nt):
                                        j = j0 + i
                                        mm(ps[bk][:], diag[sl][:, i, :], uT[:, c, base + j:base + j + 512], j == 0, j == 30,
                                           [b_diag[sl], b_uT[c][0], b_uT[c][1]], [b_ps[bk]])
                                ts("dve", stash[:, c, :], ps[bk][:], vec2[:, l, c:c + 1], None, ALU.add, None,
                                   [b_ps[bk]], [b_OT[c][0], b_OT[c][1]])
                            for c in range(4):
                                q = c % 2
                                act(sq[q][:], stash[:, c, :], AF.Copy, [b_OT[c][0]], [b_sq[q]])
                                mm(ps[6][0:1, :], onesC[:, 0:1], sq[q][:], c == 0, c == 3, [b_sq[q]], [b_ps[6]])
                            for c in range(4):
                                q = c % 2
                                act(sq[q][:], stash[:, c, :], AF.Square, [b_OT[c][0]], [b_sq[q]])
                                mm(ps[7][0:1, :], onesC[:, 0:1], sq[q][:], c == 0, c == 3, [b_sq[q]], [b_ps[7]])
                            P.op("dve", "tensor_copy", rowb[2][:], ps[6][0:1, :], reads=[b_ps[6]], writes=[b_row[2]])
                            tt("dve", rowb[0][:], rowb[2][:], rowb[2][:], ALU.mult, [b_row[2]], [b_row[0]])
                            tt("dve", rowb[0][:], ps[7][0:1, :], rowb[0][:], ALU.subtract, [b_ps[7], b_row[0]], [b_row[0]])
                            ts("dve", rowb[0][:], rowb[0][:], 4.0 * EPS, None, ALU.add, None, [b_row[0]], [b_row[0]])
                            mm(ps[6][:], onesF[:], rowb[2][:], True, True, [b_row[2]], [b_ps[6]])
                            rsqrt_bcast(0, 7)
                            for c in range(4):
                                q = c % 2
                                tt("dve", tmp[q][:], stash[:, c, :], ps[6][:], ALU.subtract, [b_OT[c][0], b_ps[6]], [b_tmp[q]])
                                tt("dve", tmp[q][:], tmp[q][:], ps[7][:], ALU.mult, [b_tmp[q], b_ps[7]], [b_tmp[q]])
                                ts("dve", tmp[q][:], tmp[q][:], vec2[:, l, 4 + c:5 + c], vec2[:, l, 8 + c:9 + c], ALU.mult, ALU.add,
                                   [b_tmp[q]], [b_tmp[q]])
                                act(sg[q][:], tmp[q][:], AF.Tanh, [b_tmp[q]], [b_sg[q]])
                                stt("dve", cT[:, c, t * 512:(t + 1) * 512], sg[q][:], 1.0, tmp[q][:], ALU.add, ALU.mult,
                                    [b_sg[q], b_tmp[q]], [b_cT[c][t]])
                        P.mark("D")
                        blocks = []
                        for h in range(4):
                            for qt in range(2):
                                Q0 = tok_base + qt * 512
                                for kb in range(0, Q0 // 128 + 4):
                                    blocks.append((h, qt, Q0, kb))
                        nb = len(blocks)
                        state = {"started": set(), "unit": None, "at_n": 0, "tr_n": 0}
                        pend_tr = {}

                        def blk_params(i):
                            h, qt, Q0, kb = blocks[i]
                            k0 = kb * 128
                            c0 = max(0, k0 - Q0)
                            n = 512 - c0
                            d0 = min(Q0 + c0 - k0, 256)
                            if half == 1 and kb >= 8:
                                Ks, Vs, bKs, bVs = curK, curV, b_curK[(kb - 8) // 4], b_curV[kb - 8]
                                kk0, vb = k0 - 1024, kb - 8
                            else:
                                Ks, Vs, bKs, bVs = kvK[l], kvV[l], b_kvK[l][kb // 4], b_kvV[l][kb]
                                kk0, vb = k0, kb
                            return h, qt, Q0, kb, k0, c0, n, d0, Ks, Vs, bKs, bVs, kk0, vb

                        def rec_S(i):
                            h, qt, Q0, kb, k0, c0, n, d0, Ks, Vs, bKs, bVs, kk0, vb = blk_params(i)
                            far = (d0 == 256)
                            for mi in range(2):
                                sbk = (i % 2) * 2 + mi
                                mm(ps[sbk][:, 0:n], Ks[64 * mi:64 * mi + 64, h, kk0:kk0 + 128],
                                   QT[64 * mi:64 * mi + 64, h, qt * 512 + c0:qt * 512 + 512],
                                   True, far, [bKs, b_QT[h][qt]], [b_ps[sbk]])
                            for mi in range(2):
                                sbk = (i % 2) * 2 + mi
                                if not far:
                                    mm(ps[sbk][:, 0:n], ident[:], biasT[:, h, d0:d0 + n], False, True, [b_const], [b_ps[sbk]])
                            for mi in range(2):
                                sbk = (i % 2) * 2 + mi
                                if far:
                                    act(ET[:, sbk, 0:n], ps[sbk][:, 0:n], AF.Exp, [b_ps[sbk]], [b_ET[sbk]], bias=cbias[:, h:h + 1])
                                else:
                                    act(ET[:, sbk, 0:n], ps[sbk][:, 0:n], AF.Exp, [b_ps[sbk]], [b_ET[sbk]])

                        def rec_AV(i):
                            h, qt, Q0, kb, k0, c0, n, d0, Ks, Vs, bKs, bVs, kk0, vb = blk_params(i)
                            if state["unit"] != (h, qt):
                                state["unit"] = (h, qt)
                                state["started"] = set()
                            started = state["started"]
                            for mi in range(2):
                                sbk = (i % 2) * 2 + mi
                                for j in range(c0 // 128, 4):
                                    a = j * 2 + mi
                                    bank, off = 4 + a // 3, (a % 3) * 160
                                    st1 = bank not in started
                                    started.add(bank)
                                    P.op("pe", "matmul", ps[bank][:, off:off + 129],
                                         ET[:, sbk, j * 128 - c0:j * 128 - c0 + 128], Vs[:, vb, h, :],
                                         start=st1, stop=(k0 == Q0 + j * 128), skip_group_check=True,
                                         reads=[b_ET[sbk], bVs], writes=[b_ps[bank]])
                            if k0 >= Q0:
                                j = (k0 - Q0) // 128
                                a1, a2 = j * 2, j * 2 + 1
                                O1 = ps[4 + a1 // 3][:, (a1 % 3) * 160:(a1 % 3) * 160 + 129]
                                O2 = ps[4 + a2 // 3][:, (a2 % 3) * 160:(a2 % 3) * 160 + 129]
                                bO1, bO2 = b_ps[4 + a1 // 3], b_ps[4 + a2 // 3]
                                z = state["at_n"] % 2
                                state["at_n"] += 1
                                st_, bst = att_s[z], b_atts[z]
                                fa, bfa = att_f[2 * z], b_attf[2 * z]
                                fo, bfo = att_f[2 * z + 1], b_attf[2 * z + 1]
                                on, bon = att_b[z], b_attb[z]
                                P.op("dve", "reciprocal", st_[:, 0:1], O1[:, 128:129], reads=[bO1], writes=[bst])
                                P.op("dve", "reciprocal", st_[:, 1:2], O2[:, 128:129], reads=[bO2], writes=[bst])
                                P.op("dve", "memset", st_[:, 3:4], 0.0, writes=[bst])
                                tt("dve", st_[:, 2:3], st_[:, 1:2], nlam[:, l:l + 1], ALU.mult, [bst], [bst])
                                ts("dve", fa[:], O1[:, 0:128], st_[:, 0:1], None, ALU.mult, None, [bst, bO1], [bfa])
                                stt("dve", fo[:], O2[:, 0:128], st_[:, 2:3], fa[:], ALU.mult, ALU.add, [bst, bO2, bfa], [bfo])
                                act(fa[:], fo[:], AF.Square, [bfo, bst], [bfa, bst], accum_out=st_[:, 3:4])
                                ts("dve", st_[:, 4:5], st_[:, 3:4], 1.0 / 128.0, EPS, ALU.mult, ALU.add, [bst], [bst])
                                tt("pool", st_[:, 5:6], st_[:, 4:5], nhalf[:, 0:1], ALU.pow, [bst], [bst])
                                stt("dve", on[:], fo[:], st_[:, 5:6], subln[:, l, :], ALU.mult, ALU.mult, [bst, bfo], [bon])
                                pend_tr[i] = (on, bon, h, qt * 512 + j * 128, qt)

                        def rec_TR(i):
                            if i not in pend_tr:
                                return
                            on, bon, h, qa, qt = pend_tr.pop(i)
                            tsl = state["tr_n"] % 4
                            state["tr_n"] += 1
                            mm(ps[7][:, tsl * 128:(tsl + 1) * 128], on[:], ident[:], True, True, [bon], [b_ps[7]])
                            act(OT[:, h, qa:qa + 128], ps[7][:, tsl * 128:(tsl + 1) * 128], AF.Copy, [b_ps[7]], [b_OT[h][qt]])

                        for i in range(nb + 2):
                            if i < nb:
                                rec_S(i)
                            if 0 <= i - 1 < nb:
                                rec_AV(i - 1)
                            if 0 <= i - 2 < nb:
                                rec_TR(i - 2)
                        P.barrier()
                        P.mark("E")
                        srcs = [(cT, b_cT), (OT, b_OT), (p2T, b_p2T)]
                        for m in range(8):
                            wg, bwg = load_pack(d_mrgg[l, m], 3072)
                            wbo, bwbo = load_pack(d_mrgb[l, m], 1536)
                            wg4 = wg[:, 0:3072].rearrange("p (b k j) -> p b k j", b=3, k=8)
                            wbo4 = wbo[:, 0:1536].rearrange("p (b k j) -> p b k j", b=3, k=4)
                            for t in range(2):
                                for br in range(3):
                                    bg, by = gen_bank(), gen_bank()
                                    for k in range(8):
                                        mm(ps[bg][:], wg4[:, br, k, :], hT[:, k, t * 512:(t + 1) * 512], k == 0, k == 7,
                                           [bwg, b_hT[k][t]], [b_ps[bg]])
                                    sT, bsT = srcs[br]
                                    for k in range(4):
                                        mm(ps[by][:], wbo4[:, br, k, :], sT[:, k, t * 512:(t + 1) * 512], k == 0, k == 3,
                                           [bwbo, bsT[k][t]], [b_ps[by]])
                                    q = br % 2
                                    act(sg[q][:], ps[bg][:], AF.Tanh, [b_ps[bg]], [b_sg[q]], scale=0.5)
                                    if br == 0:
                                        stt("dve", tmp[0][:], sg[q][:], 1.0, ps[by][:], ALU.add, ALU.mult, [b_sg[q], b_ps[by]], [b_tmp[0]])
                                    else:
                                        stt("dve", tmp[1][:], sg[q][:], 1.0, ps[by][:], ALU.add, ALU.mult, [b_sg[q], b_ps[by]], [b_tmp[1]])
                                        if br == 1:
                                            tt("dve", tmp[0][:], tmp[0][:], tmp[1][:], ALU.add, [b_tmp[1]], [b_tmp[0]])
                                        else:
                                            tt("dve", mergedT[:, m, t * 512:(t + 1) * 512], tmp[0][:], tmp[1][:], ALU.add,
                                               [b_tmp[0], b_tmp[1]], [b_mg[m][t]])
                        for t in range(2):
                            for cg in range(2):
                                w_, bw_ = load_pack(d_wout[l, cg], 4096)
                                w3 = w_[:, :].rearrange("p (k c) -> p k c", k=8)
                                for m4 in range(4):
                                    m = cg * 4 + m4
                                    b = gen_bank()
                                    for k in range(8):
                                        mm(ps[b][:], w3[:, k, m4 * 128:(m4 + 1) * 128], mergedT[:, k, t * 512:(t + 1) * 512],
                                           k == 0, k == 7, [bw_, b_mg[k][t]], [b_ps[b]])
                                    evac(yevE[:, m, :], ps[b][:], [b_ps[b]], [b_yev[m]])
                            post_norm_residual(yevE, t, 8, l, 4.0 * EPS, t)
                        P.barrier()
                        P.mark("F")
                        for t in range(2):
                            pre_norm(h2T, b_h2, t, 16, l, t, 0)
                            for fg in range(8):
                                w_, bw_ = load_pack(d_w1[l, fg], 4096)
                                w3 = w_[:, :].rearrange("p (k c) -> p k c", k=8)
                                for i in range(4):
                                    jf = fg * 4 + i
                                    b = gen_bank()
                                    for k in range(8):
                                        mm(ps[b][:], w3[:, k, i * 128:(i + 1) * 128], h2T[:, k, :], k == 0, k == 7,
                                           [bw_, b_h2[k]], [b_ps[b]])
                                    q = jf % 2
                                    act(relu_s[:, q, :], ps[b][:], AF.Relu, [b_ps[b]], [b_relu[q]])
                                    tt("dve", f1[:, jf, :], relu_s[:, q, :], relu_s[:, q, :], ALU.mult, [b_relu[q]], [b_f1[jf]])
                            for jg in range(8):
                                w_, bw_ = load_pack(d_w2[l, jg], 4096)
                                w3 = w_[:, :].rearrange("p (j c) -> p j c", j=4)
                                for jj in range(4):
                                    jf = jg * 4 + jj
                                    for m in range(8):
                                        mm(ps[m][:], w3[:, jj, m * 128:(m + 1) * 128], f1[:, jf, :], jf == 0, jf == 31,
                                           [bw_, b_f1[jf]], [b_ps[m]])
                            for m in range(8):
                                evac(yevF[:, m, :], ps[m][:], [b_ps[m]], [b_yev[m]])
                            post_norm_residual(yevF, t, 24, l, EPS, t)
                        P.barrier()
                        P.mark("G")
                        P.dma("pool", "pin", pTs, d_pT[l, sq_i, :, tok_base:tok_base + 1024].rearrange("(k p) t -> p k t", p=128),
                              writes=[b_pTs])
                        for k in range(8):
                            for t in range(2):
                                if (k + t) % 2:
                                    act(xbT[:, k, t * 512:(t + 1) * 512], xT[:, k, t * 512:(t + 1) * 512], AF.Copy,
                                        [b_x[k][t]], [b_xb[k][t]])
                                else:
                                    P.op("dve", "tensor_copy", xbT[:, k, t * 512:(t + 1) * 512], xT[:, k, t * 512:(t + 1) * 512],
                                         reads=[b_x[k][t]], writes=[b_xb[k][t]])
                        for cg in range(2):
                            w_, bw_ = load_pack(d_wpg[l, cg], 4096)
                            w3 = w_[:, :].rearrange("p (k c) -> p k c", k=8)
                            if cg == 0:
                                wpe_, bwpe = load_pack(d_wpe[l], 2048)
                                wpe3 = wpe_[:, 0:2048].rearrange("p (k c) -> p k c", k=2)
                            for m4 in range(4):
                                m = cg * 4 + m4
                                for t in range(2):
                                    bg, bp = gen_bank(), gen_bank()
                                    for k in range(8):
                                        mm(ps[bg][:], w3[:, k, m4 * 128:(m4 + 1) * 128], xbT[:, k, t * 512:(t + 1) * 512],
                                           k == 0, k == 7, [bw_, b_xb[k][t]], [b_ps[bg]])
                                    for k in range(2):
                                        mm(ps[bp][:], wpe3[:, k, m * 128:(m + 1) * 128], pTs[:, k, t * 512:(t + 1) * 512],
                                           k == 0, k == 1, [bwpe, b_pTs], [b_ps[bp]])
                                    q = t % 2
                                    act(sg[q][:], ps[bg][:], AF.Tanh, [b_ps[bg]], [b_sg[q]], scale=0.5)
                                    act(relu_s[:, q, :], ps[bp][:], AF.Copy, [b_ps[bp]], [b_relu[q]], scale=0.5)
                                    stt("dve", tmp[q][:], sg[q][:], 1.0, relu_s[:, q, :], ALU.add, ALU.mult, [b_sg[q], b_relu[q]], [b_tmp[q]])
                                    xs = xT[:, m, t * 512:(t + 1) * 512]
                                    tt("dve", xs, xs, tmp[q][:], ALU.add, [b_tmp[q]], [b_x[m][t]])
                        P.barrier()
                    for k in range(8):
                        P.dma("sp", "xout", d_out[sq_i, k * 128:(k + 1) * 128, tok_base:tok_base + 1024], xT[:, k, :],
                              reads=[b_x[k][0], b_x[k][1]])
                    P.barrier()
        P.dry = True
        walk()
        P.dry = False
        walk()
        counts = P.emit()
        counts["marks"] = P.marks
    return nc, counts


def _rel_bucket(n):
    n = np.maximum(n, 0)
    nf = np.maximum(n, 1).astype(np.float32)
    large = 16 + (np.log(nf / np.float32(16)) / np.float32(math.log(128 / 16)) * np.float32(16)).astype(np.int32)
    large = np.minimum(large, 31)
    return np.where(n < 16, n, large)


def host_layout(inp):
    f = np.float32
    g = {k: np.asarray(v, dtype=f) for k, v in inp.items()}
    L = DEPTH
    sh = {}
    w_in = g["w_in"]
    sh["win"] = np.ascontiguousarray(
        w_in[:, :, :3072].reshape(L, 8, 128, 6, 512).transpose(0, 3, 2, 1, 4)).reshape(L, 6, 128, 4096)
    gates = w_in[:, :, 3072:].reshape(L, 8, 128, 3, 8, 128)
    sh["mrgg"] = np.ascontiguousarray(gates.transpose(0, 4, 2, 3, 1, 5)).reshape(L, 8, 128, 3072)
    bo = np.stack([g["w_conv_out"], g["w_attn_out"], g["w_pool_out"]], axis=1)
    bo = bo.reshape(L, 3, 4, 128, 8, 128)
    sh["mrgb"] = np.ascontiguousarray(bo.transpose(0, 4, 3, 1, 2, 5)).reshape(L, 8, 128, 1536)

    def colgroups(w, ng):
        return np.ascontiguousarray(w.reshape(L, 8, 128, ng, 512).transpose(0, 3, 2, 1, 4)).reshape(L, ng, 128, 4096)

    sh["wout"] = colgroups(g["w_out"], 2)
    sh["w1"] = colgroups(g["w_mlp_in"], 8)
    sh["wpg"] = colgroups(g["w_ple_gate"], 2)
    sh["w2"] = np.ascontiguousarray(g["w_mlp_out"].reshape(L, 8, 4, 128, 1024).transpose(0, 1, 3, 2, 4)).reshape(L, 8, 128, 4096)
    sh["wpe"] = np.ascontiguousarray(g["w_ple_proj"].reshape(L, 2, 128, 1024).transpose(0, 2, 1, 3)).reshape(L, 128, 2048)
    sh["poolw"] = np.ascontiguousarray(g["pool_w"].transpose(0, 2, 1, 3)).reshape(L, 128, 512)
    vecs = np.zeros((L, 128, 172), f)

    def cols(v, n):
        return v.reshape(L, n, 128).transpose(0, 2, 1)
    vecs[:, :, 0:8] = cols(g["g_pre_mix"], 8)
    vecs[:, :, 8:16] = cols(g["g_post_mix"], 8)
    vecs[:, :, 16:24] = cols(g["g_pre_mlp"], 8)
    vecs[:, :, 24:32] = cols(g["g_post_mlp"], 8)
    vecs[:, :, 32:36] = cols(g["conv_dw_b"], 4)
    vecs[:, :, 36:40] = cols(g["conv_ln_g"], 4)
    vecs[:, :, 40:44] = cols(g["conv_ln_b"], 4)
    vecs[:, :, 44:48] = cols(g["pool_scale"], 4)
    cw = g["conv_dw_w"][:, :, 0, :].reshape(L, 31, 4, 128).transpose(0, 3, 2, 1)
    vecs[:, :, 48:172] = cw.reshape(L, 128, 124)
    sh["vecs"] = vecs
    sh["sublnb"] = np.ascontiguousarray(np.broadcast_to(g["subln_g"][:, None, :], (L, 128, 128)))
    sh["lamb"] = np.ascontiguousarray(np.broadcast_to(g["lam_p"].reshape(L, 1, 256), (L, 128, 256)))
    kk = np.arange(128)[:, None]
    dd = np.arange(768)[None, :]
    nrel = dd - kk
    bidx = _rel_bucket(nrel)
    tab = g["rel_bias"][bidx]
    tab = np.where((nrel >= 0)[:, :, None], tab, f(-1e30))
    sh["biasT"] = np.ascontiguousarray(tab.transpose(0, 2, 1)).reshape(128, 4 * 768).astype(f)
    sh["ident"] = np.eye(128, dtype=f)
    sh["cbias"] = np.ascontiguousarray(np.broadcast_to(g["rel_bias"][31][None, :], (128, 4)))
    rc = np.zeros((128, 16), f)
    rc[:, :] = 1.0 / (np.arange(16, dtype=f) + 1.0)
    sh["rc"] = rc
    x = g["x"]
    p = g["p"]
    per_core = []
    for c in range(8):
        d = dict(sh)
        d["xT"] = np.ascontiguousarray(x[2 * c:2 * c + 2].transpose(0, 2, 1))
        d["pT"] = np.ascontiguousarray(p[:, 2 * c:2 * c + 2].transpose(0, 1, 3, 2))
        per_core.append(d)
    return per_core


_CACHE = {}


def kernel(**inputs):
    if "nc" not in _CACHE:
        _CACHE["nc"] = build_program()[0]
    nc = _CACHE["nc"]
    in_maps = host_layout(inputs)
    res = run_bass_kernel_spmd(nc, in_maps, core_ids=list(range(8)))
    out = np.empty((16, S, D), np.float32)
    for c in range(8):
        out[2 * c:2 * c + 2] = res.results[c]["outT"].transpose(0, 2, 1)
    return out
```

```python
import math
from contextlib import ExitStack

import numpy as np
import concourse.bass as bass
import concourse.mybir as mybir
from concourse.bass_utils import run_bass_kernel_spmd

F32 = mybir.dt.float32
BF16 = mybir.dt.bfloat16
ALU = mybir.AluOpType
AF = mybir.ActivationFunctionType
AX = mybir.AxisListType

DEPTH = 2
NSEQ = 2
S = 2048
D = 1024
EPS = 1e-6
LAMBDA_INIT = [0.8 - 0.6 * math.exp(-0.3 * i) for i in range(DEPTH)]

COMPUTE = ("pe", "act", "dve", "pool")
STREAMS = ("pe", "act", "dve", "pool", "sp")


class Buf:
    __slots__ = ("name", "lw", "rd")

    def __init__(self, name):
        self.name = name
        self.lw = None
        self.rd = []


class Op:
    __slots__ = ("stream", "fn", "waits", "key", "idx", "signal", "is_dma", "cnt")


class Prog:
    def __init__(self, nc):
        self.nc = nc
        self.ops = {s: [] for s in STREAMS}
        self.cnt = {}
        self.vc = {s: {} for s in STREAMS}
        self.opvc = {}
        self.bykey = {}
        self.dma_keys = []
        self.pend = {s: {} for s in STREAMS}
        self.marks = []
        self.dry = False

    def mark(self, name):
        if not self.dry:
            self.marks.append((name, len(self.ops["pe"])))

    def barrier(self):
        if self.dry:
            return
        snap = dict(self.cnt)
        for s in STREAMS:
            for k, i in snap.items():
                if self.pend[s].get(k, 0) < i:
                    self.pend[s][k] = i

    def _record(self, stream, key, fn, reads, writes, is_dma):
        if self.dry:
            return None
        deps = set()
        for b in reads:
            if b.lw is not None:
                deps.add(b.lw)
        for b in writes:
            if b.lw is not None:
                deps.add(b.lw)
            for r in b.rd:
                deps.add(r)
        for k, i in self.pend[stream].items():
            deps.add((k, i))
        self.pend[stream] = {}
        idx = self.cnt.get(key, 0) + 1
        self.cnt[key] = idx
        me = (key, idx)
        vc = self.vc[stream]
        waits = {}
        for (k, i) in deps:
            if k == stream and not is_dma:
                continue
            if vc.get(k, 0) >= i:
                continue
            if waits.get(k, 0) < i:
                waits[k] = i
        if stream in ("act", "dve", "pool") and not is_dma:
            for b in reads:
                if b.lw is not None and b.lw[0] == stream and vc.get(("self", stream), 0) < b.lw[1]:
                    if waits.get(stream, 0) < b.lw[1]:
                        waits[stream] = b.lw[1]
        for k, i in waits.items():
            if k == stream:
                vc[("self", stream)] = max(vc.get(("self", stream), 0), i)
                continue
            dvc = self.opvc.get((k, i))
            if dvc:
                for kk, ii in dvc.items():
                    if vc.get(kk, 0) < ii:
                        vc[kk] = ii
            if vc.get(k, 0) < i:
                vc[k] = i
        op = Op()
        op.stream, op.fn, op.waits, op.key, op.idx = stream, fn, waits, key, idx
        op.signal, op.is_dma, op.cnt = is_dma, is_dma, 0
        self.ops[stream].append(op)
        self.bykey[me] = op
        snap = {k: v for k, v in vc.items() if not isinstance(k, tuple) and k != stream}
        if not is_dma:
            snap[stream] = idx
        self.opvc[me] = snap
        for b in reads:
            b.rd.append(me)
        for b in writes:
            b.lw = me
            b.rd = []
        return op

    def op(self, stream, method, *args, reads=(), writes=(), **kw):
        return self._record(stream, stream, (method, args, kw), list(reads), list(writes), False)

    def dma(self, queue, semkey, out, in_, reads=(), writes=()):
        if semkey not in self.dma_keys:
            self.dma_keys.append(semkey)
        return self._record(queue, semkey, ("dma_start", (), {"out": out, "in_": in_}), list(reads), list(writes), True)

    def emit(self):
        nc = self.nc
        for s in STREAMS:
            for op in self.ops[s]:
                for k, i in op.waits.items():
                    if k in COMPUTE:
                        self.bykey[(k, i)].signal = True
        finals = {}
        for k in COMPUTE:
            if self.cnt.get(k, 0):
                self.bykey[(k, self.cnt[k])].signal = True
                finals[k] = self.cnt[k]
        for k in self.dma_keys:
            finals[k] = self.cnt[k]
        for k in COMPUTE:
            c = 0
            for i in range(1, self.cnt.get(k, 0) + 1):
                op = self.bykey[(k, i)]
                if op.signal:
                    c += 1
                op.cnt = c
        with ExitStack() as es:
            sems = {}
            for k in COMPUTE:
                if self.cnt.get(k, 0):
                    sems[k] = es.enter_context(nc.semaphore("s_" + k))
            for k in self.dma_keys:
                sems[k] = es.enter_context(nc.semaphore("d_" + str(k)))
            block = es.enter_context(nc.Block())

            def val(k, i):
                if k in COMPUTE:
                    return self.bykey[(k, i)].cnt
                return 16 * i

            def run(stream, eng):
                for op in self.ops[stream]:
                    for k, i in op.waits.items():
                        eng.wait_ge(sems[k], val(k, i))
                    meth, args, kw = op.fn
                    ins = getattr(eng, meth)(*args, **kw)
                    if op.is_dma:
                        ins.then_inc(sems[op.key], 16)
                    elif op.signal:
                        ins.then_inc(sems[op.key], 1)
                if stream == "sp":
                    for k, i in finals.items():
                        eng.wait_ge(sems[k], val(k, i))

            @block.tensor
            def _(e):
                run("pe", e)

            @block.scalar
            def _(e):
                run("act", e)

            @block.vector
            def _(e):
                run("dve", e)

            @block.gpsimd
            def _(e):
                run("pool", e)

            @block.sync
            def _(e):
                run("sp", e)
        return {s: len(self.ops[s]) for s in STREAMS}


def build_program(layers=(0, 1), nseq=NSEQ, halves=(0, 1)):
    nc = bass.Bass("TRN2", target_bir_lowering=False)
    P = Prog(nc)

    def dram(name, shape, kind="ExternalInput"):
        return nc.dram_tensor(name, list(shape), F32, kind=kind).ap()

    d_xT = dram("xT", [NSEQ, D, S])
    d_pT = dram("pT", [DEPTH, NSEQ, 256, S])
    d_out = dram("outT", [NSEQ, D, S], kind="ExternalOutput")
    d_win = dram("win", [DEPTH, 6, 128, 4096])
    d_mrgg = dram("mrgg", [DEPTH, 8, 128, 3072])
    d_mrgb = dram("mrgb", [DEPTH, 8, 128, 1536])
    d_wout = dram("wout", [DEPTH, 2, 128, 4096])
    d_w1 = dram("w1", [DEPTH, 8, 128, 4096])
    d_w2 = dram("w2", [DEPTH, 8, 128, 4096])
    d_wpg = dram("wpg", [DEPTH, 2, 128, 4096])
    d_wpe = dram("wpe", [DEPTH, 128, 2048])
    d_poolw = dram("poolw", [DEPTH, 128, 512])
    d_vecs = dram("vecs", [DEPTH, 128, 48])
    d_cdiag = dram("cdiag", [DEPTH, 4, 128, 4096])
    d_subln = dram("sublnb", [DEPTH, 128, 128])
    d_lam = dram("lamb", [DEPTH, 128, 256])
    d_bias = dram("biasT", [128, 4 * 768])
    d_ident = dram("ident", [128, 128])
    d_rc = dram("rc", [128, 16])
    d_cb = dram("cbias", [128, 4])

    es = ExitStack()
    with es:
        def sb(name, shape, dt):
            return es.enter_context(nc.sbuf_tensor(name, list(shape), dt))

        xT = sb("xT_s", [128, 8, 1024], F32)
        kvK = [sb(f"kvK{i}", [128, 4, 1024], BF16) for i in range(2)]
        kvV = [sb(f"kvV{i}", [128, 8, 4, 129], BF16) for i in range(2)]
        NRR = 39072
        RR = sb("RR", [128, NRR], BF16)
        S2 = [sb(f"S2_{i}", [128, 528], F32) for i in range(3)]
        sq = [sb(f"sq{i}", [128, 512], BF16) for i in range(2)]
        rstd = [sb(f"rstd{i}", [128, 512], F32) for i in range(2)]
        sg = [sb(f"sg{i}", [128, 512], F32) for i in range(2)]
        tmp = [sb(f"tmp{i}", [128, 512], F32) for i in range(2)]
        ring = [sb(f"ring{i}", [128, 4096], BF16) for i in range(4)]
        biasT = sb("biasT_s", [128, 4, 768], BF16)
        ident = sb("ident_s", [128, 128], BF16)
        onesD = sb("onesD", [128, 128], BF16)
        onesC = sb("onesC", [128, 128], BF16)
        vecs = sb("vecs_s", [128, DEPTH, 48], F32)
        vec2 = sb("vec2_s", [128, DEPTH, 12], F32)
        subln = sb("subln_s", [128, DEPTH, 128], F32)
        nlam = sb("nlam", [128, DEPTH], F32)
        rc = sb("rc_s", [128, 16], F32)
        nhalf = sb("nhalf", [128, 8], F32)
        onesF = sb("onesF", [1, 128], F32)
        cbias = sb("cbias_s", [128, 4], F32)
        identF = sb("identF", [128, 128], F32)
        onesFF = sb("onesFF", [128, 128], F32)
        colb = [sb(f"colb{i}", [128, 4], F32) for i in range(2)]
        poolw = sb("poolw_s", [128, 4, 128], BF16)
        uhalo = sb("uhalo", [128, DEPTH, 4, 30], BF16)
        puhalo = sb("puhalo", [128, DEPTH, 4, 16], F32)
        att_f = [sb(f"attf{i}", [128, 128], F32) for i in range(4)]
        att_b = [sb(f"attb{i}", [128, 128], BF16) for i in range(2)]
        att_s = [sb(f"atts{i}", [128, 8], F32) for i in range(2)]
        ps = [es.enter_context(nc.psum_tensor(f"ps{i}", [128, 512], F32)) for i in range(8)]

        b_ps = [Buf(f"ps{i}") for i in range(8)]
        b_ring = [Buf(f"ring{i}") for i in range(4)]
        b_x = [[Buf(f"x{k}_{t}") for t in range(2)] for k in range(8)]
        b_sq = [Buf("sq0"), Buf("sq1")]
        b_rstd = [Buf("rstd0"), Buf("rstd1")]
        b_sg = [Buf("sg0"), Buf("sg1")]
        b_tmp = [Buf("tmp0"), Buf("tmp1")]
        b_S2 = [Buf(f"S2_{i}") for i in range(3)]
        b_const = Buf("const")
        b_kvK = [[Buf(f"kvK{i}_{t}") for t in range(2)] for i in range(2)]
        b_kvV = [[Buf(f"kvV{i}_{tb}") for tb in range(8)] for i in range(2)]
        b_uh = [Buf(f"uh{l}") for l in range(DEPTH)]
        b_ph = [[Buf(f"ph{l}_{g}") for g in range(4)] for l in range(DEPTH)]
        b_attf = [Buf(f"attf{i}") for i in range(4)]
        b_attb = [Buf(f"attb{i}") for i in range(2)]
        b_atts = [Buf(f"atts{i}") for i in range(2)]
        b_poolw = Buf("poolw")
        rowb = [rstd[0][0:1, :], rstd[1][0:1, :], tmp[1][0:1, :]]
        b_row = [b_rstd[0], b_rstd[1], b_tmp[1]]
        b_col = [Buf(f"col{i}") for i in range(2)]

        def rr3(off, a, b):
            return RR[:, off:off + a * b].rearrange("p (a b) -> p a b", a=a)

        def rrf(off_bf16, a, b):
            v = RR[:, off_bf16:off_bf16 + 2 * a * b].bitcast(F32)
            return v.rearrange("p (a b) -> p a b", a=a)

        curK = rr3(0, 4, 1024)
        curV = RR[:, 4096:4096 + 4128].rearrange("p (a h e) -> p a h e", a=8, h=4)
        hT = rr3(8224, 8, 1024)
        uT = rr3(16416, 4, 1054)
        QT = rr3(20632, 4, 1024)
        p2T = rr3(24728, 4, 1024)
        cT = rr3(28824, 4, 1024)
        OT = rr3(32920, 4, 1024)
        stash = rrf(32920, 4, 512)
        ET = rr3(37016, 4, 512)
        mergedT = rr3(0, 8, 1024)
        yevE = rrf(16416, 8, 512)
        h2T = rr3(0, 8, 512)
        h2Tb = rr3(30720, 8, 512)
        f1 = rr3(4096, 32, 512)
        relu_s = rrf(20480, 2, 512)
        yevF = rrf(22528, 8, 512)
        xbT = rr3(0, 8, 1024)
        pTs = rr3(8192, 2, 1024)
        b_curK = [Buf("curK0"), Buf("curK1")]
        b_curV = [Buf(f"curV{i}") for i in range(8)]
        b_hT = [[Buf(f"hT{k}_{t}") for t in range(2)] for k in range(8)]
        b_uT = [[Buf(f"uT{c}_{t}") for t in range(2)] for c in range(4)]
        b_QT = [[Buf(f"QT{c}_{t}") for t in range(2)] for c in range(4)]
        b_p2T = [[Buf(f"p2T{c}_{t}") for t in range(2)] for c in range(4)]
        b_cT = [[Buf(f"cT{c}_{t}") for t in range(2)] for c in range(4)]
        b_OT = [[Buf(f"OT{c}_{t}") for t in range(2)] for c in range(4)]
        b_ET = [Buf(f"ET{i}") for i in range(4)]
        b_mg = [[Buf(f"mg{k}_{t}") for t in range(2)] for k in range(8)]
        b_yev = [Buf(f"yev{m}") for m in range(8)]
        b_h2 = [Buf(f"h2_{k}") for k in range(8)]
        b_h2b = [Buf(f"h2b_{k}") for k in range(8)]
        b_f1 = [Buf(f"f1_{j}") for j in range(32)]
        b_relu = [Buf("relu0"), Buf("relu1")]
        b_xb = [[Buf(f"xb{k}_{t}") for t in range(2)] for k in range(8)]
        b_pTs = Buf("pTs")

        NSLOT = 4
        plan = []
        lastuse = {}
        st = {"pk": 0, "issued": 0, "dry_pk": 0}

        class PackTok:
            def __init__(self, k):
                self.k = k

        def load_pack(src_ap, nelem):
            if P.dry:
                plan.append((src_ap, nelem))
                k = st["dry_pk"]
                st["dry_pk"] += 1
                lastuse[k] = k + 1
                return ring[0], PackTok(k)
            k = st["pk"]
            while st["issued"] < len(plan):
                i = st["issued"]
                if i > k + NSLOT - 1:
                    break
                if i >= NSLOT and lastuse[i - NSLOT] > k:
                    assert i > k, "ring too small: slot-mate of the requested pack is still live"
                    break
                s = i % NSLOT
                P.dma("pool", f"ring{s}", ring[s][:, 0:plan[i][1]], plan[i][0], writes=[b_ring[s]])
                st["issued"] += 1
            assert st["issued"] > k
            st["pk"] += 1
            return ring[k % NSLOT], b_ring[k % NSLOT]

        gen_n = [0]

        def gen_bank():
            b = gen_n[0] % 6
            gen_n[0] += 1
            return b

        ev_n = [0]

        def evac(out, in_, reads, writes):
            ev_n[0] += 1
            if ev_n[0] % 2:
                P.op("act", "activation", out=out, in_=in_, func=AF.Copy, reads=reads, writes=writes)
            else:
                P.op("dve", "tensor_copy", out, in_, reads=reads, writes=writes)

        def mm(out, lhsT, rhs, start, stop, reads, writes):
            if P.dry:
                for b in reads:
                    if isinstance(b, PackTok):
                        lastuse[b.k] = st["dry_pk"]
                return
            P.op("pe", "matmul", out, lhsT, rhs, start=start, stop=stop, reads=reads, writes=writes)

        def tt(eng, out, in0, in1, op, reads, writes):
            P.op(eng, "tensor_tensor", out, in0, in1, op, reads=reads, writes=writes)

        def ts(eng, out, in0, s1, s2, op0, op1, reads, writes):
            if s2 is None:
                P.op(eng, "tensor_scalar", out, in0, s1, None, op0, reads=reads, writes=writes)
            else:
                P.op(eng, "tensor_scalar", out, in0, s1, s2, op0, op1, reads=reads, writes=writes)

        def stt(eng, out, in0, scalar, in1, op0, op1, reads, writes):
            P.op(eng, "scalar_tensor_tensor", out=out, in0=in0, scalar=scalar, in1=in1, op0=op0, op1=op1,
                 reads=reads, writes=writes)

        def act(out, in_, func, reads, writes, **kw):
            P.op("act", "activation", out=out, in_=in_, func=func, reads=reads, writes=writes, **kw)

        b_lamt, b_lt, b_nl, b_v2 = Buf("lamt"), Buf("lt"), Buf("nl"), Buf("v2")
        lamt = RR[:, 0:1024].bitcast(F32).rearrange("p (l c) -> p l c", l=DEPTH)
        lt = RR[:, 2048:2048 + 512].bitcast(F32)
        P.dma("sp", "c0", vecs[:], d_vecs.rearrange("l p c -> p l c"), writes=[b_const])
        P.dma("sp", "c0", subln[:], d_subln.rearrange("l p c -> p l c"), writes=[Buf("x1")])
        P.dma("sp", "c0", rc[:], d_rc, writes=[Buf("x2")])
        P.dma("sp", "c0", lamt, d_lam.rearrange("l p c -> p l c"), writes=[b_lamt])
        P.dma("pool", "c1", biasT[:], d_bias.rearrange("p (h c) -> p h c", h=4), writes=[Buf("x3")])
        P.dma("pool", "c1", ident[:], d_ident, writes=[Buf("x4")])
        P.dma("sp", "c0", identF[:], d_ident, writes=[Buf("x5")])
        P.dma("sp", "c0", cbias[:], d_cb, writes=[Buf("x6")])
        P.barrier()
        P.op("dve", "memset", onesD[:], 1.0 / 1024.0, writes=[b_const])
        P.op("dve", "memset", nhalf[:], -0.5, writes=[b_const])
        P.op("dve", "memset", onesF[:], 1.0, writes=[b_const])
        P.op("dve", "memset", onesFF[:], 1.0, writes=[b_const])
        P.op("dve", "memset", onesC[:], 1.0 / 512.0, writes=[b_const])
        P.op("dve", "memset", uhalo[:], 0.0, writes=[b_const])
        P.op("dve", "memset", puhalo[:], 0.0, writes=[b_const])
        for i in range(2):
            P.op("dve", "memset", kvV[i][:, :, :, 128:129], 1.0, writes=[b_const])
        ssum = att_s[0]
        for l in range(DEPTH):
            ts("dve", vec2[:, l, 0:4], vecs[:, l, 32:36], 2.0, None, ALU.mult, None, [b_const], [b_v2])
            ts("dve", vec2[:, l, 4:12], vecs[:, l, 36:44], 0.5, None, ALU.mult, None, [b_const], [b_v2])
            ts("dve", subln[:, l, :], subln[:, l, :], 1.0 - LAMBDA_INIT[l], None, ALU.mult, None, [b_const], [b_v2])
            for j in range(2):
                tt("dve", lt[:, 64 * j:64 * j + 64], lamt[:, l, 128 * j:128 * j + 64],
                   lamt[:, l, 128 * j + 64:128 * j + 128], ALU.mult, [b_lamt], [b_lt])
                P.op("dve", "reduce_sum", ssum[:, 2 * l + j:2 * l + j + 1], lt[:, 64 * j:64 * j + 64], AX.X,
                     reads=[b_lt], writes=[b_atts[0]])
        act(ssum[:, 4:8], ssum[:, 0:4], AF.Exp, [b_atts[0]], [b_atts[0]])
        for l in range(DEPTH):
            tt("dve", nlam[:, l:l + 1], ssum[:, 5 + 2 * l:6 + 2 * l], ssum[:, 4 + 2 * l:5 + 2 * l], ALU.subtract,
               [b_atts[0]], [b_nl])
            ts("dve", nlam[:, l:l + 1], nlam[:, l:l + 1], -LAMBDA_INIT[l], None, ALU.add, None, [b_nl], [b_nl])
        P.barrier()

        def norm_rstd(srcs, src_bufs, eps, r):
            bank = 6 + (r % 2)
            for k in range(8):
                q = k % 2
                act(sq[q][:], srcs[k], AF.Square, [src_bufs[k]], [b_sq[q]])
                mm(ps[bank][0:1, :], onesD[:, 0:1], sq[q][:], k == 0, k == 7, [b_sq[q]], [b_ps[bank]])
            ts("dve", rowb[r][:], ps[bank][0:1, :], eps, None, ALU.add, None, [b_ps[bank]], [b_row[r]])
            return rsqrt_bcast(r, bank)

        def rsqrt_bcast(r, bank):
            for blk in range(4):
                mm(ps[bank][:, blk:blk + 1], rowb[r][0:1, blk * 128:(blk + 1) * 128], onesF[0:1, 0:1], True, True,
                   [b_row[r]], [b_ps[bank]])
            P.op("dve", "tensor_copy", colb[r % 2][:], ps[bank][:, 0:4], reads=[b_ps[bank]], writes=[b_col[r % 2]])
            tt("pool", colb[r % 2][:], colb[r % 2][:], nhalf[:, 0:4], ALU.pow, [b_col[r % 2]], [b_col[r % 2]])
            for blk in range(4):
                ts("dve", rstd[r % 2][:, blk * 128:(blk + 1) * 128], identF[:], colb[r % 2][:, blk:blk + 1], None, ALU.mult, None,
                   [b_col[r % 2]], [b_rstd[r % 2]])
            for blk in range(4):
                mm(ps[bank][:, blk * 128:(blk + 1) * 128], onesFF[:], rstd[r % 2][:, blk * 128:(blk + 1) * 128], True, True,
                   [b_rstd[r % 2]], [b_ps[bank]])
            return ps[bank][:], b_ps[bank]

        def post_norm_residual(yev, t, gcol, l, eps, r):
            rs, brs = norm_rstd([yev[:, m, :] for m in range(8)], b_yev, eps, r)
            for m in range(8):
                q = m % 2
                stt("dve", tmp[q][:], yev[:, m, :], vecs[:, l, gcol + m:gcol + m + 1], rs, ALU.mult, ALU.mult,
                    [b_yev[m], brs], [b_tmp[q]])
                xs = xT[:, m, t * 512:(t + 1) * 512]
                tt("dve", xs, xs, tmp[q][:], ALU.add, [b_tmp[q]], [b_x[m][t]])

        def pre_norm(dst, dst_bufs, t, gcol, l, r, tok0):
            rs, brs = norm_rstd([xT[:, k, t * 512:(t + 1) * 512] for k in range(8)], [b_x[k][t] for k in range(8)], EPS, r)
            for k in range(8):
                stt("dve", dst[:, k, tok0:tok0 + 512], xT[:, k, t * 512:(t + 1) * 512],
                    vecs[:, l, gcol + k:gcol + k + 1], rs, ALU.mult, ALU.mult,
                    [b_x[k][t], brs], [dst_bufs[k]])

        def walk():
            gen_n[0] = 0
            ev_n[0] = 0
            for sq_i in range(nseq):
                for half in halves:
                    tok_base = half * 1024
                    for k in range(8):
                        P.dma("sp", "xin", xT[:, k, :], d_xT[sq_i, k * 128:(k + 1) * 128, tok_base:tok_base + 1024],
                              writes=[b_x[k][0], b_x[k][1]])
                    P.barrier()
                    for l in layers:
                        if half == 0:
                            Kd, Vd, bKd, bVd = kvK[l], kvV[l], b_kvK[l], b_kvV[l]
                        else:
                            Kd, Vd, bKd, bVd = curK, curV, b_curK, b_curV
                            P.op("dve", "memset", curV[:, :, :, 128:129], 1.0, writes=b_curV)
                        P.mark(f"A s{sq_i} h{half} l{l}")
                        for t in range(2):
                            pre_norm(hT, [b_hT[k][t] for k in range(8)], t, 0, l, t, t * 512)
                        P.dma("pool", "pw", poolw[:], d_poolw[l].rearrange("p (g c) -> p g c", g=4), writes=[b_poolw])
                        P.mark("B")
                        wa, bwa = load_pack(d_win[l, 0], 4096)
                        wb, bwb = load_pack(d_win[l, 1], 4096)
                        wa3 = wa[:, :].rearrange("p (k c) -> p k c", k=8)
                        wb3 = wb[:, :].rearrange("p (k c) -> p k c", k=8)
                        for c in range(4):
                            if half == 0:
                                P.op("dve", "memset", uT[:, c, 0:30], 0.0, writes=[b_uT[c][0]])
                            else:
                                P.op("dve", "tensor_copy", uT[:, c, 0:30], uhalo[:, l, c, :], reads=[b_uh[l]], writes=[b_uT[c][0]])
                        for c in range(4):
                            for t in range(2):
                                ba, bb = gen_bank(), gen_bank()
                                for k in range(8):
                                    mm(ps[ba][:], wa3[:, k, c * 128:(c + 1) * 128], hT[:, k, t * 512:(t + 1) * 512],
                                       k == 0, k == 7, [bwa, b_hT[k][t]], [b_ps[ba]])
                                for k in range(8):
                                    mm(ps[bb][:], wb3[:, k, c * 128:(c + 1) * 128], hT[:, k, t * 512:(t + 1) * 512],
                                       k == 0, k == 7, [bwb, b_hT[k][t]], [b_ps[bb]])
                                q = t % 2
                                act(sg[q][:], ps[bb][:], AF.Tanh, [b_ps[bb]], [b_sg[q]], scale=0.5)
                                stt("dve", uT[:, c, 30 + t * 512:30 + (t + 1) * 512], sg[q][:], 1.0, ps[ba][:], ALU.add, ALU.mult,
                                    [b_sg[q], b_ps[ba]], [b_uT[c][t]])
                        if half == 0:
                            for c in range(4):
                                P.op("dve", "tensor_copy", uhalo[:, l, c, :], uT[:, c, 1024:1054],
                                     reads=[b_uT[c][1]], writes=[b_uh[l]])
                        for which in (2, 3):
                            w_, bw_ = load_pack(d_win[l, which], 4096)
                            w3 = w_[:, :].rearrange("p (k c) -> p k c", k=8)
                            for c in range(4):
                                for t in range(2):
                                    b = gen_bank()
                                    for k in range(8):
                                        mm(ps[b][:], w3[:, k, c * 128:(c + 1) * 128], hT[:, k, t * 512:(t + 1) * 512],
                                           k == 0, k == 7, [bw_, b_hT[k][t]], [b_ps[b]])
                                    if which == 2:
                                        act(QT[:, c, t * 512:(t + 1) * 512], ps[b][:], AF.Copy, [b_ps[b]], [b_QT[c][t]], scale=0.125)
                                    else:
                                        P.op("dve", "tensor_copy", Kd[:, c, t * 512:(t + 1) * 512], ps[b][:],
                                             reads=[b_ps[b]], writes=[bKd[t]])
                        w_, bw_ = load_pack(d_win[l, 4], 4096)
                        w3 = w_[:, :].rearrange("p (k c) -> p k c", k=8)
                        for tb in range(8):
                            b = gen_bank()
                            for k in range(8):
                                mm(ps[b][:], hT[:, k, tb * 128:(tb + 1) * 128], w3[:, k, :],
                                   k == 0, k == 7, [bw_, b_hT[k][tb // 4]], [b_ps[b]])
                            evac(Vd[:, tb, :, 0:128], ps[b][:].rearrange("p (h e) -> p h e", h=4), [b_ps[b]], [bVd[tb]])
                        w_, bw_ = load_pack(d_win[l, 5], 4096)
                        w3 = w_[:, :].rearrange("p (k c) -> p k c", k=8)
                        B3 = [(S2[0], b_S2[0]), (S2[1], b_S2[1]), (S2[2], b_S2[2])]
                        pu, bpu = B3[0]
                        for g in range(4):
                            win_w = 2 ** (g + 1)
                            for t in range(2):
                                b = gen_bank()
                                for k in range(8):
                                    mm(ps[b][:], w3[:, k, g * 128:(g + 1) * 128], hT[:, k, t * 512:(t + 1) * 512],
                                       k == 0, k == 7, [bw_, b_hT[k][t]], [b_ps[b]])
                                if half == 0 and t == 0:
                                    P.op("dve", "memset", pu[:, 0:16], 0.0, writes=[bpu])
                                else:
                                    P.op("dve", "tensor_copy", pu[:, 0:16], puhalo[:, l, g, :], reads=[b_ph[l][g]], writes=[bpu])
                                act(pu[:, 16:528], ps[b][:], AF.Copy, [b_ps[b]], [bpu])
                                P.op("dve", "tensor_copy", puhalo[:, l, g, :], pu[:, 512:528], reads=[bpu], writes=[b_ph[l][g]])
                                si, sh, lo = 0, 1, 1
                                for st in range(g + 1):
                                    di = 1 if si != 1 else 2
                                    lo += sh
                                    sv, sbf = B3[si]
                                    dv, dbf = B3[di]
                                    tt("dve", dv[:, lo:528], sv[:, lo:528], sv[:, lo - sh:528 - sh], ALU.add, [sbf], [dbf])
                                    si, sh = di, sh * 2
                                av, abf = B3[si]
                                fi = 2 if si == 1 else 1
                                fv, fbf = B3[fi]
                                pooled = fv[:, 0:256].bitcast(BF16)
                                stt("dve", pooled, av[:, 16:528], 1.0 / win_w, pu[:, 16:528], ALU.mult, ALU.subtract,
                                    [abf, bpu], [fbf])
                                if half == 0 and t == 0:
                                    nfix = win_w - 1
                                    tt("dve", tmp[0][:, 0:nfix], av[:, 16:16 + nfix], rc[:, 0:nfix], ALU.mult, [abf], [b_tmp[0]])
                                    tt("dve", pooled[:, 0:nfix], tmp[0][:, 0:nfix], pu[:, 16:16 + nfix], ALU.subtract,
                                       [b_tmp[0], bpu], [fbf])
                                b2 = gen_bank()
                                mm(ps[b2][:], poolw[:, g, :], pooled, True, True, [b_poolw, fbf], [b_ps[b2]])
                                act(p2T[:, g, t * 512:(t + 1) * 512], ps[b2][:], AF.Copy, [b_ps[b2]], [b_p2T[g][t]],
                                    scale=vecs[:, l, 44 + g:45 + g])
                        P.mark("C")
                        ETf = RR[:, 37016:37016 + 1024].bitcast(F32)

                        def cv(t, c):
                            if t == 0:
                                return stash[:, c, :], [b_OT[c][0], b_OT[c][1]]
                            if c < 3:
                                return S2[c][:, 0:512], [b_S2[c]]
                            return ETf, [b_ET[0], b_ET[1]]

                        for c in range(4):
                            pk, bpk = load_pack(d_cdiag[l, c], 4096)
                            pk3 = pk[:, :].rearrange("p (j q) -> p j q", j=32)
                            for t in range(2):
                                base = t * 512
                                bk = gen_bank()
                                for j in range(31):
                                    mm(ps[bk][:], pk3[:, j, :], uT[:, c, base + j:base + j + 512], j == 0, j == 30,
                                       [bpk, b_uT[c][0], b_uT[c][1]], [b_ps[bk]])
                                dst, dbufs = cv(t, c)
                                ts("dve", dst, ps[bk][:], vec2[:, l, c:c + 1], None, ALU.add, None, [b_ps[bk]], dbufs)
                        for t in range(2):
                            for c in range(4):
                                q = c % 2
                                sv, sbufs = cv(t, c)
                                act(sq[q][:], sv, AF.Copy, sbufs, [b_sq[q]])
                                mm(ps[6][0:1, :], onesC[:, 0:1], sq[q][:], c == 0, c == 3, [b_sq[q]], [b_ps[6]])
                            for c in range(4):
                                q = c % 2
                                sv, sbufs = cv(t, c)
                                act(sq[q][:], sv, AF.Square, sbufs, [b_sq[q]])
                                mm(ps[7][0:1, :], onesC[:, 0:1], sq[q][:], c == 0, c == 3, [b_sq[q]], [b_ps[7]])
                            P.op("dve", "tensor_copy", rowb[2][:], ps[6][0:1, :], reads=[b_ps[6]], writes=[b_row[2]])
                            tt("dve", rowb[0][:], rowb[2][:], rowb[2][:], ALU.mult, [b_row[2]], [b_row[0]])
                            tt("dve", rowb[0][:], ps[7][0:1, :], rowb[0][:], ALU.subtract, [b_ps[7], b_row[0]], [b_row[0]])
                            ts("dve", rowb[0][:], rowb[0][:], 4.0 * EPS, None, ALU.add, None, [b_row[0]], [b_row[0]])
                            mm(ps[6][:], onesF[:], rowb[2][:], True, True, [b_row[2]], [b_ps[6]])
                            rsqrt_bcast(0, 7)
                            for c in range(4):
                                q = c % 2
                                sv, sbufs = cv(t, c)
                                tt("dve", tmp[q][:], sv, ps[6][:], ALU.subtract, sbufs + [b_ps[6]], [b_tmp[q]])
                                tt("dve", tmp[q][:], tmp[q][:], ps[7][:], ALU.mult, [b_tmp[q], b_ps[7]], [b_tmp[q]])
                                ts("dve", tmp[q][:], tmp[q][:], vec2[:, l, 4 + c:5 + c], vec2[:, l, 8 + c:9 + c], ALU.mult, ALU.add,
                                   [b_tmp[q]], [b_tmp[q]])
                                act(sg[q][:], tmp[q][:], AF.Tanh, [b_tmp[q]], [b_sg[q]])
                                stt("dve", cT[:, c, t * 512:(t + 1) * 512], sg[q][:], 1.0, tmp[q][:], ALU.add, ALU.mult,
                                    [b_sg[q], b_tmp[q]], [b_cT[c][t]])
                        P.mark("D")
                        blocks = []
                        for h in range(4):
                            for qt in range(2):
                                Q0 = tok_base + qt * 512
                                for kb in range(0, Q0 // 128 + 4):
                                    blocks.append((h, qt, Q0, kb))
                        nb = len(blocks)
                        state = {"started": set(), "unit": None, "at_n": 0, "tr_n": 0, "chains": []}
                        pend_tr = {}

                        def blk_params(i):
                            h, qt, Q0, kb = blocks[i]
                            k0 = kb * 128
                            c0 = max(0, k0 - Q0)
                            n = 512 - c0
                            d0 = min(Q0 + c0 - k0, 256)
                            if half == 1 and kb >= 8:
                                Ks, Vs, bKs, bVs = curK, curV, b_curK[(kb - 8) // 4], b_curV[kb - 8]
                                kk0, vb = k0 - 1024, kb - 8
                            else:
                                Ks, Vs, bKs, bVs = kvK[l], kvV[l], b_kvK[l][kb // 4], b_kvV[l][kb]
                                kk0, vb = k0, kb
                            return h, qt, Q0, kb, k0, c0, n, d0, Ks, Vs, bKs, bVs, kk0, vb

                        def rec_S(i):
                            h, qt, Q0, kb, k0, c0, n, d0, Ks, Vs, bKs, bVs, kk0, vb = blk_params(i)
                            far = (d0 == 256)
                            for mi in range(2):
                                sbk = (i % 2) * 2 + mi
                                mm(ps[sbk][:, 0:n], Ks[64 * mi:64 * mi + 64, h, kk0:kk0 + 128],
                                   QT[64 * mi:64 * mi + 64, h, qt * 512 + c0:qt * 512 + 512],
                                   True, far, [bKs, b_QT[h][qt]], [b_ps[sbk]])
                            for mi in range(2):
                                sbk = (i % 2) * 2 + mi
                                if not far:
                                    mm(ps[sbk][:, 0:n], ident[:], biasT[:, h, d0:d0 + n], False, True, [b_const], [b_ps[sbk]])
                            for mi in range(2):
                                sbk = (i % 2) * 2 + mi
                                if far:
                                    act(ET[:, sbk, 0:n], ps[sbk][:, 0:n], AF.Exp, [b_ps[sbk]], [b_ET[sbk]], bias=cbias[:, h:h + 1])
                                else:
                                    act(ET[:, sbk, 0:n], ps[sbk][:, 0:n], AF.Exp, [b_ps[sbk]], [b_ET[sbk]])

                        def rec_AV(i):
                            h, qt, Q0, kb, k0, c0, n, d0, Ks, Vs, bKs, bVs, kk0, vb = blk_params(i)
                            if state["unit"] != (h, qt):
                                state["unit"] = (h, qt)
                                state["started"] = set()
                            started = state["started"]
                            for mi in range(2):
                                sbk = (i % 2) * 2 + mi
                                for j in range(c0 // 128, 4):
                                    a = j * 2 + mi
                                    bank, off = 4 + a // 3, (a % 3) * 160
                                    st1 = bank not in started
                                    started.add(bank)
                                    P.op("pe", "matmul", ps[bank][:, off:off + 129],
                                         ET[:, sbk, j * 128 - c0:j * 128 - c0 + 128], Vs[:, vb, h, :],
                                         start=st1, stop=(k0 == Q0 + j * 128), skip_group_check=True,
                                         reads=[b_ET[sbk], bVs], writes=[b_ps[bank]])
                            if k0 >= Q0:
                                j = (k0 - Q0) // 128
                                z = j % 2
                                state["chains"].append(norm_chain(j, z, h, qt, i))
                                if j % 2 == 1:
                                    ca_, cb_ = state["chains"]
                                    state["chains"] = []
                                    for sa, sb_ in zip(ca_, cb_):
                                        sa()
                                        sb_()

                        def norm_chain(j, z, h, qt, i):
                            a1, a2 = j * 2, j * 2 + 1
                            O1 = ps[4 + a1 // 3][:, (a1 % 3) * 160:(a1 % 3) * 160 + 129]
                            O2 = ps[4 + a2 // 3][:, (a2 % 3) * 160:(a2 % 3) * 160 + 129]
                            bO1, bO2 = b_ps[4 + a1 // 3], b_ps[4 + a2 // 3]
                            st_, bst = att_s[z], b_atts[z]
                            fa, bfa = att_f[2 * z], b_attf[2 * z]
                            fo, bfo = att_f[2 * z + 1], b_attf[2 * z + 1]
                            on, bon = att_b[z], b_attb[z]

                            def fin():
                                stt("dve", on[:], fo[:], st_[:, 5:6], subln[:, l, :], ALU.mult, ALU.mult, [bst, bfo], [bon])
                                pend_tr[i] = (on, bon, h, qt * 512 + j * 128, qt)
                            return [
                                lambda: P.op("dve", "reciprocal", st_[:, 0:1], O1[:, 128:129], reads=[bO1], writes=[bst]),
                                lambda: P.op("dve", "reciprocal", st_[:, 1:2], O2[:, 128:129], reads=[bO2], writes=[bst]),
                                lambda: P.op("dve", "memset", st_[:, 3:4], 0.0, writes=[bst]),
                                lambda: tt("dve", st_[:, 2:3], st_[:, 1:2], nlam[:, l:l + 1], ALU.mult, [bst], [bst]),
                                lambda: ts("dve", fa[:], O1[:, 0:128], st_[:, 0:1], None, ALU.mult, None, [bst, bO1], [bfa]),
                                lambda: stt("dve", fo[:], O2[:, 0:128], st_[:, 2:3], fa[:], ALU.mult, ALU.add, [bst, bO2, bfa], [bfo]),
                                lambda: act(fa[:], fo[:], AF.Square, [bfo, bst], [bfa, bst], accum_out=st_[:, 3:4]),
                                lambda: ts("dve", st_[:, 4:5], st_[:, 3:4], 1.0 / 128.0, EPS, ALU.mult, ALU.add, [bst], [bst]),
                                lambda: tt("pool", st_[:, 5:6], st_[:, 4:5], nhalf[:, 0:1], ALU.pow, [bst], [bst]),
                                fin,
                            ]

                        def rec_TR(i):
                            if i not in pend_tr:
                                return
                            on, bon, h, qa, qt = pend_tr.pop(i)
                            tsl = state["tr_n"] % 4
                            state["tr_n"] += 1
                            mm(ps[7][:, tsl * 128:(tsl + 1) * 128], on[:], ident[:], True, True, [bon], [b_ps[7]])
                            act(OT[:, h, qa:qa + 128], ps[7][:, tsl * 128:(tsl + 1) * 128], AF.Copy, [b_ps[7]], [b_OT[h][qt]])

                        for i in range(nb + 3):
                            if i < nb:
                                rec_S(i)
                            if 0 <= i - 3 < nb:
                                rec_TR(i - 3)
                            if 0 <= i - 1 < nb:
                                rec_AV(i - 1)
                        P.barrier()
                        P.mark("E")
                        srcs = [(cT, b_cT), (OT, b_OT), (p2T, b_p2T)]
                        for m in range(8):
                            wg, bwg = load_pack(d_mrgg[l, m], 3072)
                            wbo, bwbo = load_pack(d_mrgb[l, m], 1536)
                            wg4 = wg[:, 0:3072].rearrange("p (b k j) -> p b k j", b=3, k=8)
                            wbo4 = wbo[:, 0:1536].rearrange("p (b k j) -> p b k j", b=3, k=4)
                            for t in range(2):
                                for br in range(3):
                                    bg, by = gen_bank(), gen_bank()
                                    for k in range(8):
                                        mm(ps[bg][:], wg4[:, br, k, :], hT[:, k, t * 512:(t + 1) * 512], k == 0, k == 7,
                                           [bwg, b_hT[k][t]], [b_ps[bg]])
                                    sT, bsT = srcs[br]
                                    for k in range(4):
                                        mm(ps[by][:], wbo4[:, br, k, :], sT[:, k, t * 512:(t + 1) * 512], k == 0, k == 3,
                                           [bwbo, bsT[k][t]], [b_ps[by]])
                                    q = br % 2
                                    act(sg[q][:], ps[bg][:], AF.Tanh, [b_ps[bg]], [b_sg[q]], scale=0.5)
                                    if br == 0:
                                        stt("dve", tmp[0][:], sg[q][:], 1.0, ps[by][:], ALU.add, ALU.mult, [b_sg[q], b_ps[by]], [b_tmp[0]])
                                    else:
                                        stt("dve", tmp[1][:], sg[q][:], 1.0, ps[by][:], ALU.add, ALU.mult, [b_sg[q], b_ps[by]], [b_tmp[1]])
                                        if br == 1:
                                            tt("dve", tmp[0][:], tmp[0][:], tmp[1][:], ALU.add, [b_tmp[1]], [b_tmp[0]])
                                        else:
                                            tt("dve", mergedT[:, m, t * 512:(t + 1) * 512], tmp[0][:], tmp[1][:], ALU.add,
                                               [b_tmp[0], b_tmp[1]], [b_mg[m][t]])
                        for t in range(2):
                            for cg in range(2):
                                w_, bw_ = load_pack(d_wout[l, cg], 4096)
                                w3 = w_[:, :].rearrange("p (k c) -> p k c", k=8)
                                for m4 in range(4):
                                    m = cg * 4 + m4
                                    b = gen_bank()
                                    for k in range(8):
                                        mm(ps[b][:], w3[:, k, m4 * 128:(m4 + 1) * 128], mergedT[:, k, t * 512:(t + 1) * 512],
                                           k == 0, k == 7, [bw_, b_mg[k][t]], [b_ps[b]])
                                    evac(yevE[:, m, :], ps[b][:], [b_ps[b]], [b_yev[m]])
                            post_norm_residual(yevE, t, 8, l, 4.0 * EPS, t)
                        P.barrier()
                        P.mark("F")
                        pre_norm(h2T, b_h2, 0, 16, l, 0, 0)
                        for t in range(2):
                            hcur, bh = (h2T, b_h2) if t == 0 else (h2Tb, b_h2b)
                            for fg in range(8):
                                w_, bw_ = load_pack(d_w1[l, fg], 4096)
                                w3 = w_[:, :].rearrange("p (k c) -> p k c", k=8)
                                for i in range(4):
                                    jf = fg * 4 + i
                                    b = gen_bank()
                                    for k in range(8):
                                        mm(ps[b][:], w3[:, k, i * 128:(i + 1) * 128], hcur[:, k, :], k == 0, k == 7,
                                           [bw_, bh[k]], [b_ps[b]])
                                    q = jf % 2
                                    act(relu_s[:, q, :], ps[b][:], AF.Relu, [b_ps[b]], [b_relu[q]])
                                    tt("dve", f1[:, jf, :], relu_s[:, q, :], relu_s[:, q, :], ALU.mult, [b_relu[q]], [b_f1[jf]])
                            if t == 0:
                                pre_norm(h2Tb, b_h2b, 1, 16, l, 1, 0)
                            for jg in range(8):
                                w_, bw_ = load_pack(d_w2[l, jg], 4096)
                                w3 = w_[:, :].rearrange("p (j c) -> p j c", j=4)
                                for jj in range(4):
                                    jf = jg * 4 + jj
                                    for m in range(8):
                                        mm(ps[m][:], w3[:, jj, m * 128:(m + 1) * 128], f1[:, jf, :], jf == 0, jf == 31,
                                           [bw_, b_f1[jf]], [b_ps[m]])
                            for m in range(8):
                                evac(yevF[:, m, :], ps[m][:], [b_ps[m]], [b_yev[m]])
                            post_norm_residual(yevF, t, 24, l, EPS, t)
                        P.barrier()
                        P.mark("G")
                        P.dma("pool", "pin", pTs, d_pT[l, sq_i, :, tok_base:tok_base + 1024].rearrange("(k p) t -> p k t", p=128),
                              writes=[b_pTs])
                        for k in range(8):
                            for t in range(2):
                                if (k + t) % 2:
                                    act(xbT[:, k, t * 512:(t + 1) * 512], xT[:, k, t * 512:(t + 1) * 512], AF.Copy,
                                        [b_x[k][t]], [b_xb[k][t]])
                                else:
                                    P.op("dve", "tensor_copy", xbT[:, k, t * 512:(t + 1) * 512], xT[:, k, t * 512:(t + 1) * 512],
                                         reads=[b_x[k][t]], writes=[b_xb[k][t]])
                        for cg in range(2):
                            w_, bw_ = load_pack(d_wpg[l, cg], 4096)
                            w3 = w_[:, :].rearrange("p (k c) -> p k c", k=8)
                            if cg == 0:
                                wpe_, bwpe = load_pack(d_wpe[l], 2048)
                                wpe3 = wpe_[:, 0:2048].rearrange("p (k c) -> p k c", k=2)
                            for m4 in range(4):
                                m = cg * 4 + m4
                                for t in range(2):
                                    bg, bp = gen_bank(), gen_bank()
                                    for k in range(8):
                                        mm(ps[bg][:], w3[:, k, m4 * 128:(m4 + 1) * 128], xbT[:, k, t * 512:(t + 1) * 512],
                                           k == 0, k == 7, [bw_, b_xb[k][t]], [b_ps[bg]])
                                    for k in range(2):
                                        mm(ps[bp][:], wpe3[:, k, m * 128:(m + 1) * 128], pTs[:, k, t * 512:(t + 1) * 512],
                                           k == 0, k == 1, [bwpe, b_pTs], [b_ps[bp]])
                                    q = t % 2
                                    act(sg[q][:], ps[bg][:], AF.Tanh, [b_ps[bg]], [b_sg[q]], scale=0.5)
                                    act(relu_s[:, q, :], ps[bp][:], AF.Copy, [b_ps[bp]], [b_relu[q]], scale=0.5)
                                    stt("dve", tmp[q][:], sg[q][:], 1.0, relu_s[:, q, :], ALU.add, ALU.mult, [b_sg[q], b_relu[q]], [b_tmp[q]])
                                    xs = xT[:, m, t * 512:(t + 1) * 512]
                                    tt("dve", xs, xs, tmp[q][:], ALU.add, [b_tmp[q]], [b_x[m][t]])
                        P.barrier()
                    for k in range(8):
                        P.dma("sp", "xout", d_out[sq_i, k * 128:(k + 1) * 128, tok_base:tok_base + 1024], xT[:, k, :],
                              reads=[b_x[k][0], b_x[k][1]])
                    P.barrier()
        P.dry = True
        walk()
        P.dry = False
        walk()
        counts = P.emit()
        counts["marks"] = P.marks
    return nc, counts


def _rel_bucket(n):
    n = np.maximum(n, 0)
    nf = np.maximum(n, 1).astype(np.float32)
    large = 16 + (np.log(nf / np.float32(16)) / np.float32(math.log(128 / 16)) * np.float32(16)).astype(np.int32)
    large = np.minimum(large, 31)
    return np.where(n < 16, n, large)


def host_layout(inp):
    f = np.float32
    g = {k: np.asarray(v, dtype=f) for k, v in inp.items()}
    L = DEPTH
    sh = {}
    w_in = g["w_in"]
    sh["win"] = np.ascontiguousarray(
        w_in[:, :, :3072].reshape(L, 8, 128, 6, 512).transpose(0, 3, 2, 1, 4)).reshape(L, 6, 128, 4096)
    gates = w_in[:, :, 3072:].reshape(L, 8, 128, 3, 8, 128)
    sh["mrgg"] = np.ascontiguousarray(gates.transpose(0, 4, 2, 3, 1, 5)).reshape(L, 8, 128, 3072)
    bo = np.stack([g["w_conv_out"], g["w_attn_out"], g["w_pool_out"]], axis=1)
    bo = bo.reshape(L, 3, 4, 128, 8, 128)
    sh["mrgb"] = np.ascontiguousarray(bo.transpose(0, 4, 3, 1, 2, 5)).reshape(L, 8, 128, 1536)

    def colgroups(w, ng):
        return np.ascontiguousarray(w.reshape(L, 8, 128, ng, 512).transpose(0, 3, 2, 1, 4)).reshape(L, ng, 128, 4096)

    sh["wout"] = colgroups(g["w_out"], 2)
    sh["w1"] = colgroups(g["w_mlp_in"], 8)
    sh["wpg"] = colgroups(g["w_ple_gate"], 2)
    sh["w2"] = np.ascontiguousarray(g["w_mlp_out"].reshape(L, 8, 4, 128, 1024).transpose(0, 1, 3, 2, 4)).reshape(L, 8, 128, 4096)
    sh["wpe"] = np.ascontiguousarray(g["w_ple_proj"].reshape(L, 2, 128, 1024).transpose(0, 2, 1, 3)).reshape(L, 128, 2048)
    sh["poolw"] = np.ascontiguousarray(g["pool_w"].transpose(0, 2, 1, 3)).reshape(L, 128, 512)
    vecs = np.zeros((L, 128, 48), f)

    def cols(v, n):
        return v.reshape(L, n, 128).transpose(0, 2, 1)
    vecs[:, :, 0:8] = cols(g["g_pre_mix"], 8)
    vecs[:, :, 8:16] = cols(g["g_post_mix"], 8)
    vecs[:, :, 16:24] = cols(g["g_pre_mlp"], 8)
    vecs[:, :, 24:32] = cols(g["g_post_mlp"], 8)
    vecs[:, :, 32:36] = cols(g["conv_dw_b"], 4)
    vecs[:, :, 36:40] = cols(g["conv_ln_g"], 4)
    vecs[:, :, 40:44] = cols(g["conv_ln_b"], 4)
    vecs[:, :, 44:48] = cols(g["pool_scale"], 4)
    w4 = g["conv_dw_w"][:, :, 0, :].reshape(L, 31, 4, 128).transpose(0, 2, 3, 1)
    cd = np.zeros((L, 4, 128, 32, 128), f)
    ar = np.arange(128)
    cd[:, :, ar, :31, ar] = w4.transpose(2, 0, 1, 3)
    sh["cdiag"] = cd.reshape(L, 4, 128, 4096)
    sh["vecs"] = vecs
    sh["sublnb"] = np.ascontiguousarray(np.broadcast_to(g["subln_g"][:, None, :], (L, 128, 128)))
    sh["lamb"] = np.ascontiguousarray(np.broadcast_to(g["lam_p"].reshape(L, 1, 256), (L, 128, 256)))
    kk = np.arange(128)[:, None]
    dd = np.arange(768)[None, :]
    nrel = dd - kk
    bidx = _rel_bucket(nrel)
    tab = g["rel_bias"][bidx]
    tab = np.where((nrel >= 0)[:, :, None], tab, f(-1e30))
    sh["biasT"] = np.ascontiguousarray(tab.transpose(0, 2, 1)).reshape(128, 4 * 768).astype(f)
    sh["ident"] = np.eye(128, dtype=f)
    sh["cbias"] = np.ascontiguousarray(np.broadcast_to(g["rel_bias"][31][None, :], (128, 4)))
    rc = np.zeros((128, 16), f)
    rc[:, :] = 1.0 / (np.arange(16, dtype=f) + 1.0)
    sh["rc"] = rc
    x = g["x"]
    p = g["p"]
    per_core = []
    for c in range(8):
        d = dict(sh)
        d["xT"] = np.ascontiguousarray(x[2 * c:2 * c + 2].transpose(0, 2, 1))
        d["pT"] = np.ascontiguousarray(p[:, 2 * c:2 * c + 2].transpose(0, 1, 3, 2))
        per_core.append(d)
    return per_core


_CACHE = {}


def kernel(**inputs):
    if "nc" not in _CACHE:
        _CACHE["nc"] = build_program()[0]
    nc = _CACHE["nc"]
    in_maps = host_layout(inputs)
    res = run_bass_kernel_spmd(nc, in_maps, core_ids=list(range(8)))
    out = np.empty((16, S, D), np.float32)
    for c in range(8):
        out[2 * c:2 * c + 2] = res.results[c]["outT"].transpose(0, 2, 1)
    return out
```

```python
import math
from contextlib import ExitStack

import numpy as np
import concourse.bass as bass
import concourse.mybir as mybir
from concourse.bass_utils import run_bass_kernel_spmd

F32 = mybir.dt.float32
BF16 = mybir.dt.bfloat16
ALU = mybir.AluOpType
AF = mybir.ActivationFunctionType
AX = mybir.AxisListType

DEPTH = 2
NSEQ = 2
S = 2048
D = 1024
EPS = 1e-6
LAMBDA_INIT = [0.8 - 0.6 * math.exp(-0.3 * i) for i in range(DEPTH)]

COMPUTE = ("pe", "act", "dve", "pool")
STREAMS = ("pe", "act", "dve", "pool", "sp")


class Buf:
    __slots__ = ("name", "lw", "rd")

    def __init__(self, name):
        self.name = name
        self.lw = None
        self.rd = []


class Op:
    __slots__ = ("stream", "fn", "waits", "key", "idx", "signal", "is_dma", "cnt")


class Prog:
    def __init__(self, nc):
        self.nc = nc
        self.ops = {s: [] for s in STREAMS}
        self.cnt = {}
        self.vc = {s: {} for s in STREAMS}
        self.opvc = {}
        self.bykey = {}
        self.dma_keys = []
        self.pend = {s: {} for s in STREAMS}
        self.marks = []
        self.dry = False

    def mark(self, name):
        if not self.dry:
            self.marks.append((name, len(self.ops["pe"])))

    def barrier(self):
        if self.dry:
            return
        snap = dict(self.cnt)
        for s in STREAMS:
            for k, i in snap.items():
                if self.pend[s].get(k, 0) < i:
                    self.pend[s][k] = i

    def _record(self, stream, key, fn, reads, writes, is_dma):
        if self.dry:
            return None
        deps = set()
        for b in reads:
            if b.lw is not None:
                deps.add(b.lw)
        for b in writes:
            if b.lw is not None:
                deps.add(b.lw)
            for r in b.rd:
                deps.add(r)
        for k, i in self.pend[stream].items():
            deps.add((k, i))
        self.pend[stream] = {}
        idx = self.cnt.get(key, 0) + 1
        self.cnt[key] = idx
        me = (key, idx)
        vc = self.vc[stream]
        waits = {}
        for (k, i) in deps:
            if k == stream and not is_dma:
                continue
            if vc.get(k, 0) >= i:
                continue
            if waits.get(k, 0) < i:
                waits[k] = i
        if stream in ("act", "dve", "pool") and not is_dma:
            for b in reads:
                if b.lw is not None and b.lw[0] == stream and vc.get(("self", stream), 0) < b.lw[1]:
                    if waits.get(stream, 0) < b.lw[1]:
                        waits[stream] = b.lw[1]
        for k, i in waits.items():
            if k == stream:
                vc[("self", stream)] = max(vc.get(("self", stream), 0), i)
                continue
            dvc = self.opvc.get((k, i))
            if dvc:
                for kk, ii in dvc.items():
                    if vc.get(kk, 0) < ii:
                        vc[kk] = ii
            if vc.get(k, 0) < i:
                vc[k] = i
        op = Op()
        op.stream, op.fn, op.waits, op.key, op.idx = stream, fn, waits, key, idx
        op.signal, op.is_dma, op.cnt = is_dma, is_dma, 0
        self.ops[stream].append(op)
        self.bykey[me] = op
        snap = {k: v for k, v in vc.items() if not isinstance(k, tuple) and k != stream}
        if not is_dma:
            snap[stream] = idx
        self.opvc[me] = snap
        for b in reads:
            b.rd.append(me)
        for b in writes:
            b.lw = me
            b.rd = []
        return op

    def op(self, stream, method, *args, reads=(), writes=(), **kw):
        return self._record(stream, stream, (method, args, kw), list(reads), list(writes), False)

    def dma(self, queue, semkey, out, in_, reads=(), writes=()):
        if semkey not in self.dma_keys:
            self.dma_keys.append(semkey)
        return self._record(queue, semkey, ("dma_start", (), {"out": out, "in_": in_}), list(reads), list(writes), True)

    def emit(self):
        nc = self.nc
        for s in STREAMS:
            for op in self.ops[s]:
                for k, i in op.waits.items():
                    if k in COMPUTE:
                        self.bykey[(k, i)].signal = True
        finals = {}
        for k in COMPUTE:
            if self.cnt.get(k, 0):
                self.bykey[(k, self.cnt[k])].signal = True
                finals[k] = self.cnt[k]
        for k in self.dma_keys:
            finals[k] = self.cnt[k]
        for k in COMPUTE:
            c = 0
            for i in range(1, self.cnt.get(k, 0) + 1):
                op = self.bykey[(k, i)]
                if op.signal:
                    c += 1
                op.cnt = c
        with ExitStack() as es:
            sems = {}
            for k in COMPUTE:
                if self.cnt.get(k, 0):
                    sems[k] = es.enter_context(nc.semaphore("s_" + k))
            for k in self.dma_keys:
                sems[k] = es.enter_context(nc.semaphore("d_" + str(k)))
            block = es.enter_context(nc.Block())

            def val(k, i):
                if k in COMPUTE:
                    return self.bykey[(k, i)].cnt
                return 16 * i

            def run(stream, eng):
                for op in self.ops[stream]:
                    for k, i in op.waits.items():
                        eng.wait_ge(sems[k], val(k, i))
                    meth, args, kw = op.fn
                    ins = getattr(eng, meth)(*args, **kw)
                    if op.is_dma:
                        ins.then_inc(sems[op.key], 16)
                    elif op.signal:
                        ins.then_inc(sems[op.key], 1)
                if stream == "sp":
                    for k, i in finals.items():
                        eng.wait_ge(sems[k], val(k, i))

            @block.tensor
            def _(e):
                run("pe", e)

            @block.scalar
            def _(e):
                run("act", e)

            @block.vector
            def _(e):
                run("dve", e)

            @block.gpsimd
            def _(e):
                run("pool", e)

            @block.sync
            def _(e):
                run("sp", e)
        return {s: len(self.ops[s]) for s in STREAMS}


def build_program(layers=(0, 1), nseq=NSEQ, halves=(0, 1)):
    nc = bass.Bass("TRN2", target_bir_lowering=False)
    P = Prog(nc)

    def dram(name, shape, kind="ExternalInput"):
        return nc.dram_tensor(name, list(shape), F32, kind=kind).ap()

    d_xT = dram("xT", [NSEQ, D, S])
    d_pT = dram("pT", [DEPTH, NSEQ, 256, S])
    d_out = dram("outT", [NSEQ, D, S], kind="ExternalOutput")
    d_win = dram("win", [DEPTH, 6, 128, 4096])
    d_mrgg = dram("mrgg", [DEPTH, 8, 128, 3072])
    d_mrgb = dram("mrgb", [DEPTH, 8, 128, 1536])
    d_wout = dram("wout", [DEPTH, 2, 128, 4096])
    d_w1 = dram("w1", [DEPTH, 8, 128, 4096])
    d_w2 = dram("w2", [DEPTH, 8, 128, 4096])
    d_wpg = dram("wpg", [DEPTH, 2, 128, 4096])
    d_wpe = dram("wpe", [DEPTH, 128, 2048])
    d_poolw = dram("poolw", [DEPTH, 128, 512])
    d_vecs = dram("vecs", [DEPTH, 128, 48])
    d_cdiag = dram("cdiag", [DEPTH, 4, 128, 4096])
    d_subln = dram("sublnb", [DEPTH, 128, 128])
    d_lam = dram("lamb", [DEPTH, 128, 256])
    d_bias = dram("biasT", [128, 4 * 768])
    d_ident = dram("ident", [128, 128])
    d_rc = dram("rc", [128, 16])
    d_cb = dram("cbias", [128, 4])

    es = ExitStack()
    with es:
        def sb(name, shape, dt):
            return es.enter_context(nc.sbuf_tensor(name, list(shape), dt))

        xT = sb("xT_s", [128, 8, 1024], F32)
        kvK = [sb(f"kvK{i}", [128, 4, 1024], BF16) for i in range(2)]
        kvV = [sb(f"kvV{i}", [128, 8, 4, 129], BF16) for i in range(2)]
        NRR = 39072
        RR = sb("RR", [128, NRR], BF16)
        S2 = [sb(f"S2_{i}", [128, 528], F32) for i in range(3)]
        sq = [sb(f"sq{i}", [128, 512], BF16) for i in range(2)]
        rstd = [sb(f"rstd{i}", [128, 512], F32) for i in range(2)]
        sg = [sb(f"sg{i}", [128, 512], F32) for i in range(2)]
        tmp = [sb(f"tmp{i}", [128, 512], F32) for i in range(2)]
        ring = [sb(f"ring{i}", [128, 4096], BF16) for i in range(4)]
        biasT = sb("biasT_s", [128, 4, 768], BF16)
        ident = sb("ident_s", [128, 128], BF16)
        onesD = sb("onesD", [128, 128], BF16)
        onesC = sb("onesC", [128, 128], BF16)
        vecs = sb("vecs_s", [128, DEPTH, 48], F32)
        vec2 = sb("vec2_s", [128, DEPTH, 12], F32)
        subln = sb("subln_s", [128, DEPTH, 128], F32)
        nlam = sb("nlam", [128, DEPTH], F32)
        rc = sb("rc_s", [128, 16], F32)
        nhalf = sb("nhalf", [128, 8], F32)
        onesF = sb("onesF", [1, 128], F32)
        cbias = sb("cbias_s", [128, 4], F32)
        identF = sb("identF", [128, 128], F32)
        onesFF = sb("onesFF", [128, 128], F32)
        colb = [sb(f"colb{i}", [128, 4], F32) for i in range(2)]
        poolw = sb("poolw_s", [128, 4, 128], BF16)
        uhalo = sb("uhalo", [128, DEPTH, 4, 30], BF16)
        puhalo = sb("puhalo", [128, DEPTH, 4, 16], F32)
        att_f = [sb(f"attf{i}", [128, 128], F32) for i in range(4)]
        att_b = [sb(f"attb{i}", [128, 128], BF16) for i in range(2)]
        att_s = [sb(f"atts{i}", [128, 8], F32) for i in range(2)]
        ps = [es.enter_context(nc.psum_tensor(f"ps{i}", [128, 512], F32)) for i in range(8)]

        b_ps = [Buf(f"ps{i}") for i in range(8)]
        b_ring = [Buf(f"ring{i}") for i in range(4)]
        b_x = [[Buf(f"x{k}_{t}") for t in range(2)] for k in range(8)]
        b_sq = [Buf("sq0"), Buf("sq1")]
        b_rstd = [Buf("rstd0"), Buf("rstd1")]
        b_sg = [Buf("sg0"), Buf("sg1")]
        b_tmp = [Buf("tmp0"), Buf("tmp1")]
        b_S2 = [Buf(f"S2_{i}") for i in range(3)]
        b_const = Buf("const")
        b_kvK = [[Buf(f"kvK{i}_{t}") for t in range(2)] for i in range(2)]
        b_kvV = [[Buf(f"kvV{i}_{tb}") for tb in range(8)] for i in range(2)]
        b_uh = [Buf(f"uh{l}") for l in range(DEPTH)]
        b_ph = [[Buf(f"ph{l}_{g}") for g in range(4)] for l in range(DEPTH)]
        b_attf = [Buf(f"attf{i}") for i in range(4)]
        b_attb = [Buf(f"attb{i}") for i in range(2)]
        b_atts = [Buf(f"atts{i}") for i in range(2)]
        b_poolw = Buf("poolw")
        rowb = [rstd[0][0:1, :], rstd[1][0:1, :], tmp[1][0:1, :]]
        b_row = [b_rstd[0], b_rstd[1], b_tmp[1]]
        b_col = [Buf(f"col{i}") for i in range(2)]

        def rr3(off, a, b):
            return RR[:, off:off + a * b].rearrange("p (a b) -> p a b", a=a)

        def rrf(off_bf16, a, b):
            v = RR[:, off_bf16:off_bf16 + 2 * a * b].bitcast(F32)
            return v.rearrange("p (a b) -> p a b", a=a)

        curK = rr3(0, 4, 1024)
        curV = RR[:, 4096:4096 + 4128].rearrange("p (a h e) -> p a h e", a=8, h=4)
        hT = rr3(8224, 8, 1024)
        uT = rr3(16416, 4, 1054)
        QT = rr3(20632, 4, 1024)
        p2T = rr3(24728, 4, 1024)
        cT = rr3(28824, 4, 1024)
        OT = rr3(32920, 4, 1024)
        stash = rrf(32920, 4, 512)
        ET = rr3(37016, 4, 512)
        mergedT = rr3(0, 8, 1024)
        yevE = rrf(16416, 8, 512)
        h2T = rr3(0, 8, 512)
        h2Tb = rr3(30720, 8, 512)
        f1 = rr3(4096, 32, 512)
        relu_s = rrf(20480, 2, 512)
        yevF = rrf(22528, 8, 512)
        xbT = rr3(0, 8, 1024)
        pTs = rr3(8192, 2, 1024)
        b_curK = [Buf("curK0"), Buf("curK1")]
        b_curV = [Buf(f"curV{i}") for i in range(8)]
        b_hT = [[Buf(f"hT{k}_{t}") for t in range(2)] for k in range(8)]
        b_uT = [[Buf(f"uT{c}_{t}") for t in range(2)] for c in range(4)]
        b_QT = [[Buf(f"QT{c}_{t}") for t in range(2)] for c in range(4)]
        b_p2T = [[Buf(f"p2T{c}_{t}") for t in range(2)] for c in range(4)]
        b_cT = [[Buf(f"cT{c}_{t}") for t in range(2)] for c in range(4)]
        b_OT = [[Buf(f"OT{c}_{t}") for t in range(2)] for c in range(4)]
        b_ET = [Buf(f"ET{i}") for i in range(4)]
        b_mg = [[Buf(f"mg{k}_{t}") for t in range(2)] for k in range(8)]
        b_yev = [Buf(f"yev{m}") for m in range(8)]
        b_h2 = [Buf(f"h2_{k}") for k in range(8)]
        b_h2b = [Buf(f"h2b_{k}") for k in range(8)]
        b_f1 = [Buf(f"f1_{j}") for j in range(32)]
        b_relu = [Buf("relu0"), Buf("relu1")]
        b_xb = [[Buf(f"xb{k}_{t}") for t in range(2)] for k in range(8)]
        b_pTs = Buf("pTs")

        NSLOT = 4
        plan = []
        lastuse = {}
        st = {"pk": 0, "issued": 0, "dry_pk": 0}

        class PackTok:
            def __init__(self, k):
                self.k = k

        def load_pack(src_ap, nelem):
            if P.dry:
                plan.append((src_ap, nelem))
                k = st["dry_pk"]
                st["dry_pk"] += 1
                lastuse[k] = k + 1
                return ring[0], PackTok(k)
            k = st["pk"]
            while st["issued"] < len(plan):
                i = st["issued"]
                if i > k + NSLOT - 1:
                    break
                if i >= NSLOT and lastuse[i - NSLOT] > k:
                    assert i > k, "ring too small: slot-mate of the requested pack is still live"
                    break
                s = i % NSLOT
                P.dma("pool", f"ring{s}", ring[s][:, 0:plan[i][1]], plan[i][0], writes=[b_ring[s]])
                st["issued"] += 1
            assert st["issued"] > k
            st["pk"] += 1
            return ring[k % NSLOT], b_ring[k % NSLOT]

        gen_n = [0]

        def gen_bank():
            b = gen_n[0] % 6
            gen_n[0] += 1
            return b

        ev_n = [0]

        def evac(out, in_, reads, writes):
            ev_n[0] += 1
            if ev_n[0] % 2:
                P.op("act", "activation", out=out, in_=in_, func=AF.Copy, reads=reads, writes=writes)
            else:
                P.op("dve", "tensor_copy", out, in_, reads=reads, writes=writes)

        def mm(out, lhsT, rhs, start, stop, reads, writes):
            if P.dry:
                for b in reads:
                    if isinstance(b, PackTok):
                        lastuse[b.k] = st["dry_pk"]
                return
            P.op("pe", "matmul", out, lhsT, rhs, start=start, stop=stop, reads=reads, writes=writes)

        def tt(eng, out, in0, in1, op, reads, writes):
            P.op(eng, "tensor_tensor", out, in0, in1, op, reads=reads, writes=writes)

        def ts(eng, out, in0, s1, s2, op0, op1, reads, writes):
            if s2 is None:
                P.op(eng, "tensor_scalar", out, in0, s1, None, op0, reads=reads, writes=writes)
            else:
                P.op(eng, "tensor_scalar", out, in0, s1, s2, op0, op1, reads=reads, writes=writes)

        def stt(eng, out, in0, scalar, in1, op0, op1, reads, writes):
            P.op(eng, "scalar_tensor_tensor", out=out, in0=in0, scalar=scalar, in1=in1, op0=op0, op1=op1,
                 reads=reads, writes=writes)

        def act(out, in_, func, reads, writes, **kw):
            P.op("act", "activation", out=out, in_=in_, func=func, reads=reads, writes=writes, **kw)

        b_lamt, b_lt, b_nl, b_v2 = Buf("lamt"), Buf("lt"), Buf("nl"), Buf("v2")
        lamt = RR[:, 0:1024].bitcast(F32).rearrange("p (l c) -> p l c", l=DEPTH)
        lt = RR[:, 2048:2048 + 512].bitcast(F32)
        P.dma("sp", "c0", vecs[:], d_vecs.rearrange("l p c -> p l c"), writes=[b_const])
        P.dma("sp", "c0", subln[:], d_subln.rearrange("l p c -> p l c"), writes=[Buf("x1")])
        P.dma("sp", "c0", rc[:], d_rc, writes=[Buf("x2")])
        P.dma("sp", "c0", lamt, d_lam.rearrange("l p c -> p l c"), writes=[b_lamt])
        P.dma("pool", "c1", biasT[:], d_bias.rearrange("p (h c) -> p h c", h=4), writes=[Buf("x3")])
        P.dma("pool", "c1", ident[:], d_ident, writes=[Buf("x4")])
        P.dma("sp", "c0", identF[:], d_ident, writes=[Buf("x5")])
        P.dma("sp", "c0", cbias[:], d_cb, writes=[Buf("x6")])
        P.barrier()
        P.op("dve", "memset", onesD[:], 1.0 / 1024.0, writes=[b_const])
        P.op("dve", "memset", nhalf[:], -0.5, writes=[b_const])
        P.op("dve", "memset", onesF[:], 1.0, writes=[b_const])
        P.op("dve", "memset", onesFF[:], 1.0, writes=[b_const])
        P.op("dve", "memset", onesC[:], 1.0 / 512.0, writes=[b_const])
        P.op("dve", "memset", uhalo[:], 0.0, writes=[b_const])
        P.op("dve", "memset", puhalo[:], 0.0, writes=[b_const])
        for i in range(2):
            P.op("dve", "memset", kvV[i][:, :, :, 128:129], 1.0, writes=[b_const])
        ssum = att_s[0]
        for l in range(DEPTH):
            ts("dve", vec2[:, l, 0:4], vecs[:, l, 32:36], 2.0, None, ALU.mult, None, [b_const], [b_v2])
            ts("dve", vec2[:, l, 4:12], vecs[:, l, 36:44], 0.5, None, ALU.mult, None, [b_const], [b_v2])
            ts("dve", subln[:, l, :], subln[:, l, :], 1.0 - LAMBDA_INIT[l], None, ALU.mult, None, [b_const], [b_v2])
            for j in range(2):
                tt("dve", lt[:, 64 * j:64 * j + 64], lamt[:, l, 128 * j:128 * j + 64],
                   lamt[:, l, 128 * j + 64:128 * j + 128], ALU.mult, [b_lamt], [b_lt])
                P.op("dve", "reduce_sum", ssum[:, 2 * l + j:2 * l + j + 1], lt[:, 64 * j:64 * j + 64], AX.X,
                     reads=[b_lt], writes=[b_atts[0]])
        act(ssum[:, 4:8], ssum[:, 0:4], AF.Exp, [b_atts[0]], [b_atts[0]])
        for l in range(DEPTH):
            tt("dve", nlam[:, l:l + 1], ssum[:, 5 + 2 * l:6 + 2 * l], ssum[:, 4 + 2 * l:5 + 2 * l], ALU.subtract,
               [b_atts[0]], [b_nl])
            ts("dve", nlam[:, l:l + 1], nlam[:, l:l + 1], -LAMBDA_INIT[l], None, ALU.add, None, [b_nl], [b_nl])
        P.barrier()

        def norm_rstd(srcs, src_bufs, eps, r):
            bank = 6 + (r % 2)
            for k in range(8):
                q = k % 2
                act(sq[q][:], srcs[k], AF.Square, [src_bufs[k]], [b_sq[q]])
                mm(ps[bank][0:1, :], onesD[:, 0:1], sq[q][:], k == 0, k == 7, [b_sq[q]], [b_ps[bank]])
            ts("dve", rowb[r][:], ps[bank][0:1, :], eps, None, ALU.add, None, [b_ps[bank]], [b_row[r]])
            return rsqrt_bcast(r, bank)

        def rsqrt_bcast(r, bank):
            for blk in range(4):
                mm(ps[bank][:, blk:blk + 1], rowb[r][0:1, blk * 128:(blk + 1) * 128], onesF[0:1, 0:1], True, True,
                   [b_row[r]], [b_ps[bank]])
            P.op("dve", "tensor_copy", colb[r % 2][:], ps[bank][:, 0:4], reads=[b_ps[bank]], writes=[b_col[r % 2]])
            tt("pool", colb[r % 2][:], colb[r % 2][:], nhalf[:, 0:4], ALU.pow, [b_col[r % 2]], [b_col[r % 2]])
            for blk in range(4):
                ts("dve", rstd[r % 2][:, blk * 128:(blk + 1) * 128], identF[:], colb[r % 2][:, blk:blk + 1], None, ALU.mult, None,
                   [b_col[r % 2]], [b_rstd[r % 2]])
            for blk in range(4):
                mm(ps[bank][:, blk * 128:(blk + 1) * 128], onesFF[:], rstd[r % 2][:, blk * 128:(blk + 1) * 128], True, True,
                   [b_rstd[r % 2]], [b_ps[bank]])
            return ps[bank][:], b_ps[bank]

        def post_norm_residual(yev, t, gcol, l, eps, r):
            rs, brs = norm_rstd([yev[:, m, :] for m in range(8)], b_yev, eps, r)
            for m in range(8):
                q = m % 2
                stt("dve", tmp[q][:], yev[:, m, :], vecs[:, l, gcol + m:gcol + m + 1], rs, ALU.mult, ALU.mult,
                    [b_yev[m], brs], [b_tmp[q]])
                xs = xT[:, m, t * 512:(t + 1) * 512]
                tt("dve", xs, xs, tmp[q][:], ALU.add, [b_tmp[q]], [b_x[m][t]])

        def pre_norm(dst, dst_bufs, t, gcol, l, r, tok0):
            rs, brs = norm_rstd([xT[:, k, t * 512:(t + 1) * 512] for k in range(8)], [b_x[k][t] for k in range(8)], EPS, r)
            for k in range(8):
                stt("dve", dst[:, k, tok0:tok0 + 512], xT[:, k, t * 512:(t + 1) * 512],
                    vecs[:, l, gcol + k:gcol + k + 1], rs, ALU.mult, ALU.mult,
                    [b_x[k][t], brs], [dst_bufs[k]])

        def walk():
            gen_n[0] = 0
            ev_n[0] = 0
            for sq_i in range(nseq):
                for half in halves:
                    tok_base = half * 1024
                    for k in range(8):
                        P.dma("sp", "xin", xT[:, k, :], d_xT[sq_i, k * 128:(k + 1) * 128, tok_base:tok_base + 1024],
                              writes=[b_x[k][0], b_x[k][1]])
                    P.barrier()
                    for l in layers:
                        if half == 0:
                            Kd, Vd, bKd, bVd = kvK[l], kvV[l], b_kvK[l], b_kvV[l]
                        else:
                            Kd, Vd, bKd, bVd = curK, curV, b_curK, b_curV
                            P.op("dve", "memset", curV[:, :, :, 128:129], 1.0, writes=b_curV)
                        P.mark(f"A s{sq_i} h{half} l{l}")
                        for t in range(2):
                            pre_norm(hT, [b_hT[k][t] for k in range(8)], t, 0, l, t, t * 512)
                        P.dma("pool", "pw", poolw[:], d_poolw[l].rearrange("p (g c) -> p g c", g=4), writes=[b_poolw])
                        P.mark("B")
                        wa, bwa = load_pack(d_win[l, 0], 4096)
                        wb, bwb = load_pack(d_win[l, 1], 4096)
                        wa3 = wa[:, :].rearrange("p (k c) -> p k c", k=8)
                        wb3 = wb[:, :].rearrange("p (k c) -> p k c", k=8)
                        for c in range(4):
                            if half == 0:
                                P.op("dve", "memset", uT[:, c, 0:30], 0.0, writes=[b_uT[c][0]])
                            else:
                                P.op("dve", "tensor_copy", uT[:, c, 0:30], uhalo[:, l, c, :], reads=[b_uh[l]], writes=[b_uT[c][0]])
                        for c in range(4):
                            for t in range(2):
                                ba, bb = gen_bank(), gen_bank()
                                for k in range(8):
                                    mm(ps[ba][:], wa3[:, k, c * 128:(c + 1) * 128], hT[:, k, t * 512:(t + 1) * 512],
                                       k == 0, k == 7, [bwa, b_hT[k][t]], [b_ps[ba]])
                                for k in range(8):
                                    mm(ps[bb][:], wb3[:, k, c * 128:(c + 1) * 128], hT[:, k, t * 512:(t + 1) * 512],
                                       k == 0, k == 7, [bwb, b_hT[k][t]], [b_ps[bb]])
                                q = t % 2
                                act(sg[q][:], ps[bb][:], AF.Tanh, [b_ps[bb]], [b_sg[q]], scale=0.5)
                                stt("dve", uT[:, c, 30 + t * 512:30 + (t + 1) * 512], sg[q][:], 1.0, ps[ba][:], ALU.add, ALU.mult,
                                    [b_sg[q], b_ps[ba]], [b_uT[c][t]])
                        if half == 0:
                            for c in range(4):
                                P.op("dve", "tensor_copy", uhalo[:, l, c, :], uT[:, c, 1024:1054],
                                     reads=[b_uT[c][1]], writes=[b_uh[l]])
                        for which in (2, 3):
                            w_, bw_ = load_pack(d_win[l, which], 4096)
                            w3 = w_[:, :].rearrange("p (k c) -> p k c", k=8)
                            for c in range(4):
                                for t in range(2):
                                    b = gen_bank()
                                    for k in range(8):
                                        mm(ps[b][:], w3[:, k, c * 128:(c + 1) * 128], hT[:, k, t * 512:(t + 1) * 512],
                                           k == 0, k == 7, [bw_, b_hT[k][t]], [b_ps[b]])
                                    if which == 2:
                                        act(QT[:, c, t * 512:(t + 1) * 512], ps[b][:], AF.Copy, [b_ps[b]], [b_QT[c][t]], scale=0.125)
                                    else:
                                        P.op("dve", "tensor_copy", Kd[:, c, t * 512:(t + 1) * 512], ps[b][:],
                                             reads=[b_ps[b]], writes=[bKd[t]])
                        w_, bw_ = load_pack(d_win[l, 4], 4096)
                        w3 = w_[:, :].rearrange("p (k c) -> p k c", k=8)
                        for tb in range(8):
                            b = gen_bank()
                            for k in range(8):
                                mm(ps[b][:], hT[:, k, tb * 128:(tb + 1) * 128], w3[:, k, :],
                                   k == 0, k == 7, [bw_, b_hT[k][tb // 4]], [b_ps[b]])
                            evac(Vd[:, tb, :, 0:128], ps[b][:].rearrange("p (h e) -> p h e", h=4), [b_ps[b]], [bVd[tb]])
                        w_, bw_ = load_pack(d_win[l, 5], 4096)
                        w3 = w_[:, :].rearrange("p (k c) -> p k c", k=8)
                        B3 = [(S2[0], b_S2[0]), (S2[1], b_S2[1]), (S2[2], b_S2[2])]
                        pu, bpu = B3[0]
                        for g in range(4):
                            win_w = 2 ** (g + 1)
                            for t in range(2):
                                b = gen_bank()
                                for k in range(8):
                                    mm(ps[b][:], w3[:, k, g * 128:(g + 1) * 128], hT[:, k, t * 512:(t + 1) * 512],
                                       k == 0, k == 7, [bw_, b_hT[k][t]], [b_ps[b]])
                                if half == 0 and t == 0:
                                    P.op("dve", "memset", pu[:, 0:16], 0.0, writes=[bpu])
                                else:
                                    P.op("dve", "tensor_copy", pu[:, 0:16], puhalo[:, l, g, :], reads=[b_ph[l][g]], writes=[bpu])
                                act(pu[:, 16:528], ps[b][:], AF.Copy, [b_ps[b]], [bpu])
                                P.op("dve", "tensor_copy", puhalo[:, l, g, :], pu[:, 512:528], reads=[bpu], writes=[b_ph[l][g]])
                                si, sh, lo = 0, 1, 1
                                for st in range(g + 1):
                                    di = 1 if si != 1 else 2
                                    lo += sh
                                    sv, sbf = B3[si]
                                    dv, dbf = B3[di]
                                    tt("dve", dv[:, lo:528], sv[:, lo:528], sv[:, lo - sh:528 - sh], ALU.add, [sbf], [dbf])
                                    si, sh = di, sh * 2
                                av, abf = B3[si]
                                fi = 2 if si == 1 else 1
                                fv, fbf = B3[fi]
                                pooled = fv[:, 0:256].bitcast(BF16)
                                stt("dve", pooled, av[:, 16:528], 1.0 / win_w, pu[:, 16:528], ALU.mult, ALU.subtract,
                                    [abf, bpu], [fbf])
                                if half == 0 and t == 0:
                                    nfix = win_w - 1
                                    tt("dve", tmp[0][:, 0:nfix], av[:, 16:16 + nfix], rc[:, 0:nfix], ALU.mult, [abf], [b_tmp[0]])
                                    tt("dve", pooled[:, 0:nfix], tmp[0][:, 0:nfix], pu[:, 16:16 + nfix], ALU.subtract,
                                       [b_tmp[0], bpu], [fbf])
                                b2 = gen_bank()
                                mm(ps[b2][:], poolw[:, g, :], pooled, True, True, [b_poolw, fbf], [b_ps[b2]])
                                act(p2T[:, g, t * 512:(t + 1) * 512], ps[b2][:], AF.Copy, [b_ps[b2]], [b_p2T[g][t]],
                                    scale=vecs[:, l, 44 + g:45 + g])
                        P.mark("C")
                        ETf = RR[:, 37016:37016 + 1024].bitcast(F32)

                        def cv(t, c):
                            if t == 0:
                                return stash[:, c, :], [b_OT[c][0], b_OT[c][1]]
                            if c < 3:
                                return S2[c][:, 0:512], [b_S2[c]]
                            return ETf, [b_ET[0], b_ET[1]]

                        for c in range(4):
                            pk, bpk = load_pack(d_cdiag[l, c], 4096)
                            pk3 = pk[:, :].rearrange("p (j q) -> p j q", j=32)
                            for t in range(2):
                                base = t * 512
                                bk = gen_bank()
                                for j in range(31):
                                    mm(ps[bk][:], pk3[:, j, :], uT[:, c, base + j:base + j + 512], j == 0, j == 30,
                                       [bpk, b_uT[c][0], b_uT[c][1]], [b_ps[bk]])
                                dst, dbufs = cv(t, c)
                                ts("dve", dst, ps[bk][:], vec2[:, l, c:c + 1], None, ALU.add, None, [b_ps[bk]], dbufs)
                        for t in range(2):
                            for c in range(4):
                                q = c % 2
                                sv, sbufs = cv(t, c)
                                act(sq[q][:], sv, AF.Copy, sbufs, [b_sq[q]])
                                mm(ps[6][0:1, :], onesC[:, 0:1], sq[q][:], c == 0, c == 3, [b_sq[q]], [b_ps[6]])
                            for c in range(4):
                                q = c % 2
                                sv, sbufs = cv(t, c)
                                act(sq[q][:], sv, AF.Square, sbufs, [b_sq[q]])
                                mm(ps[7][0:1, :], onesC[:, 0:1], sq[q][:], c == 0, c == 3, [b_sq[q]], [b_ps[7]])
                            P.op("dve", "tensor_copy", rowb[2][:], ps[6][0:1, :], reads=[b_ps[6]], writes=[b_row[2]])
                            tt("dve", rowb[0][:], rowb[2][:], rowb[2][:], ALU.mult, [b_row[2]], [b_row[0]])
                            tt("dve", rowb[0][:], ps[7][0:1, :], rowb[0][:], ALU.subtract, [b_ps[7], b_row[0]], [b_row[0]])
                            ts("dve", rowb[0][:], rowb[0][:], 4.0 * EPS, None, ALU.add, None, [b_row[0]], [b_row[0]])
                            mm(ps[6][:], onesF[:], rowb[2][:], True, True, [b_row[2]], [b_ps[6]])
                            rsqrt_bcast(0, 7)
                            for c in range(4):
                                q = c % 2
                                sv, sbufs = cv(t, c)
                                tt("dve", tmp[q][:], sv, ps[6][:], ALU.subtract, sbufs + [b_ps[6]], [b_tmp[q]])
                                tt("dve", tmp[q][:], tmp[q][:], ps[7][:], ALU.mult, [b_tmp[q], b_ps[7]], [b_tmp[q]])
                                ts("dve", tmp[q][:], tmp[q][:], vec2[:, l, 4 + c:5 + c], vec2[:, l, 8 + c:9 + c], ALU.mult, ALU.add,
                                   [b_tmp[q]], [b_tmp[q]])
                                act(sg[q][:], tmp[q][:], AF.Tanh, [b_tmp[q]], [b_sg[q]])
                                stt("dve", cT[:, c, t * 512:(t + 1) * 512], sg[q][:], 1.0, tmp[q][:], ALU.add, ALU.mult,
                                    [b_sg[q], b_tmp[q]], [b_cT[c][t]])
                        P.mark("D")
                        blocks = []
                        for h in range(4):
                            for qt in range(2):
                                Q0 = tok_base + qt * 512
                                for kb in range(0, Q0 // 128 + 4):
                                    blocks.append((h, qt, Q0, kb))
                        nb = len(blocks)
                        state = {"started": set(), "unit": None, "at_n": 0, "tr_n": 0, "chains": []}
                        pend_tr = {}
                        sched = {}

                        def blk_params(i):
                            h, qt, Q0, kb = blocks[i]
                            k0 = kb * 128
                            c0 = max(0, k0 - Q0)
                            n = 512 - c0
                            d0 = min(Q0 + c0 - k0, 256)
                            if half == 1 and kb >= 8:
                                Ks, Vs, bKs, bVs = curK, curV, b_curK[(kb - 8) // 4], b_curV[kb - 8]
                                kk0, vb = k0 - 1024, kb - 8
                            else:
                                Ks, Vs, bKs, bVs = kvK[l], kvV[l], b_kvK[l][kb // 4], b_kvV[l][kb]
                                kk0, vb = k0, kb
                            return h, qt, Q0, kb, k0, c0, n, d0, Ks, Vs, bKs, bVs, kk0, vb

                        def rec_S(i):
                            h, qt, Q0, kb, k0, c0, n, d0, Ks, Vs, bKs, bVs, kk0, vb = blk_params(i)
                            far = (d0 == 256)
                            for mi in range(2):
                                sbk = (i % 2) * 2 + mi
                                mm(ps[sbk][:, 0:n], Ks[64 * mi:64 * mi + 64, h, kk0:kk0 + 128],
                                   QT[64 * mi:64 * mi + 64, h, qt * 512 + c0:qt * 512 + 512],
                                   True, far, [bKs, b_QT[h][qt]], [b_ps[sbk]])
                            for mi in range(2):
                                sbk = (i % 2) * 2 + mi
                                if not far:
                                    mm(ps[sbk][:, 0:n], ident[:], biasT[:, h, d0:d0 + n], False, True, [b_const], [b_ps[sbk]])
                            for mi in range(2):
                                sbk = (i % 2) * 2 + mi
                                if far:
                                    act(ET[:, sbk, 0:n], ps[sbk][:, 0:n], AF.Exp, [b_ps[sbk]], [b_ET[sbk]], bias=cbias[:, h:h + 1])
                                else:
                                    act(ET[:, sbk, 0:n], ps[sbk][:, 0:n], AF.Exp, [b_ps[sbk]], [b_ET[sbk]])

                        def rec_AV(i):
                            h, qt, Q0, kb, k0, c0, n, d0, Ks, Vs, bKs, bVs, kk0, vb = blk_params(i)
                            if state["unit"] != (h, qt):
                                state["unit"] = (h, qt)
                                state["started"] = set()
                            started = state["started"]
                            for mi in range(2):
                                sbk = (i % 2) * 2 + mi
                                for j in range(c0 // 128, 4):
                                    a = j * 2 + mi
                                    bank, off = 4 + a // 3, (a % 3) * 160
                                    st1 = bank not in started
                                    started.add(bank)
                                    P.op("pe", "matmul", ps[bank][:, off:off + 129],
                                         ET[:, sbk, j * 128 - c0:j * 128 - c0 + 128], Vs[:, vb, h, :],
                                         start=st1, stop=(k0 == Q0 + j * 128), skip_group_check=True,
                                         reads=[b_ET[sbk], bVs], writes=[b_ps[bank]])
                            if k0 >= Q0:
                                j = (k0 - Q0) // 128
                                z = j % 2
                                state["chains"].append((norm_chain(j, z, h, qt, i), i))
                                if j % 2 == 1:
                                    (ca_, ia), (cb_, ib) = state["chains"]
                                    state["chains"] = []
                                    for sa, sb_ in zip(ca_[:6], cb_[:6]):
                                        sa()
                                        sb_()
                                    it = state["iter"]

                                    def second(ca_=ca_, cb_=cb_):
                                        for sa, sb_ in zip(ca_[6:], cb_[6:]):
                                            sa()
                                            sb_()
                                    sched.setdefault(it + 1, []).append(second)
                                    sched.setdefault(it + 3, []).append(lambda ia=ia, ib=ib: (rec_TR(ia), rec_TR(ib)))

                        def norm_chain(j, z, h, qt, i):
                            a1, a2 = j * 2, j * 2 + 1
                            O1 = ps[4 + a1 // 3][:, (a1 % 3) * 160:(a1 % 3) * 160 + 129]
                            O2 = ps[4 + a2 // 3][:, (a2 % 3) * 160:(a2 % 3) * 160 + 129]
                            bO1, bO2 = b_ps[4 + a1 // 3], b_ps[4 + a2 // 3]
                            st_, bst = att_s[z], b_atts[z]
                            fa, bfa = att_f[2 * z], b_attf[2 * z]
                            fo, bfo = att_f[2 * z + 1], b_attf[2 * z + 1]
                            on, bon = att_b[z], b_attb[z]

                            def fin():
                                stt("dve", on[:], fo[:], st_[:, 5:6], subln[:, l, :], ALU.mult, ALU.mult, [bst, bfo], [bon])
                                pend_tr[i] = (on, bon, h, qt * 512 + j * 128, qt)
                            return [
                                lambda: P.op("dve", "reciprocal", st_[:, 0:1], O1[:, 128:129], reads=[bO1], writes=[bst]),
                                lambda: P.op("dve", "reciprocal", st_[:, 1:2], O2[:, 128:129], reads=[bO2], writes=[bst]),
                                lambda: P.op("dve", "memset", st_[:, 3:4], 0.0, writes=[bst]),
                                lambda: tt("dve", st_[:, 2:3], st_[:, 1:2], nlam[:, l:l + 1], ALU.mult, [bst], [bst]),
                                lambda: ts("dve", fa[:], O1[:, 0:128], st_[:, 0:1], None, ALU.mult, None, [bst, bO1], [bfa]),
                                lambda: stt("dve", fo[:], O2[:, 0:128], st_[:, 2:3], fa[:], ALU.mult, ALU.add, [bst, bO2, bfa], [bfo]),
                                lambda: act(fa[:], fo[:], AF.Square, [bfo, bst], [bfa, bst], accum_out=st_[:, 3:4]),
                                lambda: ts("dve", st_[:, 4:5], st_[:, 3:4], 1.0 / 128.0, EPS, ALU.mult, ALU.add, [bst], [bst]),
                                lambda: tt("pool", st_[:, 5:6], st_[:, 4:5], nhalf[:, 0:1], ALU.pow, [bst], [bst]),
                                fin,
                            ]

                        def rec_TR(i):
                            if i not in pend_tr:
                                return
                            on, bon, h, qa, qt = pend_tr.pop(i)
                            tsl = state["tr_n"] % 4
                            state["tr_n"] += 1
                            mm(ps[7][:, tsl * 128:(tsl + 1) * 128], on[:], ident[:], True, True, [bon], [b_ps[7]])
                            act(OT[:, h, qa:qa + 128], ps[7][:, tsl * 128:(tsl + 1) * 128], AF.Copy, [b_ps[7]], [b_OT[h][qt]])

                        for i in range(nb + 5):
                            state["iter"] = i
                            if i < nb:
                                rec_S(i)
                            for fn_ in sched.pop(i, []):
                                fn_()
                            if 0 <= i - 1 < nb:
                                rec_AV(i - 1)
                        assert not sched and not pend_tr
                        P.barrier()
                        P.mark("E")
                        srcs = [(cT, b_cT), (OT, b_OT), (p2T, b_p2T)]
                        for m in range(8):
                            wg, bwg = load_pack(d_mrgg[l, m], 3072)
                            wbo, bwbo = load_pack(d_mrgb[l, m], 1536)
                            wg4 = wg[:, 0:3072].rearrange("p (b k j) -> p b k j", b=3, k=8)
                            wbo4 = wbo[:, 0:1536].rearrange("p (b k j) -> p b k j", b=3, k=4)
                            for t in range(2):
                                for br in range(3):
                                    bg, by = gen_bank(), gen_bank()
                                    for k in range(8):
                                        mm(ps[bg][:], wg4[:, br, k, :], hT[:, k, t * 512:(t + 1) * 512], k == 0, k == 7,
                                           [bwg, b_hT[k][t]], [b_ps[bg]])
                                    sT, bsT = srcs[br]
                                    for k in range(4):
                                        mm(ps[by][:], wbo4[:, br, k, :], sT[:, k, t * 512:(t + 1) * 512], k == 0, k == 3,
                                           [bwbo, bsT[k][t]], [b_ps[by]])
                                    q = br % 2
                                    act(sg[q][:], ps[bg][:], AF.Tanh, [b_ps[bg]], [b_sg[q]], scale=0.5)
                                    if br == 0:
                                        stt("dve", tmp[0][:], sg[q][:], 1.0, ps[by][:], ALU.add, ALU.mult, [b_sg[q], b_ps[by]], [b_tmp[0]])
                                    else:
                                        stt("dve", tmp[1][:], sg[q][:], 1.0, ps[by][:], ALU.add, ALU.mult, [b_sg[q], b_ps[by]], [b_tmp[1]])
                                        if br == 1:
                                            tt("dve", tmp[0][:], tmp[0][:], tmp[1][:], ALU.add, [b_tmp[1]], [b_tmp[0]])
                                        else:
                                            tt("dve", mergedT[:, m, t * 512:(t + 1) * 512], tmp[0][:], tmp[1][:], ALU.add,
                                               [b_tmp[0], b_tmp[1]], [b_mg[m][t]])
                        for t in range(2):
                            for cg in range(2):
                                w_, bw_ = load_pack(d_wout[l, cg], 4096)
                                w3 = w_[:, :].rearrange("p (k c) -> p k c", k=8)
                                for m4 in range(4):
                                    m = cg * 4 + m4
                                    b = gen_bank()
                                    for k in range(8):
                                        mm(ps[b][:], w3[:, k, m4 * 128:(m4 + 1) * 128], mergedT[:, k, t * 512:(t + 1) * 512],
                                           k == 0, k == 7, [bw_, b_mg[k][t]], [b_ps[b]])
                                    evac(yevE[:, m, :], ps[b][:], [b_ps[b]], [b_yev[m]])
                            post_norm_residual(yevE, t, 8, l, 4.0 * EPS, t)
                        P.barrier()
                        P.mark("F")
                        pre_norm(h2T, b_h2, 0, 16, l, 0, 0)
                        for t in range(2):
                            hcur, bh = (h2T, b_h2) if t == 0 else (h2Tb, b_h2b)
                            for fg in range(8):
                                w_, bw_ = load_pack(d_w1[l, fg], 4096)
                                w3 = w_[:, :].rearrange("p (k c) -> p k c", k=8)
                                for i in range(4):
                                    jf = fg * 4 + i
                                    b = gen_bank()
                                    for k in range(8):
                                        mm(ps[b][:], w3[:, k, i * 128:(i + 1) * 128], hcur[:, k, :], k == 0, k == 7,
                                           [bw_, bh[k]], [b_ps[b]])
                                    q = jf % 2
                                    act(relu_s[:, q, :], ps[b][:], AF.Relu, [b_ps[b]], [b_relu[q]])
                                    tt("dve", f1[:, jf, :], relu_s[:, q, :], relu_s[:, q, :], ALU.mult, [b_relu[q]], [b_f1[jf]])
                            if t == 0:
                                pre_norm(h2Tb, b_h2b, 1, 16, l, 1, 0)
                            for jg in range(8):
                                w_, bw_ = load_pack(d_w2[l, jg], 4096)
                                w3 = w_[:, :].rearrange("p (j c) -> p j c", j=4)
                                for jj in range(4):
                                    jf = jg * 4 + jj
                                    for m in range(8):
                                        mm(ps[m][:], w3[:, jj, m * 128:(m + 1) * 128], f1[:, jf, :], jf == 0, jf == 31,
                                           [bw_, b_f1[jf]], [b_ps[m]])
                            for m in range(8):
                                evac(yevF[:, m, :], ps[m][:], [b_ps[m]], [b_yev[m]])
                            post_norm_residual(yevF, t, 24, l, EPS, t)
                        P.barrier()
                        P.mark("G")
                        P.dma("pool", "pin", pTs, d_pT[l, sq_i, :, tok_base:tok_base + 1024].rearrange("(k p) t -> p k t", p=128),
                              writes=[b_pTs])
                        for k in range(8):
                            for t in range(2):
                                if (k + t) % 2:
                                    act(xbT[:, k, t * 512:(t + 1) * 512], xT[:, k, t * 512:(t + 1) * 512], AF.Copy,
                                        [b_x[k][t]], [b_xb[k][t]])
                                else:
                                    P.op("dve", "tensor_copy", xbT[:, k, t * 512:(t + 1) * 512], xT[:, k, t * 512:(t + 1) * 512],
                                         reads=[b_x[k][t]], writes=[b_xb[k][t]])
                        for cg in range(2):
                            w_, bw_ = load_pack(d_wpg[l, cg], 4096)
                            w3 = w_[:, :].rearrange("p (k c) -> p k c", k=8)
                            if cg == 0:
                                wpe_, bwpe = load_pack(d_wpe[l], 2048)
                                wpe3 = wpe_[:, 0:2048].rearrange("p (k c) -> p k c", k=2)
                            for m4 in range(4):
                                m = cg * 4 + m4
                                for t in range(2):
                                    bg, bp = gen_bank(), gen_bank()
                                    for k in range(8):
                                        mm(ps[bg][:], w3[:, k, m4 * 128:(m4 + 1) * 128], xbT[:, k, t * 512:(t + 1) * 512],
                                           k == 0, k == 7, [bw_, b_xb[k][t]], [b_ps[bg]])
                                    for k in range(2):
                                        mm(ps[bp][:], wpe3[:, k, m * 128:(m + 1) * 128], pTs[:, k, t * 512:(t + 1) * 512],
                                           k == 0, k == 1, [bwpe, b_pTs], [b_ps[bp]])
                                    q = t % 2
                                    act(sg[q][:], ps[bg][:], AF.Tanh, [b_ps[bg]], [b_sg[q]], scale=0.5)
                                    act(relu_s[:, q, :], ps[bp][:], AF.Copy, [b_ps[bp]], [b_relu[q]], scale=0.5)
                                    stt("dve", tmp[q][:], sg[q][:], 1.0, relu_s[:, q, :], ALU.add, ALU.mult, [b_sg[q], b_relu[q]], [b_tmp[q]])
                                    xs = xT[:, m, t * 512:(t + 1) * 512]
                                    tt("dve", xs, xs, tmp[q][:], ALU.add, [b_tmp[q]], [b_x[m][t]])
                        P.barrier()
                    for k in range(8):
                        P.dma("sp", "xout", d_out[sq_i, k * 128:(k + 1) * 128, tok_base:tok_base + 1024], xT[:, k, :],
                              reads=[b_x[k][0], b_x[k][1]])
                    P.barrier()
        P.dry = True
        walk()
        P.dry = False
        walk()
        counts = P.emit()
        counts["marks"] = P.marks
    return nc, counts


def _rel_bucket(n):
    n = np.maximum(n, 0)
    nf = np.maximum(n, 1).astype(np.float32)
    large = 16 + (np.log(nf / np.float32(16)) / np.float32(math.log(128 / 16)) * np.float32(16)).astype(np.int32)
    large = np.minimum(large, 31)
    return np.where(n < 16, n, large)


def host_layout(inp):
    f = np.float32
    g = {k: np.asarray(v, dtype=f) for k, v in inp.items()}
    L = DEPTH
    sh = {}
    w_in = g["w_in"]
    sh["win"] = np.ascontiguousarray(
        w_in[:, :, :3072].reshape(L, 8, 128, 6, 512).transpose(0, 3, 2, 1, 4)).reshape(L, 6, 128, 4096)
    gates = w_in[:, :, 3072:].reshape(L, 8, 128, 3, 8, 128)
    sh["mrgg"] = np.ascontiguousarray(gates.transpose(0, 4, 2, 3, 1, 5)).reshape(L, 8, 128, 3072)
    bo = np.stack([g["w_conv_out"], g["w_attn_out"], g["w_pool_out"]], axis=1)
    bo = bo.reshape(L, 3, 4, 128, 8, 128)
    sh["mrgb"] = np.ascontiguousarray(bo.transpose(0, 4, 3, 1, 2, 5)).reshape(L, 8, 128, 1536)

    def colgroups(w, ng):
        return np.ascontiguousarray(w.reshape(L, 8, 128, ng, 512).transpose(0, 3, 2, 1, 4)).reshape(L, ng, 128, 4096)

    sh["wout"] = colgroups(g["w_out"], 2)
    sh["w1"] = colgroups(g["w_mlp_in"], 8)
    sh["wpg"] = colgroups(g["w_ple_gate"], 2)
    sh["w2"] = np.ascontiguousarray(g["w_mlp_out"].reshape(L, 8, 4, 128, 1024).transpose(0, 1, 3, 2, 4)).reshape(L, 8, 128, 4096)
    sh["wpe"] = np.ascontiguousarray(g["w_ple_proj"].reshape(L, 2, 128, 1024).transpose(0, 2, 1, 3)).reshape(L, 128, 2048)
    sh["poolw"] = np.ascontiguousarray(g["pool_w"].transpose(0, 2, 1, 3)).reshape(L, 128, 512)
    vecs = np.zeros((L, 128, 48), f)

    def cols(v, n):
        return v.reshape(L, n, 128).transpose(0, 2, 1)
    vecs[:, :, 0:8] = cols(g["g_pre_mix"], 8)
    vecs[:, :, 8:16] = cols(g["g_post_mix"], 8)
    vecs[:, :, 16:24] = cols(g["g_pre_mlp"], 8)
    vecs[:, :, 24:32] = cols(g["g_post_mlp"], 8)
    vecs[:, :, 32:36] = cols(g["conv_dw_b"], 4)
    vecs[:, :, 36:40] = cols(g["conv_ln_g"], 4)
    vecs[:, :, 40:44] = cols(g["conv_ln_b"], 4)
    vecs[:, :, 44:48] = cols(g["pool_scale"], 4)
    w4 = g["conv_dw_w"][:, :, 0, :].reshape(L, 31, 4, 128).transpose(0, 2, 3, 1)
    cd = np.zeros((L, 4, 128, 32, 128), f)
    ar = np.arange(128)
    cd[:, :, ar, :31, ar] = w4.transpose(2, 0, 1, 3)
    sh["cdiag"] = cd.reshape(L, 4, 128, 4096)
    sh["vecs"] = vecs
    sh["sublnb"] = np.ascontiguousarray(np.broadcast_to(g["subln_g"][:, None, :], (L, 128, 128)))
    sh["lamb"] = np.ascontiguousarray(np.broadcast_to(g["lam_p"].reshape(L, 1, 256), (L, 128, 256)))
    kk = np.arange(128)[:, None]
    dd = np.arange(768)[None, :]
    nrel = dd - kk
    bidx = _rel_bucket(nrel)
    tab = g["rel_bias"][bidx]
    tab = np.where((nrel >= 0)[:, :, None], tab, f(-1e30))
    sh["biasT"] = np.ascontiguousarray(tab.transpose(0, 2, 1)).reshape(128, 4 * 768).astype(f)
    sh["ident"] = np.eye(128, dtype=f)
    sh["cbias"] = np.ascontiguousarray(np.broadcast_to(g["rel_bias"][31][None, :], (128, 4)))
    rc = np.zeros((128, 16), f)
    rc[:, :] = 1.0 / (np.arange(16, dtype=f) + 1.0)
    sh["rc"] = rc
    x = g["x"]
    p = g["p"]
    per_core = []
    for c in range(8):
        d = dict(sh)
        d["xT"] = np.ascontiguousarray(x[2 * c:2 * c + 2].transpose(0, 2, 1))
        d["pT"] = np.ascontiguousarray(p[:, 2 * c:2 * c + 2].transpose(0, 1, 3, 2))
        per_core.append(d)
    return per_core


_CACHE = {}


def kernel(**inputs):
    if "nc" not in _CACHE:
        _CACHE["nc"] = build_program()[0]
    nc = _CACHE["nc"]
    in_maps = host_layout(inputs)
    res = run_bass_kernel_spmd(nc, in_maps, core_ids=list(range(8)))
    out = np.empty((16, S, D), np.float32)
    for c in range(8):
        out[2 * c:2 * c + 2] = res.results[c]["outT"].transpose(0, 2, 1)
    return out
```

```python
import math
from contextlib import ExitStack

import numpy as np
import concourse.bass as bass
import concourse.mybir as mybir
from concourse.bass_utils import run_bass_kernel_spmd

F32 = mybir.dt.float32
BF16 = mybir.dt.bfloat16
ALU = mybir.AluOpType
AF = mybir.ActivationFunctionType
AX = mybir.AxisListType

DEPTH = 2
NSEQ = 2
S = 2048
D = 1024
EPS = 1e-6
LAMBDA_INIT = [0.8 - 0.6 * math.exp(-0.3 * i) for i in range(DEPTH)]

COMPUTE = ("pe", "act", "dve", "pool")
STREAMS = ("pe", "act", "dve", "pool", "sp")


class Buf:
    __slots__ = ("name", "lw", "rd")

    def __init__(self, name):
        self.name = name
        self.lw = None
        self.rd = []


class Op:
    __slots__ = ("stream", "fn", "waits", "key", "idx", "signal", "is_dma", "cnt")


class Prog:
    def __init__(self, nc):
        self.nc = nc
        self.ops = {s: [] for s in STREAMS}
        self.cnt = {}
        self.vc = {s: {} for s in STREAMS}
        self.opvc = {}
        self.bykey = {}
        self.dma_keys = []
        self.pend = {s: {} for s in STREAMS}
        self.marks = []
        self.dry = False

    def mark(self, name):
        if not self.dry:
            self.marks.append((name, len(self.ops["pe"])))

    def barrier(self):
        if self.dry:
            return
        snap = dict(self.cnt)
        for s in STREAMS:
            for k, i in snap.items():
                if self.pend[s].get(k, 0) < i:
                    self.pend[s][k] = i

    def _record(self, stream, key, fn, reads, writes, is_dma):
        if self.dry:
            return None
        deps = set()
        for b in reads:
            if b.lw is not None:
                deps.add(b.lw)
        for b in writes:
            if b.lw is not None:
                deps.add(b.lw)
            for r in b.rd:
                deps.add(r)
        for k, i in self.pend[stream].items():
            deps.add((k, i))
        self.pend[stream] = {}
        idx = self.cnt.get(key, 0) + 1
        self.cnt[key] = idx
        me = (key, idx)
        vc = self.vc[stream]
        waits = {}
        for (k, i) in deps:
            if k == stream and not is_dma:
                continue
            if vc.get(k, 0) >= i:
                continue
            if waits.get(k, 0) < i:
                waits[k] = i
        if stream in ("act", "dve", "pool") and not is_dma:
            own = 0
            for b in reads:
                if b.lw is not None and b.lw[0] == stream:
                    own = max(own, b.lw[1])
            for b in writes:
                if b.lw is not None and b.lw[0] == stream:
                    own = max(own, b.lw[1])
                for r in b.rd:
                    if r[0] == stream:
                        own = max(own, r[1])
            if own > vc.get(("self", stream), 0):
                waits[stream] = own
        for k, i in waits.items():
            if k == stream:
                vc[("self", stream)] = max(vc.get(("self", stream), 0), i)
                continue
            dvc = self.opvc.get((k, i))
            if dvc:
                for kk, ii in dvc.items():
                    if vc.get(kk, 0) < ii:
                        vc[kk] = ii
            if vc.get(k, 0) < i:
                vc[k] = i
        op = Op()
        op.stream, op.fn, op.waits, op.key, op.idx = stream, fn, waits, key, idx
        op.signal, op.is_dma, op.cnt = is_dma, is_dma, 0
        self.ops[stream].append(op)
        self.bykey[me] = op
        snap = {k: v for k, v in vc.items() if not isinstance(k, tuple) and k != stream}
        if not is_dma:
            snap[stream] = idx
        self.opvc[me] = snap
        for b in reads:
            b.rd.append(me)
        for b in writes:
            b.lw = me
            b.rd = []
        return op

    def op(self, stream, method, *args, reads=(), writes=(), **kw):
        return self._record(stream, stream, (method, args, kw), list(reads), list(writes), False)

    def dma(self, queue, semkey, out, in_, reads=(), writes=()):
        if semkey not in self.dma_keys:
            self.dma_keys.append(semkey)
        return self._record(queue, semkey, ("dma_start", (), {"out": out, "in_": in_}), list(reads), list(writes), True)

    def emit(self):
        nc = self.nc
        for s in STREAMS:
            for op in self.ops[s]:
                for k, i in op.waits.items():
                    if k in COMPUTE:
                        self.bykey[(k, i)].signal = True
        finals = {}
        for k in COMPUTE:
            if self.cnt.get(k, 0):
                self.bykey[(k, self.cnt[k])].signal = True
                finals[k] = self.cnt[k]
        for k in self.dma_keys:
            finals[k] = self.cnt[k]
        for k in COMPUTE:
            c = 0
            for i in range(1, self.cnt.get(k, 0) + 1):
                op = self.bykey[(k, i)]
                if op.signal:
                    c += 1
                op.cnt = c
        with ExitStack() as es:
            sems = {}
            for k in COMPUTE:
                if self.cnt.get(k, 0):
                    sems[k] = es.enter_context(nc.semaphore("s_" + k))
            for k in self.dma_keys:
                sems[k] = es.enter_context(nc.semaphore("d_" + str(k)))
            block = es.enter_context(nc.Block())

            def val(k, i):
                if k in COMPUTE:
                    return self.bykey[(k, i)].cnt
                return 16 * i

            def run(stream, eng):
                for op in self.ops[stream]:
                    for k, i in op.waits.items():
                        eng.wait_ge(sems[k], val(k, i))
                    meth, args, kw = op.fn
                    ins = getattr(eng, meth)(*args, **kw)
                    if op.is_dma:
                        ins.then_inc(sems[op.key], 16)
                    elif op.signal:
                        ins.then_inc(sems[op.key], 1)
                if stream == "sp":
                    for k, i in finals.items():
                        eng.wait_ge(sems[k], val(k, i))

            @block.tensor
            def _(e):
                run("pe", e)

            @block.scalar
            def _(e):
                run("act", e)

            @block.vector
            def _(e):
                run("dve", e)

            @block.gpsimd
            def _(e):
                run("pool", e)

            @block.sync
            def _(e):
                run("sp", e)
        return {s: len(self.ops[s]) for s in STREAMS}


def build_program(layers=(0, 1), nseq=NSEQ, halves=(0, 1)):
    nc = bass.Bass("TRN2", target_bir_lowering=False)
    P = Prog(nc)

    def dram(name, shape, kind="ExternalInput"):
        return nc.dram_tensor(name, list(shape), F32, kind=kind).ap()

    d_xT = dram("xT", [NSEQ, D, S])
    d_pT = dram("pT", [DEPTH, NSEQ, 256, S])
    d_out = dram("outT", [NSEQ, D, S], kind="ExternalOutput")
    d_win = dram("win", [DEPTH, 6, 128, 4096])
    d_mrgg = dram("mrgg", [DEPTH, 8, 128, 3072])
    d_mrgb = dram("mrgb", [DEPTH, 8, 128, 1536])
    d_wout = dram("wout", [DEPTH, 2, 128, 4096])
    d_w1 = dram("w1", [DEPTH, 8, 128, 4096])
    d_w2 = dram("w2", [DEPTH, 8, 128, 4096])
    d_wpg = dram("wpg", [DEPTH, 2, 128, 4096])
    d_wpe = dram("wpe", [DEPTH, 128, 2048])
    d_poolw = dram("poolw", [DEPTH, 128, 512])
    d_vecs = dram("vecs", [DEPTH, 128, 48])
    d_cdiag = dram("cdiag", [DEPTH, 4, 128, 4096])
    d_subln = dram("sublnb", [DEPTH, 128, 128])
    d_lam = dram("lamb", [DEPTH, 128, 256])
    d_bias = dram("biasT", [128, 4 * 768])
    d_ident = dram("ident", [128, 128])
    d_rc = dram("rc", [128, 16])
    d_cb = dram("cbias", [128, 4])

    es = ExitStack()
    with es:
        def sb(name, shape, dt):
            return es.enter_context(nc.sbuf_tensor(name, list(shape), dt))

        xT = sb("xT_s", [128, 8, 1024], F32)
        kvK = [sb(f"kvK{i}", [128, 4, 1024], BF16) for i in range(2)]
        kvV = [sb(f"kvV{i}", [128, 8, 4, 129], BF16) for i in range(2)]
        NRR = 39072
        RR = sb("RR", [128, NRR], BF16)
        S2 = [sb(f"S2_{i}", [128, 528], F32) for i in range(3)]
        sq = [sb(f"sq{i}", [128, 512], BF16) for i in range(2)]
        rstd = [sb(f"rstd{i}", [128, 512], F32) for i in range(2)]
        sg = [sb(f"sg{i}", [128, 512], F32) for i in range(2)]
        tmp = [sb(f"tmp{i}", [128, 512], F32) for i in range(2)]
        ring = [sb(f"ring{i}", [128, 4096], BF16) for i in range(4)]
        biasT = sb("biasT_s", [128, 4, 768], BF16)
        ident = sb("ident_s", [128, 128], BF16)
        onesD = sb("onesD", [128, 128], BF16)
        onesC = sb("onesC", [128, 128], BF16)
        vecs = sb("vecs_s", [128, DEPTH, 48], F32)
        vec2 = sb("vec2_s", [128, DEPTH, 12], F32)
        subln = sb("subln_s", [128, DEPTH, 128], F32)
        nlam = sb("nlam", [128, DEPTH], F32)
        rc = sb("rc_s", [128, 16], F32)
        nhalf = sb("nhalf", [128, 8], F32)
        onesF = sb("onesF", [1, 128], F32)
        cbias = sb("cbias_s", [128, 4], F32)
        identF = sb("identF", [128, 128], F32)
        onesFF = sb("onesFF", [128, 128], F32)
        colb = [sb(f"colb{i}", [128, 4], F32) for i in range(2)]
        poolw = sb("poolw_s", [128, 4, 128], BF16)
        uhalo = sb("uhalo", [128, DEPTH, 4, 30], BF16)
        puhalo = sb("puhalo", [128, DEPTH, 4, 16], F32)
        att_f = [sb(f"attf{i}", [128, 128], F32) for i in range(4)]
        att_b = [sb(f"attb{i}", [128, 128], BF16) for i in range(2)]
        att_s = [sb(f"atts{i}", [128, 8], F32) for i in range(2)]
        ps = [es.enter_context(nc.psum_tensor(f"ps{i}", [128, 512], F32)) for i in range(8)]

        b_ps = [Buf(f"ps{i}") for i in range(8)]
        b_ring = [Buf(f"ring{i}") for i in range(4)]
        b_x = [[Buf(f"x{k}_{t}") for t in range(2)] for k in range(8)]
        b_sq = [Buf("sq0"), Buf("sq1")]
        b_rstd = [Buf("rstd0"), Buf("rstd1")]
        b_sg = [Buf("sg0"), Buf("sg1")]
        b_tmp = [Buf("tmp0"), Buf("tmp1")]
        b_S2 = [Buf(f"S2_{i}") for i in range(3)]
        b_const = Buf("const")
        b_kvK = [[Buf(f"kvK{i}_{t}") for t in range(2)] for i in range(2)]
        b_kvV = [[Buf(f"kvV{i}_{tb}") for tb in range(8)] for i in range(2)]
        b_uh = [Buf(f"uh{l}") for l in range(DEPTH)]
        b_ph = [[Buf(f"ph{l}_{g}") for g in range(4)] for l in range(DEPTH)]
        b_attf = [Buf(f"attf{i}") for i in range(4)]
        b_attb = [Buf(f"attb{i}") for i in range(2)]
        b_atts = [Buf(f"atts{i}") for i in range(2)]
        b_poolw = Buf("poolw")
        rowb = [rstd[0][0:1, :], rstd[1][0:1, :], tmp[1][0:1, :]]
        b_row = [b_rstd[0], b_rstd[1], b_tmp[1]]
        b_col = [Buf(f"col{i}") for i in range(2)]

        def rr3(off, a, b):
            return RR[:, off:off + a * b].rearrange("p (a b) -> p a b", a=a)

        def rrf(off_bf16, a, b):
            v = RR[:, off_bf16:off_bf16 + 2 * a * b].bitcast(F32)
            return v.rearrange("p (a b) -> p a b", a=a)

        curK = rr3(0, 4, 1024)
        curV = RR[:, 4096:4096 + 4128].rearrange("p (a h e) -> p a h e", a=8, h=4)
        hT = rr3(8224, 8, 1024)
        uT = rr3(16416, 4, 1054)
        QT = rr3(20632, 4, 1024)
        p2T = rr3(24728, 4, 1024)
        cT = rr3(28824, 4, 1024)
        OT = rr3(32920, 4, 1024)
        stash = rrf(32920, 4, 512)
        ET = rr3(37016, 4, 512)
        mergedT = rr3(0, 8, 1024)
        yevE = rrf(16416, 8, 512)
        h2T = rr3(0, 8, 512)
        h2Tb = rr3(30720, 8, 512)
        f1 = rr3(4096, 32, 512)
        relu_s = rrf(20480, 2, 512)
        yevF = rrf(22528, 8, 512)
        xbT = rr3(0, 8, 1024)
        pTs = rr3(8192, 2, 1024)
        b_curK = [Buf("curK0"), Buf("curK1")]
        b_curV = [Buf(f"curV{i}") for i in range(8)]
        b_hT = [[Buf(f"hT{k}_{t}") for t in range(2)] for k in range(8)]
        b_uT = [[Buf(f"uT{c}_{t}") for t in range(2)] for c in range(4)]
        b_QT = [[Buf(f"QT{c}_{t}") for t in range(2)] for c in range(4)]
        b_p2T = [[Buf(f"p2T{c}_{t}") for t in range(2)] for c in range(4)]
        b_cT = [[Buf(f"cT{c}_{t}") for t in range(2)] for c in range(4)]
        b_OT = [[Buf(f"OT{c}_{t}") for t in range(2)] for c in range(4)]
        b_ET = [Buf(f"ET{i}") for i in range(4)]
        b_mg = [[Buf(f"mg{k}_{t}") for t in range(2)] for k in range(8)]
        b_yev = [Buf(f"yev{m}") for m in range(8)]
        b_h2 = [Buf(f"h2_{k}") for k in range(8)]
        b_h2b = [Buf(f"h2b_{k}") for k in range(8)]
        b_f1 = [Buf(f"f1_{j}") for j in range(32)]
        b_relu = [Buf("relu0"), Buf("relu1")]
        b_xb = [[Buf(f"xb{k}_{t}") for t in range(2)] for k in range(8)]
        b_pTs = Buf("pTs")

        NSLOT = 4
        plan = []
        lastuse = {}
        st = {"pk": 0, "issued": 0, "dry_pk": 0}

        class PackTok:
            def __init__(self, k):
                self.k = k

        def load_pack(src_ap, nelem):
            if P.dry:
                plan.append((src_ap, nelem))
                k = st["dry_pk"]
                st["dry_pk"] += 1
                lastuse[k] = k + 1
                return ring[0], PackTok(k)
            k = st["pk"]
            while st["issued"] < len(plan):
                i = st["issued"]
                if i > k + NSLOT - 1:
                    break
                if i >= NSLOT and lastuse[i - NSLOT] > k:
                    assert i > k, "ring too small: slot-mate of the requested pack is still live"
                    break
                s = i % NSLOT
                P.dma("pool", f"ring{s}", ring[s][:, 0:plan[i][1]], plan[i][0], writes=[b_ring[s]])
                st["issued"] += 1
            assert st["issued"] > k
            st["pk"] += 1
            return ring[k % NSLOT], b_ring[k % NSLOT]

        gen_n = [0]

        def gen_bank():
            b = gen_n[0] % 6
            gen_n[0] += 1
            return b

        ev_n = [0]

        def evac(out, in_, reads, writes):
            ev_n[0] += 1
            if ev_n[0] % 2:
                P.op("act", "activation", out=out, in_=in_, func=AF.Copy, reads=reads, writes=writes)
            else:
                P.op("dve", "tensor_copy", out, in_, reads=reads, writes=writes)

        def mm(out, lhsT, rhs, start, stop, reads, writes):
            if P.dry:
                for b in reads:
                    if isinstance(b, PackTok):
                        lastuse[b.k] = st["dry_pk"]
                return
            P.op("pe", "matmul", out, lhsT, rhs, start=start, stop=stop, reads=reads, writes=writes)

        def tt(eng, out, in0, in1, op, reads, writes):
            P.op(eng, "tensor_tensor", out, in0, in1, op, reads=reads, writes=writes)

        def ts(eng, out, in0, s1, s2, op0, op1, reads, writes):
            if s2 is None:
                P.op(eng, "tensor_scalar", out, in0, s1, None, op0, reads=reads, writes=writes)
            else:
                P.op(eng, "tensor_scalar", out, in0, s1, s2, op0, op1, reads=reads, writes=writes)

        def stt(eng, out, in0, scalar, in1, op0, op1, reads, writes):
            P.op(eng, "scalar_tensor_tensor", out=out, in0=in0, scalar=scalar, in1=in1, op0=op0, op1=op1,
                 reads=reads, writes=writes)

        def act(out, in_, func, reads, writes, **kw):
            P.op("act", "activation", out=out, in_=in_, func=func, reads=reads, writes=writes, **kw)

        b_lamt, b_lt, b_nl, b_v2 = Buf("lamt"), Buf("lt"), Buf("nl"), Buf("v2")
        lamt = RR[:, 0:1024].bitcast(F32).rearrange("p (l c) -> p l c", l=DEPTH)
        lt = RR[:, 2048:2048 + 512].bitcast(F32)
        P.dma("sp", "c0", vecs[:], d_vecs.rearrange("l p c -> p l c"), writes=[b_const])
        P.dma("sp", "c0", subln[:], d_subln.rearrange("l p c -> p l c"), writes=[Buf("x1")])
        P.dma("sp", "c0", rc[:], d_rc, writes=[Buf("x2")])
        P.dma("sp", "c0", lamt, d_lam.rearrange("l p c -> p l c"), writes=[b_lamt])
        P.dma("pool", "c1", biasT[:], d_bias.rearrange("p (h c) -> p h c", h=4), writes=[Buf("x3")])
        P.dma("pool", "c1", ident[:], d_ident, writes=[Buf("x4")])
        P.dma("sp", "c0", identF[:], d_ident, writes=[Buf("x5")])
        P.dma("sp", "c0", cbias[:], d_cb, writes=[Buf("x6")])
        P.barrier()
        P.op("dve", "memset", onesD[:], 1.0 / 1024.0, writes=[b_const])
        P.op("dve", "memset", nhalf[:], -0.5, writes=[b_const])
        P.op("dve", "memset", onesF[:], 1.0, writes=[b_const])
        P.op("dve", "memset", onesFF[:], 1.0, writes=[b_const])
        P.op("dve", "memset", onesC[:], 1.0 / 512.0, writes=[b_const])
        P.op("dve", "memset", uhalo[:], 0.0, writes=[b_const])
        P.op("dve", "memset", puhalo[:], 0.0, writes=[b_const])
        for i in range(2):
            P.op("dve", "memset", kvV[i][:, :, :, 128:129], 1.0, writes=[b_const])
        ssum = att_s[0]
        for l in range(DEPTH):
            ts("dve", vec2[:, l, 0:4], vecs[:, l, 32:36], 2.0, None, ALU.mult, None, [b_const], [b_v2])
            ts("dve", vec2[:, l, 4:12], vecs[:, l, 36:44], 0.5, None, ALU.mult, None, [b_const], [b_v2])
            ts("dve", subln[:, l, :], subln[:, l, :], 1.0 - LAMBDA_INIT[l], None, ALU.mult, None, [b_const], [b_v2])
            for j in range(2):
                tt("dve", lt[:, 64 * j:64 * j + 64], lamt[:, l, 128 * j:128 * j + 64],
                   lamt[:, l, 128 * j + 64:128 * j + 128], ALU.mult, [b_lamt], [b_lt])
                P.op("dve", "reduce_sum", ssum[:, 2 * l + j:2 * l + j + 1], lt[:, 64 * j:64 * j + 64], AX.X,
                     reads=[b_lt], writes=[b_atts[0]])
        act(ssum[:, 4:8], ssum[:, 0:4], AF.Exp, [b_atts[0]], [b_atts[0]])
        for l in range(DEPTH):
            tt("dve", nlam[:, l:l + 1], ssum[:, 5 + 2 * l:6 + 2 * l], ssum[:, 4 + 2 * l:5 + 2 * l], ALU.subtract,
               [b_atts[0]], [b_nl])
            ts("dve", nlam[:, l:l + 1], nlam[:, l:l + 1], -LAMBDA_INIT[l], None, ALU.add, None, [b_nl], [b_nl])
        P.barrier()

        def norm_rstd(srcs, src_bufs, eps, r):
            bank = 6 + (r % 2)
            for k in range(8):
                q = k % 2
                act(sq[q][:], srcs[k], AF.Square, [src_bufs[k]], [b_sq[q]])
                mm(ps[bank][0:1, :], onesD[:, 0:1], sq[q][:], k == 0, k == 7, [b_sq[q]], [b_ps[bank]])
            ts("dve", rowb[r][:], ps[bank][0:1, :], eps, None, ALU.add, None, [b_ps[bank]], [b_row[r]])
            return rsqrt_bcast(r, bank)

        def rsqrt_bcast(r, bank):
            for blk in range(4):
                mm(ps[bank][:, blk:blk + 1], rowb[r][0:1, blk * 128:(blk + 1) * 128], onesF[0:1, 0:1], True, True,
                   [b_row[r]], [b_ps[bank]])
            P.op("dve", "tensor_copy", colb[r % 2][:], ps[bank][:, 0:4], reads=[b_ps[bank]], writes=[b_col[r % 2]])
            tt("pool", colb[r % 2][:], colb[r % 2][:], nhalf[:, 0:4], ALU.pow, [b_col[r % 2]], [b_col[r % 2]])
            for blk in range(4):
                ts("dve", rstd[r % 2][:, blk * 128:(blk + 1) * 128], identF[:], colb[r % 2][:, blk:blk + 1], None, ALU.mult, None,
                   [b_col[r % 2]], [b_rstd[r % 2]])
            for blk in range(4):
                mm(ps[bank][:, blk * 128:(blk + 1) * 128], onesFF[:], rstd[r % 2][:, blk * 128:(blk + 1) * 128], True, True,
                   [b_rstd[r % 2]], [b_ps[bank]])
            return ps[bank][:], b_ps[bank]

        def post_norm_residual(yev, t, gcol, l, eps, r):
            rs, brs = norm_rstd([yev[:, m, :] for m in range(8)], b_yev, eps, r)
            for m in range(8):
                q = m % 2
                stt("dve", tmp[q][:], yev[:, m, :], vecs[:, l, gcol + m:gcol + m + 1], rs, ALU.mult, ALU.mult,
                    [b_yev[m], brs], [b_tmp[q]])
                xs = xT[:, m, t * 512:(t + 1) * 512]
                tt("dve", xs, xs, tmp[q][:], ALU.add, [b_tmp[q]], [b_x[m][t]])

        def pre_norm_stages(dst, dst_bufs, t, gcol, l, r, tok0, extra_writes=()):
            bank = 6 + (r % 2)
            srcs = [xT[:, k, t * 512:(t + 1) * 512] for k in range(8)]
            sb_ = [b_x[k][t] for k in range(8)]

            def s1():
                for k in range(8):
                    q = k % 2
                    act(sq[q][:], srcs[k], AF.Square, [sb_[k]], [b_sq[q]])
                    mm(ps[bank][0:1, :], onesD[:, 0:1], sq[q][:], k == 0, k == 7, [b_sq[q]], [b_ps[bank]])
                ts("dve", rowb[r][:], ps[bank][0:1, :], EPS, None, ALU.add, None, [b_ps[bank]], [b_row[r]])

            def s2():
                for blk in range(4):
                    mm(ps[bank][:, blk:blk + 1], rowb[r][0:1, blk * 128:(blk + 1) * 128], onesF[0:1, 0:1], True, True,
                       [b_row[r]], [b_ps[bank]])
                P.op("dve", "tensor_copy", colb[r % 2][:], ps[bank][:, 0:4], reads=[b_ps[bank]], writes=[b_col[r % 2]])
                tt("pool", colb[r % 2][:], colb[r % 2][:], nhalf[:, 0:4], ALU.pow, [b_col[r % 2]], [b_col[r % 2]])

            def s3():
                for blk in range(4):
                    ts("dve", rstd[r % 2][:, blk * 128:(blk + 1) * 128], identF[:], colb[r % 2][:, blk:blk + 1], None, ALU.mult, None,
                       [b_col[r % 2]], [b_rstd[r % 2]])
                for blk in range(4):
                    mm(ps[bank][:, blk * 128:(blk + 1) * 128], onesFF[:], rstd[r % 2][:, blk * 128:(blk + 1) * 128], True, True,
                       [b_rstd[r % 2]], [b_ps[bank]])

            def s4():
                for k in range(8):
                    stt("dve", dst[:, k, tok0:tok0 + 512], srcs[k], vecs[:, l, gcol + k:gcol + k + 1], ps[bank][:],
                        ALU.mult, ALU.mult, [sb_[k], b_ps[bank]], [dst_bufs[k]] + list(extra_writes))
            return [s1, s2, s3, s4]

        def pre_norm(dst, dst_bufs, t, gcol, l, r, tok0):
            rs, brs = norm_rstd([xT[:, k, t * 512:(t + 1) * 512] for k in range(8)], [b_x[k][t] for k in range(8)], EPS, r)
            for k in range(8):
                stt("dve", dst[:, k, tok0:tok0 + 512], xT[:, k, t * 512:(t + 1) * 512],
                    vecs[:, l, gcol + k:gcol + k + 1], rs, ALU.mult, ALU.mult,
                    [b_x[k][t], brs], [dst_bufs[k]])

        def walk():
            gen_n[0] = 0
            ev_n[0] = 0
            for sq_i in range(nseq):
                for half in halves:
                    tok_base = half * 1024
                    for k in range(8):
                        P.dma("sp", "xin", xT[:, k, :], d_xT[sq_i, k * 128:(k + 1) * 128, tok_base:tok_base + 1024],
                              writes=[b_x[k][0], b_x[k][1]])
                    P.barrier()
                    for l in layers:
                        if half == 0:
                            Kd, Vd, bKd, bVd = kvK[l], kvV[l], b_kvK[l], b_kvV[l]
                        else:
                            Kd, Vd, bKd, bVd = curK, curV, b_curK, b_curV
                            P.op("dve", "memset", curV[:, :, :, 128:129], 1.0, writes=b_curV)
                        P.mark(f"A s{sq_i} h{half} l{l}")
                        for t in range(2):
                            pre_norm(hT, [b_hT[k][t] for k in range(8)], t, 0, l, t, t * 512)
                        P.dma("pool", "pw", poolw[:], d_poolw[l].rearrange("p (g c) -> p g c", g=4), writes=[b_poolw])
                        P.mark("B")
                        wa, bwa = load_pack(d_win[l, 0], 4096)
                        wb, bwb = load_pack(d_win[l, 1], 4096)
                        wa3 = wa[:, :].rearrange("p (k c) -> p k c", k=8)
                        wb3 = wb[:, :].rearrange("p (k c) -> p k c", k=8)
                        for c in range(4):
                            if half == 0:
                                P.op("dve", "memset", uT[:, c, 0:30], 0.0, writes=[b_uT[c][0]])
                            else:
                                P.op("dve", "tensor_copy", uT[:, c, 0:30], uhalo[:, l, c, :], reads=[b_uh[l]], writes=[b_uT[c][0]])
                        for c in range(4):
                            for t in range(2):
                                ba, bb = gen_bank(), gen_bank()
                                for k in range(8):
                                    mm(ps[ba][:], wa3[:, k, c * 128:(c + 1) * 128], hT[:, k, t * 512:(t + 1) * 512],
                                       k == 0, k == 7, [bwa, b_hT[k][t]], [b_ps[ba]])
                                for k in range(8):
                                    mm(ps[bb][:], wb3[:, k, c * 128:(c + 1) * 128], hT[:, k, t * 512:(t + 1) * 512],
                                       k == 0, k == 7, [bwb, b_hT[k][t]], [b_ps[bb]])
                                q = t % 2
                                act(sg[q][:], ps[bb][:], AF.Tanh, [b_ps[bb]], [b_sg[q]], scale=0.5)
                                stt("dve", uT[:, c, 30 + t * 512:30 + (t + 1) * 512], sg[q][:], 1.0, ps[ba][:], ALU.add, ALU.mult,
                                    [b_sg[q], b_ps[ba]], [b_uT[c][t]])
                        if half == 0:
                            for c in range(4):
                                P.op("dve", "tensor_copy", uhalo[:, l, c, :], uT[:, c, 1024:1054],
                                     reads=[b_uT[c][1]], writes=[b_uh[l]])
                        for which in (2, 3):
                            w_, bw_ = load_pack(d_win[l, which], 4096)
                            w3 = w_[:, :].rearrange("p (k c) -> p k c", k=8)
                            for c in range(4):
                                for t in range(2):
                                    b = gen_bank()
                                    for k in range(8):
                                        mm(ps[b][:], w3[:, k, c * 128:(c + 1) * 128], hT[:, k, t * 512:(t + 1) * 512],
                                           k == 0, k == 7, [bw_, b_hT[k][t]], [b_ps[b]])
                                    if which == 2:
                                        act(QT[:, c, t * 512:(t + 1) * 512], ps[b][:], AF.Copy, [b_ps[b]], [b_QT[c][t]], scale=0.125)
                                    else:
                                        P.op("dve", "tensor_copy", Kd[:, c, t * 512:(t + 1) * 512], ps[b][:],
                                             reads=[b_ps[b]], writes=[bKd[t]])
                        w_, bw_ = load_pack(d_win[l, 4], 4096)
                        w3 = w_[:, :].rearrange("p (k c) -> p k c", k=8)
                        for tb in range(8):
                            b = gen_bank()
                            for k in range(8):
                                mm(ps[b][:], hT[:, k, tb * 128:(tb + 1) * 128], w3[:, k, :],
                                   k == 0, k == 7, [bw_, b_hT[k][tb // 4]], [b_ps[b]])
                            evac(Vd[:, tb, :, 0:128], ps[b][:].rearrange("p (h e) -> p h e", h=4), [b_ps[b]], [bVd[tb]])
                        w_, bw_ = load_pack(d_win[l, 5], 4096)
                        w3 = w_[:, :].rearrange("p (k c) -> p k c", k=8)
                        B3 = [(S2[0], b_S2[0]), (S2[1], b_S2[1]), (S2[2], b_S2[2])]
                        pu, bpu = B3[0]
                        for g in range(4):
                            win_w = 2 ** (g + 1)
                            for t in range(2):
                                b = gen_bank()
                                for k in range(8):
                                    mm(ps[b][:], w3[:, k, g * 128:(g + 1) * 128], hT[:, k, t * 512:(t + 1) * 512],
                                       k == 0, k == 7, [bw_, b_hT[k][t]], [b_ps[b]])
                                if half == 0 and t == 0:
                                    P.op("dve", "memset", pu[:, 0:16], 0.0, writes=[bpu])
                                else:
                                    P.op("dve", "tensor_copy", pu[:, 0:16], puhalo[:, l, g, :], reads=[b_ph[l][g]], writes=[bpu])
                                act(pu[:, 16:528], ps[b][:], AF.Copy, [b_ps[b]], [bpu])
                                P.op("dve", "tensor_copy", puhalo[:, l, g, :], pu[:, 512:528], reads=[bpu], writes=[b_ph[l][g]])
                                si, sh, lo = 0, 1, 1
                                for st in range(g + 1):
                                    di = 1 if si != 1 else 2
                                    lo += sh
                                    sv, sbf = B3[si]
                                    dv, dbf = B3[di]
                                    tt("dve", dv[:, lo:528], sv[:, lo:528], sv[:, lo - sh:528 - sh], ALU.add, [sbf], [dbf])
                                    si, sh = di, sh * 2
                                av, abf = B3[si]
                                fi = 2 if si == 1 else 1
                                fv, fbf = B3[fi]
                                pooled = fv[:, 0:256].bitcast(BF16)
                                stt("dve", pooled, av[:, 16:528], 1.0 / win_w, pu[:, 16:528], ALU.mult, ALU.subtract,
                                    [abf, bpu], [fbf])
                                if half == 0 and t == 0:
                                    nfix = win_w - 1
                                    tt("dve", tmp[0][:, 0:nfix], av[:, 16:16 + nfix], rc[:, 0:nfix], ALU.mult, [abf], [b_tmp[0]])
                                    tt("dve", pooled[:, 0:nfix], tmp[0][:, 0:nfix], pu[:, 16:16 + nfix], ALU.subtract,
                                       [b_tmp[0], bpu], [fbf])
                                b2 = gen_bank()
                                mm(ps[b2][:], poolw[:, g, :], pooled, True, True, [b_poolw, fbf], [b_ps[b2]])
                                act(p2T[:, g, t * 512:(t + 1) * 512], ps[b2][:], AF.Copy, [b_ps[b2]], [b_p2T[g][t]],
                                    scale=vecs[:, l, 44 + g:45 + g])
                        P.mark("C")
                        ETf = RR[:, 37016:37016 + 1024].bitcast(F32)

                        def cv(t, c):
                            if t == 0:
                                return stash[:, c, :], [b_OT[c][0], b_OT[c][1]]
                            if c < 3:
                                return S2[c][:, 0:512], [b_S2[c]]
                            return ETf, [b_ET[0], b_ET[1]]

                        for c in range(4):
                            pk, bpk = load_pack(d_cdiag[l, c], 4096)
                            pk3 = pk[:, :].rearrange("p (j q) -> p j q", j=32)
                            for t in range(2):
                                base = t * 512
                                bk = gen_bank()
                                for j in range(31):
                                    mm(ps[bk][:], pk3[:, j, :], uT[:, c, base + j:base + j + 512], j == 0, j == 30,
                                       [bpk, b_uT[c][0], b_uT[c][1]], [b_ps[bk]])
                                dst, dbufs = cv(t, c)
                                ts("dve", dst, ps[bk][:], vec2[:, l, c:c + 1], None, ALU.add, None, [b_ps[bk]], dbufs)
                        for t in range(2):
                            for c in range(4):
                                q = c % 2
                                sv, sbufs = cv(t, c)
                                act(sq[q][:], sv, AF.Copy, sbufs, [b_sq[q]])
                                mm(ps[6][0:1, :], onesC[:, 0:1], sq[q][:], c == 0, c == 3, [b_sq[q]], [b_ps[6]])
                            for c in range(4):
                                q = c % 2
                                sv, sbufs = cv(t, c)
                                act(sq[q][:], sv, AF.Square, sbufs, [b_sq[q]])
                                mm(ps[7][0:1, :], onesC[:, 0:1], sq[q][:], c == 0, c == 3, [b_sq[q]], [b_ps[7]])
                            P.op("dve", "tensor_copy", rowb[2][:], ps[6][0:1, :], reads=[b_ps[6]], writes=[b_row[2]])
                            tt("dve", rowb[0][:], rowb[2][:], rowb[2][:], ALU.mult, [b_row[2]], [b_row[0]])
                            tt("dve", rowb[0][:], ps[7][0:1, :], rowb[0][:], ALU.subtract, [b_ps[7], b_row[0]], [b_row[0]])
                            ts("dve", rowb[0][:], rowb[0][:], 4.0 * EPS, None, ALU.add, None, [b_row[0]], [b_row[0]])
                            mm(ps[6][:], onesF[:], rowb[2][:], True, True, [b_row[2]], [b_ps[6]])
                            rsqrt_bcast(0, 7)
                            for c in range(4):
                                q = c % 2
                                sv, sbufs = cv(t, c)
                                tt("dve", tmp[q][:], sv, ps[6][:], ALU.subtract, sbufs + [b_ps[6]], [b_tmp[q]])
                                tt("dve", tmp[q][:], tmp[q][:], ps[7][:], ALU.mult, [b_tmp[q], b_ps[7]], [b_tmp[q]])
                                ts("dve", tmp[q][:], tmp[q][:], vec2[:, l, 4 + c:5 + c], vec2[:, l, 8 + c:9 + c], ALU.mult, ALU.add,
                                   [b_tmp[q]], [b_tmp[q]])
                                act(sg[q][:], tmp[q][:], AF.Tanh, [b_tmp[q]], [b_sg[q]])
                                stt("dve", cT[:, c, t * 512:(t + 1) * 512], sg[q][:], 1.0, tmp[q][:], ALU.add, ALU.mult,
                                    [b_sg[q], b_tmp[q]], [b_cT[c][t]])
                        P.mark("D")
                        blocks = []
                        for h in range(4):
                            for qt in range(2):
                                Q0 = tok_base + qt * 512
                                for kb in range(0, Q0 // 128 + 4):
                                    blocks.append((h, qt, Q0, kb))
                        nb = len(blocks)
                        state = {"started": set(), "unit": None, "at_n": 0, "tr_n": 0, "chains": []}
                        pend_tr = {}
                        sched = {}

                        def blk_params(i):
                            h, qt, Q0, kb = blocks[i]
                            k0 = kb * 128
                            c0 = max(0, k0 - Q0)
                            n = 512 - c0
                            d0 = min(Q0 + c0 - k0, 256)
                            if half == 1 and kb >= 8:
                                Ks, Vs, bKs, bVs = curK, curV, b_curK[(kb - 8) // 4], b_curV[kb - 8]
                                kk0, vb = k0 - 1024, kb - 8
                            else:
                                Ks, Vs, bKs, bVs = kvK[l], kvV[l], b_kvK[l][kb // 4], b_kvV[l][kb]
                                kk0, vb = k0, kb
                            return h, qt, Q0, kb, k0, c0, n, d0, Ks, Vs, bKs, bVs, kk0, vb

                        def rec_S(i):
                            h, qt, Q0, kb, k0, c0, n, d0, Ks, Vs, bKs, bVs, kk0, vb = blk_params(i)
                            far = (d0 == 256)
                            for mi in range(2):
                                sbk = (i % 2) * 2 + mi
                                mm(ps[sbk][:, 0:n], Ks[64 * mi:64 * mi + 64, h, kk0:kk0 + 128],
                                   QT[64 * mi:64 * mi + 64, h, qt * 512 + c0:qt * 512 + 512],
                                   True, far, [bKs, b_QT[h][qt]], [b_ps[sbk]])
                            for mi in range(2):
                                sbk = (i % 2) * 2 + mi
                                if not far:
                                    mm(ps[sbk][:, 0:n], ident[:], biasT[:, h, d0:d0 + n], False, True, [b_const], [b_ps[sbk]])
                            for mi in range(2):
                                sbk = (i % 2) * 2 + mi
                                if far:
                                    act(ET[:, sbk, 0:n], ps[sbk][:, 0:n], AF.Exp, [b_ps[sbk]], [b_ET[sbk]], bias=cbias[:, h:h + 1])
                                else:
                                    act(ET[:, sbk, 0:n], ps[sbk][:, 0:n], AF.Exp, [b_ps[sbk]], [b_ET[sbk]])

                        def rec_AV(i):
                            h, qt, Q0, kb, k0, c0, n, d0, Ks, Vs, bKs, bVs, kk0, vb = blk_params(i)
                            if state["unit"] != (h, qt):
                                state["unit"] = (h, qt)
                                state["started"] = set()
                            started = state["started"]
                            for mi in range(2):
                                sbk = (i % 2) * 2 + mi
                                for j in range(c0 // 128, 4):
                                    a = j * 2 + mi
                                    bank, off = 4 + a // 3, (a % 3) * 160
                                    st1 = bank not in started
                                    started.add(bank)
                                    P.op("pe", "matmul", ps[bank][:, off:off + 129],
                                         ET[:, sbk, j * 128 - c0:j * 128 - c0 + 128], Vs[:, vb, h, :],
                                         start=st1, stop=(k0 == Q0 + j * 128), skip_group_check=True,
                                         reads=[b_ET[sbk], bVs], writes=[b_ps[bank]])
                            if k0 >= Q0:
                                j = (k0 - Q0) // 128
                                z = j % 2
                                state["chains"].append((norm_chain(j, z, h, qt, i), i))
                                if j % 2 == 1:
                                    (ca_, ia), (cb_, ib) = state["chains"]
                                    state["chains"] = []
                                    for sa, sb_ in zip(ca_[:6], cb_[:6]):
                                        sa()
                                        sb_()
                                    it = state["iter"]

                                    def second(ca_=ca_, cb_=cb_):
                                        for sa, sb_ in zip(ca_[6:], cb_[6:]):
                                            sa()
                                            sb_()
                                    sched.setdefault(it + 1, []).append(second)
                                    sched.setdefault(it + 3, []).append(lambda ia=ia, ib=ib: (rec_TR(ia), rec_TR(ib)))

                        def norm_chain(j, z, h, qt, i):
                            a1, a2 = j * 2, j * 2 + 1
                            O1 = ps[4 + a1 // 3][:, (a1 % 3) * 160:(a1 % 3) * 160 + 129]
                            O2 = ps[4 + a2 // 3][:, (a2 % 3) * 160:(a2 % 3) * 160 + 129]
                            bO1, bO2 = b_ps[4 + a1 // 3], b_ps[4 + a2 // 3]
                            st_, bst = att_s[z], b_atts[z]
                            fa, bfa = att_f[2 * z], b_attf[2 * z]
                            fo, bfo = att_f[2 * z + 1], b_attf[2 * z + 1]
                            on, bon = att_b[z], b_attb[z]

                            def fin():
                                stt("dve", on[:], fo[:], st_[:, 5:6], subln[:, l, :], ALU.mult, ALU.mult, [bst, bfo], [bon])
                                pend_tr[i] = (on, bon, h, qt * 512 + j * 128, qt)
                            return [
                                lambda: P.op("dve", "reciprocal", st_[:, 0:1], O1[:, 128:129], reads=[bO1], writes=[bst]),
                                lambda: P.op("dve", "reciprocal", st_[:, 1:2], O2[:, 128:129], reads=[bO2], writes=[bst]),
                                lambda: P.op("dve", "memset", st_[:, 3:4], 0.0, writes=[bst]),
                                lambda: tt("dve", st_[:, 2:3], st_[:, 1:2], nlam[:, l:l + 1], ALU.mult, [bst], [bst]),
                                lambda: ts("dve", fa[:], O1[:, 0:128], st_[:, 0:1], None, ALU.mult, None, [bst, bO1], [bfa]),
                                lambda: stt("dve", fo[:], O2[:, 0:128], st_[:, 2:3], fa[:], ALU.mult, ALU.add, [bst, bO2, bfa], [bfo]),
                                lambda: act(fa[:], fo[:], AF.Square, [bfo, bst], [bfa, bst], accum_out=st_[:, 3:4]),
                                lambda: ts("dve", st_[:, 4:5], st_[:, 3:4], 1.0 / 128.0, EPS, ALU.mult, ALU.add, [bst], [bst]),
                                lambda: tt("pool", st_[:, 5:6], st_[:, 4:5], nhalf[:, 0:1], ALU.pow, [bst], [bst]),
                                fin,
                            ]

                        def rec_TR(i):
                            if i not in pend_tr:
                                return
                            on, bon, h, qa, qt = pend_tr.pop(i)
                            tsl = state["tr_n"] % 4
                            state["tr_n"] += 1
                            mm(ps[7][:, tsl * 128:(tsl + 1) * 128], on[:], ident[:], True, True, [bon], [b_ps[7]])
                            act(OT[:, h, qa:qa + 128], ps[7][:, tsl * 128:(tsl + 1) * 128], AF.Copy, [b_ps[7]], [b_OT[h][qt]])

                        for i in range(nb + 5):
                            state["iter"] = i
                            if i < nb:
                                rec_S(i)
                            for fn_ in sched.pop(i, []):
                                fn_()
                            if 0 <= i - 1 < nb:
                                rec_AV(i - 1)
                        assert not sched and not pend_tr
                        P.barrier()
                        P.mark("E")
                        srcs = [(cT, b_cT), (OT, b_OT), (p2T, b_p2T)]
                        for m in range(8):
                            wg, bwg = load_pack(d_mrgg[l, m], 3072)
                            wbo, bwbo = load_pack(d_mrgb[l, m], 1536)
                            wg4 = wg[:, 0:3072].rearrange("p (b k j) -> p b k j", b=3, k=8)
                            wbo4 = wbo[:, 0:1536].rearrange("p (b k j) -> p b k j", b=3, k=4)
                            for t in range(2):
                                for br in range(3):
                                    bg, by = gen_bank(), gen_bank()
                                    for k in range(8):
                                        mm(ps[bg][:], wg4[:, br, k, :], hT[:, k, t * 512:(t + 1) * 512], k == 0, k == 7,
                                           [bwg, b_hT[k][t]], [b_ps[bg]])
                                    sT, bsT = srcs[br]
                                    for k in range(4):
                                        mm(ps[by][:], wbo4[:, br, k, :], sT[:, k, t * 512:(t + 1) * 512], k == 0, k == 3,
                                           [bwbo, bsT[k][t]], [b_ps[by]])
                                    q = br % 2
                                    act(sg[q][:], ps[bg][:], AF.Tanh, [b_ps[bg]], [b_sg[q]], scale=0.5)
                                    if br == 0:
                                        stt("dve", tmp[0][:], sg[q][:], 1.0, ps[by][:], ALU.add, ALU.mult, [b_sg[q], b_ps[by]], [b_tmp[0]])
                                    else:
                                        stt("dve", tmp[1][:], sg[q][:], 1.0, ps[by][:], ALU.add, ALU.mult, [b_sg[q], b_ps[by]], [b_tmp[1]])
                                        if br == 1:
                                            tt("dve", tmp[0][:], tmp[0][:], tmp[1][:], ALU.add, [b_tmp[1]], [b_tmp[0]])
                                        else:
                                            tt("dve", mergedT[:, m, t * 512:(t + 1) * 512], tmp[0][:], tmp[1][:], ALU.add,
                                               [b_tmp[0], b_tmp[1]], [b_mg[m][t]])
                        stg = []
                        for t in range(2):
                            for cg in range(2):
                                w_, bw_ = load_pack(d_wout[l, cg], 4096)
                                w3 = w_[:, :].rearrange("p (k c) -> p k c", k=8)
                                for m4 in range(4):
                                    m = cg * 4 + m4
                                    b = gen_bank()
                                    for k in range(8):
                                        mm(ps[b][:], w3[:, k, m4 * 128:(m4 + 1) * 128], mergedT[:, k, t * 512:(t + 1) * 512],
                                           k == 0, k == 7, [bw_, b_mg[k][t]], [b_ps[b]])
                                    evac(yevE[:, m, :], ps[b][:], [b_ps[b]], [b_yev[m]])
                                    if stg and m % 2 == 1:
                                        stg.pop(0)()
                            post_norm_residual(yevE, t, 8, l, 4.0 * EPS, t)
                            if t == 0:
                                alias = [b_cT[c][tt_] for c in (1, 2, 3) for tt_ in range(2)] + \
                                        [b_OT[c][tt_] for c in (0, 1) for tt_ in range(2)]
                                stg = pre_norm_stages(h2Tb, b_h2b, 0, 16, l, 0, 0, extra_writes=alias)
                        while stg:
                            stg.pop(0)()
                        P.barrier()
                        P.mark("F")
                        stg = pre_norm_stages(h2T, b_h2, 1, 16, l, 1, 0)
                        for t in range(2):
                            hcur, bh = (h2Tb, b_h2b) if t == 0 else (h2T, b_h2)
                            for fg in range(8):
                                w_, bw_ = load_pack(d_w1[l, fg], 4096)
                                w3 = w_[:, :].rearrange("p (k c) -> p k c", k=8)
                                for i in range(4):
                                    jf = fg * 4 + i
                                    b = gen_bank()
                                    for k in range(8):
                                        mm(ps[b][:], w3[:, k, i * 128:(i + 1) * 128], hcur[:, k, :], k == 0, k == 7,
                                           [bw_, bh[k]], [b_ps[b]])
                                    q = jf % 2
                                    act(relu_s[:, q, :], ps[b][:], AF.Relu, [b_ps[b]], [b_relu[q]])
                                    tt("dve", f1[:, jf, :], relu_s[:, q, :], relu_s[:, q, :], ALU.mult, [b_relu[q]], [b_f1[jf]])
                            if stg and fg % 2 == 0:
                                stg.pop(0)()
                            while stg:
                                stg.pop(0)()
                            for jg in range(8):
                                w_, bw_ = load_pack(d_w2[l, jg], 4096)
                                w3 = w_[:, :].rearrange("p (j c) -> p j c", j=4)
                                for jj in range(4):
                                    jf = jg * 4 + jj
                                    for m in range(8):
                                        mm(ps[m][:], w3[:, jj, m * 128:(m + 1) * 128], f1[:, jf, :], jf == 0, jf == 31,
                                           [bw_, b_f1[jf]], [b_ps[m]])
                            for m in range(8):
                                evac(yevF[:, m, :], ps[m][:], [b_ps[m]], [b_yev[m]])
                            post_norm_residual(yevF, t, 24, l, EPS, t)
                        P.barrier()
                        P.mark("G")
                        P.dma("pool", "pin", pTs, d_pT[l, sq_i, :, tok_base:tok_base + 1024].rearrange("(k p) t -> p k t", p=128),
                              writes=[b_pTs])
                        for k in range(8):
                            for t in range(2):
                                if (k + t) % 2:
                                    act(xbT[:, k, t * 512:(t + 1) * 512], xT[:, k, t * 512:(t + 1) * 512], AF.Copy,
                                        [b_x[k][t]], [b_xb[k][t]])
                                else:
                                    P.op("dve", "tensor_copy", xbT[:, k, t * 512:(t + 1) * 512], xT[:, k, t * 512:(t + 1) * 512],
                                         reads=[b_x[k][t]], writes=[b_xb[k][t]])
                        for cg in range(2):
                            w_, bw_ = load_pack(d_wpg[l, cg], 4096)
                            w3 = w_[:, :].rearrange("p (k c) -> p k c", k=8)
                            if cg == 0:
                                wpe_, bwpe = load_pack(d_wpe[l], 2048)
                                wpe3 = wpe_[:, 0:2048].rearrange("p (k c) -> p k c", k=2)
                            for m4 in range(4):
                                m = cg * 4 + m4
                                for t in range(2):
                                    bg, bp = gen_bank(), gen_bank()
                                    for k in range(8):
                                        mm(ps[bg][:], w3[:, k, m4 * 128:(m4 + 1) * 128], xbT[:, k, t * 512:(t + 1) * 512],
                                           k == 0, k == 7, [bw_, b_xb[k][t]], [b_ps[bg]])
                                    for k in range(2):
                                        mm(ps[bp][:], wpe3[:, k, m * 128:(m + 1) * 128], pTs[:, k, t * 512:(t + 1) * 512],
                                           k == 0, k == 1, [bwpe, b_pTs], [b_ps[bp]])
                                    q = t % 2
                                    act(sg[q][:], ps[bg][:], AF.Tanh, [b_ps[bg]], [b_sg[q]], scale=0.5)
                                    act(relu_s[:, q, :], ps[bp][:], AF.Copy, [b_ps[bp]], [b_relu[q]], scale=0.5)
                                    stt("dve", tmp[q][:], sg[q][:], 1.0, relu_s[:, q, :], ALU.add, ALU.mult, [b_sg[q], b_relu[q]], [b_tmp[q]])
                                    xs = xT[:, m, t * 512:(t + 1) * 512]
                                    tt("dve", xs, xs, tmp[q][:], ALU.add, [b_tmp[q]], [b_x[m][t]])
                        P.barrier()
                    for k in range(8):
                        P.dma("sp", "xout", d_out[sq_i, k * 128:(k + 1) * 128, tok_base:tok_base + 1024], xT[:, k, :],
                              reads=[b_x[k][0], b_x[k][1]])
                    P.barrier()
        P.dry = True
        walk()
        P.dry = False
        walk()
        counts = P.emit()
        counts["marks"] = P.marks
    return nc, counts


def _rel_bucket(n):
    n = np.maximum(n, 0)
    nf = np.maximum(n, 1).astype(np.float32)
    large = 16 + (np.log(nf / np.float32(16)) / np.float32(math.log(128 / 16)) * np.float32(16)).astype(np.int32)
    large = np.minimum(large, 31)
    return np.where(n < 16, n, large)


def host_layout(inp):
    f = np.float32
    g = {k: np.asarray(v, dtype=f) for k, v in inp.items()}
    L = DEPTH
    sh = {}
    w_in = g["w_in"]
    sh["win"] = np.ascontiguousarray(
        w_in[:, :, :3072].reshape(L, 8, 128, 6, 512).transpose(0, 3, 2, 1, 4)).reshape(L, 6, 128, 4096)
    gates = w_in[:, :, 3072:].reshape(L, 8, 128, 3, 8, 128)
    sh["mrgg"] = np.ascontiguousarray(gates.transpose(0, 4, 2, 3, 1, 5)).reshape(L, 8, 128, 3072)
    bo = np.stack([g["w_conv_out"], g["w_attn_out"], g["w_pool_out"]], axis=1)
    bo = bo.reshape(L, 3, 4, 128, 8, 128)
    sh["mrgb"] = np.ascontiguousarray(bo.transpose(0, 4, 3, 1, 2, 5)).reshape(L, 8, 128, 1536)

    def colgroups(w, ng):
        return np.ascontiguousarray(w.reshape(L, 8, 128, ng, 512).transpose(0, 3, 2, 1, 4)).reshape(L, ng, 128, 4096)

    sh["wout"] = colgroups(g["w_out"], 2)
    sh["w1"] = colgroups(g["w_mlp_in"], 8)
    sh["wpg"] = colgroups(g["w_ple_gate"], 2)
    sh["w2"] = np.ascontiguousarray(g["w_mlp_out"].reshape(L, 8, 4, 128, 1024).transpose(0, 1, 3, 2, 4)).reshape(L, 8, 128, 4096)
    sh["wpe"] = np.ascontiguousarray(g["w_ple_proj"].reshape(L, 2, 128, 1024).transpose(0, 2, 1, 3)).reshape(L, 128, 2048)
    sh["poolw"] = np.ascontiguousarray(g["pool_w"].transpose(0, 2, 1, 3)).reshape(L, 128, 512)
    vecs = np.zeros((L, 128, 48), f)

    def cols(v, n):
        return v.reshape(L, n, 128).transpose(0, 2, 1)
    vecs[:, :, 0:8] = cols(g["g_pre_mix"], 8)
    vecs[:, :, 8:16] = cols(g["g_post_mix"], 8)
    vecs[:, :, 16:24] = cols(g["g_pre_mlp"], 8)
    vecs[:, :, 24:32] = cols(g["g_post_mlp"], 8)
    vecs[:, :, 32:36] = cols(g["conv_dw_b"], 4)
    vecs[:, :, 36:40] = cols(g["conv_ln_g"], 4)
    vecs[:, :, 40:44] = cols(g["conv_ln_b"], 4)
    vecs[:, :, 44:48] = cols(g["pool_scale"], 4)
    w4 = g["conv_dw_w"][:, :, 0, :].reshape(L, 31, 4, 128).transpose(0, 2, 3, 1)
    cd = np.zeros((L, 4, 128, 32, 128), f)
    ar = np.arange(128)
    cd[:, :, ar, :31, ar] = w4.transpose(2, 0, 1, 3)
    sh["cdiag"] = cd.reshape(L, 4, 128, 4096)
    sh["vecs"] = vecs
    sh["sublnb"] = np.ascontiguousarray(np.broadcast_to(g["subln_g"][:, None, :], (L, 128, 128)))
    sh["lamb"] = np.ascontiguousarray(np.broadcast_to(g["lam_p"].reshape(L, 1, 256), (L, 128, 256)))
    kk = np.arange(128)[:, None]
    dd = np.arange(768)[None, :]
    nrel = dd - kk
    bidx = _rel_bucket(nrel)
    tab = g["rel_bias"][bidx]
    tab = np.where((nrel >= 0)[:, :, None], tab, f(-1e30))
    sh["biasT"] = np.ascontiguousarray(tab.transpose(0, 2, 1)).reshape(128, 4 * 768).astype(f)
    sh["ident"] = np.eye(128, dtype=f)
    sh["cbias"] = np.ascontiguousarray(np.broadcast_to(g["rel_bias"][31][None, :], (128, 4)))
    rc = np.zeros((128, 16), f)
    rc[:, :] = 1.0 / (np.arange(16, dtype=f) + 1.0)
    sh["rc"] = rc
    x = g["x"]
    p = g["p"]
    per_core = []
    for c in range(8):
        d = dict(sh)
        d["xT"] = np.ascontiguousarray(x[2 * c:2 * c + 2].transpose(0, 2, 1))
        d["pT"] = np.ascontiguousarray(p[:, 2 * c:2 * c + 2].transpose(0, 1, 3, 2))
        per_core.append(d)
    return per_core


_CACHE = {}


def kernel(**inputs):
    if "nc" not in _CACHE:
        _CACHE["nc"] = build_program()[0]
    nc = _CACHE["nc"]
    in_maps = host_layout(inputs)
    res = run_bass_kernel_spmd(nc, in_maps, core_ids=list(range(8)))
    out = np.empty((16, S, D), np.float32)
    for c in range(8):
        out[2 * c:2 * c + 2] = res.results[c]["outT"].transpose(0, 2, 1)
    return out
```

```python
import math
from contextlib import ExitStack

import numpy as np
import concourse.bass as bass
import concourse.mybir as mybir
from concourse.bass_utils import run_bass_kernel_spmd

F32 = mybir.dt.float32
BF16 = mybir.dt.bfloat16
ALU = mybir.AluOpType
AF = mybir.ActivationFunctionType
AX = mybir.AxisListType

DEPTH = 2
NSEQ = 2
S = 2048
D = 1024
EPS = 1e-6
LAMBDA_INIT = [0.8 - 0.6 * math.exp(-0.3 * i) for i in range(DEPTH)]

COMPUTE = ("pe", "act", "dve", "pool")
STREAMS = ("pe", "act", "dve", "pool", "sp")


class Buf:
    __slots__ = ("name", "lw", "rd")

    def __init__(self, name):
        self.name = name
        self.lw = None
        self.rd = []


class Op:
    __slots__ = ("stream", "fn", "waits", "key", "idx", "signal", "is_dma", "cnt")


class Prog:
    def __init__(self, nc):
        self.nc = nc
        self.ops = {s: [] for s in STREAMS}
        self.cnt = {}
        self.vc = {s: {} for s in STREAMS}
        self.opvc = {}
        self.bykey = {}
        self.dma_keys = []
        self.pend = {s: {} for s in STREAMS}
        self.marks = []
        self.dry = False

    def mark(self, name):
        if not self.dry:
            self.marks.append((name, len(self.ops["pe"])))

    def barrier(self):
        if self.dry:
            return
        snap = dict(self.cnt)
        for s in STREAMS:
            for k, i in snap.items():
                if self.pend[s].get(k, 0) < i:
                    self.pend[s][k] = i

    def _record(self, stream, key, fn, reads, writes, is_dma):
        if self.dry:
            return None
        deps = set()
        for b in reads:
            if b.lw is not None:
                deps.add(b.lw)
        for b in writes:
            if b.lw is not None:
                deps.add(b.lw)
            for r in b.rd:
                deps.add(r)
        for k, i in self.pend[stream].items():
            deps.add((k, i))
        self.pend[stream] = {}
        idx = self.cnt.get(key, 0) + 1
        self.cnt[key] = idx
        me = (key, idx)
        vc = self.vc[stream]
        waits = {}
        for (k, i) in deps:
            if k == stream and not is_dma:
                continue
            if vc.get(k, 0) >= i:
                continue
            if waits.get(k, 0) < i:
                waits[k] = i
        if stream in ("act", "dve", "pool") and not is_dma:
            own = 0
            for b in reads:
                if b.lw is not None and b.lw[0] == stream:
                    own = max(own, b.lw[1])
            for b in writes:
                if b.lw is not None and b.lw[0] == stream:
                    own = max(own, b.lw[1])
                for r in b.rd:
                    if r[0] == stream:
                        own = max(own, r[1])
            if own > vc.get(("self", stream), 0):
                waits[stream] = own
        for k, i in waits.items():
            if k == stream:
                vc[("self", stream)] = max(vc.get(("self", stream), 0), i)
                continue
            dvc = self.opvc.get((k, i))
            if dvc:
                for kk, ii in dvc.items():
                    if vc.get(kk, 0) < ii:
                        vc[kk] = ii
            if vc.get(k, 0) < i:
                vc[k] = i
        op = Op()
        op.stream, op.fn, op.waits, op.key, op.idx = stream, fn, waits, key, idx
        op.signal, op.is_dma, op.cnt = is_dma, is_dma, 0
        self.ops[stream].append(op)
        self.bykey[me] = op
        snap = {k: v for k, v in vc.items() if not isinstance(k, tuple) and k != stream}
        if not is_dma:
            snap[stream] = idx
        self.opvc[me] = snap
        for b in reads:
            b.rd.append(me)
        for b in writes:
            b.lw = me
            b.rd = []
        return op

    def op(self, stream, method, *args, reads=(), writes=(), **kw):
        return self._record(stream, stream, (method, args, kw), list(reads), list(writes), False)

    def dma(self, queue, semkey, out, in_, reads=(), writes=()):
        if semkey not in self.dma_keys:
            self.dma_keys.append(semkey)
        return self._record(queue, semkey, ("dma_start", (), {"out": out, "in_": in_}), list(reads), list(writes), True)

    def emit(self):
        nc = self.nc
        for s in STREAMS:
            for op in self.ops[s]:
                for k, i in op.waits.items():
                    if k in COMPUTE:
                        self.bykey[(k, i)].signal = True
        finals = {}
        for k in COMPUTE:
            if self.cnt.get(k, 0):
                self.bykey[(k, self.cnt[k])].signal = True
                finals[k] = self.cnt[k]
        for k in self.dma_keys:
            finals[k] = self.cnt[k]
        for k in COMPUTE:
            c = 0
            for i in range(1, self.cnt.get(k, 0) + 1):
                op = self.bykey[(k, i)]
                if op.signal:
                    c += 1
                op.cnt = c
        with ExitStack() as es:
            sems = {}
            for k in COMPUTE:
                if self.cnt.get(k, 0):
                    sems[k] = es.enter_context(nc.semaphore("s_" + k))
            for k in self.dma_keys:
                sems[k] = es.enter_context(nc.semaphore("d_" + str(k)))
            block = es.enter_context(nc.Block())

            def val(k, i):
                if k in COMPUTE:
                    return self.bykey[(k, i)].cnt
                return 16 * i

            def run(stream, eng):
                for op in self.ops[stream]:
                    for k, i in op.waits.items():
                        eng.wait_ge(sems[k], val(k, i))
                    meth, args, kw = op.fn
                    ins = getattr(eng, meth)(*args, **kw)
                    if op.is_dma:
                        ins.then_inc(sems[op.key], 16)
                    elif op.signal:
                        ins.then_inc(sems[op.key], 1)
                if stream == "sp":
                    for k, i in finals.items():
                        eng.wait_ge(sems[k], val(k, i))

            @block.tensor
            def _(e):
                run("pe", e)

            @block.scalar
            def _(e):
                run("act", e)

            @block.vector
            def _(e):
                run("dve", e)

            @block.gpsimd
            def _(e):
                run("pool", e)

            @block.sync
            def _(e):
                run("sp", e)
        return {s: len(self.ops[s]) for s in STREAMS}


def build_program(layers=(0, 1), nseq=NSEQ, halves=(0, 1)):
    nc = bass.Bass("TRN2", target_bir_lowering=False)
    P = Prog(nc)

    def dram(name, shape, kind="ExternalInput"):
        return nc.dram_tensor(name, list(shape), F32, kind=kind).ap()

    d_xT = dram("xT", [NSEQ, D, S])
    d_pT = dram("pT", [DEPTH, NSEQ, 256, S])
    d_out = dram("outT", [NSEQ, D, S], kind="ExternalOutput")
    d_win = dram("win", [DEPTH, 6, 128, 4096])
    d_mrgg = dram("mrgg", [DEPTH, 8, 128, 3072])
    d_mrgb = dram("mrgb", [DEPTH, 8, 128, 1536])
    d_wout = dram("wout", [DEPTH, 2, 128, 4096])
    d_w1 = dram("w1", [DEPTH, 8, 128, 4096])
    d_w2 = dram("w2", [DEPTH, 8, 128, 4096])
    d_wpg = dram("wpg", [DEPTH, 2, 128, 4096])
    d_wpe = dram("wpe", [DEPTH, 128, 2048])
    d_poolw = dram("poolw", [DEPTH, 128, 512])
    d_vecs = dram("vecs", [DEPTH, 128, 48])
    d_cdiag = dram("cdiag", [DEPTH, 4, 128, 4096])
    d_subln = dram("sublnb", [DEPTH, 128, 128])
    d_lam = dram("lamb", [DEPTH, 128, 256])
    d_bias = dram("biasT", [128, 4 * 768])
    d_ident = dram("ident", [128, 128])
    d_rc = dram("rc", [128, 16])
    d_cb = dram("cbias", [128, 4])

    es = ExitStack()
    with es:
        def sb(name, shape, dt):
            return es.enter_context(nc.sbuf_tensor(name, list(shape), dt))

        xT = sb("xT_s", [128, 8, 1024], F32)
        kvK = [sb(f"kvK{i}", [128, 4, 1024], BF16) for i in range(2)]
        kvV = [sb(f"kvV{i}", [128, 8, 4, 129], BF16) for i in range(2)]
        NRR = 39072
        RR = sb("RR", [128, NRR], BF16)
        S2 = [sb(f"S2_{i}", [128, 528], F32) for i in range(3)]
        sq = [sb(f"sq{i}", [128, 512], BF16) for i in range(2)]
        rstd = [sb(f"rstd{i}", [128, 512], F32) for i in range(2)]
        sg = [sb(f"sg{i}", [128, 512], F32) for i in range(2)]
        tmp = [sb(f"tmp{i}", [128, 512], F32) for i in range(2)]
        ring = [sb(f"ring{i}", [128, 4096], BF16) for i in range(4)]
        biasT = sb("biasT_s", [128, 4, 768], BF16)
        ident = sb("ident_s", [128, 128], BF16)
        onesD = sb("onesD", [128, 128], BF16)
        onesC = sb("onesC", [128, 128], BF16)
        vecs = sb("vecs_s", [128, DEPTH, 48], F32)
        vec2 = sb("vec2_s", [128, DEPTH, 12], F32)
        subln = sb("subln_s", [128, DEPTH, 128], F32)
        nlam = sb("nlam", [128, DEPTH], F32)
        rc = sb("rc_s", [128, 16], F32)
        nhalf = sb("nhalf", [128, 8], F32)
        onesF = sb("onesF", [1, 128], F32)
        cbias = sb("cbias_s", [128, 4], F32)
        identF = sb("identF", [128, 128], F32)
        onesFF = sb("onesFF", [128, 128], F32)
        colb = [sb(f"colb{i}", [128, 4], F32) for i in range(2)]
        poolw = sb("poolw_s", [128, 4, 128], BF16)
        uhalo = sb("uhalo", [128, DEPTH, 4, 30], BF16)
        puhalo = sb("puhalo", [128, DEPTH, 4, 16], F32)
        att_f = [sb(f"attf{i}", [128, 128], F32) for i in range(4)]
        att_b = [sb(f"attb{i}", [128, 128], BF16) for i in range(2)]
        att_s = [sb(f"atts{i}", [128, 8], F32) for i in range(2)]
        ps = [es.enter_context(nc.psum_tensor(f"ps{i}", [128, 512], F32)) for i in range(8)]

        b_ps = [Buf(f"ps{i}") for i in range(8)]
        b_ring = [Buf(f"ring{i}") for i in range(4)]
        b_x = [[Buf(f"x{k}_{t}") for t in range(2)] for k in range(8)]
        b_sq = [Buf("sq0"), Buf("sq1")]
        b_rstd = [Buf("rstd0"), Buf("rstd1")]
        b_sg = [Buf("sg0"), Buf("sg1")]
        b_tmp = [Buf("tmp0"), Buf("tmp1")]
        b_S2 = [Buf(f"S2_{i}") for i in range(3)]
        b_const = Buf("const")
        b_kvK = [[Buf(f"kvK{i}_{t}") for t in range(2)] for i in range(2)]
        b_kvV = [[Buf(f"kvV{i}_{tb}") for tb in range(8)] for i in range(2)]
        b_uh = [Buf(f"uh{l}") for l in range(DEPTH)]
        b_ph = [[Buf(f"ph{l}_{g}") for g in range(4)] for l in range(DEPTH)]
        b_attf = [Buf(f"attf{i}") for i in range(4)]
        b_attb = [Buf(f"attb{i}") for i in range(2)]
        b_atts = [Buf(f"atts{i}") for i in range(2)]
        b_poolw = Buf("poolw")
        rowb = [rstd[0][0:1, :], rstd[1][0:1, :], tmp[1][0:1, :]]
        b_row = [b_rstd[0], b_rstd[1], b_tmp[1]]
        b_col = [Buf(f"col{i}") for i in range(2)]

        def rr3(off, a, b):
            return RR[:, off:off + a * b].rearrange("p (a b) -> p a b", a=a)

        def rrf(off_bf16, a, b):
            v = RR[:, off_bf16:off_bf16 + 2 * a * b].bitcast(F32)
            return v.rearrange("p (a b) -> p a b", a=a)

        curK = rr3(0, 4, 1024)
        curV = RR[:, 4096:4096 + 4128].rearrange("p (a h e) -> p a h e", a=8, h=4)
        hT = rr3(8224, 8, 1024)
        uT = rr3(16416, 4, 1054)
        QT = rr3(20632, 4, 1024)
        p2T = rr3(24728, 4, 1024)
        cT = rr3(28824, 4, 1024)
        OT = rr3(32920, 4, 1024)
        stash = rrf(32920, 4, 512)
        ET = rr3(37016, 4, 512)
        mergedT = rr3(0, 8, 1024)
        yevE = rrf(16416, 8, 512)
        h2T = rr3(0, 8, 512)
        h2Tb = rr3(30720, 8, 512)
        f1 = rr3(4096, 32, 512)
        relu_s = rrf(20480, 2, 512)
        yevF = rrf(22528, 8, 512)
        xbT = rr3(0, 8, 1024)
        pTs = rr3(8192, 2, 1024)
        b_curK = [Buf("curK0"), Buf("curK1")]
        b_curV = [Buf(f"curV{i}") for i in range(8)]
        b_hT = [[Buf(f"hT{k}_{t}") for t in range(2)] for k in range(8)]
        b_uT = [[Buf(f"uT{c}_{t}") for t in range(2)] for c in range(4)]
        b_QT = [[Buf(f"QT{c}_{t}") for t in range(2)] for c in range(4)]
        b_p2T = [[Buf(f"p2T{c}_{t}") for t in range(2)] for c in range(4)]
        b_cT = [[Buf(f"cT{c}_{t}") for t in range(2)] for c in range(4)]
        b_OT = [[Buf(f"OT{c}_{t}") for t in range(2)] for c in range(4)]
        b_ET = [Buf(f"ET{i}") for i in range(4)]
        b_mg = [[Buf(f"mg{k}_{t}") for t in range(2)] for k in range(8)]
        b_yev = [Buf(f"yev{m}") for m in range(8)]
        b_h2 = [Buf(f"h2_{k}") for k in range(8)]
        b_h2b = [Buf(f"h2b_{k}") for k in range(8)]
        b_f1 = [Buf(f"f1_{j}") for j in range(32)]
        b_relu = [Buf("relu0"), Buf("relu1")]
        b_xb = [[Buf(f"xb{k}_{t}") for t in range(2)] for k in range(8)]
        b_pTs = Buf("pTs")

        NSLOT = 4
        plan = []
        lastuse = {}
        st = {"pk": 0, "issued": 0, "dry_pk": 0}

        class PackTok:
            def __init__(self, k):
                self.k = k

        def load_pack(src_ap, nelem):
            if P.dry:
                plan.append((src_ap, nelem))
                k = st["dry_pk"]
                st["dry_pk"] += 1
                lastuse[k] = k + 1
                return ring[0], PackTok(k)
            k = st["pk"]
            while st["issued"] < len(plan):
                i = st["issued"]
                if i > k + NSLOT - 1:
                    break
                if i >= NSLOT and lastuse[i - NSLOT] > k:
                    assert i > k, "ring too small: slot-mate of the requested pack is still live"
                    break
                s = i % NSLOT
                P.dma("pool", f"ring{s}", ring[s][:, 0:plan[i][1]], plan[i][0], writes=[b_ring[s]])
                st["issued"] += 1
            assert st["issued"] > k
            st["pk"] += 1
            return ring[k % NSLOT], b_ring[k % NSLOT]

        gen_n = [0]

        def gen_bank():
            b = gen_n[0] % 6
            gen_n[0] += 1
            return b

        ev_n = [0]

        def evac(out, in_, reads, writes):
            ev_n[0] += 1
            if ev_n[0] % 2:
                P.op("act", "activation", out=out, in_=in_, func=AF.Copy, reads=reads, writes=writes)
            else:
                P.op("dve", "tensor_copy", out, in_, reads=reads, writes=writes)

        def mm(out, lhsT, rhs, start, stop, reads, writes):
            if P.dry:
                for b in reads:
                    if isinstance(b, PackTok):
                        lastuse[b.k] = st["dry_pk"]
                return
            P.op("pe", "matmul", out, lhsT, rhs, start=start, stop=stop, reads=reads, writes=writes)

        def tt(eng, out, in0, in1, op, reads, writes):
            P.op(eng, "tensor_tensor", out, in0, in1, op, reads=reads, writes=writes)

        def ts(eng, out, in0, s1, s2, op0, op1, reads, writes):
            if s2 is None:
                P.op(eng, "tensor_scalar", out, in0, s1, None, op0, reads=reads, writes=writes)
            else:
                P.op(eng, "tensor_scalar", out, in0, s1, s2, op0, op1, reads=reads, writes=writes)

        def stt(eng, out, in0, scalar, in1, op0, op1, reads, writes):
            P.op(eng, "scalar_tensor_tensor", out=out, in0=in0, scalar=scalar, in1=in1, op0=op0, op1=op1,
                 reads=reads, writes=writes)

        def act(out, in_, func, reads, writes, **kw):
            P.op("act", "activation", out=out, in_=in_, func=func, reads=reads, writes=writes, **kw)

        b_lamt, b_lt, b_nl, b_v2 = Buf("lamt"), Buf("lt"), Buf("nl"), Buf("v2")
        lamt = RR[:, 0:1024].bitcast(F32).rearrange("p (l c) -> p l c", l=DEPTH)
        lt = RR[:, 2048:2048 + 512].bitcast(F32)
        P.dma("sp", "c0", vecs[:], d_vecs.rearrange("l p c -> p l c"), writes=[b_const])
        P.dma("sp", "c0", subln[:], d_subln.rearrange("l p c -> p l c"), writes=[Buf("x1")])
        P.dma("sp", "c0", rc[:], d_rc, writes=[Buf("x2")])
        P.dma("sp", "c0", lamt, d_lam.rearrange("l p c -> p l c"), writes=[b_lamt])
        P.dma("pool", "c1", biasT[:], d_bias.rearrange("p (h c) -> p h c", h=4), writes=[Buf("x3")])
        P.dma("pool", "c1", ident[:], d_ident, writes=[Buf("x4")])
        P.dma("sp", "c0", identF[:], d_ident, writes=[Buf("x5")])
        P.dma("sp", "c0", cbias[:], d_cb, writes=[Buf("x6")])
        P.barrier()
        P.op("dve", "memset", onesD[:], 1.0 / 1024.0, writes=[b_const])
        P.op("dve", "memset", nhalf[:], -0.5, writes=[b_const])
        P.op("dve", "memset", onesF[:], 1.0, writes=[b_const])
        P.op("dve", "memset", onesFF[:], 1.0, writes=[b_const])
        P.op("dve", "memset", onesC[:], 1.0 / 512.0, writes=[b_const])
        P.op("dve", "memset", uhalo[:], 0.0, writes=[b_const])
        P.op("dve", "memset", puhalo[:], 0.0, writes=[b_const])
        for i in range(2):
            P.op("dve", "memset", kvV[i][:, :, :, 128:129], 1.0, writes=[b_const])
        ssum = att_s[0]
        for l in range(DEPTH):
            ts("dve", vec2[:, l, 0:4], vecs[:, l, 32:36], 2.0, None, ALU.mult, None, [b_const], [b_v2])
            ts("dve", vec2[:, l, 4:12], vecs[:, l, 36:44], 0.5, None, ALU.mult, None, [b_const], [b_v2])
            ts("dve", subln[:, l, :], subln[:, l, :], 1.0 - LAMBDA_INIT[l], None, ALU.mult, None, [b_const], [b_v2])
            for j in range(2):
                tt("dve", lt[:, 64 * j:64 * j + 64], lamt[:, l, 128 * j:128 * j + 64],
                   lamt[:, l, 128 * j + 64:128 * j + 128], ALU.mult, [b_lamt], [b_lt])
                P.op("dve", "reduce_sum", ssum[:, 2 * l + j:2 * l + j + 1], lt[:, 64 * j:64 * j + 64], AX.X,
                     reads=[b_lt], writes=[b_atts[0]])
        act(ssum[:, 4:8], ssum[:, 0:4], AF.Exp, [b_atts[0]], [b_atts[0]])
        for l in range(DEPTH):
            tt("dve", nlam[:, l:l + 1], ssum[:, 5 + 2 * l:6 + 2 * l], ssum[:, 4 + 2 * l:5 + 2 * l], ALU.subtract,
               [b_atts[0]], [b_nl])
            ts("dve", nlam[:, l:l + 1], nlam[:, l:l + 1], -LAMBDA_INIT[l], None, ALU.add, None, [b_nl], [b_nl])
        P.barrier()

        def norm_rstd(srcs, src_bufs, eps, r):
            bank = 6 + (r % 2)
            for k in range(8):
                q = k % 2
                act(sq[q][:], srcs[k], AF.Square, [src_bufs[k]], [b_sq[q]])
                mm(ps[bank][0:1, :], onesD[:, 0:1], sq[q][:], k == 0, k == 7, [b_sq[q]], [b_ps[bank]])
            ts("dve", rowb[r][:], ps[bank][0:1, :], eps, None, ALU.add, None, [b_ps[bank]], [b_row[r]])
            return rsqrt_bcast(r, bank)

        def rsqrt_bcast(r, bank):
            for blk in range(4):
                mm(ps[bank][:, blk:blk + 1], rowb[r][0:1, blk * 128:(blk + 1) * 128], onesF[0:1, 0:1], True, True,
                   [b_row[r]], [b_ps[bank]])
            P.op("dve", "tensor_copy", colb[r % 2][:], ps[bank][:, 0:4], reads=[b_ps[bank]], writes=[b_col[r % 2]])
            tt("pool", colb[r % 2][:], colb[r % 2][:], nhalf[:, 0:4], ALU.pow, [b_col[r % 2]], [b_col[r % 2]])
            for blk in range(4):
                ts("dve", rstd[r % 2][:, blk * 128:(blk + 1) * 128], identF[:], colb[r % 2][:, blk:blk + 1], None, ALU.mult, None,
                   [b_col[r % 2]], [b_rstd[r % 2]])
            for blk in range(4):
                mm(ps[bank][:, blk * 128:(blk + 1) * 128], onesFF[:], rstd[r % 2][:, blk * 128:(blk + 1) * 128], True, True,
                   [b_rstd[r % 2]], [b_ps[bank]])
            return ps[bank][:], b_ps[bank]

        def post_norm_residual(yev, t, gcol, l, eps, r):
            rs, brs = norm_rstd([yev[:, m, :] for m in range(8)], b_yev, eps, r)
            for m in range(8):
                q = m % 2
                stt("dve", tmp[q][:], yev[:, m, :], vecs[:, l, gcol + m:gcol + m + 1], rs, ALU.mult, ALU.mult,
                    [b_yev[m], brs], [b_tmp[q]])
                xs = xT[:, m, t * 512:(t + 1) * 512]
                tt("dve", xs, xs, tmp[q][:], ALU.add, [b_tmp[q]], [b_x[m][t]])

        def pre_norm(dst, dst_bufs, t, gcol, l, r, tok0):
            rs, brs = norm_rstd([xT[:, k, t * 512:(t + 1) * 512] for k in range(8)], [b_x[k][t] for k in range(8)], EPS, r)
            for k in range(8):
                stt("dve", dst[:, k, tok0:tok0 + 512], xT[:, k, t * 512:(t + 1) * 512],
                    vecs[:, l, gcol + k:gcol + k + 1], rs, ALU.mult, ALU.mult,
                    [b_x[k][t], brs], [dst_bufs[k]])

        def walk():
            gen_n[0] = 0
            ev_n[0] = 0
            for sq_i in range(nseq):
                for half in halves:
                    tok_base = half * 1024
                    for k in range(8):
                        P.dma("sp", "xin", xT[:, k, :], d_xT[sq_i, k * 128:(k + 1) * 128, tok_base:tok_base + 1024],
                              writes=[b_x[k][0], b_x[k][1]])
                    P.barrier()
                    for l in layers:
                        if half == 0:
                            Kd, Vd, bKd, bVd = kvK[l], kvV[l], b_kvK[l], b_kvV[l]
                        else:
                            Kd, Vd, bKd, bVd = curK, curV, b_curK, b_curV
                            P.op("dve", "memset", curV[:, :, :, 128:129], 1.0, writes=b_curV)
                        P.mark(f"A s{sq_i} h{half} l{l}")
                        for t in range(2):
                            pre_norm(hT, [b_hT[k][t] for k in range(8)], t, 0, l, t, t * 512)
                        P.dma("pool", "pw", poolw[:], d_poolw[l].rearrange("p (g c) -> p g c", g=4), writes=[b_poolw])
                        P.mark("B")
                        wa, bwa = load_pack(d_win[l, 0], 4096)
                        wb, bwb = load_pack(d_win[l, 1], 4096)
                        wa3 = wa[:, :].rearrange("p (k c) -> p k c", k=8)
                        wb3 = wb[:, :].rearrange("p (k c) -> p k c", k=8)
                        for c in range(4):
                            if half == 0:
                                P.op("dve", "memset", uT[:, c, 0:30], 0.0, writes=[b_uT[c][0]])
                            else:
                                P.op("dve", "tensor_copy", uT[:, c, 0:30], uhalo[:, l, c, :], reads=[b_uh[l]], writes=[b_uT[c][0]])
                        for c in range(4):
                            for t in range(2):
                                ba, bb = gen_bank(), gen_bank()
                                for k in range(8):
                                    mm(ps[ba][:], wa3[:, k, c * 128:(c + 1) * 128], hT[:, k, t * 512:(t + 1) * 512],
                                       k == 0, k == 7, [bwa, b_hT[k][t]], [b_ps[ba]])
                                for k in range(8):
                                    mm(ps[bb][:], wb3[:, k, c * 128:(c + 1) * 128], hT[:, k, t * 512:(t + 1) * 512],
                                       k == 0, k == 7, [bwb, b_hT[k][t]], [b_ps[bb]])
                                q = t % 2
                                act(sg[q][:], ps[bb][:], AF.Tanh, [b_ps[bb]], [b_sg[q]], scale=0.5)
                                stt("dve", uT[:, c, 30 + t * 512:30 + (t + 1) * 512], sg[q][:], 1.0, ps[ba][:], ALU.add, ALU.mult,
                                    [b_sg[q], b_ps[ba]], [b_uT[c][t]])
                        if half == 0:
                            for c in range(4):
                                P.op("dve", "tensor_copy", uhalo[:, l, c, :], uT[:, c, 1024:1054],
                                     reads=[b_uT[c][1]], writes=[b_uh[l]])
                        for which in (2, 3):
                            w_, bw_ = load_pack(d_win[l, which], 4096)
                            w3 = w_[:, :].rearrange("p (k c) -> p k c", k=8)
                            for c in range(4):
                                for t in range(2):
                                    b = gen_bank()
                                    for k in range(8):
                                        mm(ps[b][:], w3[:, k, c * 128:(c + 1) * 128], hT[:, k, t * 512:(t + 1) * 512],
                                           k == 0, k == 7, [bw_, b_hT[k][t]], [b_ps[b]])
                                    if which == 2:
                                        act(QT[:, c, t * 512:(t + 1) * 512], ps[b][:], AF.Copy, [b_ps[b]], [b_QT[c][t]], scale=0.125)
                                    else:
                                        P.op("dve", "tensor_copy", Kd[:, c, t * 512:(t + 1) * 512], ps[b][:],
                                             reads=[b_ps[b]], writes=[bKd[t]])
                        w_, bw_ = load_pack(d_win[l, 4], 4096)
                        w3 = w_[:, :].rearrange("p (k c) -> p k c", k=8)
                        for tb in range(8):
                            b = gen_bank()
                            for k in range(8):
                                mm(ps[b][:], hT[:, k, tb * 128:(tb + 1) * 128], w3[:, k, :],
                                   k == 0, k == 7, [bw_, b_hT[k][tb // 4]], [b_ps[b]])
                            evac(Vd[:, tb, :, 0:128], ps[b][:].rearrange("p (h e) -> p h e", h=4), [b_ps[b]], [bVd[tb]])
                        w_, bw_ = load_pack(d_win[l, 5], 4096)
                        w3 = w_[:, :].rearrange("p (k c) -> p k c", k=8)
                        B3 = [(S2[0], b_S2[0]), (S2[1], b_S2[1]), (S2[2], b_S2[2])]
                        pu, bpu = B3[0]
                        for g in range(4):
                            win_w = 2 ** (g + 1)
                            for t in range(2):
                                b = gen_bank()
                                for k in range(8):
                                    mm(ps[b][:], w3[:, k, g * 128:(g + 1) * 128], hT[:, k, t * 512:(t + 1) * 512],
                                       k == 0, k == 7, [bw_, b_hT[k][t]], [b_ps[b]])
                                if half == 0 and t == 0:
                                    P.op("dve", "memset", pu[:, 0:16], 0.0, writes=[bpu])
                                else:
                                    P.op("dve", "tensor_copy", pu[:, 0:16], puhalo[:, l, g, :], reads=[b_ph[l][g]], writes=[bpu])
                                act(pu[:, 16:528], ps[b][:], AF.Copy, [b_ps[b]], [bpu])
                                P.op("dve", "tensor_copy", puhalo[:, l, g, :], pu[:, 512:528], reads=[bpu], writes=[b_ph[l][g]])
                                si, sh, lo = 0, 1, 1
                                for st in range(g + 1):
                                    di = 1 if si != 1 else 2
                                    lo += sh
                                    sv, sbf = B3[si]
                                    dv, dbf = B3[di]
                                    tt("dve", dv[:, lo:528], sv[:, lo:528], sv[:, lo - sh:528 - sh], ALU.add, [sbf], [dbf])
                                    si, sh = di, sh * 2
                                av, abf = B3[si]
                                fi = 2 if si == 1 else 1
                                fv, fbf = B3[fi]
                                pooled = fv[:, 0:256].bitcast(BF16)
                                stt("dve", pooled, av[:, 16:528], 1.0 / win_w, pu[:, 16:528], ALU.mult, ALU.subtract,
                                    [abf, bpu], [fbf])
                                if half == 0 and t == 0:
                                    nfix = win_w - 1
                                    tt("dve", tmp[0][:, 0:nfix], av[:, 16:16 + nfix], rc[:, 0:nfix], ALU.mult, [abf], [b_tmp[0]])
                                    tt("dve", pooled[:, 0:nfix], tmp[0][:, 0:nfix], pu[:, 16:16 + nfix], ALU.subtract,
                                       [b_tmp[0], bpu], [fbf])
                                b2 = gen_bank()
                                mm(ps[b2][:], poolw[:, g, :], pooled, True, True, [b_poolw, fbf], [b_ps[b2]])
                                act(p2T[:, g, t * 512:(t + 1) * 512], ps[b2][:], AF.Copy, [b_ps[b2]], [b_p2T[g][t]],
                                    scale=vecs[:, l, 44 + g:45 + g])
                        P.mark("C")
                        ETf = RR[:, 37016:37016 + 1024].bitcast(F32)

                        def cv(t, c):
                            if t == 0:
                                return stash[:, c, :], [b_OT[c][0], b_OT[c][1]]
                            if c < 3:
                                return S2[c][:, 0:512], [b_S2[c]]
                            return ETf, [b_ET[0], b_ET[1]]

                        for c in range(4):
                            pk, bpk = load_pack(d_cdiag[l, c], 4096)
                            pk3 = pk[:, :].rearrange("p (j q) -> p j q", j=32)
                            for t in range(2):
                                base = t * 512
                                bk = gen_bank()
                                for j in range(31):
                                    mm(ps[bk][:], pk3[:, j, :], uT[:, c, base + j:base + j + 512], j == 0, j == 30,
                                       [bpk, b_uT[c][0], b_uT[c][1]], [b_ps[bk]])
                                dst, dbufs = cv(t, c)
                                ts("dve", dst, ps[bk][:], vec2[:, l, c:c + 1], None, ALU.add, None, [b_ps[bk]], dbufs)
                        for t in range(2):
                            for c in range(4):
                                q = c % 2
                                sv, sbufs = cv(t, c)
                                act(sq[q][:], sv, AF.Copy, sbufs, [b_sq[q]])
                                mm(ps[6][0:1, :], onesC[:, 0:1], sq[q][:], c == 0, c == 3, [b_sq[q]], [b_ps[6]])
                            for c in range(4):
                                q = c % 2
                                sv, sbufs = cv(t, c)
                                act(sq[q][:], sv, AF.Square, sbufs, [b_sq[q]])
                                mm(ps[7][0:1, :], onesC[:, 0:1], sq[q][:], c == 0, c == 3, [b_sq[q]], [b_ps[7]])
                            P.op("dve", "tensor_copy", rowb[2][:], ps[6][0:1, :], reads=[b_ps[6]], writes=[b_row[2]])
                            tt("dve", rowb[0][:], rowb[2][:], rowb[2][:], ALU.mult, [b_row[2]], [b_row[0]])
                            tt("dve", rowb[0][:], ps[7][0:1, :], rowb[0][:], ALU.subtract, [b_ps[7], b_row[0]], [b_row[0]])
                            ts("dve", rowb[0][:], rowb[0][:], 4.0 * EPS, None, ALU.add, None, [b_row[0]], [b_row[0]])
                            mm(ps[6][:], onesF[:], rowb[2][:], True, True, [b_row[2]], [b_ps[6]])
                            rsqrt_bcast(0, 7)
                            for c in range(4):
                                q = c % 2
                                sv, sbufs = cv(t, c)
                                tt("dve", tmp[q][:], sv, ps[6][:], ALU.subtract, sbufs + [b_ps[6]], [b_tmp[q]])
                                tt("dve", tmp[q][:], tmp[q][:], ps[7][:], ALU.mult, [b_tmp[q], b_ps[7]], [b_tmp[q]])
                                ts("dve", tmp[q][:], tmp[q][:], vec2[:, l, 4 + c:5 + c], vec2[:, l, 8 + c:9 + c], ALU.mult, ALU.add,
                                   [b_tmp[q]], [b_tmp[q]])
                                act(sg[q][:], tmp[q][:], AF.Tanh, [b_tmp[q]], [b_sg[q]])
                                stt("dve", cT[:, c, t * 512:(t + 1) * 512], sg[q][:], 1.0, tmp[q][:], ALU.add, ALU.mult,
                                    [b_sg[q], b_tmp[q]], [b_cT[c][t]])
                        P.mark("D")
                        blocks = []
                        for h in range(4):
                            for qt in range(2):
                                Q0 = tok_base + qt * 512
                                for kb in range(0, Q0 // 128 + 4):
                                    blocks.append((h, qt, Q0, kb))
                        nb = len(blocks)
                        state = {"started": set(), "unit": None, "at_n": 0, "tr_n": 0, "chains": []}
                        pend_tr = {}
                        sched = {}

                        def blk_params(i):
                            h, qt, Q0, kb = blocks[i]
                            k0 = kb * 128
                            c0 = max(0, k0 - Q0)
                            n = 512 - c0
                            d0 = min(Q0 + c0 - k0, 256)
                            if half == 1 and kb >= 8:
                                Ks, Vs, bKs, bVs = curK, curV, b_curK[(kb - 8) // 4], b_curV[kb - 8]
                                kk0, vb = k0 - 1024, kb - 8
                            else:
                                Ks, Vs, bKs, bVs = kvK[l], kvV[l], b_kvK[l][kb // 4], b_kvV[l][kb]
                                kk0, vb = k0, kb
                            return h, qt, Q0, kb, k0, c0, n, d0, Ks, Vs, bKs, bVs, kk0, vb

                        def rec_S(i):
                            h, qt, Q0, kb, k0, c0, n, d0, Ks, Vs, bKs, bVs, kk0, vb = blk_params(i)
                            far = (d0 == 256)
                            for mi in range(2):
                                sbk = (i % 2) * 2 + mi
                                mm(ps[sbk][:, 0:n], Ks[64 * mi:64 * mi + 64, h, kk0:kk0 + 128],
                                   QT[64 * mi:64 * mi + 64, h, qt * 512 + c0:qt * 512 + 512],
                                   True, far, [bKs, b_QT[h][qt]], [b_ps[sbk]])
                            for mi in range(2):
                                sbk = (i % 2) * 2 + mi
                                if not far:
                                    mm(ps[sbk][:, 0:n], ident[:], biasT[:, h, d0:d0 + n], False, True, [b_const], [b_ps[sbk]])
                            for mi in range(2):
                                sbk = (i % 2) * 2 + mi
                                if far:
                                    act(ET[:, sbk, 0:n], ps[sbk][:, 0:n], AF.Exp, [b_ps[sbk]], [b_ET[sbk]], bias=cbias[:, h:h + 1])
                                else:
                                    act(ET[:, sbk, 0:n], ps[sbk][:, 0:n], AF.Exp, [b_ps[sbk]], [b_ET[sbk]])

                        def rec_AV(i):
                            h, qt, Q0, kb, k0, c0, n, d0, Ks, Vs, bKs, bVs, kk0, vb = blk_params(i)
                            if state["unit"] != (h, qt):
                                state["unit"] = (h, qt)
                                state["started"] = set()
                            started = state["started"]
                            for mi in range(2):
                                sbk = (i % 2) * 2 + mi
                                for j in range(c0 // 128, 4):
                                    a = j * 2 + mi
                                    bank, off = 4 + a // 3, (a % 3) * 160
                                    st1 = bank not in started
                                    started.add(bank)
                                    P.op("pe", "matmul", ps[bank][:, off:off + 129],
                                         ET[:, sbk, j * 128 - c0:j * 128 - c0 + 128], Vs[:, vb, h, :],
                                         start=st1, stop=(k0 == Q0 + j * 128), skip_group_check=True,
                                         reads=[b_ET[sbk], bVs], writes=[b_ps[bank]])
                            if k0 >= Q0:
                                j = (k0 - Q0) // 128
                                z = j % 2
                                state["chains"].append((norm_chain(j, z, h, qt, i), i))
                                if j % 2 == 1:
                                    (ca_, ia), (cb_, ib) = state["chains"]
                                    state["chains"] = []
                                    for sa, sb_ in zip(ca_[:6], cb_[:6]):
                                        sa()
                                        sb_()
                                    it = state["iter"]

                                    def second(ca_=ca_, cb_=cb_):
                                        for sa, sb_ in zip(ca_[6:], cb_[6:]):
                                            sa()
                                            sb_()
                                    sched.setdefault(it + 1, []).append(second)
                                    sched.setdefault(it + 3, []).append(lambda ia=ia, ib=ib: (rec_TR(ia), rec_TR(ib)))

                        def norm_chain(j, z, h, qt, i):
                            a1, a2 = j * 2, j * 2 + 1
                            O1 = ps[4 + a1 // 3][:, (a1 % 3) * 160:(a1 % 3) * 160 + 129]
                            O2 = ps[4 + a2 // 3][:, (a2 % 3) * 160:(a2 % 3) * 160 + 129]
                            bO1, bO2 = b_ps[4 + a1 // 3], b_ps[4 + a2 // 3]
                            st_, bst = att_s[z], b_atts[z]
                            fa, bfa = att_f[2 * z], b_attf[2 * z]
                            fo, bfo = att_f[2 * z + 1], b_attf[2 * z + 1]
                            on, bon = att_b[z], b_attb[z]

                            def fin():
                                stt("dve", on[:], fo[:], st_[:, 5:6], subln[:, l, :], ALU.mult, ALU.mult, [bst, bfo], [bon])
                                pend_tr[i] = (on, bon, h, qt * 512 + j * 128, qt)
                            return [
                                lambda: P.op("dve", "reciprocal", st_[:, 0:1], O1[:, 128:129], reads=[bO1], writes=[bst]),
                                lambda: P.op("dve", "reciprocal", st_[:, 1:2], O2[:, 128:129], reads=[bO2], writes=[bst]),
                                lambda: P.op("dve", "memset", st_[:, 3:4], 0.0, writes=[bst]),
                                lambda: tt("dve", st_[:, 2:3], st_[:, 1:2], nlam[:, l:l + 1], ALU.mult, [bst], [bst]),
                                lambda: ts("dve", fa[:], O1[:, 0:128], st_[:, 0:1], None, ALU.mult, None, [bst, bO1], [bfa]),
                                lambda: stt("dve", fo[:], O2[:, 0:128], st_[:, 2:3], fa[:], ALU.mult, ALU.add, [bst, bO2, bfa], [bfo]),
                                lambda: act(fa[:], fo[:], AF.Square, [bfo, bst], [bfa, bst], accum_out=st_[:, 3:4]),
                                lambda: ts("dve", st_[:, 4:5], st_[:, 3:4], 1.0 / 128.0, EPS, ALU.mult, ALU.add, [bst], [bst]),
                                lambda: tt("pool", st_[:, 5:6], st_[:, 4:5], nhalf[:, 0:1], ALU.pow, [bst], [bst]),
                                fin,
                            ]

                        def rec_TR(i):
                            if i not in pend_tr:
                                return
                            on, bon, h, qa, qt = pend_tr.pop(i)
                            tsl = state["tr_n"] % 4
                            state["tr_n"] += 1
                            mm(ps[7][:, tsl * 128:(tsl + 1) * 128], on[:], ident[:], True, True, [bon], [b_ps[7]])
                            act(OT[:, h, qa:qa + 128], ps[7][:, tsl * 128:(tsl + 1) * 128], AF.Copy, [b_ps[7]], [b_OT[h][qt]])

                        for i in range(nb + 5):
                            state["iter"] = i
                            if i < nb:
                                rec_S(i)
                            for fn_ in sched.pop(i, []):
                                fn_()
                            if 0 <= i - 1 < nb:
                                rec_AV(i - 1)
                        assert not sched and not pend_tr
                        P.barrier()
                        P.mark("E")
                        srcs = [(cT, b_cT), (OT, b_OT), (p2T, b_p2T)]
                        for m in range(8):
                            wg, bwg = load_pack(d_mrgg[l, m], 3072)
                            wbo, bwbo = load_pack(d_mrgb[l, m], 1536)
                            wg4 = wg[:, 0:3072].rearrange("p (b k j) -> p b k j", b=3, k=8)
                            wbo4 = wbo[:, 0:1536].rearrange("p (b k j) -> p b k j", b=3, k=4)
                            for t in range(2):
                                for br in range(3):
                                    bg, by = gen_bank(), gen_bank()
                                    for k in range(8):
                                        mm(ps[bg][:], wg4[:, br, k, :], hT[:, k, t * 512:(t + 1) * 512], k == 0, k == 7,
                                           [bwg, b_hT[k][t]], [b_ps[bg]])
                                    sT, bsT = srcs[br]
                                    for k in range(4):
                                        mm(ps[by][:], wbo4[:, br, k, :], sT[:, k, t * 512:(t + 1) * 512], k == 0, k == 3,
                                           [bwbo, bsT[k][t]], [b_ps[by]])
                                    q = br % 2
                                    act(sg[q][:], ps[bg][:], AF.Tanh, [b_ps[bg]], [b_sg[q]], scale=0.5)
                                    if br == 0:
                                        stt("dve", tmp[0][:], sg[q][:], 1.0, ps[by][:], ALU.add, ALU.mult, [b_sg[q], b_ps[by]], [b_tmp[0]])
                                    else:
                                        stt("dve", tmp[1][:], sg[q][:], 1.0, ps[by][:], ALU.add, ALU.mult, [b_sg[q], b_ps[by]], [b_tmp[1]])
                                        if br == 1:
                                            tt("dve", tmp[0][:], tmp[0][:], tmp[1][:], ALU.add, [b_tmp[1]], [b_tmp[0]])
                                        else:
                                            tt("dve", mergedT[:, m, t * 512:(t + 1) * 512], tmp[0][:], tmp[1][:], ALU.add,
                                               [b_tmp[0], b_tmp[1]], [b_mg[m][t]])
                        for t in range(2):
                            for cg in range(2):
                                w_, bw_ = load_pack(d_wout[l, cg], 4096)
                                w3 = w_[:, :].rearrange("p (k c) -> p k c", k=8)
                                for m4 in range(4):
                                    m = cg * 4 + m4
                                    b = gen_bank()
                                    for k in range(8):
                                        mm(ps[b][:], w3[:, k, m4 * 128:(m4 + 1) * 128], mergedT[:, k, t * 512:(t + 1) * 512],
                                           k == 0, k == 7, [bw_, b_mg[k][t]], [b_ps[b]])
                                    evac(yevE[:, m, :], ps[b][:], [b_ps[b]], [b_yev[m]])
                            post_norm_residual(yevE, t, 8, l, 4.0 * EPS, t)
                        P.barrier()
                        P.mark("F")
                        pre_norm(h2T, b_h2, 0, 16, l, 0, 0)
                        for t in range(2):
                            hcur, bh = (h2T, b_h2) if t == 0 else (h2Tb, b_h2b)
                            for fg in range(8):
                                w_, bw_ = load_pack(d_w1[l, fg], 4096)
                                w3 = w_[:, :].rearrange("p (k c) -> p k c", k=8)
                                for i in range(4):
                                    jf = fg * 4 + i
                                    b = gen_bank()
                                    for k in range(8):
                                        mm(ps[b][:], w3[:, k, i * 128:(i + 1) * 128], hcur[:, k, :], k == 0, k == 7,
                                           [bw_, bh[k]], [b_ps[b]])
                                    q = jf % 2
                                    act(relu_s[:, q, :], ps[b][:], AF.Relu, [b_ps[b]], [b_relu[q]])
                                    tt("dve", f1[:, jf, :], relu_s[:, q, :], relu_s[:, q, :], ALU.mult, [b_relu[q]], [b_f1[jf]])
                            if t == 0:
                                pre_norm(h2Tb, b_h2b, 1, 16, l, 1, 0)
                            for jg in range(8):
                                w_, bw_ = load_pack(d_w2[l, jg], 4096)
                                w3 = w_[:, :].rearrange("p (j c) -> p j c", j=4)
                                for jj in range(4):
                                    jf = jg * 4 + jj
                                    for m in range(8):
                                        mm(ps[m][:], w3[:, jj, m * 128:(m + 1) * 128], f1[:, jf, :], jf == 0, jf == 31,
                                           [bw_, b_f1[jf]], [b_ps[m]])
                            for m in range(8):
                                evac(yevF[:, m, :], ps[m][:], [b_ps[m]], [b_yev[m]])
                            post_norm_residual(yevF, t, 24, l, EPS, t)
                        P.barrier()
                        P.mark("G")
                        P.dma("pool", "pin", pTs, d_pT[l, sq_i, :, tok_base:tok_base + 1024].rearrange("(k p) t -> p k t", p=128),
                              writes=[b_pTs])
                        for k in range(8):
                            for t in range(2):
                                if (k + t) % 2:
                                    act(xbT[:, k, t * 512:(t + 1) * 512], xT[:, k, t * 512:(t + 1) * 512], AF.Copy,
                                        [b_x[k][t]], [b_xb[k][t]])
                                else:
                                    P.op("dve", "tensor_copy", xbT[:, k, t * 512:(t + 1) * 512], xT[:, k, t * 512:(t + 1) * 512],
                                         reads=[b_x[k][t]], writes=[b_xb[k][t]])
                        for cg in range(2):
                            w_, bw_ = load_pack(d_wpg[l, cg], 4096)
                            w3 = w_[:, :].rearrange("p (k c) -> p k c", k=8)
                            if cg == 0:
                                wpe_, bwpe = load_pack(d_wpe[l], 2048)
                                wpe3 = wpe_[:, 0:2048].rearrange("p (k c) -> p k c", k=2)
                            for m4 in range(4):
                                m = cg * 4 + m4
                                for t in range(2):
                                    bg, bp = gen_bank(), gen_bank()
                                    for k in range(8):
                                        mm(ps[bg][:], w3[:, k, m4 * 128:(m4 + 1) * 128], xbT[:, k, t * 512:(t + 1) * 512],
                                           k == 0, k == 7, [bw_, b_xb[k][t]], [b_ps[bg]])
                                    for k in range(2):
                                        mm(ps[bp][:], wpe3[:, k, m * 128:(m + 1) * 128], pTs[:, k, t * 512:(t + 1) * 512],
                                           k == 0, k == 1, [bwpe, b_pTs], [b_ps[bp]])
                                    q = t % 2
                                    act(sg[q][:], ps[bg][:], AF.Tanh, [b_ps[bg]], [b_sg[q]], scale=0.5)
                                    act(relu_s[:, q, :], ps[bp][:], AF.Copy, [b_ps[bp]], [b_relu[q]], scale=0.5)
                                    stt("dve", tmp[q][:], sg[q][:], 1.0, relu_s[:, q, :], ALU.add, ALU.mult, [b_sg[q], b_relu[q]], [b_tmp[q]])
                                    xs = xT[:, m, t * 512:(t + 1) * 512]
                                    tt("dve", xs, xs, tmp[q][:], ALU.add, [b_tmp[q]], [b_x[m][t]])
                        P.barrier()
                    for k in range(8):
                        P.dma("sp", "xout", d_out[sq_i, k * 128:(k + 1) * 128, tok_base:tok_base + 1024], xT[:, k, :],
                              reads=[b_x[k][0], b_x[k][1]])
                    P.barrier()
        P.dry = True
        walk()
        P.dry = False
        walk()
        counts = P.emit()
        counts["marks"] = P.marks
    return nc, counts


def _rel_bucket(n):
    n = np.maximum(n, 0)
    nf = np.maximum(n, 1).astype(np.float32)
    large = 16 + (np.log(nf / np.float32(16)) / np.float32(math.log(128 / 16)) * np.float32(16)).astype(np.int32)
    large = np.minimum(large, 31)
    return np.where(n < 16, n, large)


def host_layout(inp):
    f = np.float32
    g = {k: np.asarray(v, dtype=f) for k, v in inp.items()}
    L = DEPTH
    sh = {}
    w_in = g["w_in"]
    sh["win"] = np.ascontiguousarray(
        w_in[:, :, :3072].reshape(L, 8, 128, 6, 512).transpose(0, 3, 2, 1, 4)).reshape(L, 6, 128, 4096)
    gates = w_in[:, :, 3072:].reshape(L, 8, 128, 3, 8, 128)
    sh["mrgg"] = np.ascontiguousarray(gates.transpose(0, 4, 2, 3, 1, 5)).reshape(L, 8, 128, 3072)
    bo = np.stack([g["w_conv_out"], g["w_attn_out"], g["w_pool_out"]], axis=1)
    bo = bo.reshape(L, 3, 4, 128, 8, 128)
    sh["mrgb"] = np.ascontiguousarray(bo.transpose(0, 4, 3, 1, 2, 5)).reshape(L, 8, 128, 1536)

    def colgroups(w, ng):
        return np.ascontiguousarray(w.reshape(L, 8, 128, ng, 512).transpose(0, 3, 2, 1, 4)).reshape(L, ng, 128, 4096)

    sh["wout"] = colgroups(g["w_out"], 2)
    sh["w1"] = colgroups(g["w_mlp_in"], 8)
    sh["wpg"] = colgroups(g["w_ple_gate"], 2)
    sh["w2"] = np.ascontiguousarray(g["w_mlp_out"].reshape(L, 8, 4, 128, 1024).transpose(0, 1, 3, 2, 4)).reshape(L, 8, 128, 4096)
    sh["wpe"] = np.ascontiguousarray(g["w_ple_proj"].reshape(L, 2, 128, 1024).transpose(0, 2, 1, 3)).reshape(L, 128, 2048)
    sh["poolw"] = np.ascontiguousarray(g["pool_w"].transpose(0, 2, 1, 3)).reshape(L, 128, 512)
    vecs = np.zeros((L, 128, 48), f)

    def cols(v, n):
        return v.reshape(L, n, 128).transpose(0, 2, 1)
    vecs[:, :, 0:8] = cols(g["g_pre_mix"], 8)
    vecs[:, :, 8:16] = cols(g["g_post_mix"], 8)
    vecs[:, :, 16:24] = cols(g["g_pre_mlp"], 8)
    vecs[:, :, 24:32] = cols(g["g_post_mlp"], 8)
    vecs[:, :, 32:36] = cols(g["conv_dw_b"], 4)
    vecs[:, :, 36:40] = cols(g["conv_ln_g"], 4)
    vecs[:, :, 40:44] = cols(g["conv_ln_b"], 4)
    vecs[:, :, 44:48] = cols(g["pool_scale"], 4)
    w4 = g["conv_dw_w"][:, :, 0, :].reshape(L, 31, 4, 128).transpose(0, 2, 3, 1)
    cd = np.zeros((L, 4, 128, 32, 128), f)
    ar = np.arange(128)
    cd[:, :, ar, :31, ar] = w4.transpose(2, 0, 1, 3)
    sh["cdiag"] = cd.reshape(L, 4, 128, 4096)
    sh["vecs"] = vecs
    sh["sublnb"] = np.ascontiguousarray(np.broadcast_to(g["subln_g"][:, None, :], (L, 128, 128)))
    sh["lamb"] = np.ascontiguousarray(np.broadcast_to(g["lam_p"].reshape(L, 1, 256), (L, 128, 256)))
    kk = np.arange(128)[:, None]
    dd = np.arange(768)[None, :]
    nrel = dd - kk
    bidx = _rel_bucket(nrel)
    tab = g["rel_bias"][bidx]
    tab = np.where((nrel >= 0)[:, :, None], tab, f(-1e30))
    sh["biasT"] = np.ascontiguousarray(tab.transpose(0, 2, 1)).reshape(128, 4 * 768).astype(f)
    sh["ident"] = np.eye(128, dtype=f)
    sh["cbias"] = np.ascontiguousarray(np.broadcast_to(g["rel_bias"][31][None, :], (128, 4)))
    rc = np.zeros((128, 16), f)
    rc[:, :] = 1.0 / (np.arange(16, dtype=f) + 1.0)
    sh["rc"] = rc
    x = g["x"]
    p = g["p"]
    per_core = []
    for c in range(8):
        d = dict(sh)
        d["xT"] = np.ascontiguousarray(x[2 * c:2 * c + 2].transpose(0, 2, 1))
        d["pT"] = np.ascontiguousarray(p[:, 2 * c:2 * c + 2].transpose(0, 1, 3, 2))
        per_core.append(d)
    return per_core


_CACHE = {}


def kernel(**inputs):
    if "nc" not in _CACHE:
        _CACHE["nc"] = build_program()[0]
    nc = _CACHE["nc"]
    in_maps = host_layout(inputs)
    res = run_bass_kernel_spmd(nc, in_maps, core_ids=list(range(8)))
    out = np.empty((16, S, D), np.float32)
    for c in range(8):
        out[2 * c:2 * c + 2] = res.results[c]["outT"].transpose(0, 2, 1)
    return out
```
